# Optimizing a Trainium2 kernel written in Bass

```python
import math
import jax
import jax.numpy as jnp
from jax import lax
import numpy as np

D_MODEL = 1024
BATCH = 8
SEQ = 2048
DEPTH = 2

MEM_LEN = 256
N_EVEN = (DEPTH + 1) // 2
N_ODD = DEPTH // 2
ALPHA = (2.0 * DEPTH) ** 0.25
BETA = (8.0 * DEPTH) ** -0.25
LN_EPS = 1e-5

GMLP_WIDTH = D_MODEL // 2
GMLP_GROUPS = 4
GMLP_GDIM = GMLP_WIDTH // GMLP_GROUPS
GMLP_CHUNK = 128
HGRN_WIDTH = D_MODEL // 2
HGRN_HEADS = 4
HGRN_DK = HGRN_WIDTH // HGRN_HEADS
HGRN_CHUNK = 64
EVEN_IN_WIDTH = 2 * GMLP_WIDTH + 4 * HGRN_WIDTH

MOBA_HEADS = 16
MOBA_HDIM = D_MODEL // MOBA_HEADS
MOBA_BLOCK = 256
MOBA_TOPK = 3
MOBA_QCHUNK = 16

MEM_HEADS = 4
MEM_HDIM = D_MODEL // MEM_HEADS

D_FF = int(math.ceil(8 * D_MODEL / 3 / 256)) * 256

kernel_name = 'hybrid_gmlp_hgrn2_moba_deepnorm'


def layer_norm(x, g, b):
    xf = x.astype(jnp.float32)
    mu = jnp.mean(xf, axis=-1, keepdims=True)
    var = jnp.mean(jnp.square(xf - mu), axis=-1, keepdims=True)
    return ((xf - mu) * lax.rsqrt(var + LN_EPS)).astype(x.dtype) * g + b


def rms_norm(x, g):
    xf = x.astype(jnp.float32)
    return (xf * lax.rsqrt(jnp.mean(jnp.square(xf), axis=-1, keepdims=True) + LN_EPS)).astype(x.dtype) * g


def alibi_slopes(n_heads):
    return jnp.asarray(2.0 ** (-8.0 * np.arange(1, n_heads + 1) / n_heads), dtype=jnp.float32)


def spatial_gating_unit(u, v, ws, bs, ln_g, ln_b):
    bsz, t_len, _ = u.shape
    v = layer_norm(v.reshape(bsz, t_len, GMLP_GROUPS, GMLP_GDIM),
                   ln_g.reshape(GMLP_GROUPS, GMLP_GDIM), ln_b.reshape(GMLP_GROUPS, GMLP_GDIM))
    v = v.reshape(bsz, t_len // GMLP_CHUNK, GMLP_CHUNK, GMLP_GROUPS, GMLP_GDIM)
    causal = jnp.tril(jnp.ones((GMLP_CHUNK, GMLP_CHUNK), dtype=bool))
    w = jnp.where(causal[None], ws, jnp.zeros_like(ws))
    s = jnp.einsum('gts,bnsgc->bntgc', w, v) + bs.T[:, :, None]
    return u * s.reshape(bsz, t_len, GMLP_WIDTH)


def hgrn2_recurrence(q, f_logit, i, lb):
    bsz, t_len, _ = q.shape
    n_chunks = t_len // HGRN_CHUNK

    def heads(z):
        return z.astype(jnp.float32).reshape(bsz, n_chunks, HGRN_CHUNK, HGRN_HEADS, HGRN_DK).transpose(0, 3, 1, 2, 4)

    f = lb + (1.0 - lb) * jax.nn.sigmoid(f_logit.astype(jnp.float32))
    q_c, k_c, v_c = heads(q), heads(1.0 - f), heads(i)
    cum = jnp.cumsum(heads(jnp.log(f)), axis=3)
    q_dec = q_c * jnp.exp(cum)
    k_dec = k_c * jnp.exp(-cum)
    k_tail = k_c * jnp.exp(cum[..., -1:, :] - cum)
    causal = jnp.tril(jnp.ones((HGRN_CHUNK, HGRN_CHUNK), dtype=bool))
    attn = jnp.where(causal, jnp.einsum('bhntk,bhnsk->bhnts', q_dec, k_dec), 0.0)
    o_intra = jnp.einsum('bhnts,bhnsv->bhntv', attn, v_c)
    chunk_decay = jnp.exp(cum[..., -1, :])
    chunk_update = jnp.einsum('bhnsk,bhnsv->bhnkv', k_tail, v_c)

    def step(state, inp):
        dec, upd = inp
        return dec[..., None] * state + upd, state

    s0 = jnp.zeros((bsz, HGRN_HEADS, HGRN_DK, HGRN_DK), jnp.float32)
    _, s_in = lax.scan(step, s0, (jnp.moveaxis(chunk_decay, 2, 0), jnp.moveaxis(chunk_update, 2, 0)))
    s_in = jnp.moveaxis(s_in, 0, 2)
    o = o_intra + jnp.einsum('bhntk,bhnkv->bhntv', q_dec, s_in)
    return o.transpose(0, 2, 3, 1, 4).reshape(bsz, t_len, HGRN_HEADS, HGRN_DK)


def even_mixer(x, w_in, w_out, a_ws, a_bs, a_ln_g, a_ln_b, b_norm_g, lb):
    bsz, t_len, _ = x.shape
    h = x @ w_in
    cuts = [GMLP_WIDTH, 2 * GMLP_WIDTH, 2 * GMLP_WIDTH + HGRN_WIDTH,
            2 * GMLP_WIDTH + 2 * HGRN_WIDTH, 2 * GMLP_WIDTH + 3 * HGRN_WIDTH]
    a_u, a_v, b_q, b_f, b_i, b_g = jnp.split(h, cuts, axis=-1)
    y_a = spatial_gating_unit(jax.nn.gelu(a_u), jax.nn.gelu(a_v), a_ws, a_bs, a_ln_g, a_ln_b)
    o = hgrn2_recurrence(b_q, b_f, jax.nn.silu(b_i), lb)
    gate = jax.nn.silu(b_g.astype(jnp.float32)).reshape(bsz, t_len, HGRN_HEADS, HGRN_DK)
    y_b = (rms_norm(o, b_norm_g.reshape(HGRN_HEADS, HGRN_DK).astype(jnp.float32)) * gate).astype(x.dtype)
    y = jnp.concatenate([y_a, y_b.reshape(bsz, t_len, HGRN_WIDTH)], axis=-1)
    return y @ w_out


def moba_attention(x, w_qkv, w_out):
    bsz, t_len, _ = x.shape
    n_blocks = -(-t_len // MOBA_BLOCK)
    pad = n_blocks * MOBA_BLOCK - t_len
    top_k = min(MOBA_TOPK, max(n_blocks - 1, 1))
    scale = MOBA_HDIM ** -0.5
    slopes = alibi_slopes(MOBA_HEADS)[None, :, None]
    qkv = (x @ w_qkv).reshape(bsz, t_len, 3, MOBA_HEADS, MOBA_HDIM).transpose(2, 0, 3, 1, 4)
    q, k, v = qkv[0], qkv[1], qkv[2]
    pad_cfg = ((0, 0), (0, 0), (0, pad), (0, 0))
    k_blocks = jnp.pad(k, pad_cfg).reshape(bsz, MOBA_HEADS, n_blocks, MOBA_BLOCK, MOBA_HDIM)
    v_blocks = jnp.pad(v, pad_cfg).reshape(bsz, MOBA_HEADS, n_blocks, MOBA_BLOCK, MOBA_HDIM)
    k_mean = jnp.mean(k_blocks.astype(jnp.float32), axis=3)
    q_block = jnp.arange(t_len) // MOBA_BLOCK
    fully_past = jnp.arange(n_blocks)[None, :] < q_block[:, None]
    affinity = jnp.einsum('bhtd,bhnd->bhtn', q.astype(jnp.float32), k_mean)
    affinity = jnp.where(fully_past, affinity, -jnp.inf)
    _, sel = lax.top_k(affinity, top_k)
    sel_valid = sel < q_block[:, None]
    b_idx = jnp.arange(bsz)[:, None, None, None]
    h_idx = jnp.arange(MOBA_HEADS)[None, :, None, None]
    key_offsets = jnp.arange(MOBA_BLOCK)

    def query_chunk(c):
        t0 = c * MOBA_QCHUNK
        q_c = lax.dynamic_slice_in_dim(q, t0, MOBA_QCHUNK, axis=2)
        sel_c = lax.dynamic_slice_in_dim(sel, t0, MOBA_QCHUNK, axis=2)
        valid_c = lax.dynamic_slice_in_dim(sel_valid, t0, MOBA_QCHUNK, axis=2)
        own = t0 // MOBA_BLOCK
        tq = t0 + jnp.arange(MOBA_QCHUNK)
        k_sel = k_blocks[b_idx, h_idx, sel_c]
        v_sel = v_blocks[b_idx, h_idx, sel_c]
        k_own = lax.dynamic_index_in_dim(k_blocks, own, axis=2, keepdims=False)
        v_own = lax.dynamic_index_in_dim(v_blocks, own, axis=2, keepdims=False)
        dist_sel = (tq[:, None, None] - (sel_c[..., None] * MOBA_BLOCK + key_offsets)).astype(jnp.float32)
        s_sel = (jnp.einsum('bhqd,bhqjsd->bhqjs', q_c, k_sel).astype(jnp.float32) * scale
                 - slopes[..., None, None] * dist_sel)
        s_sel = jnp.where(valid_c[..., None], s_sel, -jnp.inf).reshape(bsz, MOBA_HEADS, MOBA_QCHUNK, top_k * MOBA_BLOCK)
        dist_own = (tq[:, None] - (own * MOBA_BLOCK + key_offsets)[None, :]).astype(jnp.float32)
        s_own = (jnp.einsum('bhqd,bhsd->bhqs', q_c, k_own).astype(jnp.float32) * scale
                 - slopes[..., None] * dist_own)
        s_own = jnp.where(dist_own >= 0, s_own, -jnp.inf)
        p = jax.nn.softmax(jnp.concatenate([s_sel, s_own], axis=-1), axis=-1).astype(v.dtype)
        p_sel = p[..., :top_k * MOBA_BLOCK].reshape(bsz, MOBA_HEADS, MOBA_QCHUNK, top_k, MOBA_BLOCK)
        p_own = p[..., top_k * MOBA_BLOCK:]
        return (jnp.einsum('bhqjs,bhqjsd->bhqd', p_sel, v_sel)
                + jnp.einsum('bhqs,bhsd->bhqd', p_own, v_own))

    out = lax.map(query_chunk, jnp.arange(t_len // MOBA_QCHUNK))
    out = out.transpose(1, 0, 3, 2, 4).reshape(bsz, t_len, D_MODEL)
    return out @ w_out


def memory_cross_attention(x, mem, w_q, w_kv, w_o):
    bsz, t_len, _ = x.shape
    q = (x @ w_q).reshape(bsz, t_len, MEM_HEADS, MEM_HDIM)
    kv = (mem @ w_kv).reshape(bsz, mem.shape[1], 2, MEM_HEADS, MEM_HDIM)
    s = jnp.einsum('bthd,bmhd->bhtm', q, kv[:, :, 0]).astype(jnp.float32) * (MEM_HDIM ** -0.5)
    p = jax.nn.softmax(s, axis=-1).astype(x.dtype)
    o = jnp.einsum('bhtm,bmhd->bthd', p, kv[:, :, 1]).reshape(bsz, t_len, D_MODEL)
    return o @ w_o


def swiglu_ffn(x, w_in, w_out):
    gate, up = jnp.split(x @ w_in, 2, axis=-1)
    return (jax.nn.silu(gate) * up) @ w_out


def setup_inputs(seed: int = 0) -> dict:
    key = jax.random.key(seed)
    ks = jax.random.split(key, 24)

    def nrm(k, shape, std):
        return jax.random.normal(k, shape, jnp.float32) * std

    d_inv = D_MODEL ** -0.5
    mix_w = GMLP_WIDTH + HGRN_WIDTH
    return {
        'x': nrm(ks[0], (BATCH, SEQ, D_MODEL), 1.0),
        'mem': nrm(ks[1], (BATCH, MEM_LEN, D_MODEL), 1.0),
        'ln_g': 1.0 + nrm(ks[2], (DEPTH, 3, D_MODEL), 0.02),
        'ln_b': nrm(ks[3], (DEPTH, 3, D_MODEL), 0.02),
        'x_wq': nrm(ks[4], (DEPTH, D_MODEL, D_MODEL), d_inv),
        'x_wkv': jnp.concatenate([nrm(ks[5], (DEPTH, D_MODEL, D_MODEL), d_inv),
                                  nrm(ks[6], (DEPTH, D_MODEL, D_MODEL), d_inv * BETA)], axis=-1),
        'x_wo': nrm(ks[7], (DEPTH, D_MODEL, D_MODEL), d_inv * BETA),
        'ffn_w_in': nrm(ks[8], (DEPTH, D_MODEL, 2 * D_FF), d_inv),
        'ffn_w_out': nrm(ks[9], (DEPTH, D_FF, D_MODEL), D_FF ** -0.5 * BETA),
        'ev_w_in': nrm(ks[10], (N_EVEN, D_MODEL, EVEN_IN_WIDTH), d_inv),
        'ev_w_out': nrm(ks[11], (N_EVEN, mix_w, D_MODEL), mix_w ** -0.5 * BETA),
        'a_ws': nrm(ks[12], (N_EVEN, GMLP_GROUPS, GMLP_CHUNK, GMLP_CHUNK), GMLP_CHUNK ** -0.5),
        'a_bs': 1.0 + nrm(ks[13], (N_EVEN, GMLP_GROUPS, GMLP_CHUNK), 0.1),
        'a_ln_g': 1.0 + nrm(ks[14], (N_EVEN, GMLP_WIDTH), 0.02),
        'a_ln_b': nrm(ks[15], (N_EVEN, GMLP_WIDTH), 0.02),
        'b_norm_g': 1.0 + nrm(ks[16], (N_EVEN, HGRN_WIDTH), 0.02),
        'hgrn_lb_logits': nrm(ks[17], (N_EVEN + 1, HGRN_WIDTH), 0.1),
        'od_w_qkv': jnp.concatenate([nrm(ks[18], (N_ODD, D_MODEL, 2 * D_MODEL), d_inv),
                                     nrm(ks[19], (N_ODD, D_MODEL, D_MODEL), d_inv * BETA)], axis=-1),
        'od_w_out': nrm(ks[20], (N_ODD, D_MODEL, D_MODEL), d_inv * BETA),
    }


def reference(x, mem, ln_g, ln_b, x_wq, x_wkv, x_wo, ffn_w_in, ffn_w_out, ev_w_in, ev_w_out,
              a_ws, a_bs, a_ln_g, a_ln_b, b_norm_g, hgrn_lb_logits, od_w_qkv, od_w_out):
    lb_all = jnp.cumsum(jax.nn.softmax(hgrn_lb_logits.astype(jnp.float32), axis=0), axis=0)
    for l in range(DEPTH):
        j = l // 2
        if l % 2 == 0:
            y = even_mixer(x, ev_w_in[j], ev_w_out[j], a_ws[j], a_bs[j], a_ln_g[j], a_ln_b[j],
                           b_norm_g[j], lb_all[j])
        else:
            y = moba_attention(x, od_w_qkv[j], od_w_out[j])
        x = layer_norm(ALPHA * x + y, ln_g[l, 0], ln_b[l, 0])
        x = layer_norm(ALPHA * x + memory_cross_attention(x, mem, x_wq[l], x_wkv[l], x_wo[l]),
                       ln_g[l, 1], ln_b[l, 1])
        x = layer_norm(ALPHA * x + swiglu_ffn(x, ffn_w_in[l], ffn_w_out[l]), ln_g[l, 2], ln_b[l, 2])
    return x
```

```python
import math
from contextlib import ExitStack

import numpy as np
import concourse.bass as bass
import concourse.mybir as mybir
from concourse.bass_utils import run_bass_kernel_spmd

F32 = mybir.dt.float32
BF16 = mybir.dt.bfloat16
AF = mybir.ActivationFunctionType
ALU = mybir.AluOpType
AX = mybir.AxisListType

T = 2048
D = 1024
NT = T // 128
KC = D // 128
MEM = 256
DFF = 2816
NJ = DFF // 128
ALPHA = 4.0 ** 0.25
EPS = 1e-5
NCORES = 8


class Tok:
    __slots__ = ("name", "w", "r", "rd")

    def __init__(self, name=""):
        self.name = name
        self.w = None
        self.r = {}
        self.rd = []


class Op:
    __slots__ = ("eng", "fn", "deps", "sig", "is_dma", "need", "idx", "guard")

    def __init__(self, eng, fn, is_dma):
        self.eng = eng
        self.fn = fn
        self.deps = []
        self.sig = None
        self.is_dma = is_dma
        self.need = False
        self.guard = None


import os
SERIAL = bool(os.environ.get('BASS_SERIAL'))


class Prog:
    ENGS = ("pe", "act", "dve", "pool", "sp")

    def __init__(self):
        self.ops = {e: [] for e in self.ENGS}
        self.fence_deps = []
        self.last = None
        self.fenced = set(self.ENGS)
        self.dma_ops = {"sp": [], "pool": [], "act": []}

    def fence(self):
        self.fence_deps = [lst[-1] for lst in self.ops.values() if lst]
        self.fenced = set()

    def op(self, eng, fn, reads=(), writes=(), dma=False):
        o = Op(eng, fn, dma)
        deps = []
        if eng not in self.fenced:
            deps.extend(self.fence_deps)
            self.fenced.add(eng)
        if SERIAL and self.last is not None:
            deps.append(self.last)
        self.last = o
        for t in reads:
            if t.w is not None:
                deps.append(t.w)
        for t in writes:
            if t.w is not None:
                deps.append(t.w)
            deps.extend(t.r.values())
            deps.extend(t.rd)
        seen = set()
        for d in deps:
            if id(d) in seen or d is o:
                continue
            seen.add(id(d))
            if (not d.is_dma) and (not dma) and d.eng == eng and eng == "pe":
                continue
            d.need = True
            o.deps.append(d)
        for t in reads:
            if dma:
                t.rd.append(o)
            else:
                t.r[eng] = o
        for t in writes:
            t.w = o
            t.r = {}
            t.rd = []
        self.ops[eng].append(o)
        if dma:
            self.dma_ops[eng].append(o)
        return o

    def emit(self, nc, engines, sems, dma_sems):
        NDS = {q: len(dma_sems[q]) for q in dma_sems}
        for e in self.ENGS:
            cnt = 0
            for o in self.ops[e]:
                if o.is_dma:
                    continue
                if o.need:
                    cnt += 1
                    o.sig = (sems[e], cnt)
        for q, lst in self.dma_ops.items():
            for k, o in enumerate(lst):
                s = dma_sems[q][k % NDS[q]]
                gen = k // NDS[q]
                o.sig = (s, 16 * (gen + 1))
                o.guard = (s, 16 * gen) if gen > 0 else None

        def run(e, eng):
            waited = {}
            for o in self.ops[e]:
                need = {}
                if o.guard is not None:
                    need[o.guard[0]] = o.guard[1]
                for d in o.deps:
                    s, v = d.sig
                    if need.get(s, 0) < v:
                        need[s] = v
                for s, v in need.items():
                    if waited.get(s, 0) >= v:
                        continue
                    eng.wait_ge(s, v)
                    waited[s] = v
                ins = o.fn(eng)
                if o.is_dma:
                    ins.then_inc(o.sig[0], 16)
                elif o.sig is not None:
                    ins.then_inc(o.sig[0], 1)
            return waited

        with nc.Block() as block:
            @block.tensor
            def _(pe):
                run("pe", pe)

            @block.scalar
            def _(act):
                run("act", act)

            @block.vector
            def _(dve):
                run("dve", dve)

            @block.gpsimd
            def _(pool):
                run("pool", pool)

            @block.sync
            def _(sp):
                w = run("sp", sp)
                for q, lst in self.dma_ops.items():
                    last = {}
                    for o in lst:
                        last[o.sig[0]] = o.sig[1]
                    for s, v in last.items():
                        if w.get(s, 0) < v:
                            sp.wait_ge(s, v)


GELU_NATIVE = False
DEBUG_ZERO = bool(os.environ.get('DEBUG_ZERO'))
ARENA_BYTES = 98 * 1024

C_IDENT = 0
C_TRI = 128
C_M2 = 256
C_MC = 384
C_LCOL = 640
C_SBB = 768
C_ALB = 1280
C_NTRI = C_ALB + 256
CONST_W = C_NTRI + 128


def alibi_slope(h):
    return float(np.float32(2.0 ** (-8.0 * (h + 1) / 16)))


def _bf16_round(v):
    a = np.asarray(v, np.float32).reshape(1)
    u = a.view(np.uint32)
    r = ((u + 0x7FFF + ((u >> 16) & 1)) & 0xFFFF0000).astype(np.uint32)
    return float(r.view(np.float32)[0])


def make_consts():
    c = np.zeros((128, CONST_W), np.float32)
    p = np.arange(128)
    c[:, C_IDENT:C_IDENT + 128] = np.eye(128, dtype=np.float32)
    c[:, C_TRI:C_TRI + 128] = (p[:, None] <= p[None, :]).astype(np.float32)
    c[:, C_M2:C_M2 + 128] = ((p[:, None] <= p[None, :]) & ((p[:, None] // 64) == (p[None, :] // 64))).astype(np.float32)
    c[:, C_NTRI:C_NTRI + 128] = np.where(p[:, None] <= p[None, :], 0.0, -30000.0).astype(np.float32)
    t = np.arange(256)
    c[:, C_MC:C_MC + 256] = (t % 64 != 0).astype(np.float32)[None, :]
    for G in range(2):
        for hl in range(8):
            sl = alibi_slope(G * 8 + hl)
            hi = _bf16_round(sl)
            lo = _bf16_round(sl - hi)
            for kb in range(8):
                col = C_LCOL + G * 64 + hl * 8 + kb
                c[hl * 16 + kb, col] = 1.0
    for j in range(4):
        for hl in range(8):
            base = C_SBB + j * 128 + hl * 16
            c[:, base + 8] = (j % 2) * 128 + p
            c[:, base + 9] = 256 * (j // 2)
            c[:, base + 10] = (j % 2) * 128 + p
            c[:, base + 11] = 256 * (j // 2)
    for h in range(16):
        sl = alibi_slope(h)
        for m in range(16):
            c[:, C_ALB + h * 16 + m] = np.float32(sl) * (p - 64.0 - 128.0 * m).astype(np.float32)
    return c


class Ring:
    def __init__(self, items):
        self.items = items
        self.i = 0

    def next(self):
        it = self.items[self.i % len(self.items)]
        self.i += 1
        return it


class K:
    def __init__(self, nc, stages):
        self.nc = nc
        self.P = Prog()
        self.es = ExitStack()
        self.stages = stages
        self.uid = 0

    def sb(self, shape, dt, name=None):
        self.uid += 1
        return self.es.enter_context(self.nc.sbuf_tensor(f"{name or 't'}_{self.uid}", list(shape), dt))

    def salloc(self, shape, dt):
        n = 1
        for s in shape[1:]:
            n *= s
        nbytes = n * (4 if dt == F32 else 2)
        off = (self.aoff + 31) // 32 * 32
        self.aoff = off + nbytes
        if os.environ.get('DEBUG_ALLOC'):
            print('salloc', shape, off, off + nbytes)
        assert self.aoff <= ARENA_BYTES, f"arena overflow {self.aoff}"
        v = self.arena[:, off // 2:(off + nbytes) // 2]
        if dt == F32:
            v = v.bitcast(F32)
        if len(shape) == 3:
            v = v.rearrange("p (a b) -> p a b", a=shape[1])
        elif len(shape) == 4:
            v = v.rearrange("p (a b c) -> p a b c", a=shape[1], b=shape[2])
        if shape[0] < 128:
            v = v[0:shape[0]]
        return v

    def ring(self, n, shape, dt, name="r"):
        return Ring([(self.salloc(shape, dt), Tok(name)) for _ in range(n)])

    def begin_stage(self):
        self.aoff = 0
        self.P.fence()

    def dram_in(self, name, shape, dt=F32):
        return self.nc.dram_tensor(name, list(shape), dt, kind="ExternalInput").ap()

    def pe(self, fn, r=(), w=()):
        return self.P.op("pe", fn, r, w)

    def act(self, fn, r=(), w=()):
        return self.P.op("act", fn, r, w)

    def dve(self, fn, r=(), w=()):
        return self.P.op("dve", fn, r, w)

    def pool(self, fn, r=(), w=()):
        return self.P.op("pool", fn, r, w)

    def dma(self, q, out, in_, r=(), w=()):
        return self.P.op(q, lambda e: e.dma_start(out=out, in_=in_), r, w, dma=True)

    def dump(self, ap, tok, psum=False):
        if not os.environ.get('DEBUG_DUMP'):
            return
        n = ap.shape[-1]
        c0 = self.dump_off
        self.dump_off += n
        print("DUMP", c0, n)
        if psum:
            scr = self.salloc([128, n], F32)
            st = Tok("dscr")
            self.dve(lambda e: e.tensor_copy(out=scr, in_=ap), r=[tok], w=[st])
            self.dma("pool", self.d["dbg"][:, c0:c0 + n], scr, r=[st])
        else:
            self.dma("pool", self.d["dbg"][:, c0:c0 + n], ap, r=[tok])

    def bank(self):
        b = self.banks[self.bank_i % 6]
        self.bank_i += 1
        return b

    def obank(self):
        b = self.banks[6 + self.obank_i % 2]
        self.obank_i += 1
        return b

    def mm(self, out, lhsT, rhs, start, stop, r, w, **kw):
        return self.pe(lambda e: e.matmul(out, lhsT=lhsT, rhs=rhs, start=start, stop=stop, **kw), r=r, w=w)

    def mmB(self, ps, pst, W, wt, col0, tok0, n):
        xts = [self.xT_t[q] for q in range(tok0 // 128, (tok0 + n + 127) // 128)]
        for kc in range(KC):
            self.mm(ps, W[:, kc, col0:col0 + 128], self.xT[:, kc, tok0:tok0 + n], kc == 0, kc == KC - 1, [wt] + xts, [pst])

    def mmA(self, ps, pst, i, W, wt, col0, n):
        for kc in range(KC):
            self.mm(ps, self.xT[:, kc, i * 128:(i + 1) * 128], W[:, kc, col0:col0 + n], kc == 0, kc == KC - 1,
                    [wt, self.xT_t[i]], [pst])

    def wload(self, dst, src, tok):
        self.dma("pool", dst, src, w=[tok])

    def wload_w(self, dst, W_d, col0, n, tok, step=512):
        for c in range(0, n, step):
            m = min(step, n - c)
            self.wload(dst[:, :, c:c + m], W_d[:, col0 + c:col0 + c + m].rearrange("(k p) n -> p k n", p=128), tok)

    def build(self):
        nc = self.nc
        d = {}
        d["x"] = self.dram_in("x", [T, D])
        d["mem"] = self.dram_in("mem", [MEM, D])
        d["lng"] = self.dram_in("ln_g", [6, D])
        d["lnb"] = self.dram_in("ln_b", [6, D])
        d["xwq"] = self.dram_in("x_wq", [2, D, D])
        d["xwkv"] = self.dram_in("x_wkv", [2, D, 2 * D])
        d["xwo"] = self.dram_in("x_wo", [2, D, D])
        d["fwi"] = self.dram_in("ffn_w_in", [2, D, 2 * DFF])
        d["fwo"] = self.dram_in("ffn_w_out", [2, DFF, D])
        d["evi"] = self.dram_in("ev_w_in", [D, 3072])
        d["evo"] = self.dram_in("ev_w_out", [D, D])
        d["awsT"] = self.dram_in("a_wsT", [128, 4, 128])
        d["abs"] = self.dram_in("a_bs", [1, 512])
        d["alng"] = self.dram_in("a_ln_g", [1, 512])
        d["alnb"] = self.dram_in("a_ln_b", [1, 512])
        d["bng"] = self.dram_in("b_norm_gT", [128, 4])
        d["lbl"] = self.dram_in("lb_logitsT", [128, 2, 4])
        d["oqkv"] = self.dram_in("od_w_qkv", [D, 3072])
        d["owo"] = self.dram_in("od_w_out", [D, D])
        d["cst"] = self.dram_in("consts", [128, CONST_W])
        d["out"] = nc.dram_tensor("out", [T, D], F32, kind="ExternalOutput").ap()
        if os.environ.get('DEBUG_DUMP'):
            d["dbg"] = nc.dram_tensor("dbg", [128, 8192], F32, kind="ExternalOutput").ap()
        self.dump_off = 0
        self.d = d

        self.x_tok = self.sb([128, NT, D], F32, "x_tok")
        self.xT = self.sb([128, KC, T], BF16, "xT")
        self.xtok_t = [Tok(f"xtok{i}") for i in range(NT)]
        self.xT_t = [Tok(f"xT{i}") for i in range(NT)]
        self.cst = self.sb([128, CONST_W], F32, "cst")
        self.cst_t = Tok("cst")
        self.ident = self.sb([128, 128], BF16, "ident")
        self.ident_t = Tok("ident")
        self.xb_ring = Ring([(self.sb([128, D], BF16, "xb"), Tok("xb")) for _ in range(2)])
        self.st_ring = Ring([(self.sb([128, 32], F32, "lnst"), Tok("lnst")) for _ in range(3)])
        self.negh = self.sb([128, 512], F32, "negh")
        self.negh_t = Tok("negh")
        self.arena = self.sb([128, ARENA_BYTES // 2], BF16, "arena")
        self.aoff = 0
        self.banks = []
        for i in range(8):
            pt = self.es.enter_context(nc.psum_tensor(f"bank{i}", [128, 512], F32))
            self.banks.append((pt, Tok(f"bank{i}")))
        self.bank_i = 0
        self.obank_i = 0
        self.cp_i = 0

        self.dma("sp", self.cst[:], d["cst"], w=[self.cst_t])
        self.dve(lambda e: e.tensor_copy(out=self.ident[:], in_=self.cst[:, C_IDENT:C_IDENT + 128]),
                 r=[self.cst_t], w=[self.ident_t])
        self.pool(lambda e: e.memset(self.negh[:], -0.5), w=[self.negh_t])

        self.load_x()
        for s in self.stages:
            getattr(self, "stage_" + s[0])(*s[1:])
        self.P.fence()
        self.store_out()

        sems = {e: self.es.enter_context(nc.semaphore(f"s_{e}")) for e in Prog.ENGS}
        dma_sems = {}
        for q, n in (("sp", 24), ("pool", 32), ("act", 2)):
            dma_sems[q] = [self.es.enter_context(nc.semaphore(f"d_{q}{i}")) for i in range(n)]
        self.P.emit(nc, None, sems, dma_sems)
        self.es.close()

    def copy(self, out, in_, r, w):
        self.cp_i += 1
        if self.cp_i % 2:
            return self.act(lambda e: e.copy(out=out, in_=in_), r=r, w=w)
        return self.dve(lambda e: e.tensor_copy(out=out, in_=in_), r=r, w=w)

    def load_x(self):
        for i in range(NT):
            self.dma("sp", self.x_tok[:, i, :], self.d["x"][i * 128:(i + 1) * 128, :], w=[self.xtok_t[i]])
        for i in range(NT):
            self.to_xT(i)

    def store_out(self):
        for i in range(NT):
            self.dma("sp", self.d["out"][i * 128:(i + 1) * 128, :], self.x_tok[:, i, :], r=[self.xtok_t[i]])

    def transpose_to(self, dst_view, src_tiles, r, w):
        bk, bt = self.bank()
        psb = bk[:].bitcast(BF16)
        n = len(src_tiles)
        for k, src in enumerate(src_tiles):
            self.pe(lambda e, k=k, src=src: e.transpose(out=psb[:, k * 128:(k + 1) * 128], in_=src, identity=self.ident[:]),
                    r=list(r) + [self.ident_t], w=[bt])
        self.copy(dst_view, psb[:, 0:n * 128].rearrange("p (k t) -> p k t", k=n), [bt], w)

    def to_xT(self, i):
        xb, xbt = self.xb_ring.next()
        self.act(lambda e: e.copy(out=xb[:], in_=self.x_tok[:, i, :]), r=[self.xtok_t[i]], w=[xbt])
        self.transpose_to(self.xT[:, :, i * 128:(i + 1) * 128], [xb[:, kc * 128:(kc + 1) * 128] for kc in range(KC)],
                          [xbt], [self.xT_t[i]])

    def load_ln(self, idx):
        g = self.salloc([128, D], F32)
        b = self.salloc([128, D], F32)
        t = Tok("lnp")
        self.dma("sp", g, self.d["lng"][idx:idx + 1, :].to_broadcast([128, D]), w=[t])
        self.dma("sp", b, self.d["lnb"][idx:idx + 1, :].to_broadcast([128, D]), w=[t])
        return g, b, t

    def rstd_small(self, out, var, r, w, eps=EPS):
        n = out.shape[-1]
        self.pool(lambda e: e.tensor_scalar(out=out, in0=var, scalar1=eps, scalar2=None, op0=ALU.add), r=r, w=w)
        self.pool(lambda e: e.tensor_tensor(out=out, in0=out, in1=self.negh[:, 0:n], op=ALU.pow), r=list(w) + [self.negh_t], w=w)

    def layer_norm(self, i, lnp):
        g, b, gt = lnp
        xt = self.x_tok[:, i, :]
        xtok = self.xtok_t[i]
        stt, stk = self.st_ring.next()
        self.dve(lambda e: e.bn_stats(out=stt[:, 0:6], in_=self.x_tok[:, i, 0:512]), r=[xtok], w=[stk])
        self.dve(lambda e: e.bn_stats(out=stt[:, 6:12], in_=self.x_tok[:, i, 512:1024]), r=[xtok], w=[stk])
        self.dve(lambda e: e.bn_aggr(out=stt[:, 12:14], in_=stt[:, 0:12].rearrange("p (a b) -> p a b", a=2)), r=[stk], w=[stk])
        self.rstd_small(stt[:, 15:16], stt[:, 13:14], [stk], [stk])
        self.dve(lambda e: e.scalar_tensor_tensor(out=stt[:, 16:17], in0=stt[:, 12:13], scalar=-1.0, in1=stt[:, 15:16],
                                                  op0=ALU.mult, op1=ALU.mult), r=[stk], w=[stk])
        self.act(lambda e: e.activation(out=xt, in_=xt, func=AF.Identity, bias=stt[:, 16:17], scale=stt[:, 15:16]),
                 r=[stk, xtok], w=[xtok])
        self.pool(lambda e: e.tensor_tensor(out=xt, in0=xt, in1=g, op=ALU.mult), r=[xtok, gt], w=[xtok])
        self.dve(lambda e: e.tensor_tensor(out=xt, in0=xt, in1=b, op=ALU.add), r=[xtok, gt], w=[xtok])
        self.to_xT(i)

    def accum(self, i, half, ps, pst, first):
        dst = self.x_tok[:, i, half * 512:(half + 1) * 512]
        if first:
            self.dve(lambda e: e.scalar_tensor_tensor(out=dst, in0=dst, scalar=ALPHA, in1=ps, op0=ALU.mult, op1=ALU.add),
                     r=[pst, self.xtok_t[i]], w=[self.xtok_t[i]])
        else:
            self.dve(lambda e: e.tensor_tensor(out=dst, in0=dst, in1=ps, op=ALU.add),
                     r=[pst, self.xtok_t[i]], w=[self.xtok_t[i]])

    def out_proj(self, i, lhs_fn, nk, lhs_toks, wo, wot, first):
        for half in range(2):
            ps, pst = self.bank()
            for k in range(nk):
                self.mm(ps[:], lhs_fn(k), wo[:, k, half * 512:(half + 1) * 512], k == 0, k == nk - 1, list(lhs_toks) + [wot], [pst])
            self.accum(i, half, ps[:], pst, first)

    def stage_ln_only(self, idx):
        self.begin_stage()
        lnp = self.load_ln(idx)
        for i in range(NT):
            self.layer_norm(i, lnp)

    def stage_ffn(self, l):
        self.begin_stage()
        groups = [list(range(0, 6)), list(range(6, 12)), list(range(12, 17)), list(range(17, 22))]
        fwi = self.d["fwi"][l]
        fwo = self.d["fwo"][l]
        lnp = self.load_ln(l * 3 + 2)
        wi_r = self.ring(3, [128, KC, 256], BF16, "fwi")
        wo_r = self.ring(2, [128, 6, D], BF16, "fwo")
        hT, hTt = self.salloc([128, 6, T], BF16), Tok("hT")
        sg_r = self.ring(2, [128, 512], F32, "sg")
        for gi, js in enumerate(groups):
            wo, wot = wo_r.next()
            j0 = js[0]
            self.wload(wo[:, 0:len(js), :], fwo[j0 * 128:(j0 + len(js)) * 128, :].rearrange("(j p) n -> p j n", p=128), wot)
            for jl, j in enumerate(js):
                wi, wit = wi_r.next()
                self.wload(wi[:, :, 0:128], fwi[:, j * 128:(j + 1) * 128].rearrange("(k p) n -> p k n", p=128), wit)
                self.wload(wi[:, :, 128:256], fwi[:, DFF + j * 128:DFF + (j + 1) * 128].rearrange("(k p) n -> p k n", p=128), wit)
                for tb in range(4):
                    pg, pgt = self.bank()
                    pu, put = self.bank()
                    self.mmB(pg[:], pgt, wi, wit, 0, tb * 512, 512)
                    self.mmB(pu[:], put, wi, wit, 128, tb * 512, 512)
                    sg, sgt = sg_r.next()
                    self.act(lambda e, sg=sg, pg=pg: e.activation(out=sg, in_=pg[:], func=AF.Silu), r=[pgt], w=[sgt])
                    self.dve(lambda e, sg=sg, pu=pu, jl=jl, tb=tb: e.tensor_tensor(out=hT[:, jl, tb * 512:(tb + 1) * 512], in0=sg, in1=pu[:], op=ALU.mult),
                             r=[sgt, put], w=[hTt])
            last = gi == len(groups) - 1
            for i in range(NT):
                self.out_proj(i, lambda k, i=i: hT[:, k, i * 128:(i + 1) * 128], len(js), [hTt], wo, wot, first=(gi == 0))
                if last:
                    self.layer_norm(i, lnp)

    def stage_xattn(self, l):
        self.begin_stage()
        d = self.d
        SC = 1.0 / 16.0
        lnp = self.load_ln(l * 3 + 1)
        wq, wqt = self.salloc([128, KC, D], BF16), Tok("wq")
        wo, wot = self.salloc([128, KC, D], BF16), Tok("wo")
        kT, kTt = self.salloc([128, KC, MEM], BF16), Tok("kT")
        vS, vSt = self.salloc([128, 2, D], BF16), Tok("vS")
        memb, membt = self.salloc([128, 2, D], BF16), Tok("memb")
        memT, memTt = self.salloc([128, KC, MEM], BF16), Tok("memT")
        wkv_r = self.ring(1, [128, KC, 512], BF16, "wkv")
        qT_r = self.ring(1, [128, KC, 512], BF16, "qT")
        p32_r = self.ring(2, [128, 4, 256], F32, "p32")
        pb_r = self.ring(2, [128, 4, 256], BF16, "pb")
        pT_r = self.ring(2, [128, 8, 128], BF16, "pT")
        oT_r = self.ring(2, [128, 8, 128], BF16, "oT")
        sm_r = self.ring(3, [128, 16], F32, "sm")
        self.wload(memb, d["mem"].rearrange("(m p) n -> p m n", p=128), membt)
        for mt in range(2):
            self.transpose_to(memT[:, :, mt * 128:(mt + 1) * 128], [memb[:, mt, kc * 128:(kc + 1) * 128] for kc in range(KC)],
                              [membt], [memTt])
        for c in range(4):
            wk, wkt = wkv_r.next()
            self.wload_w(wk, d["xwkv"][l], c * 512, 512, wkt)
            if c < 2:
                for cc in range(4):
                    fc = c * 4 + cc
                    ps, pst = self.bank()
                    for kc in range(KC):
                        self.mm(ps[:, 0:MEM], wk[:, kc, cc * 128:(cc + 1) * 128], memT[:, kc, :], kc == 0, kc == KC - 1, [wkt, memTt], [pst])
                    self.copy(kT[:, fc, :], ps[:, 0:MEM], [pst], [kTt])
            else:
                for mt in range(2):
                    ps, pst = self.bank()
                    for kc in range(KC):
                        self.mm(ps[:], memT[:, kc, mt * 128:(mt + 1) * 128], wk[:, kc, :], kc == 0, kc == KC - 1, [wkt, memTt], [pst])
                    self.copy(vS[:, mt, (c - 2) * 512:(c - 1) * 512], ps[:], [pst], [vSt])
        self.wload_w(wq, d["xwq"][l], 0, D, wqt)
        self.wload_w(wo, d["xwo"][l], 0, D, wot)
        for tb in range(4):
            qT, qTt = qT_r.next()
            for c in range(KC):
                ps, pst = self.bank()
                self.mmB(ps[:], pst, wq, wqt, c * 128, tb * 512, 512)
                self.copy(qT[:, c, :], ps[:], [pst], [qTt])
            for il in range(4):
                i = tb * 4 + il
                sA, sAt = self.bank()
                sB, sBt = self.bank()
                for h in range(4):
                    bk, bkt = (sA, sAt) if h < 2 else (sB, sBt)
                    for k2 in range(2):
                        self.mm(bk[:, (h % 2) * 256:(h % 2 + 1) * 256], qT[:, 2 * h + k2, il * 128:(il + 1) * 128], kT[:, 2 * h + k2, :],
                                k2 == 0, k2 == 1, [qTt, kTt], [bkt])
                sm, smt = sm_r.next()
                self.dve(lambda e, sm=sm, sA=sA: e.tensor_reduce(out=sm[:, 0:2], in_=sA[:].rearrange("p (h m) -> p h m", h=2), axis=AX.X, op=ALU.max), r=[sAt], w=[smt])
                self.dve(lambda e, sm=sm, sB=sB: e.tensor_reduce(out=sm[:, 2:4], in_=sB[:].rearrange("p (h m) -> p h m", h=2), axis=AX.X, op=ALU.max), r=[sBt], w=[smt])
                self.dve(lambda e, sm=sm: e.tensor_scalar(out=sm[:, 4:8], in0=sm[:, 0:4], scalar1=-SC, scalar2=None, op0=ALU.mult), r=[smt], w=[smt])
                p32, p32t = p32_r.next()
                for h in range(4):
                    bk, bkt = (sA, sAt) if h < 2 else (sB, sBt)
                    self.act(lambda e, h=h, bk=bk, sm=sm, p32=p32: e.activation(out=p32[:, h, :], in_=bk[:, (h % 2) * 256:(h % 2 + 1) * 256], func=AF.Exp,
                                                                             bias=sm[:, 4 + h:5 + h], scale=SC, accum_out=sm[:, 8 + h:9 + h]),
                             r=[bkt, smt], w=[p32t, smt])
                self.dve(lambda e, sm=sm: e.reciprocal(out=sm[:, 12:16], in_=sm[:, 8:12]), r=[smt], w=[smt])
                pb, pbt = pb_r.next()
                for h in range(4):
                    self.pool(lambda e, h=h, pb=pb, p32=p32, sm=sm: e.tensor_scalar(out=pb[:, h, :], in0=p32[:, h, :], scalar1=sm[:, 12 + h:13 + h], scalar2=None, op0=ALU.mult),
                              r=[p32t, smt], w=[pbt])
                pT, pTt = pT_r.next()
                self.transpose_to(pT, [pb[:, h, mt * 128:(mt + 1) * 128] for h in range(4) for mt in range(2)], [pbt], [pTt])
                oA, oAt = self.bank()
                oB, oBt = self.bank()
                oT, oTt = oT_r.next()
                for c in range(8):
                    bk, bkt = (oA, oAt) if c < 4 else (oB, oBt)
                    h = c // 2
                    for mt in range(2):
                        self.mm(bk[:, (c % 4) * 128:(c % 4 + 1) * 128], vS[:, mt, c * 128:(c + 1) * 128], pT[:, h * 2 + mt, :],
                                mt == 0, mt == 1, [vSt, pTt], [bkt])
                self.copy(oT[:, 0:4, :], oA[:].rearrange("p (c t) -> p c t", c=4), [oAt], [oTt])
                self.copy(oT[:, 4:8, :], oB[:].rearrange("p (c t) -> p c t", c=4), [oBt], [oTt])
                self.out_proj(i, lambda k, oT=oT: oT[:, k, :], KC, [oTt], wo, wot, first=True)
                self.layer_norm(i, lnp)

    def gelu(self, out, ps, pst, outt, scr_r, half=True):
        if GELU_NATIVE:
            self.act(lambda e: e.activation(out=out, in_=ps, func=AF.Gelu_apprx_tanh), r=[pst], w=[outt])
            return 1.0
        C0 = 0.7978845608028654
        C1 = 0.044715
        s, stk = scr_r.next()
        n = ps.shape[-1]
        sv = s[:, 0:n]
        self.act(lambda e: e.activation(out=sv, in_=ps, func=AF.Square), r=[pst], w=[stk])
        self.dve(lambda e: e.tensor_scalar(out=sv, in0=sv, scalar1=C1, scalar2=1.0, op0=ALU.mult, op1=ALU.add), r=[stk], w=[stk])
        self.dve(lambda e: e.tensor_tensor(out=sv, in0=sv, in1=ps, op=ALU.mult), r=[stk, pst], w=[stk])
        self.act(lambda e: e.activation(out=sv, in_=sv, func=AF.Tanh, scale=C0), r=[stk], w=[stk])
        self.dve(lambda e: e.scalar_tensor_tensor(out=out, in0=sv, scalar=1.0, in1=ps, op0=ALU.add, op1=ALU.mult), r=[stk, pst], w=[outt])
        return 0.5

    def stage_mix0(self):
        d = self.d
        self.begin_stage()
        wuv, wut, wvt = self.salloc([128, KC, 1024], BF16), Tok("wu"), Tok("wv")
        woA, woAt = self.salloc([128, 4, D], BF16), Tok("woA")
        ws32, ws32t = self.salloc([128, 4, 128], F32), Tok("ws32")
        wsb, wsbt = self.salloc([128, 4, 128], BF16), Tok("wsb")
        bsB, bsBt = self.salloc([128, 512], F32), Tok("bsB")
        lgB, lgBt = self.salloc([128, 512], F32), Tok("lgB")
        lbB, lbBt = self.salloc([128, 512], F32), Tok("lbB")
        uT_r = self.ring(2, [128, 4, 512], BF16, "uT")
        scr_r = self.ring(3, [128, 512], F32, "gscr")
        vg_r = self.ring(2, [128, 512], F32, "vg")
        vb_r = self.ring(2, [128, 512], BF16, "vb")
        ss_r = self.ring(2, [128, 512], F32, "ssum")
        ya_r = self.ring(2, [128, 4, 128], BF16, "yaT")
        st_r = self.ring(2, [128, 48], F32, "gst")
        self.wload_w(wuv[:, :, 0:512], d["evi"], 0, 512, wut)
        self.wload_w(wuv[:, :, 512:1024], d["evi"], 512, 512, wvt)
        self.wload(woA, d["evo"][0:512, :].rearrange("(g p) n -> p g n", p=128), woAt)
        self.dma("sp", ws32, d["awsT"], w=[ws32t])
        self.dma("sp", bsB, d["abs"].to_broadcast([128, 512]), w=[bsBt])
        self.dma("sp", lgB, d["alng"].to_broadcast([128, 512]), w=[lgBt])
        self.dma("sp", lbB, d["alnb"].to_broadcast([128, 512]), w=[lbBt])
        for g in range(4):
            self.dve(lambda e, g=g: e.tensor_tensor(out=wsb[:, g, :], in0=ws32[:, g, :], in1=self.cst[:, C_TRI:C_TRI + 128], op=ALU.mult),
                     r=[ws32t, self.cst_t], w=[wsbt])
        for tb in range(4):
            uT, uTt = uT_r.next()
            gf = 1.0
            for c in range(4):
                ps, pst = self.bank()
                self.mmB(ps[:], pst, wuv, wut, c * 128, tb * 512, 512)
                gf = self.gelu(uT[:, c, :], ps[:], pst, uTt, scr_r)
            for il in range(4):
                i = tb * 4 + il
                ps, pst = self.bank()
                self.mmA(ps[:], pst, i, wuv, wvt, 512, 512)
                vg, vgt = vg_r.next()
                gv = self.gelu(vg, ps[:], pst, vgt, scr_r)
                st, stt = st_r.next()
                for g in range(4):
                    self.dve(lambda e, g=g, st=st, vg=vg: e.bn_stats(out=st[:, g * 6:(g + 1) * 6], in_=vg[:, g * 128:(g + 1) * 128]), r=[vgt], w=[stt])
                for g in range(4):
                    self.dve(lambda e, g=g, st=st: e.bn_aggr(out=st[:, 24 + g * 2:26 + g * 2], in_=st[:, g * 6:(g + 1) * 6]), r=[stt], w=[stt])
                mv = st[:, 24:32].rearrange("p (g two) -> p g two", two=2)
                self.pool(lambda e, st=st, mv=mv: e.tensor_scalar(out=st[:, 32:36], in0=mv[:, :, 1], scalar1=gv * gv, scalar2=EPS, op0=ALU.mult, op1=ALU.add), r=[stt], w=[stt])
                self.pool(lambda e, st=st: e.tensor_tensor(out=st[:, 32:36], in0=st[:, 32:36], in1=self.negh[:, 0:4], op=ALU.pow), r=[stt, self.negh_t], w=[stt])
                self.dve(lambda e, st=st: e.tensor_scalar(out=st[:, 36:40], in0=st[:, 32:36], scalar1=gv, scalar2=None, op0=ALU.mult), r=[stt], w=[stt])
                self.dve(lambda e, st=st, mv=mv: e.scalar_tensor_tensor(out=st[:, 40:44], in0=mv[:, :, 0], scalar=-1.0, in1=st[:, 36:40], op0=ALU.mult, op1=ALU.mult), r=[stt], w=[stt])
                for g in range(4):
                    self.act(lambda e, g=g, st=st, vg=vg: e.activation(out=vg[:, g * 128:(g + 1) * 128], in_=vg[:, g * 128:(g + 1) * 128], func=AF.Identity,
                                                                      bias=st[:, 40 + g:41 + g], scale=st[:, 36 + g:37 + g]), r=[stt, vgt], w=[vgt])
                self.dve(lambda e, vg=vg: e.tensor_tensor(out=vg, in0=vg, in1=lgB, op=ALU.mult), r=[vgt, lgBt], w=[vgt])
                vb, vbt = vb_r.next()
                self.pool(lambda e, vg=vg, vb=vb: e.tensor_tensor(out=vb, in0=vg, in1=lbB, op=ALU.add), r=[vgt, lbBt], w=[vbt])
                pss, psst = self.bank()
                for g in range(4):
                    self.mm(pss[:, g * 128:(g + 1) * 128], vb[:, g * 128:(g + 1) * 128], wsb[:, g, :], True, True, [vbt, wsbt], [psst])
                ss, sst = ss_r.next()
                self.dve(lambda e, ss=ss, pss=pss: e.tensor_tensor(out=ss, in0=pss[:], in1=bsB, op=ALU.add), r=[psst, bsBt], w=[sst])
                ya, yat = ya_r.next()
                self.dve(lambda e, ya=ya, uT=uT, ss=ss, il=il: e.scalar_tensor_tensor(out=ya, in0=uT[:, :, il * 128:(il + 1) * 128], scalar=gf,
                                                                                   in1=ss.rearrange("p (g t) -> p g t", g=4), op0=ALU.mult, op1=ALU.mult),
                         r=[uTt, sst], w=[yat])
                self.out_proj(i, lambda k, ya=ya: ya[:, k, :], 4, [yat], woA, woAt, first=True)

        self.begin_stage()
        NB = 256
        lnp = self.load_ln(0)
        wB, wBt = self.salloc([128, KC, 2048], BF16), [Tok("wBq"), Tok("wBf"), Tok("wBi"), Tok("wBg")]
        woB, woBt = self.salloc([128, 4, D], BF16), Tok("woB")
        sm, smt = self.salloc([128, 32], F32), Tok("hsm")
        S, St = self.salloc([128, 4, 128], F32), Tok("S")
        Sb_r = self.ring(3, [128, 4, 128], BF16, "Sb")
        m2x4, m2t = self.salloc([128, 512], BF16), Tok("m2x4")
        ones, onest = self.salloc([128, 128], BF16), Tok("ones")
        f1 = self.ring(1, [128, 4, NB], F32, "f1")
        f2 = self.ring(1, [128, 4, NB], F32, "f2")
        f3 = self.ring(1, [128, 4, NB], F32, "f3")
        E_r = self.ring(1, [128, 4, NB], F32, "E")
        qd_r = self.ring(1, [128, 4, NB], BF16, "qdT")
        kd_r = self.ring(1, [128, 4, NB], BF16, "kdT")
        gs_r = self.ring(1, [128, 4, NB], BF16, "gsT")
        vi_r = self.ring(1, [128, 2, 512], BF16, "vi")
        kt_r = self.ring(1, [128, 2, 512], BF16, "kdtok")
        am_r = self.ring(2, [128, 512], BF16, "am")
        tmp_r = self.ring(1, [128, 512], F32, "stmp")
        sq_r = self.ring(2, [128, 512], BF16, "sq")
        r_r = self.ring(1, [128, 512], F32, "rr")
        t1_r = self.ring(1, [128, 512], F32, "t1")
        yb_r = self.ring(2, [128, 4, 128], BF16, "ybT")
        for c in range(4):
            self.wload_w(wB[:, :, c * 512:(c + 1) * 512], d["evi"], 1024 + c * 512, 512, wBt[c])
        self.wload(woB, d["evo"][512:1024, :].rearrange("(g p) n -> p g n", p=128), woBt)
        self.dma("sp", sm[:, 0:8], d["lbl"].rearrange("p l h -> p (l h)"), w=[smt])
        self.dma("sp", sm[:, 20:24], d["bng"], w=[smt])
        self.dve(lambda e: e.tensor_tensor(out=sm[:, 8:12], in0=sm[:, 0:4], in1=sm[:, 4:8], op=ALU.subtract), r=[smt], w=[smt])
        self.act(lambda e: e.activation(out=sm[:, 12:16], in_=sm[:, 8:12], func=AF.Sigmoid), r=[smt], w=[smt])
        self.dve(lambda e: e.tensor_scalar(out=sm[:, 16:20], in0=sm[:, 12:16], scalar1=-1.0, scalar2=1.0, op0=ALU.mult, op1=ALU.add), r=[smt], w=[smt])
        self.pool(lambda e: e.memset(S.rearrange("p h v -> p (h v)"), 0.0), w=[St])
        Sb, Sbt = Sb_r.next()
        self.pool(lambda e, Sb=Sb: e.memset(Sb.rearrange("p h v -> p (h v)"), 0.0), w=[Sbt])
        self.pool(lambda e: e.memset(ones, 1.0), w=[onest])
        for h in range(4):
            self.dve(lambda e, h=h: e.tensor_copy(out=m2x4[:, h * 128:(h + 1) * 128], in_=self.cst[:, C_M2:C_M2 + 128]), r=[self.cst_t], w=[m2t])
        maskc = self.cst[:, C_MC:C_MC + NB]
        for blk in range(T // NB):
            tok0 = blk * NB
            s1, s1t = f1.next()
            s2, s2t = f2.next()
            s3, s3t = f3.next()
            E, Et = E_r.next()
            qd, qdt = qd_r.next()
            kd, kdt = kd_r.next()
            gs, gst = gs_r.next()
            qf = []
            for h in range(4):
                b1, b1t = self.bank()
                self.mmB(b1[:, 0:NB], b1t, wB, wBt[0], h * 128, tok0, NB)
                self.mmB(b1[:, NB:2 * NB], b1t, wB, wBt[1], 512 + h * 128, tok0, NB)
                qf.append((b1, b1t))
                self.act(lambda e, h=h, b1=b1, s1=s1: e.activation(out=s1[:, h, :], in_=b1[:, NB:2 * NB], func=AF.Sigmoid), r=[b1t], w=[s1t])
            for h in range(4):
                self.dve(lambda e, h=h, s1=s1: e.tensor_scalar(out=s1[:, h, :], in0=s1[:, h, :], scalar1=sm[:, 16 + h:17 + h], scalar2=sm[:, 12 + h:13 + h],
                                                              op0=ALU.mult, op1=ALU.add), r=[s1t, smt], w=[s1t])
            self.act(lambda e, s1=s1, s2=s2: e.activation(out=s2, in_=s1, func=AF.Ln), r=[s1t], w=[s2t])
            for h in range(4):
                self.dve(lambda e, h=h, s2=s2, s3=s3: e.tensor_tensor_scan(out=s3[:, h, :], data0=maskc, data1=s2[:, h, :], initial=0.0, op0=ALU.mult, op1=ALU.add),
                         r=[s2t, self.cst_t], w=[s3t])
            self.pool(lambda e, s1=s1: e.tensor_scalar(out=s1, in0=s1, scalar1=-1.0, scalar2=1.0, op0=ALU.mult, op1=ALU.add), r=[s1t], w=[s1t])
            self.act(lambda e, E=E, s3=s3: e.activation(out=E, in_=s3, func=AF.Exp), r=[s3t], w=[Et])
            self.act(lambda e, s2=s2, s3=s3: e.activation(out=s2, in_=s3, func=AF.Exp, scale=-1.0), r=[s3t, s2t], w=[s2t])
            for h in range(4):
                b1, b1t = qf[h]
                self.dve(lambda e, h=h, b1=b1, E=E, qd=qd: e.tensor_tensor(out=qd[:, h, :], in0=b1[:, 0:NB], in1=E[:, h, :], op=ALU.mult), r=[b1t, Et], w=[qdt])
            self.pool(lambda e, s1=s1, s2=s2, kd=kd: e.tensor_tensor(out=kd, in0=s1, in1=s2, op=ALU.mult), r=[s1t, s2t], w=[kdt])
            for h in range(0, 4, 2):
                b2, b2t = self.bank()
                self.mmB(b2[:, 0:NB], b2t, wB, wBt[3], 1536 + h * 128, tok0, NB)
                self.mmB(b2[:, NB:2 * NB], b2t, wB, wBt[3], 1536 + (h + 1) * 128, tok0, NB)
                self.act(lambda e, h=h, b2=b2, gs=gs: e.activation(out=gs[:, h:h + 2, :], in_=b2[:].rearrange("p (a t) -> p a t", a=2), func=AF.Silu), r=[b2t], w=[gst])
            vi, vit = vi_r.next()
            ktk, ktkt = kt_r.next()
            for il in range(2):
                b3, b3t = self.bank()
                self.mmA(b3[:], b3t, blk * 2 + il, wB, wBt[2], 1024, 512)
                self.act(lambda e, il=il, b3=b3, vi=vi: e.activation(out=vi[:, il, :], in_=b3[:], func=AF.Silu), r=[b3t], w=[vit])
                self.transpose_to(ktk[:, il, :].rearrange("p (h k) -> p h k", h=4), [kd[:, h, il * 128:(il + 1) * 128] for h in range(4)], [kdt], [ktkt])
            for il in range(2):
                i = blk * 2 + il
                tc0 = il * 128
                U = [self.bank(), self.bank()]
                for c in range(2):
                    for h in range(4):
                        self.mm(U[c][0][:, h * 128:(h + 1) * 128], ktk[c * 64:(c + 1) * 64, il, h * 128:(h + 1) * 128],
                                vi[c * 64:(c + 1) * 64, il, h * 128:(h + 1) * 128], True, True, [ktkt, vit], [U[c][1]])
                A, At = self.bank()
                for h in range(4):
                    self.mm(A[:, h * 128:(h + 1) * 128], kd[:, h, tc0:tc0 + 128], qd[:, h, tc0:tc0 + 128], True, True, [kdt, qdt], [At])
                am, amt = am_r.next()
                self.dve(lambda e, am=am, A=A: e.tensor_tensor(out=am, in0=A[:], in1=m2x4, op=ALU.mult), r=[At, m2t], w=[amt])
                Sbs = [(Sb, Sbt)]
                for c in range(2):
                    tmp, tmpt = tmp_r.next()
                    Uc, Uct = U[c]
                    self.dve(lambda e, tmp=tmp, Uc=Uc: e.tensor_tensor(out=tmp, in0=Uc[:], in1=S.rearrange("p h v -> p (h v)"), op=ALU.add), r=[Uct, St], w=[tmpt])
                    Sb, Sbt = Sb_r.next()
                    col = tc0 + c * 64 + 63
                    for h in range(4):
                        self.dve(lambda e, h=h, tmp=tmp, E=E, col=col: e.tensor_scalar(out=S[:, h, :], in0=tmp[:, h * 128:(h + 1) * 128], scalar1=E[:, h, col:col + 1],
                                                                                    scalar2=None, op0=ALU.mult), r=[tmpt, Et], w=[St])
                        self.act(lambda e, h=h, tmp=tmp, E=E, col=col, Sb=Sb: e.activation(out=Sb[:, h, :], in_=tmp[:, h * 128:(h + 1) * 128], func=AF.Copy,
                                                                                        scale=E[:, h, col:col + 1]), r=[tmpt, Et], w=[Sbt])
                    Sbs.append((Sb, Sbt))
                O, Ot = self.bank()
                for h in range(4):
                    self.mm(O[:, h * 128:(h + 1) * 128], vi[:, il, h * 128:(h + 1) * 128], am[:, h * 128:(h + 1) * 128], True, False, [vit, amt], [Ot])
                    for c in range(2):
                        Sc, Sct = Sbs[c]
                        self.mm(O[:, h * 128 + c * 64:h * 128 + (c + 1) * 64], Sc[:, h, :], qd[:, h, tc0 + c * 64:tc0 + (c + 1) * 64], False, c == 1,
                                [Sct, qdt], [Ot], skip_group_check=True)
                sq, sqt = sq_r.next()
                self.act(lambda e, sq=sq, O=O: e.activation(out=sq, in_=O[:], func=AF.Square), r=[Ot], w=[sqt])
                Q, Qt = self.bank()
                self.mm(Q[:], ones, sq, True, True, [onest, sqt], [Qt])
                rr, rrt = r_r.next()
                self.dve(lambda e, rr=rr, Q=Q: e.tensor_scalar(out=rr, in0=Q[:], scalar1=1.0 / 128.0, scalar2=EPS, op0=ALU.mult, op1=ALU.add), r=[Qt], w=[rrt])
                self.pool(lambda e, rr=rr: e.tensor_tensor(out=rr, in0=rr, in1=self.negh[:], op=ALU.pow), r=[rrt, self.negh_t], w=[rrt])
                t1, t1t = t1_r.next()
                self.dve(lambda e, t1=t1, O=O, rr=rr: e.tensor_tensor(out=t1, in0=O[:], in1=rr, op=ALU.mult), r=[Ot, rrt], w=[t1t])
                yb, ybt = yb_r.next()
                for h in range(4):
                    self.dve(lambda e, h=h, yb=yb, t1=t1, gs=gs, tc0=tc0: e.scalar_tensor_tensor(out=yb[:, h, :], in0=t1[:, h * 128:(h + 1) * 128], scalar=sm[:, 20 + h:21 + h],
                                                                                              in1=gs[:, h, tc0:tc0 + 128], op0=ALU.mult, op1=ALU.mult),
                             r=[t1t, gst, smt], w=[ybt])
                self.out_proj(i, lambda k, yb=yb: yb[:, k, :], 4, [ybt], woB, woBt, first=False)
                self.layer_norm(i, lnp)

    def stage_moba(self):
        for G in range(2):
            self.moba_group(G)

    def moba_group(self, G):
        d = self.d
        if True:
            self.begin_stage()
            if DEBUG_ZERO:
                zt = Tok("z")
                lo, hi = [int(v) for v in os.environ['DEBUG_ZERO'].split(':')]
                self.pool(lambda e: e.memset(self.arena[:, lo // 2:hi // 2], 0.0), w=[zt])
                self.P.fence()
            lnp = self.load_ln(3) if G == 1 else None
            kT, kTt = self.salloc([128, 4, T], BF16), Tok("kT")
            va, vat = self.salloc([128, NT, 8, 65], BF16), Tok("va")
            wq, wqt = self.salloc([128, KC, 512], BF16), Tok("wq")
            woG, woGt = self.salloc([128, 4, D], BF16), Tok("woG")
            wk_r = self.ring(2, [128, KC, 256], BF16, "wk")
            qz_r = self.ring(1, [128, 8, 512], BF16, "qz")
            R_r = self.ring(1, [128, 512], BF16, "R")
            Rc, Rct = self.salloc([128, 512], BF16), Tok("Rc")
            pT_r = self.ring(3, [128, 512], BF16, "pT")
            otok, otokt = self.salloc([128, 4, 512], BF16), Tok("otok")
            oTb, oTbt = self.salloc([128, 4, 512], BF16), Tok("oTb")
            Lm_r = self.ring(2, [128, 8, 128], BF16, "Lm")
            lcol, lcolt = self.salloc([128, 64], BF16), Tok("lcol")
            tri, trit = self.salloc([128, 128], BF16), Tok("tri")
            SBb, SBbt = self.salloc([128, 4, 128], BF16), Tok("SBb")
            SB_r = self.ring(4, [128, 128], BF16, "SB")
            km32, km32t = self.salloc([128, 4, 8], F32), Tok("km32")
            kmh, kmht = self.salloc([128, 4, 8], BF16), Tok("kmh")
            kml, kmlt = self.salloc([128, 4, 8], BF16), Tok("kml")
            aff_r = self.ring(2, [128, 8, 8], F32, "aff")
            cmp_, cmpt = self.salloc([128, 8, 8, 8], F32), Tok("cmp")
            cnt, cntt = self.salloc([128, 8, 8], F32), Tok("cnt")
            sm_r = self.ring(2, [128, 8], F32, "msm")
            zb = os.environ.get('DEBUG_ZBUF')
            if zb:
                cand = dict(qz=[qz_r.items[0][0]], pT=[x[0] for x in pT_r.items], Lm=[x[0] for x in Lm_r.items], otok=[otok], oTb=[oTb],
                            R=[R_r.items[0][0]], SB=[x[0] for x in SB_r.items], kT=[kT], va=[va], small=[km32, kmh, kml, cmp_, cnt] + [x[0] for x in aff_r.items] + [x[0] for x in sm_r.items])
                for ap in cand[zb]:
                    nd = len(ap.shape)
                    flat = ap if nd == 2 else ap.rearrange({3: "p a b -> p (a b)", 4: "p a b c -> p (a b c)"}[nd])
                    self.pool(lambda e, flat=flat: e.memset(flat, float(os.environ.get('DEBUG_ZVAL', '0'))), w=[Tok("z")])
                self.P.fence()
            self.dve(lambda e: e.tensor_copy(out=lcol, in_=self.cst[:, C_LCOL + G * 64:C_LCOL + (G + 1) * 64]), r=[self.cst_t], w=[lcolt])
            self.dve(lambda e: e.tensor_copy(out=tri, in_=self.cst[:, C_TRI:C_TRI + 128]), r=[self.cst_t], w=[trit])
            self.dve(lambda e: e.tensor_copy(out=SBb, in_=self.cst[:, C_SBB:C_SBB + 512].rearrange("p (j c) -> p j c", j=4)), r=[self.cst_t], w=[SBbt])
            self.transpose_to(Rc.rearrange("p (j t) -> p j t", j=4), [SBb[:, j, :] for j in range(4)], [SBbt], [Rct])
            if G == 0:
                self.dump(Rc[:, 0:128], Rct)
                self.dump(SBb[:, 0, :], SBbt)
            for s_ in range(qz_r.items.__len__()):
                qz0, qz0t = qz_r.items[s_]
                self.pool(lambda e, qz0=qz0: e.memset(qz0.rearrange("p h t -> p (h t)"), 0.0), w=[qz0t])
            self.pool(lambda e: e.memset(va.rearrange("p a b c -> p (a b c)"), 1.0), w=[vat])
            self.wload_w(wq, d["oqkv"], G * 512, 512, wqt)
            self.wload(woG, d["owo"][G * 512:(G + 1) * 512, :].rearrange("(g p) n -> p g n", p=128), woGt)
            for c2 in range(2):
                wk, wkt = wk_r.next()
                self.wload_w(wk, d["oqkv"], 1024 + G * 512 + c2 * 256, 256, wkt, step=256)
                for pp in range(2):
                    p = c2 * 2 + pp
                    for tb in range(4):
                        ps, pst = self.bank()
                        self.mmB(ps[:], pst, wk, wkt, pp * 128, tb * 512, 512)
                        self.copy(kT[:, p, tb * 512:(tb + 1) * 512], ps[:], [pst], [kTt])
            for c2 in range(2):
                wk, wkt = wk_r.next()
                self.wload_w(wk, d["oqkv"], 2048 + G * 512 + c2 * 256, 256, wkt, step=256)
                for i in range(NT):
                    ps, pst = self.bank()
                    self.mmA(ps[:, 0:256], pst, i, wk, wkt, 0, 256)
                    self.copy(va[:, i, c2 * 4:(c2 + 1) * 4, 0:64], ps[:, 0:256].rearrange("p (h c) -> p h c", h=4), [pst], [vat])
            for p in range(4):
                self.dve(lambda e, p=p: e.tensor_reduce(out=km32[:, p, :], in_=kT[:, p, :].rearrange("p (b t) -> p b t", b=8), axis=AX.X, op=ALU.add), r=[kTt], w=[km32t])
            self.dve(lambda e: e.tensor_scalar(out=km32, in0=km32, scalar1=1.0 / 256.0, scalar2=None, op0=ALU.mult), r=[km32t], w=[km32t])
            self.dve(lambda e: e.tensor_copy(out=kmh, in_=km32), r=[km32t], w=[kmht])
            self.dve(lambda e: e.tensor_tensor(out=km32, in0=km32, in1=kmh, op=ALU.subtract), r=[km32t, kmht], w=[km32t])
            self.dve(lambda e: e.tensor_copy(out=kml, in_=km32), r=[km32t], w=[kmlt])

            for qc in range(4):
                qz, qzt = qz_r.next()
                for p in range(4):
                    ps, pst = self.bank()
                    self.mmB(ps[:], pst, wq, wqt, p * 128, qc * 512, 512)
                    self.act(lambda e, p=p, ps=ps, qz=qz: e.copy(out=qz[0:64, 2 * p, :], in_=ps[0:64, :]), r=[pst], w=[qzt])
                    self.dve(lambda e, p=p, ps=ps, qz=qz: e.tensor_copy(out=qz[64:128, 2 * p + 1, :], in_=ps[64:128, :]), r=[pst], w=[qzt])
                if qc < 2:
                    R, Rt = Rc, Rct
                else:
                    R, Rt = R_r.next()
                    sbs = []
                    for j in range(4):
                        qb = (qc * 4 + j) // 2
                        ab, abt = self.bank()
                        for hl in range(8):
                            self.mm(ab[:, hl * 8:(hl + 1) * 8], qz[:, hl, j * 128:(j + 1) * 128], kmh[:, hl // 2, :], True, False, [qzt, kmht], [abt])
                            self.mm(ab[:, hl * 8:(hl + 1) * 8], qz[:, hl, j * 128:(j + 1) * 128], kml[:, hl // 2, :], False, True, [qzt, kmlt], [abt])
                        aff, afft = aff_r.next()
                        self.dve(lambda e, aff=aff, ab=ab: e.tensor_copy(out=aff, in_=ab[:, 0:64].rearrange("p (h k) -> p h k", h=8)), r=[abt], w=[afft])
                        self.dve(lambda e, aff=aff, qb=qb: e.tensor_tensor(out=cmp_[:, :, 0:qb, 0:qb],
                                                                        in0=aff[:, :, 0:qb].unsqueeze(2).to_broadcast([128, 8, qb, qb]),
                                                                        in1=aff[:, :, 0:qb].unsqueeze(3).to_broadcast([128, 8, qb, qb]), op=ALU.is_gt),
                                 r=[afft], w=[cmpt])
                        self.dve(lambda e, qb=qb: e.tensor_reduce(out=cnt[:, :, 0:qb], in_=cmp_[:, :, 0:qb, 0:qb], axis=AX.X, op=ALU.add), r=[cmpt], w=[cntt])
                        SB, SBt = SB_r.next()
                        self.pool(lambda e, SB=SB, j=j: e.tensor_copy(out=SB, in_=SBb[:, j, :]), r=[SBbt], w=[SBt])
                        self.dve(lambda e, SB=SB, qb=qb: e.tensor_scalar(out=SB.rearrange("p (h s) -> p h s", h=8)[:, :, 0:qb], in0=cnt[:, :, 0:qb],
                                                                      scalar1=3.0, scalar2=-32768.0, op0=ALU.is_ge, op1=ALU.mult), r=[cntt, SBt], w=[SBt])
                        sbs.append((SB, SBt))
                    self.transpose_to(R.rearrange("p (j t) -> p j t", j=4), [s[0] for s in sbs], [s[1] for s in sbs], [Rt])
                nkt = 4 * qc + 4
                for hl in range(8):
                    h = G * 8 + hl
                    Lm, Lmt = Lm_r.next()
                    for kb in range(2 * qc + 2):
                        self.pool(lambda e, Lm=Lm, kb=kb, hl=hl: e.tensor_copy(out=Lm[:, kb, :], in_=lcol[:, hl * 8 + kb:hl * 8 + kb + 1].to_broadcast([128, 128])),
                                  r=[lcolt], w=[Lmt])
                    O, Ot = self.obank()
                    first = True
                    for kt in range(nkt):
                        j0 = max(0, kt - 4 * qc)
                        c0 = j0 * 128
                        kb = kt // 2
                        S_, S_t = self.bank()
                        self.mm(S_[:, c0:512], kT[:, hl // 2, kt * 128:(kt + 1) * 128], qz[:, hl, c0:512], True, False, [kTt, qzt], [S_t])
                        self.mm(S_[:, c0:512], Lm[:, kb, :], R[:, c0:512], False, True, [Lmt, Rt], [S_t])
                        if kt >= 4 * qc:
                            self.dve(lambda e, S_=S_, c0=c0: e.tensor_tensor(out=S_[:, c0:c0 + 128], in0=S_[:, c0:c0 + 128], in1=self.cst[:, C_NTRI:C_NTRI + 128], op=ALU.add),
                                     r=[S_t, self.cst_t], w=[S_t])
                        pT, pTt = pT_r.next()
                        for j in range(j0, 4):
                            bcol = C_ALB + h * 16 + (4 * qc + j - kt)
                            self.act(lambda e, pT=pT, S_=S_, j=j, bcol=bcol: e.activation(out=pT[:, j * 128:(j + 1) * 128], in_=S_[:, j * 128:(j + 1) * 128], func=AF.Exp,
                                                                                       bias=self.cst[:, bcol:bcol + 1], scale=0.125),
                                     r=[S_t, self.cst_t], w=[pTt])
                        if G == 0 and qc == 0 and hl == 0 and kt == 0:
                            self.dump(S_[:, 0:512], S_t, psum=True)
                            self.dump(pT, pTt)
                            self.dump(va[:, 0, 0, :], vat)
                            self.dump(qz[:, 0, :], qzt)
                            self.dump(kT[:, 0, 0:128], kTt)
                            self.dump(Lm[:, 0, :], Lmt)
                            self.dump(R[:, 0:128], Rt)
                            self.dump(va[:, 0, :, :].rearrange("p h c -> p (h c)"), vat)
                        for j in range(j0, 4):
                            self.mm(O[:, j * 65:(j + 1) * 65], pT[:, j * 128:(j + 1) * 128], va[:, kt, hl, :], first, kt == 4 * qc + j,
                                    [pTt, vat], [Ot], skip_group_check=True)
                            first = False
                    if G == 0 and qc == 0 and hl == 0:
                        self.dump(O[:, 0:260], Ot, psum=True)
                    sm, smt = sm_r.next()
                    self.dve(lambda e, sm=sm, O=O: e.reciprocal(out=sm[:, 0:4], in_=O[:, 0:260].rearrange("p (j c) -> p j c", j=4)[:, :, 64]), r=[Ot], w=[smt])
                    for j in range(4):
                        if j % 2:
                            self.dve(lambda e, sm=sm, O=O, j=j, hl=hl: e.tensor_scalar(out=otok[:, j, hl * 64:(hl + 1) * 64], in0=O[:, j * 65:j * 65 + 64],
                                                                                    scalar1=sm[:, j:j + 1], scalar2=None, op0=ALU.mult), r=[Ot, smt], w=[otokt])
                        else:
                            self.act(lambda e, sm=sm, O=O, j=j, hl=hl: e.activation(out=otok[:, j, hl * 64:(hl + 1) * 64], in_=O[:, j * 65:j * 65 + 64],
                                                                                 func=AF.Copy, scale=sm[:, j:j + 1]), r=[Ot, smt], w=[otokt])
                if G == 0 and qc == 0:
                    self.dump(otok.rearrange("p j c -> p (j c)"), otokt)
                for j in range(4):
                    i = qc * 4 + j
                    self.transpose_to(oTb[:, :, j * 128:(j + 1) * 128], [otok[:, j, c * 128:(c + 1) * 128] for c in range(4)], [otokt], [oTbt])
                    self.out_proj(i, lambda k, j=j: oTb[:, k, j * 128:(j + 1) * 128], 4, [oTbt], woG, woGt, first=(G == 0))
                    if G == 1:
                        self.layer_norm(i, lnp)


def build_program(stages):
    nc = bass.Bass("TRN2", target_bir_lowering=False)
    k = K(nc, stages)
    k.build()
    return nc


FULL_STAGES = [("mix0",), ("xattn", 0), ("ffn", 0), ("moba",), ("xattn", 1), ("ffn", 1)]


def make_in_maps(inputs, ncores=NCORES):
    f = lambda a: np.ascontiguousarray(np.asarray(a, dtype=np.float32))
    x = f(inputs["x"])
    mem = f(inputs["mem"])
    shared = dict(
        ln_g=f(inputs["ln_g"]).reshape(6, D),
        ln_b=f(inputs["ln_b"]).reshape(6, D),
        x_wq=f(inputs["x_wq"]), x_wkv=f(inputs["x_wkv"]), x_wo=f(inputs["x_wo"]),
        ffn_w_in=f(inputs["ffn_w_in"]), ffn_w_out=f(inputs["ffn_w_out"]),
        ev_w_in=f(inputs["ev_w_in"])[0], ev_w_out=f(inputs["ev_w_out"])[0],
        a_wsT=f(np.transpose(np.asarray(inputs["a_ws"])[0], (2, 0, 1))),
        a_bs=f(inputs["a_bs"]).reshape(1, 512),
        a_ln_g=f(inputs["a_ln_g"]).reshape(1, 512),
        a_ln_b=f(inputs["a_ln_b"]).reshape(1, 512),
        b_norm_gT=f(np.asarray(inputs["b_norm_g"]).reshape(4, 128).T),
        lb_logitsT=f(np.transpose(np.asarray(inputs["hgrn_lb_logits"]).reshape(2, 4, 128), (2, 0, 1))),
        od_w_qkv=f(inputs["od_w_qkv"])[0], od_w_out=f(inputs["od_w_out"])[0],
        consts=make_consts(),
    )
    maps = []
    for c in range(ncores):
        m = dict(shared)
        m["x"] = x[c]
        m["mem"] = mem[c]
        maps.append(m)
    return maps


def kernel(**inputs):
    nc = build_program(FULL_STAGES)
    in_maps = make_in_maps(inputs)
    res = run_bass_kernel_spmd(nc, in_maps, core_ids=list(range(NCORES)))
    return np.stack([np.asarray(r["out"], dtype=np.float32) for r in res.results], axis=0)
```

```python
import math
import os
from contextlib import ExitStack

import numpy as np
import concourse.bass as bass
import concourse.mybir as mybir
from concourse.bass_utils import run_bass_kernel_spmd

F32 = mybir.dt.float32
BF16 = mybir.dt.bfloat16
AF = mybir.ActivationFunctionType
ALU = mybir.AluOpType
AX = mybir.AxisListType

T = 2048
D = 1024
NT = T // 128
KC = D // 128
MEM = 256
DFF = 2816
NJ = DFF // 128
ALPHA = 4.0 ** 0.25
EPS = 1e-5
NCORES = 8


class Tok:
    __slots__ = ("name", "w", "r")

    def __init__(self, name=""):
        self.name = name
        self.w = None
        self.r = []


class Op:
    __slots__ = ("eng", "fn", "deps", "sig", "is_dma", "need", "idx", "guard", "cost", "stage", "nleft", "users", "ready", "fin")

    def __init__(self, eng, fn, is_dma, cost):
        self.eng = eng
        self.fn = fn
        self.deps = []
        self.sig = None
        self.is_dma = is_dma
        self.need = False
        self.guard = None
        self.cost = cost
        self.users = []


DEFAULT_COST = {"pe": 0.06, "act": 0.45, "dve": 0.35, "pool": 0.6, "sp": 0.1}
SCHED_WINDOW = 48


class Prog:
    ENGS = ("pe", "act", "dve", "pool", "sp")

    def __init__(self):
        self.all = []
        self.stage = 0

    def fence(self):
        self.stage += 1

    def op(self, eng, fn, reads=(), writes=(), dma=False, cost=None):
        if cost is None:
            cost = 3.0 if dma else DEFAULT_COST[eng]
        o = Op(eng, fn, dma, cost)
        o.stage = self.stage
        o.idx = len(self.all)
        deps = []
        for t in reads:
            if t.w is not None:
                deps.append(t.w)
        for t in writes:
            if t.w is not None:
                deps.append(t.w)
            deps.extend(t.r)
        seen = set()
        for d in deps:
            if id(d) in seen or d is o:
                continue
            seen.add(id(d))
            o.deps.append(d)
            d.users.append(o)
        for t in reads:
            t.r.append(o)
        for t in writes:
            t.w = o
            t.r = []
        self.all.append(o)
        return o

    def schedule(self):
        order = {e: [] for e in self.ENGS}
        nst = self.stage + 1
        stages = [[] for _ in range(nst)]
        for o in self.all:
            stages[o.stage].append(o)
        t_stage = 0.0
        for ops in stages:
            if not ops:
                continue
            inst = set(id(o) for o in ops)
            pend = {e: [] for e in self.ENGS}
            for o in ops:
                o.nleft = sum(1 for d in o.deps if id(d) in inst)
                o.ready = t_stage
                pend[o.eng].append(o)
            free = {e: t_stage for e in self.ENGS}
            n = len(ops)
            tmax = t_stage
            while n:
                best = None
                for e in self.ENGS:
                    lst = pend[e]
                    cnt = 0
                    for o in lst:
                        if o.nleft == 0:
                            st = o.ready if o.ready > free[e] else free[e]
                            if best is None or st < best[0] - 1e-9:
                                best = (st, o)
                        cnt += 1
                        if cnt >= SCHED_WINDOW:
                            break
                st, o = best
                e = o.eng
                pend[e].remove(o)
                if o.is_dma:
                    free[e] = st + 0.1
                    o.fin = st + o.cost
                else:
                    o.fin = st + o.cost
                    free[e] = o.fin
                tmax = max(tmax, o.fin)
                for u in o.users:
                    if id(u) in inst:
                        u.nleft -= 1
                        lat = 0.05 if (u.eng == e and not o.is_dma) else 0.35
                        if o.fin + lat > u.ready:
                            u.ready = o.fin + lat
                order[e].append(o)
                n -= 1
            t_stage = tmax
        self.est_us = t_stage
        return order

    def emit(self, nc, engines, sems, dma_sems):
        order = self.schedule()
        last_by_stage = {}
        for e in self.ENGS:
            for o in order[e]:
                last_by_stage[(e, o.stage)] = o
        for e in self.ENGS:
            prev_stage = None
            for o in order[e]:
                if o.stage != prev_stage:
                    for e2 in self.ENGS:
                        cands = [v for (ee, st), v in last_by_stage.items() if ee == e2 and st < o.stage]
                        if cands:
                            d = max(cands, key=lambda v: v.stage)
                            if d is not o and d not in o.deps:
                                o.deps.append(d)
                    prev_stage = o.stage
        for e in self.ENGS:
            for o in order[e]:
                for d in o.deps:
                    if (not d.is_dma) and (not o.is_dma) and d.eng == o.eng and o.eng == "pe":
                        continue
                    d.need = True
        NDS = {q: len(dma_sems[q]) for q in dma_sems}
        all_dma = {q: [] for q in dma_sems}
        for e in self.ENGS:
            cnt = 0
            k = 0
            for o in order[e]:
                if o.is_dma:
                    s = dma_sems[e][k % NDS[e]]
                    gen = k // NDS[e]
                    o.sig = (s, 16 * (gen + 1))
                    o.guard = (s, 16 * gen) if gen > 0 else None
                    all_dma[e].append(o)
                    k += 1
                elif o.need:
                    cnt += 1
                    o.sig = (sems[e], cnt)

        def run(e, eng):
            waited = {}
            for o in order[e]:
                need = {}
                if o.guard is not None:
                    need[o.guard[0]] = o.guard[1]
                for d in o.deps:
                    if (not d.is_dma) and (not o.is_dma) and d.eng == e and e == "pe":
                        continue
                    s, v = d.sig
                    if need.get(s, 0) < v:
                        need[s] = v
                for s, v in need.items():
                    if waited.get(s, 0) >= v:
                        continue
                    eng.wait_ge(s, v)
                    waited[s] = v
                ins = o.fn(eng)
                if o.is_dma:
                    ins.then_inc(o.sig[0], 16)
                elif o.sig is not None:
                    ins.then_inc(o.sig[0], 1)
            return waited

        with nc.Block() as block:
            @block.tensor
            def _(pe):
                run("pe", pe)

            @block.scalar
            def _(act):
                run("act", act)

            @block.vector
            def _(dve):
                run("dve", dve)

            @block.gpsimd
            def _(pool):
                run("pool", pool)

            @block.sync
            def _(sp):
                w = run("sp", sp)
                for q, lst in all_dma.items():
                    last = {}
                    for o in lst:
                        last[o.sig[0]] = o.sig[1]
                    for s, v in last.items():
                        if w.get(s, 0) < v:
                            sp.wait_ge(s, v)


GELU_NATIVE = False
DEBUG_ZERO = bool(os.environ.get('DEBUG_ZERO'))
ARENA_BYTES = 98 * 1024

C_IDENT = 0
C_TRI = 128
C_M2 = 256
C_MC = 384
C_LCOL = 640
C_SBB = 768
C_ALB = 1280
C_NTRI = C_ALB + 256
CONST_W = C_NTRI + 128


def alibi_slope(h):
    return float(np.float32(2.0 ** (-8.0 * (h + 1) / 16)))


def _bf16_round(v):
    a = np.asarray(v, np.float32).reshape(1)
    u = a.view(np.uint32)
    r = ((u + 0x7FFF + ((u >> 16) & 1)) & 0xFFFF0000).astype(np.uint32)
    return float(r.view(np.float32)[0])


def make_consts():
    c = np.zeros((128, CONST_W), np.float32)
    p = np.arange(128)
    c[:, C_IDENT:C_IDENT + 128] = np.eye(128, dtype=np.float32)
    c[:, C_TRI:C_TRI + 128] = (p[:, None] <= p[None, :]).astype(np.float32)
    c[:, C_M2:C_M2 + 128] = ((p[:, None] <= p[None, :]) & ((p[:, None] // 64) == (p[None, :] // 64))).astype(np.float32)
    c[:, C_NTRI:C_NTRI + 128] = np.where(p[:, None] <= p[None, :], 0.0, -30000.0).astype(np.float32)
    t = np.arange(256)
    c[:, C_MC:C_MC + 256] = (t % 64 != 0).astype(np.float32)[None, :]
    for G in range(2):
        for hl in range(8):
            sl = alibi_slope(G * 8 + hl)
            hi = _bf16_round(sl)
            lo = _bf16_round(sl - hi)
            for kb in range(8):
                col = C_LCOL + G * 64 + hl * 8 + kb
                c[hl * 16 + kb, col] = 1.0
    for j in range(4):
        for hl in range(8):
            base = C_SBB + j * 128 + hl * 16
            c[:, base + 8] = (j % 2) * 128 + p
            c[:, base + 9] = 256 * (j // 2)
            c[:, base + 10] = (j % 2) * 128 + p
            c[:, base + 11] = 256 * (j // 2)
    for h in range(16):
        sl = alibi_slope(h)
        for m in range(16):
            c[:, C_ALB + h * 16 + m] = np.float32(sl) * (p - 64.0 - 128.0 * m).astype(np.float32)
    return c


class Ring:
    def __init__(self, items):
        self.items = items
        self.i = 0

    def next(self):
        it = self.items[self.i % len(self.items)]
        self.i += 1
        return it


class K:
    def __init__(self, nc, stages):
        self.nc = nc
        self.P = Prog()
        self.es = ExitStack()
        self.stages = stages
        self.uid = 0

    def sb(self, shape, dt, name=None):
        self.uid += 1
        return self.es.enter_context(self.nc.sbuf_tensor(f"{name or 't'}_{self.uid}", list(shape), dt))

    def salloc(self, shape, dt):
        n = 1
        for s in shape[1:]:
            n *= s
        nbytes = n * (4 if dt == F32 else 2)
        off = (self.aoff + 31) // 32 * 32
        self.aoff = off + nbytes
        if os.environ.get('DEBUG_ALLOC'):
            print('salloc', shape, off, off + nbytes)
        assert self.aoff <= ARENA_BYTES, f"arena overflow {self.aoff}"
        v = self.arena[:, off // 2:(off + nbytes) // 2]
        if dt == F32:
            v = v.bitcast(F32)
        if len(shape) == 3:
            v = v.rearrange("p (a b) -> p a b", a=shape[1])
        elif len(shape) == 4:
            v = v.rearrange("p (a b c) -> p a b c", a=shape[1], b=shape[2])
        if shape[0] < 128:
            v = v[0:shape[0]]
        return v

    def ring(self, n, shape, dt, name="r"):
        return Ring([(self.salloc(shape, dt), Tok(name)) for _ in range(n)])

    def begin_stage(self):
        self.aoff = 0
        self.P.fence()

    def dram_in(self, name, shape, dt=F32):
        return self.nc.dram_tensor(name, list(shape), dt, kind="ExternalInput").ap()

    def pe(self, fn, r=(), w=()):
        return self.P.op("pe", fn, r, w)

    def act(self, fn, r=(), w=()):
        return self.P.op("act", fn, r, w)

    def dve(self, fn, r=(), w=()):
        return self.P.op("dve", fn, r, w)

    def pool(self, fn, r=(), w=()):
        return self.P.op("pool", fn, r, w)

    def dma(self, q, out, in_, r=(), w=()):
        return self.P.op(q, lambda e: e.dma_start(out=out, in_=in_), r, w, dma=True)

    def dump(self, ap, tok, psum=False):
        if not os.environ.get('DEBUG_DUMP'):
            return
        n = ap.shape[-1]
        c0 = self.dump_off
        self.dump_off += n
        print("DUMP", c0, n)
        if psum:
            scr = self.salloc([128, n], F32)
            st = Tok("dscr")
            self.dve(lambda e: e.tensor_copy(out=scr, in_=ap), r=[tok], w=[st])
            self.dma("pool", self.d["dbg"][:, c0:c0 + n], scr, r=[st])
        else:
            self.dma("pool", self.d["dbg"][:, c0:c0 + n], ap, r=[tok])

    def bank(self):
        b = self.banks[self.bank_i % 6]
        self.bank_i += 1
        return b

    def obank(self):
        b = self.banks[6 + self.obank_i % 2]
        self.obank_i += 1
        return b

    def mm(self, out, lhsT, rhs, start, stop, r, w, **kw):
        n = out.shape[-1]
        return self.P.op("pe", lambda e: e.matmul(out, lhsT=lhsT, rhs=rhs, start=start, stop=stop, **kw), r, w, cost=max(n, 64) / 2400.0 + 0.012)

    def mmB(self, ps, pst, W, wt, col0, tok0, n):
        xts = [self.xT_t[q] for q in range(tok0 // 128, (tok0 + n + 127) // 128)]
        for kc in range(KC):
            self.mm(ps, W[:, kc, col0:col0 + 128], self.xT[:, kc, tok0:tok0 + n], kc == 0, kc == KC - 1, [wt] + xts, [pst])

    def mmA(self, ps, pst, i, W, wt, col0, n):
        for kc in range(KC):
            self.mm(ps, self.xT[:, kc, i * 128:(i + 1) * 128], W[:, kc, col0:col0 + n], kc == 0, kc == KC - 1,
                    [wt, self.xT_t[i]], [pst])

    def wload(self, dst, src, tok):
        self.dma("pool", dst, src, w=[tok])

    def wload_w(self, dst, W_d, col0, n, tok, step=512):
        for c in range(0, n, step):
            m = min(step, n - c)
            self.wload(dst[:, :, c:c + m], W_d[:, col0 + c:col0 + c + m].rearrange("(k p) n -> p k n", p=128), tok)

    def build(self):
        nc = self.nc
        d = {}
        d["x"] = self.dram_in("x", [T, D])
        d["mem"] = self.dram_in("mem", [MEM, D])
        d["lng"] = self.dram_in("ln_g", [6, D])
        d["lnb"] = self.dram_in("ln_b", [6, D])
        d["xwq"] = self.dram_in("x_wq", [2, D, D])
        d["xwkv"] = self.dram_in("x_wkv", [2, D, 2 * D])
        d["xwo"] = self.dram_in("x_wo", [2, D, D])
        d["fwi"] = self.dram_in("ffn_w_in", [2, D, 2 * DFF])
        d["fwo"] = self.dram_in("ffn_w_out", [2, DFF, D])
        d["evi"] = self.dram_in("ev_w_in", [D, 3072])
        d["evo"] = self.dram_in("ev_w_out", [D, D])
        d["awsT"] = self.dram_in("a_wsT", [128, 4, 128])
        d["abs"] = self.dram_in("a_bs", [1, 512])
        d["alng"] = self.dram_in("a_ln_g", [1, 512])
        d["alnb"] = self.dram_in("a_ln_b", [1, 512])
        d["bng"] = self.dram_in("b_norm_gT", [128, 4])
        d["lbl"] = self.dram_in("lb_logitsT", [128, 2, 4])
        d["oqkv"] = self.dram_in("od_w_qkv", [D, 3072])
        d["owo"] = self.dram_in("od_w_out", [D, D])
        d["cst"] = self.dram_in("consts", [128, CONST_W])
        d["out"] = nc.dram_tensor("out", [T, D], F32, kind="ExternalOutput").ap()
        if os.environ.get('DEBUG_DUMP'):
            d["dbg"] = nc.dram_tensor("dbg", [128, 8192], F32, kind="ExternalOutput").ap()
        self.dump_off = 0
        self.d = d

        self.x_tok = self.sb([128, NT, D], F32, "x_tok")
        self.xT = self.sb([128, KC, T], BF16, "xT")
        self.xtok_t = [Tok(f"xtok{i}") for i in range(NT)]
        self.xT_t = [Tok(f"xT{i}") for i in range(NT)]
        self.cst = self.sb([128, CONST_W], F32, "cst")
        self.cst_t = Tok("cst")
        self.ident = self.sb([128, 128], BF16, "ident")
        self.ident_t = Tok("ident")
        self.xb_ring = Ring([(self.sb([128, D], BF16, "xb"), Tok("xb")) for _ in range(2)])
        self.st_ring = Ring([(self.sb([128, 32], F32, "lnst"), Tok("lnst")) for _ in range(3)])
        self.negh = self.sb([128, 512], F32, "negh")
        self.negh_t = Tok("negh")
        self.arena = self.sb([128, ARENA_BYTES // 2], BF16, "arena")
        self.aoff = 0
        self.banks = []
        for i in range(8):
            pt = self.es.enter_context(nc.psum_tensor(f"bank{i}", [128, 512], F32))
            self.banks.append((pt, Tok(f"bank{i}")))
        self.bank_i = 0
        self.obank_i = 0
        self.cp_i = 0

        self.dma("sp", self.cst[:], d["cst"], w=[self.cst_t])
        self.dve(lambda e: e.tensor_copy(out=self.ident[:], in_=self.cst[:, C_IDENT:C_IDENT + 128]),
                 r=[self.cst_t], w=[self.ident_t])
        self.pool(lambda e: e.memset(self.negh[:], -0.5), w=[self.negh_t])

        self.load_x()
        for s in self.stages:
            getattr(self, "stage_" + s[0])(*s[1:])
        self.P.fence()
        self.store_out()
        self.P.fence()

        sems = {e: self.es.enter_context(nc.semaphore(f"s_{e}")) for e in Prog.ENGS}
        dma_sems = {}
        for q, n in (("sp", 24), ("pool", 32), ("act", 2)):
            dma_sems[q] = [self.es.enter_context(nc.semaphore(f"d_{q}{i}")) for i in range(n)]
        self.P.emit(nc, None, sems, dma_sems)
        self.es.close()

    def copy(self, out, in_, r, w):
        self.cp_i += 1
        if self.cp_i % 2:
            return self.act(lambda e: e.copy(out=out, in_=in_), r=r, w=w)
        return self.dve(lambda e: e.tensor_copy(out=out, in_=in_), r=r, w=w)

    def load_x(self):
        for i in range(NT):
            self.dma("sp", self.x_tok[:, i, :], self.d["x"][i * 128:(i + 1) * 128, :], w=[self.xtok_t[i]])
        for i in range(NT):
            self.to_xT(i)

    def store_out(self):
        for i in range(NT):
            self.dma("sp", self.d["out"][i * 128:(i + 1) * 128, :], self.x_tok[:, i, :], r=[self.xtok_t[i]])

    def transpose_to(self, dst_view, src_tiles, r, w):
        bk, bt = self.bank()
        psb = bk[:].bitcast(BF16)
        n = len(src_tiles)
        for k, src in enumerate(src_tiles):
            self.pe(lambda e, k=k, src=src: e.transpose(out=psb[:, k * 128:(k + 1) * 128], in_=src, identity=self.ident[:]),
                    r=list(r) + [self.ident_t], w=[bt])
        self.copy(dst_view, psb[:, 0:n * 128].rearrange("p (k t) -> p k t", k=n), [bt], w)

    def to_xT(self, i):
        xb, xbt = self.xb_ring.next()
        self.act(lambda e: e.copy(out=xb[:], in_=self.x_tok[:, i, :]), r=[self.xtok_t[i]], w=[xbt])
        self.transpose_to(self.xT[:, :, i * 128:(i + 1) * 128], [xb[:, kc * 128:(kc + 1) * 128] for kc in range(KC)],
                          [xbt], [self.xT_t[i]])

    def load_ln(self, idx):
        g = self.salloc([128, D], F32)
        b = self.salloc([128, D], F32)
        t = Tok("lnp")
        self.dma("sp", g, self.d["lng"][idx:idx + 1, :].to_broadcast([128, D]), w=[t])
        self.dma("sp", b, self.d["lnb"][idx:idx + 1, :].to_broadcast([128, D]), w=[t])
        return g, b, t

    def rstd_small(self, out, var, r, w, eps=EPS):
        n = out.shape[-1]
        self.pool(lambda e: e.tensor_scalar(out=out, in0=var, scalar1=eps, scalar2=None, op0=ALU.add), r=r, w=w)
        self.pool(lambda e: e.tensor_tensor(out=out, in0=out, in1=self.negh[:, 0:n], op=ALU.pow), r=list(w) + [self.negh_t], w=w)

    def layer_norm(self, i, lnp):
        g, b, gt = lnp
        xt = self.x_tok[:, i, :]
        xtok = self.xtok_t[i]
        stt, stk = self.st_ring.next()
        self.dve(lambda e: e.bn_stats(out=stt[:, 0:6], in_=self.x_tok[:, i, 0:512]), r=[xtok], w=[stk])
        self.dve(lambda e: e.bn_stats(out=stt[:, 6:12], in_=self.x_tok[:, i, 512:1024]), r=[xtok], w=[stk])
        self.dve(lambda e: e.bn_aggr(out=stt[:, 12:14], in_=stt[:, 0:12].rearrange("p (a b) -> p a b", a=2)), r=[stk], w=[stk])
        self.rstd_small(stt[:, 15:16], stt[:, 13:14], [stk], [stk])
        self.dve(lambda e: e.scalar_tensor_tensor(out=stt[:, 16:17], in0=stt[:, 12:13], scalar=-1.0, in1=stt[:, 15:16],
                                                  op0=ALU.mult, op1=ALU.mult), r=[stk], w=[stk])
        self.act(lambda e: e.activation(out=xt, in_=xt, func=AF.Identity, bias=stt[:, 16:17], scale=stt[:, 15:16]),
                 r=[stk, xtok], w=[xtok])
        self.pool(lambda e: e.tensor_tensor(out=xt, in0=xt, in1=g, op=ALU.mult), r=[xtok, gt], w=[xtok])
        self.dve(lambda e: e.tensor_tensor(out=xt, in0=xt, in1=b, op=ALU.add), r=[xtok, gt], w=[xtok])
        self.to_xT(i)

    def accum(self, i, half, ps, pst, first):
        dst = self.x_tok[:, i, half * 512:(half + 1) * 512]
        if first:
            self.dve(lambda e: e.scalar_tensor_tensor(out=dst, in0=dst, scalar=ALPHA, in1=ps, op0=ALU.mult, op1=ALU.add),
                     r=[pst, self.xtok_t[i]], w=[self.xtok_t[i]])
        else:
            self.dve(lambda e: e.tensor_tensor(out=dst, in0=dst, in1=ps, op=ALU.add),
                     r=[pst, self.xtok_t[i]], w=[self.xtok_t[i]])

    def out_proj(self, i, lhs_fn, nk, lhs_toks, wo, wot, first):
        for half in range(2):
            ps, pst = self.bank()
            for k in range(nk):
                self.mm(ps[:], lhs_fn(k), wo[:, k, half * 512:(half + 1) * 512], k == 0, k == nk - 1, list(lhs_toks) + [wot], [pst])
            self.accum(i, half, ps[:], pst, first)

    def stage_ln_only(self, idx):
        self.begin_stage()
        lnp = self.load_ln(idx)
        for i in range(NT):
            self.layer_norm(i, lnp)

    def stage_ffn(self, l):
        self.begin_stage()
        groups = [list(range(0, 6)), list(range(6, 12)), list(range(12, 17)), list(range(17, 22))]
        fwi = self.d["fwi"][l]
        fwo = self.d["fwo"][l]
        lnp = self.load_ln(l * 3 + 2)
        wi_r = self.ring(3, [128, KC, 256], BF16, "fwi")
        wo_r = self.ring(2, [128, 6, D], BF16, "fwo")
        hT, hTt = self.salloc([128, 6, T], BF16), Tok("hT")
        sg_r = self.ring(2, [128, 512], F32, "sg")
        for gi, js in enumerate(groups):
            wo, wot = wo_r.next()
            j0 = js[0]
            self.wload(wo[:, 0:len(js), :], fwo[j0 * 128:(j0 + len(js)) * 128, :].rearrange("(j p) n -> p j n", p=128), wot)
            for jl, j in enumerate(js):
                wi, wit = wi_r.next()
                self.wload(wi[:, :, 0:128], fwi[:, j * 128:(j + 1) * 128].rearrange("(k p) n -> p k n", p=128), wit)
                self.wload(wi[:, :, 128:256], fwi[:, DFF + j * 128:DFF + (j + 1) * 128].rearrange("(k p) n -> p k n", p=128), wit)
                for tb in range(4):
                    pg, pgt = self.bank()
                    pu, put = self.bank()
                    self.mmB(pg[:], pgt, wi, wit, 0, tb * 512, 512)
                    self.mmB(pu[:], put, wi, wit, 128, tb * 512, 512)
                    sg, sgt = sg_r.next()
                    self.act(lambda e, sg=sg, pg=pg: e.activation(out=sg, in_=pg[:], func=AF.Silu), r=[pgt], w=[sgt])
                    self.dve(lambda e, sg=sg, pu=pu, jl=jl, tb=tb: e.tensor_tensor(out=hT[:, jl, tb * 512:(tb + 1) * 512], in0=sg, in1=pu[:], op=ALU.mult),
                             r=[sgt, put], w=[hTt])
            last = gi == len(groups) - 1
            for i in range(NT):
                self.out_proj(i, lambda k, i=i: hT[:, k, i * 128:(i + 1) * 128], len(js), [hTt], wo, wot, first=(gi == 0))
                if last:
                    self.layer_norm(i, lnp)

    def stage_xattn(self, l):
        self.begin_stage()
        d = self.d
        SC = 1.0 / 16.0
        lnp = self.load_ln(l * 3 + 1)
        wq, wqt = self.salloc([128, KC, D], BF16), Tok("wq")
        wo, wot = self.salloc([128, KC, D], BF16), Tok("wo")
        kT, kTt = self.salloc([128, KC, MEM], BF16), Tok("kT")
        vS, vSt = self.salloc([128, 2, D], BF16), Tok("vS")
        memb, membt = self.salloc([128, 2, D], BF16), Tok("memb")
        memT, memTt = self.salloc([128, KC, MEM], BF16), Tok("memT")
        wkv_r = self.ring(1, [128, KC, 512], BF16, "wkv")
        qT_r = self.ring(1, [128, KC, 512], BF16, "qT")
        p32_r = self.ring(2, [128, 4, 256], F32, "p32")
        pb_r = self.ring(2, [128, 4, 256], BF16, "pb")
        pT_r = self.ring(2, [128, 8, 128], BF16, "pT")
        oT_r = self.ring(2, [128, 8, 128], BF16, "oT")
        sm_r = self.ring(3, [128, 16], F32, "sm")
        self.wload(memb, d["mem"].rearrange("(m p) n -> p m n", p=128), membt)
        for mt in range(2):
            self.transpose_to(memT[:, :, mt * 128:(mt + 1) * 128], [memb[:, mt, kc * 128:(kc + 1) * 128] for kc in range(KC)],
                              [membt], [memTt])
        for c in range(4):
            wk, wkt = wkv_r.next()
            self.wload_w(wk, d["xwkv"][l], c * 512, 512, wkt)
            if c < 2:
                for cc in range(4):
                    fc = c * 4 + cc
                    ps, pst = self.bank()
                    for kc in range(KC):
                        self.mm(ps[:, 0:MEM], wk[:, kc, cc * 128:(cc + 1) * 128], memT[:, kc, :], kc == 0, kc == KC - 1, [wkt, memTt], [pst])
                    self.copy(kT[:, fc, :], ps[:, 0:MEM], [pst], [kTt])
            else:
                for mt in range(2):
                    ps, pst = self.bank()
                    for kc in range(KC):
                        self.mm(ps[:], memT[:, kc, mt * 128:(mt + 1) * 128], wk[:, kc, :], kc == 0, kc == KC - 1, [wkt, memTt], [pst])
                    self.copy(vS[:, mt, (c - 2) * 512:(c - 1) * 512], ps[:], [pst], [vSt])
        self.wload_w(wq, d["xwq"][l], 0, D, wqt)
        self.wload_w(wo, d["xwo"][l], 0, D, wot)
        for tb in range(4):
            qT, qTt = qT_r.next()
            for c in range(KC):
                ps, pst = self.bank()
                self.mmB(ps[:], pst, wq, wqt, c * 128, tb * 512, 512)
                self.copy(qT[:, c, :], ps[:], [pst], [qTt])
            for il in range(4):
                i = tb * 4 + il
                sA, sAt = self.bank()
                sB, sBt = self.bank()
                for h in range(4):
                    bk, bkt = (sA, sAt) if h < 2 else (sB, sBt)
                    for k2 in range(2):
                        self.mm(bk[:, (h % 2) * 256:(h % 2 + 1) * 256], qT[:, 2 * h + k2, il * 128:(il + 1) * 128], kT[:, 2 * h + k2, :],
                                k2 == 0, k2 == 1, [qTt, kTt], [bkt])
                sm, smt = sm_r.next()
                self.dve(lambda e, sm=sm, sA=sA: e.tensor_reduce(out=sm[:, 0:2], in_=sA[:].rearrange("p (h m) -> p h m", h=2), axis=AX.X, op=ALU.max), r=[sAt], w=[smt])
                self.dve(lambda e, sm=sm, sB=sB: e.tensor_reduce(out=sm[:, 2:4], in_=sB[:].rearrange("p (h m) -> p h m", h=2), axis=AX.X, op=ALU.max), r=[sBt], w=[smt])
                self.dve(lambda e, sm=sm: e.tensor_scalar(out=sm[:, 4:8], in0=sm[:, 0:4], scalar1=-SC, scalar2=None, op0=ALU.mult), r=[smt], w=[smt])
                p32, p32t = p32_r.next()
                for h in range(4):
                    bk, bkt = (sA, sAt) if h < 2 else (sB, sBt)
                    self.act(lambda e, h=h, bk=bk, sm=sm, p32=p32: e.activation(out=p32[:, h, :], in_=bk[:, (h % 2) * 256:(h % 2 + 1) * 256], func=AF.Exp,
                                                                             bias=sm[:, 4 + h:5 + h], scale=SC, accum_out=sm[:, 8 + h:9 + h]),
                             r=[bkt, smt], w=[p32t, smt])
                self.dve(lambda e, sm=sm: e.reciprocal(out=sm[:, 12:16], in_=sm[:, 8:12]), r=[smt], w=[smt])
                pb, pbt = pb_r.next()
                for h in range(4):
                    self.pool(lambda e, h=h, pb=pb, p32=p32, sm=sm: e.tensor_scalar(out=pb[:, h, :], in0=p32[:, h, :], scalar1=sm[:, 12 + h:13 + h], scalar2=None, op0=ALU.mult),
                              r=[p32t, smt], w=[pbt])
                pT, pTt = pT_r.next()
                self.transpose_to(pT, [pb[:, h, mt * 128:(mt + 1) * 128] for h in range(4) for mt in range(2)], [pbt], [pTt])
                oA, oAt = self.bank()
                oB, oBt = self.bank()
                oT, oTt = oT_r.next()
                for c in range(8):
                    bk, bkt = (oA, oAt) if c < 4 else (oB, oBt)
                    h = c // 2
                    for mt in range(2):
                        self.mm(bk[:, (c % 4) * 128:(c % 4 + 1) * 128], vS[:, mt, c * 128:(c + 1) * 128], pT[:, h * 2 + mt, :],
                                mt == 0, mt == 1, [vSt, pTt], [bkt])
                self.copy(oT[:, 0:4, :], oA[:].rearrange("p (c t) -> p c t", c=4), [oAt], [oTt])
                self.copy(oT[:, 4:8, :], oB[:].rearrange("p (c t) -> p c t", c=4), [oBt], [oTt])
                self.out_proj(i, lambda k, oT=oT: oT[:, k, :], KC, [oTt], wo, wot, first=True)
                self.layer_norm(i, lnp)

    def gelu(self, out, ps, pst, outt, scr_r, half=True):
        if GELU_NATIVE:
            self.act(lambda e: e.activation(out=out, in_=ps, func=AF.Gelu_apprx_tanh), r=[pst], w=[outt])
            return 1.0
        C0 = 0.7978845608028654
        C1 = 0.044715
        s, stk = scr_r.next()
        n = ps.shape[-1]
        sv = s[:, 0:n]
        self.act(lambda e: e.activation(out=sv, in_=ps, func=AF.Square), r=[pst], w=[stk])
        self.dve(lambda e: e.tensor_scalar(out=sv, in0=sv, scalar1=C1, scalar2=1.0, op0=ALU.mult, op1=ALU.add), r=[stk], w=[stk])
        self.dve(lambda e: e.tensor_tensor(out=sv, in0=sv, in1=ps, op=ALU.mult), r=[stk, pst], w=[stk])
        self.act(lambda e: e.activation(out=sv, in_=sv, func=AF.Tanh, scale=C0), r=[stk], w=[stk])
        self.dve(lambda e: e.scalar_tensor_tensor(out=out, in0=sv, scalar=1.0, in1=ps, op0=ALU.add, op1=ALU.mult), r=[stk, pst], w=[outt])
        return 0.5

    def stage_mix0(self):
        d = self.d
        self.begin_stage()
        wuv, wut, wvt = self.salloc([128, KC, 1024], BF16), Tok("wu"), Tok("wv")
        woA, woAt = self.salloc([128, 4, D], BF16), Tok("woA")
        ws32, ws32t = self.salloc([128, 4, 128], F32), Tok("ws32")
        wsb, wsbt = self.salloc([128, 4, 128], BF16), Tok("wsb")
        bsB, bsBt = self.salloc([128, 512], F32), Tok("bsB")
        lgB, lgBt = self.salloc([128, 512], F32), Tok("lgB")
        lbB, lbBt = self.salloc([128, 512], F32), Tok("lbB")
        uT_r = self.ring(2, [128, 4, 512], BF16, "uT")
        scr_r = self.ring(3, [128, 512], F32, "gscr")
        vg_r = self.ring(2, [128, 512], F32, "vg")
        vb_r = self.ring(2, [128, 512], BF16, "vb")
        ss_r = self.ring(2, [128, 512], F32, "ssum")
        ya_r = self.ring(2, [128, 4, 128], BF16, "yaT")
        st_r = self.ring(2, [128, 48], F32, "gst")
        self.wload_w(wuv[:, :, 0:512], d["evi"], 0, 512, wut)
        self.wload_w(wuv[:, :, 512:1024], d["evi"], 512, 512, wvt)
        self.wload(woA, d["evo"][0:512, :].rearrange("(g p) n -> p g n", p=128), woAt)
        self.dma("sp", ws32, d["awsT"], w=[ws32t])
        self.dma("sp", bsB, d["abs"].to_broadcast([128, 512]), w=[bsBt])
        self.dma("sp", lgB, d["alng"].to_broadcast([128, 512]), w=[lgBt])
        self.dma("sp", lbB, d["alnb"].to_broadcast([128, 512]), w=[lbBt])
        for g in range(4):
            self.dve(lambda e, g=g: e.tensor_tensor(out=wsb[:, g, :], in0=ws32[:, g, :], in1=self.cst[:, C_TRI:C_TRI + 128], op=ALU.mult),
                     r=[ws32t, self.cst_t], w=[wsbt])
        for tb in range(4):
            uT, uTt = uT_r.next()
            gf = 1.0
            for c in range(4):
                ps, pst = self.bank()
                self.mmB(ps[:], pst, wuv, wut, c * 128, tb * 512, 512)
                gf = self.gelu(uT[:, c, :], ps[:], pst, uTt, scr_r)
            for il in range(4):
                i = tb * 4 + il
                ps, pst = self.bank()
                self.mmA(ps[:], pst, i, wuv, wvt, 512, 512)
                vg, vgt = vg_r.next()
                gv = self.gelu(vg, ps[:], pst, vgt, scr_r)
                st, stt = st_r.next()
                for g in range(4):
                    self.dve(lambda e, g=g, st=st, vg=vg: e.bn_stats(out=st[:, g * 6:(g + 1) * 6], in_=vg[:, g * 128:(g + 1) * 128]), r=[vgt], w=[stt])
                for g in range(4):
                    self.dve(lambda e, g=g, st=st: e.bn_aggr(out=st[:, 24 + g * 2:26 + g * 2], in_=st[:, g * 6:(g + 1) * 6]), r=[stt], w=[stt])
                mv = st[:, 24:32].rearrange("p (g two) -> p g two", two=2)
                self.pool(lambda e, st=st, mv=mv: e.tensor_scalar(out=st[:, 32:36], in0=mv[:, :, 1], scalar1=gv * gv, scalar2=EPS, op0=ALU.mult, op1=ALU.add), r=[stt], w=[stt])
                self.pool(lambda e, st=st: e.tensor_tensor(out=st[:, 32:36], in0=st[:, 32:36], in1=self.negh[:, 0:4], op=ALU.pow), r=[stt, self.negh_t], w=[stt])
                self.dve(lambda e, st=st: e.tensor_scalar(out=st[:, 36:40], in0=st[:, 32:36], scalar1=gv, scalar2=None, op0=ALU.mult), r=[stt], w=[stt])
                self.dve(lambda e, st=st, mv=mv: e.scalar_tensor_tensor(out=st[:, 40:44], in0=mv[:, :, 0], scalar=-1.0, in1=st[:, 36:40], op0=ALU.mult, op1=ALU.mult), r=[stt], w=[stt])
                for g in range(4):
                    self.act(lambda e, g=g, st=st, vg=vg: e.activation(out=vg[:, g * 128:(g + 1) * 128], in_=vg[:, g * 128:(g + 1) * 128], func=AF.Identity,
                                                                      bias=st[:, 40 + g:41 + g], scale=st[:, 36 + g:37 + g]), r=[stt, vgt], w=[vgt])
                self.dve(lambda e, vg=vg: e.tensor_tensor(out=vg, in0=vg, in1=lgB, op=ALU.mult), r=[vgt, lgBt], w=[vgt])
                vb, vbt = vb_r.next()
                self.pool(lambda e, vg=vg, vb=vb: e.tensor_tensor(out=vb, in0=vg, in1=lbB, op=ALU.add), r=[vgt, lbBt], w=[vbt])
                pss, psst = self.bank()
                for g in range(4):
                    self.mm(pss[:, g * 128:(g + 1) * 128], vb[:, g * 128:(g + 1) * 128], wsb[:, g, :], True, True, [vbt, wsbt], [psst])
                ss, sst = ss_r.next()
                self.dve(lambda e, ss=ss, pss=pss: e.tensor_tensor(out=ss, in0=pss[:], in1=bsB, op=ALU.add), r=[psst, bsBt], w=[sst])
                ya, yat = ya_r.next()
                self.dve(lambda e, ya=ya, uT=uT, ss=ss, il=il: e.scalar_tensor_tensor(out=ya, in0=uT[:, :, il * 128:(il + 1) * 128], scalar=gf,
                                                                                   in1=ss.rearrange("p (g t) -> p g t", g=4), op0=ALU.mult, op1=ALU.mult),
                         r=[uTt, sst], w=[yat])
                self.out_proj(i, lambda k, ya=ya: ya[:, k, :], 4, [yat], woA, woAt, first=True)

        self.begin_stage()
        NB = 256
        lnp = self.load_ln(0)
        wB, wBt = self.salloc([128, KC, 2048], BF16), [Tok("wBq"), Tok("wBf"), Tok("wBi"), Tok("wBg")]
        woB, woBt = self.salloc([128, 4, D], BF16), Tok("woB")
        sm, smt = self.salloc([128, 32], F32), Tok("hsm")
        S, St = self.salloc([128, 4, 128], F32), Tok("S")
        Sb_r = self.ring(3, [128, 4, 128], BF16, "Sb")
        m2x4, m2t = self.salloc([128, 512], BF16), Tok("m2x4")
        ones, onest = self.salloc([128, 128], BF16), Tok("ones")
        f1 = self.ring(1, [128, 4, NB], F32, "f1")
        f2 = self.ring(1, [128, 4, NB], F32, "f2")
        f3 = self.ring(1, [128, 4, NB], F32, "f3")
        E_r = self.ring(1, [128, 4, NB], F32, "E")
        qd_r = self.ring(1, [128, 4, NB], BF16, "qdT")
        kd_r = self.ring(1, [128, 4, NB], BF16, "kdT")
        gs_r = self.ring(1, [128, 4, NB], BF16, "gsT")
        vi_r = self.ring(1, [128, 2, 512], BF16, "vi")
        kt_r = self.ring(1, [128, 2, 512], BF16, "kdtok")
        am_r = self.ring(2, [128, 512], BF16, "am")
        tmp_r = self.ring(1, [128, 512], F32, "stmp")
        sq_r = self.ring(2, [128, 512], BF16, "sq")
        r_r = self.ring(1, [128, 512], F32, "rr")
        t1_r = self.ring(1, [128, 512], F32, "t1")
        yb_r = self.ring(2, [128, 4, 128], BF16, "ybT")
        for c in range(4):
            self.wload_w(wB[:, :, c * 512:(c + 1) * 512], d["evi"], 1024 + c * 512, 512, wBt[c])
        self.wload(woB, d["evo"][512:1024, :].rearrange("(g p) n -> p g n", p=128), woBt)
        self.dma("sp", sm[:, 0:8], d["lbl"].rearrange("p l h -> p (l h)"), w=[smt])
        self.dma("sp", sm[:, 20:24], d["bng"], w=[smt])
        self.dve(lambda e: e.tensor_tensor(out=sm[:, 8:12], in0=sm[:, 0:4], in1=sm[:, 4:8], op=ALU.subtract), r=[smt], w=[smt])
        self.act(lambda e: e.activation(out=sm[:, 12:16], in_=sm[:, 8:12], func=AF.Sigmoid), r=[smt], w=[smt])
        self.dve(lambda e: e.tensor_scalar(out=sm[:, 16:20], in0=sm[:, 12:16], scalar1=-1.0, scalar2=1.0, op0=ALU.mult, op1=ALU.add), r=[smt], w=[smt])
        self.pool(lambda e: e.memset(S.rearrange("p h v -> p (h v)"), 0.0), w=[St])
        Sb, Sbt = Sb_r.next()
        self.pool(lambda e, Sb=Sb: e.memset(Sb.rearrange("p h v -> p (h v)"), 0.0), w=[Sbt])
        self.pool(lambda e: e.memset(ones, 1.0), w=[onest])
        for h in range(4):
            self.dve(lambda e, h=h: e.tensor_copy(out=m2x4[:, h * 128:(h + 1) * 128], in_=self.cst[:, C_M2:C_M2 + 128]), r=[self.cst_t], w=[m2t])
        maskc = self.cst[:, C_MC:C_MC + NB]
        for blk in range(T // NB):
            tok0 = blk * NB
            s1, s1t = f1.next()
            s2, s2t = f2.next()
            s3, s3t = f3.next()
            E, Et = E_r.next()
            qd, qdt = qd_r.next()
            kd, kdt = kd_r.next()
            gs, gst = gs_r.next()
            qf = []
            for h in range(4):
                b1, b1t = self.bank()
                self.mmB(b1[:, 0:NB], b1t, wB, wBt[0], h * 128, tok0, NB)
                self.mmB(b1[:, NB:2 * NB], b1t, wB, wBt[1], 512 + h * 128, tok0, NB)
                qf.append((b1, b1t))
                self.act(lambda e, h=h, b1=b1, s1=s1: e.activation(out=s1[:, h, :], in_=b1[:, NB:2 * NB], func=AF.Sigmoid), r=[b1t], w=[s1t])
            for h in range(4):
                self.dve(lambda e, h=h, s1=s1: e.tensor_scalar(out=s1[:, h, :], in0=s1[:, h, :], scalar1=sm[:, 16 + h:17 + h], scalar2=sm[:, 12 + h:13 + h],
                                                              op0=ALU.mult, op1=ALU.add), r=[s1t, smt], w=[s1t])
            self.act(lambda e, s1=s1, s2=s2: e.activation(out=s2, in_=s1, func=AF.Ln), r=[s1t], w=[s2t])
            for h in range(4):
                self.dve(lambda e, h=h, s2=s2, s3=s3: e.tensor_tensor_scan(out=s3[:, h, :], data0=maskc, data1=s2[:, h, :], initial=0.0, op0=ALU.mult, op1=ALU.add),
                         r=[s2t, self.cst_t], w=[s3t])
            self.pool(lambda e, s1=s1: e.tensor_scalar(out=s1, in0=s1, scalar1=-1.0, scalar2=1.0, op0=ALU.mult, op1=ALU.add), r=[s1t], w=[s1t])
            self.act(lambda e, E=E, s3=s3: e.activation(out=E, in_=s3, func=AF.Exp), r=[s3t], w=[Et])
            self.act(lambda e, s2=s2, s3=s3: e.activation(out=s2, in_=s3, func=AF.Exp, scale=-1.0), r=[s3t, s2t], w=[s2t])
            for h in range(4):
                b1, b1t = qf[h]
                self.dve(lambda e, h=h, b1=b1, E=E, qd=qd: e.tensor_tensor(out=qd[:, h, :], in0=b1[:, 0:NB], in1=E[:, h, :], op=ALU.mult), r=[b1t, Et], w=[qdt])
            self.pool(lambda e, s1=s1, s2=s2, kd=kd: e.tensor_tensor(out=kd, in0=s1, in1=s2, op=ALU.mult), r=[s1t, s2t], w=[kdt])
            for h in range(0, 4, 2):
                b2, b2t = self.bank()
                self.mmB(b2[:, 0:NB], b2t, wB, wBt[3], 1536 + h * 128, tok0, NB)
                self.mmB(b2[:, NB:2 * NB], b2t, wB, wBt[3], 1536 + (h + 1) * 128, tok0, NB)
                self.act(lambda e, h=h, b2=b2, gs=gs: e.activation(out=gs[:, h:h + 2, :], in_=b2[:].rearrange("p (a t) -> p a t", a=2), func=AF.Silu), r=[b2t], w=[gst])
            vi, vit = vi_r.next()
            ktk, ktkt = kt_r.next()
            for il in range(2):
                b3, b3t = self.bank()
                self.mmA(b3[:], b3t, blk * 2 + il, wB, wBt[2], 1024, 512)
                self.act(lambda e, il=il, b3=b3, vi=vi: e.activation(out=vi[:, il, :], in_=b3[:], func=AF.Silu), r=[b3t], w=[vit])
                self.transpose_to(ktk[:, il, :].rearrange("p (h k) -> p h k", h=4), [kd[:, h, il * 128:(il + 1) * 128] for h in range(4)], [kdt], [ktkt])
            for il in range(2):
                i = blk * 2 + il
                tc0 = il * 128
                U = [self.bank(), self.bank()]
                for c in range(2):
                    for h in range(4):
                        self.mm(U[c][0][:, h * 128:(h + 1) * 128], ktk[c * 64:(c + 1) * 64, il, h * 128:(h + 1) * 128],
                                vi[c * 64:(c + 1) * 64, il, h * 128:(h + 1) * 128], True, True, [ktkt, vit], [U[c][1]])
                A, At = self.bank()
                for h in range(4):
                    self.mm(A[:, h * 128:(h + 1) * 128], kd[:, h, tc0:tc0 + 128], qd[:, h, tc0:tc0 + 128], True, True, [kdt, qdt], [At])
                am, amt = am_r.next()
                self.dve(lambda e, am=am, A=A: e.tensor_tensor(out=am, in0=A[:], in1=m2x4, op=ALU.mult), r=[At, m2t], w=[amt])
                Sbs = [(Sb, Sbt)]
                for c in range(2):
                    tmp, tmpt = tmp_r.next()
                    Uc, Uct = U[c]
                    self.dve(lambda e, tmp=tmp, Uc=Uc: e.tensor_tensor(out=tmp, in0=Uc[:], in1=S.rearrange("p h v -> p (h v)"), op=ALU.add), r=[Uct, St], w=[tmpt])
                    Sb, Sbt = Sb_r.next()
                    col = tc0 + c * 64 + 63
                    for h in range(4):
                        self.dve(lambda e, h=h, tmp=tmp, E=E, col=col: e.tensor_scalar(out=S[:, h, :], in0=tmp[:, h * 128:(h + 1) * 128], scalar1=E[:, h, col:col + 1],
                                                                                    scalar2=None, op0=ALU.mult), r=[tmpt, Et], w=[St])
                        self.act(lambda e, h=h, tmp=tmp, E=E, col=col, Sb=Sb: e.activation(out=Sb[:, h, :], in_=tmp[:, h * 128:(h + 1) * 128], func=AF.Copy,
                                                                                        scale=E[:, h, col:col + 1]), r=[tmpt, Et], w=[Sbt])
                    Sbs.append((Sb, Sbt))
                O, Ot = self.bank()
                for h in range(4):
                    self.mm(O[:, h * 128:(h + 1) * 128], vi[:, il, h * 128:(h + 1) * 128], am[:, h * 128:(h + 1) * 128], True, False, [vit, amt], [Ot])
                    for c in range(2):
                        Sc, Sct = Sbs[c]
                        self.mm(O[:, h * 128 + c * 64:h * 128 + (c + 1) * 64], Sc[:, h, :], qd[:, h, tc0 + c * 64:tc0 + (c + 1) * 64], False, c == 1,
                                [Sct, qdt], [Ot], skip_group_check=True)
                sq, sqt = sq_r.next()
                self.act(lambda e, sq=sq, O=O: e.activation(out=sq, in_=O[:], func=AF.Square), r=[Ot], w=[sqt])
                Q, Qt = self.bank()
                self.mm(Q[:], ones, sq, True, True, [onest, sqt], [Qt])
                rr, rrt = r_r.next()
                self.dve(lambda e, rr=rr, Q=Q: e.tensor_scalar(out=rr, in0=Q[:], scalar1=1.0 / 128.0, scalar2=EPS, op0=ALU.mult, op1=ALU.add), r=[Qt], w=[rrt])
                self.pool(lambda e, rr=rr: e.tensor_tensor(out=rr, in0=rr, in1=self.negh[:], op=ALU.pow), r=[rrt, self.negh_t], w=[rrt])
                t1, t1t = t1_r.next()
                self.dve(lambda e, t1=t1, O=O, rr=rr: e.tensor_tensor(out=t1, in0=O[:], in1=rr, op=ALU.mult), r=[Ot, rrt], w=[t1t])
                yb, ybt = yb_r.next()
                for h in range(4):
                    self.dve(lambda e, h=h, yb=yb, t1=t1, gs=gs, tc0=tc0: e.scalar_tensor_tensor(out=yb[:, h, :], in0=t1[:, h * 128:(h + 1) * 128], scalar=sm[:, 20 + h:21 + h],
                                                                                              in1=gs[:, h, tc0:tc0 + 128], op0=ALU.mult, op1=ALU.mult),
                             r=[t1t, gst, smt], w=[ybt])
                self.out_proj(i, lambda k, yb=yb: yb[:, k, :], 4, [ybt], woB, woBt, first=False)
                self.layer_norm(i, lnp)

    def stage_moba(self):
        for G in range(2):
            self.moba_group(G)

    def moba_group(self, G):
        d = self.d
        if True:
            self.begin_stage()
            if DEBUG_ZERO:
                zt = Tok("z")
                lo, hi = [int(v) for v in os.environ['DEBUG_ZERO'].split(':')]
                self.pool(lambda e: e.memset(self.arena[:, lo // 2:hi // 2], 0.0), w=[zt])
                self.P.fence()
            lnp = self.load_ln(3) if G == 1 else None
            kT, kTt = self.salloc([128, 4, T], BF16), Tok("kT")
            va, vat = self.salloc([128, NT, 8, 65], BF16), Tok("va")
            wq, wqt = self.salloc([128, KC, 512], BF16), Tok("wq")
            woG, woGt = self.salloc([128, 4, D], BF16), Tok("woG")
            wk_r = self.ring(2, [128, KC, 256], BF16, "wk")
            qz_r = self.ring(1, [128, 8, 512], BF16, "qz")
            R_r = self.ring(1, [128, 512], BF16, "R")
            Rc, Rct = self.salloc([128, 512], BF16), Tok("Rc")
            pT_r = self.ring(3, [128, 512], BF16, "pT")
            otok, otokt = self.salloc([128, 4, 512], BF16), Tok("otok")
            oTb, oTbt = self.salloc([128, 4, 512], BF16), Tok("oTb")
            Lm_r = self.ring(2, [128, 8, 128], BF16, "Lm")
            lcol, lcolt = self.salloc([128, 64], BF16), Tok("lcol")
            tri, trit = self.salloc([128, 128], BF16), Tok("tri")
            SBb, SBbt = self.salloc([128, 4, 128], BF16), Tok("SBb")
            SB_r = self.ring(4, [128, 128], BF16, "SB")
            km32, km32t = self.salloc([128, 4, 8], F32), Tok("km32")
            kmh, kmht = self.salloc([128, 4, 8], BF16), Tok("kmh")
            kml, kmlt = self.salloc([128, 4, 8], BF16), Tok("kml")
            aff_r = self.ring(2, [128, 8, 8], F32, "aff")
            cmp_, cmpt = self.salloc([128, 8, 8, 8], F32), Tok("cmp")
            cnt, cntt = self.salloc([128, 8, 8], F32), Tok("cnt")
            sm_r = self.ring(2, [128, 8], F32, "msm")
            zb = os.environ.get('DEBUG_ZBUF')
            if zb:
                cand = dict(qz=[qz_r.items[0][0]], pT=[x[0] for x in pT_r.items], Lm=[x[0] for x in Lm_r.items], otok=[otok], oTb=[oTb],
                            R=[R_r.items[0][0]], SB=[x[0] for x in SB_r.items], kT=[kT], va=[va], small=[km32, kmh, kml, cmp_, cnt] + [x[0] for x in aff_r.items] + [x[0] for x in sm_r.items])
                for ap in cand[zb]:
                    nd = len(ap.shape)
                    flat = ap if nd == 2 else ap.rearrange({3: "p a b -> p (a b)", 4: "p a b c -> p (a b c)"}[nd])
                    self.pool(lambda e, flat=flat: e.memset(flat, float(os.environ.get('DEBUG_ZVAL', '0'))), w=[Tok("z")])
                self.P.fence()
            self.dve(lambda e: e.tensor_copy(out=lcol, in_=self.cst[:, C_LCOL + G * 64:C_LCOL + (G + 1) * 64]), r=[self.cst_t], w=[lcolt])
            self.dve(lambda e: e.tensor_copy(out=tri, in_=self.cst[:, C_TRI:C_TRI + 128]), r=[self.cst_t], w=[trit])
            self.dve(lambda e: e.tensor_copy(out=SBb, in_=self.cst[:, C_SBB:C_SBB + 512].rearrange("p (j c) -> p j c", j=4)), r=[self.cst_t], w=[SBbt])
            self.transpose_to(Rc.rearrange("p (j t) -> p j t", j=4), [SBb[:, j, :] for j in range(4)], [SBbt], [Rct])
            if G == 0:
                self.dump(Rc[:, 0:128], Rct)
                self.dump(SBb[:, 0, :], SBbt)
            for s_ in range(qz_r.items.__len__()):
                qz0, qz0t = qz_r.items[s_]
                self.pool(lambda e, qz0=qz0: e.memset(qz0.rearrange("p h t -> p (h t)"), 0.0), w=[qz0t])
            self.pool(lambda e: e.memset(va.rearrange("p a b c -> p (a b c)"), 1.0), w=[vat])
            self.wload_w(wq, d["oqkv"], G * 512, 512, wqt)
            self.wload(woG, d["owo"][G * 512:(G + 1) * 512, :].rearrange("(g p) n -> p g n", p=128), woGt)
            for c2 in range(2):
                wk, wkt = wk_r.next()
                self.wload_w(wk, d["oqkv"], 1024 + G * 512 + c2 * 256, 256, wkt, step=256)
                for pp in range(2):
                    p = c2 * 2 + pp
                    for tb in range(4):
                        ps, pst = self.bank()
                        self.mmB(ps[:], pst, wk, wkt, pp * 128, tb * 512, 512)
                        self.copy(kT[:, p, tb * 512:(tb + 1) * 512], ps[:], [pst], [kTt])
            for c2 in range(2):
                wk, wkt = wk_r.next()
                self.wload_w(wk, d["oqkv"], 2048 + G * 512 + c2 * 256, 256, wkt, step=256)
                for i in range(NT):
                    ps, pst = self.bank()
                    self.mmA(ps[:, 0:256], pst, i, wk, wkt, 0, 256)
                    self.copy(va[:, i, c2 * 4:(c2 + 1) * 4, 0:64], ps[:, 0:256].rearrange("p (h c) -> p h c", h=4), [pst], [vat])
            for p in range(4):
                self.dve(lambda e, p=p: e.tensor_reduce(out=km32[:, p, :], in_=kT[:, p, :].rearrange("p (b t) -> p b t", b=8), axis=AX.X, op=ALU.add), r=[kTt], w=[km32t])
            self.dve(lambda e: e.tensor_scalar(out=km32, in0=km32, scalar1=1.0 / 256.0, scalar2=None, op0=ALU.mult), r=[km32t], w=[km32t])
            self.dve(lambda e: e.tensor_copy(out=kmh, in_=km32), r=[km32t], w=[kmht])
            self.dve(lambda e: e.tensor_tensor(out=km32, in0=km32, in1=kmh, op=ALU.subtract), r=[km32t, kmht], w=[km32t])
            self.dve(lambda e: e.tensor_copy(out=kml, in_=km32), r=[km32t], w=[kmlt])

            for qc in range(4):
                qz, qzt = qz_r.next()
                for p in range(4):
                    ps, pst = self.bank()
                    self.mmB(ps[:], pst, wq, wqt, p * 128, qc * 512, 512)
                    self.act(lambda e, p=p, ps=ps, qz=qz: e.copy(out=qz[0:64, 2 * p, :], in_=ps[0:64, :]), r=[pst], w=[qzt])
                    self.dve(lambda e, p=p, ps=ps, qz=qz: e.tensor_copy(out=qz[64:128, 2 * p + 1, :], in_=ps[64:128, :]), r=[pst], w=[qzt])
                if qc < 2:
                    R, Rt = Rc, Rct
                else:
                    R, Rt = R_r.next()
                    sbs = []
                    for j in range(4):
                        qb = (qc * 4 + j) // 2
                        ab, abt = self.bank()
                        for hl in range(8):
                            self.mm(ab[:, hl * 8:(hl + 1) * 8], qz[:, hl, j * 128:(j + 1) * 128], kmh[:, hl // 2, :], True, False, [qzt, kmht], [abt])
                            self.mm(ab[:, hl * 8:(hl + 1) * 8], qz[:, hl, j * 128:(j + 1) * 128], kml[:, hl // 2, :], False, True, [qzt, kmlt], [abt])
                        aff, afft = aff_r.next()
                        self.dve(lambda e, aff=aff, ab=ab: e.tensor_copy(out=aff, in_=ab[:, 0:64].rearrange("p (h k) -> p h k", h=8)), r=[abt], w=[afft])
                        self.dve(lambda e, aff=aff, qb=qb: e.tensor_tensor(out=cmp_[:, :, 0:qb, 0:qb],
                                                                        in0=aff[:, :, 0:qb].unsqueeze(2).to_broadcast([128, 8, qb, qb]),
                                                                        in1=aff[:, :, 0:qb].unsqueeze(3).to_broadcast([128, 8, qb, qb]), op=ALU.is_gt),
                                 r=[afft], w=[cmpt])
                        self.dve(lambda e, qb=qb: e.tensor_reduce(out=cnt[:, :, 0:qb], in_=cmp_[:, :, 0:qb, 0:qb], axis=AX.X, op=ALU.add), r=[cmpt], w=[cntt])
                        SB, SBt = SB_r.next()
                        self.pool(lambda e, SB=SB, j=j: e.tensor_copy(out=SB, in_=SBb[:, j, :]), r=[SBbt], w=[SBt])
                        self.dve(lambda e, SB=SB, qb=qb: e.tensor_scalar(out=SB.rearrange("p (h s) -> p h s", h=8)[:, :, 0:qb], in0=cnt[:, :, 0:qb],
                                                                      scalar1=3.0, scalar2=-32768.0, op0=ALU.is_ge, op1=ALU.mult), r=[cntt, SBt], w=[SBt])
                        sbs.append((SB, SBt))
                    self.transpose_to(R.rearrange("p (j t) -> p j t", j=4), [s[0] for s in sbs], [s[1] for s in sbs], [Rt])
                nkt = 4 * qc + 4
                for hl in range(8):
                    h = G * 8 + hl
                    Lm, Lmt = Lm_r.next()
                    for kb in range(2 * qc + 2):
                        self.pool(lambda e, Lm=Lm, kb=kb, hl=hl: e.tensor_copy(out=Lm[:, kb, :], in_=lcol[:, hl * 8 + kb:hl * 8 + kb + 1].to_broadcast([128, 128])),
                                  r=[lcolt], w=[Lmt])
                    O, Ot = self.obank()
                    first = True
                    for kt in range(nkt):
                        j0 = max(0, kt - 4 * qc)
                        c0 = j0 * 128
                        kb = kt // 2
                        S_, S_t = self.bank()
                        self.mm(S_[:, c0:512], kT[:, hl // 2, kt * 128:(kt + 1) * 128], qz[:, hl, c0:512], True, False, [kTt, qzt], [S_t])
                        self.mm(S_[:, c0:512], Lm[:, kb, :], R[:, c0:512], False, True, [Lmt, Rt], [S_t])
                        if kt >= 4 * qc:
                            self.dve(lambda e, S_=S_, c0=c0: e.tensor_tensor(out=S_[:, c0:c0 + 128], in0=S_[:, c0:c0 + 128], in1=self.cst[:, C_NTRI:C_NTRI + 128], op=ALU.add),
                                     r=[S_t, self.cst_t], w=[S_t])
                        pT, pTt = pT_r.next()
                        for j in range(j0, 4):
                            bcol = C_ALB + h * 16 + (4 * qc + j - kt)
                            self.act(lambda e, pT=pT, S_=S_, j=j, bcol=bcol: e.activation(out=pT[:, j * 128:(j + 1) * 128], in_=S_[:, j * 128:(j + 1) * 128], func=AF.Exp,
                                                                                       bias=self.cst[:, bcol:bcol + 1], scale=0.125),
                                     r=[S_t, self.cst_t], w=[pTt])
                        if G == 0 and qc == 0 and hl == 0 and kt == 0:
                            self.dump(S_[:, 0:512], S_t, psum=True)
                            self.dump(pT, pTt)
                            self.dump(va[:, 0, 0, :], vat)
                            self.dump(qz[:, 0, :], qzt)
                            self.dump(kT[:, 0, 0:128], kTt)
                            self.dump(Lm[:, 0, :], Lmt)
                            self.dump(R[:, 0:128], Rt)
                            self.dump(va[:, 0, :, :].rearrange("p h c -> p (h c)"), vat)
                        for j in range(j0, 4):
                            self.mm(O[:, j * 65:(j + 1) * 65], pT[:, j * 128:(j + 1) * 128], va[:, kt, hl, :], first, kt == 4 * qc + j,
                                    [pTt, vat], [Ot], skip_group_check=True)
                            first = False
                    if G == 0 and qc == 0 and hl == 0:
                        self.dump(O[:, 0:260], Ot, psum=True)
                    sm, smt = sm_r.next()
                    self.dve(lambda e, sm=sm, O=O: e.reciprocal(out=sm[:, 0:4], in_=O[:, 0:260].rearrange("p (j c) -> p j c", j=4)[:, :, 64]), r=[Ot], w=[smt])
                    for j in range(4):
                        if j % 2:
                            self.dve(lambda e, sm=sm, O=O, j=j, hl=hl: e.tensor_scalar(out=otok[:, j, hl * 64:(hl + 1) * 64], in0=O[:, j * 65:j * 65 + 64],
                                                                                    scalar1=sm[:, j:j + 1], scalar2=None, op0=ALU.mult), r=[Ot, smt], w=[otokt])
                        else:
                            self.act(lambda e, sm=sm, O=O, j=j, hl=hl: e.activation(out=otok[:, j, hl * 64:(hl + 1) * 64], in_=O[:, j * 65:j * 65 + 64],
                                                                                 func=AF.Copy, scale=sm[:, j:j + 1]), r=[Ot, smt], w=[otokt])
                if G == 0 and qc == 0:
                    self.dump(otok.rearrange("p j c -> p (j c)"), otokt)
                for j in range(4):
                    i = qc * 4 + j
                    self.transpose_to(oTb[:, :, j * 128:(j + 1) * 128], [otok[:, j, c * 128:(c + 1) * 128] for c in range(4)], [otokt], [oTbt])
                    self.out_proj(i, lambda k, j=j: oTb[:, k, j * 128:(j + 1) * 128], 4, [oTbt], woG, woGt, first=(G == 0))
                    if G == 1:
                        self.layer_norm(i, lnp)


def build_program(stages):
    nc = bass.Bass("TRN2", target_bir_lowering=False)
    k = K(nc, stages)
    k.build()
    return nc


FULL_STAGES = [("mix0",), ("xattn", 0), ("ffn", 0), ("moba",), ("xattn", 1), ("ffn", 1)]


def make_in_maps(inputs, ncores=NCORES):
    f = lambda a: np.ascontiguousarray(np.asarray(a, dtype=np.float32))
    x = f(inputs["x"])
    mem = f(inputs["mem"])
    shared = dict(
        ln_g=f(inputs["ln_g"]).reshape(6, D),
        ln_b=f(inputs["ln_b"]).reshape(6, D),
        x_wq=f(inputs["x_wq"]), x_wkv=f(inputs["x_wkv"]), x_wo=f(inputs["x_wo"]),
        ffn_w_in=f(inputs["ffn_w_in"]), ffn_w_out=f(inputs["ffn_w_out"]),
        ev_w_in=f(inputs["ev_w_in"])[0], ev_w_out=f(inputs["ev_w_out"])[0],
        a_wsT=f(np.transpose(np.asarray(inputs["a_ws"])[0], (2, 0, 1))),
        a_bs=f(inputs["a_bs"]).reshape(1, 512),
        a_ln_g=f(inputs["a_ln_g"]).reshape(1, 512),
        a_ln_b=f(inputs["a_ln_b"]).reshape(1, 512),
        b_norm_gT=f(np.asarray(inputs["b_norm_g"]).reshape(4, 128).T),
        lb_logitsT=f(np.transpose(np.asarray(inputs["hgrn_lb_logits"]).reshape(2, 4, 128), (2, 0, 1))),
        od_w_qkv=f(inputs["od_w_qkv"])[0], od_w_out=f(inputs["od_w_out"])[0],
        consts=make_consts(),
    )
    maps = []
    for c in range(ncores):
        m = dict(shared)
        m["x"] = x[c]
        m["mem"] = mem[c]
        maps.append(m)
    return maps


def kernel(**inputs):
    nc = build_program(FULL_STAGES)
    in_maps = make_in_maps(inputs)
    res = run_bass_kernel_spmd(nc, in_maps, core_ids=list(range(NCORES)))
    return np.stack([np.asarray(r["out"], dtype=np.float32) for r in res.results], axis=0)
```

```python
import math
import os
from contextlib import ExitStack

import numpy as np
import concourse.bass as bass
import concourse.mybir as mybir
from concourse.bass_utils import run_bass_kernel_spmd

F32 = mybir.dt.float32
BF16 = mybir.dt.bfloat16
AF = mybir.ActivationFunctionType
ALU = mybir.AluOpType
AX = mybir.AxisListType

T = 2048
D = 1024
NT = T // 128
KC = D // 128
MEM = 256
DFF = 2816
NJ = DFF // 128
ALPHA = 4.0 ** 0.25
EPS = 1e-5
NCORES = 8


class Tok:
    __slots__ = ("name", "w", "r")

    def __init__(self, name=""):
        self.name = name
        self.w = None
        self.r = []


class Op:
    __slots__ = ("eng", "fn", "deps", "sig", "is_dma", "need", "idx", "guard", "cost", "stage", "nleft", "users", "ready", "fin")

    def __init__(self, eng, fn, is_dma, cost):
        self.eng = eng
        self.fn = fn
        self.deps = []
        self.sig = None
        self.is_dma = is_dma
        self.need = False
        self.guard = None
        self.cost = cost
        self.users = []


DEFAULT_COST = {"pe": 0.06, "act": 0.45, "dve": 0.35, "pool": 0.6, "sp": 0.1}
SCHED_WINDOW = 48


class Prog:
    ENGS = ("pe", "act", "dve", "pool", "sp")

    def __init__(self):
        self.all = []
        self.stage = 0

    def fence(self):
        self.stage += 1

    def op(self, eng, fn, reads=(), writes=(), dma=False, cost=None):
        if cost is None:
            cost = 3.0 if dma else DEFAULT_COST[eng]
        o = Op(eng, fn, dma, cost)
        o.stage = self.stage
        o.idx = len(self.all)
        deps = []
        for t in reads:
            if t.w is not None:
                deps.append(t.w)
        for t in writes:
            if t.w is not None:
                deps.append(t.w)
            deps.extend(t.r)
        seen = set()
        for d in deps:
            if id(d) in seen or d is o:
                continue
            seen.add(id(d))
            o.deps.append(d)
            d.users.append(o)
        for t in reads:
            t.r.append(o)
        for t in writes:
            t.w = o
            t.r = []
        self.all.append(o)
        return o

    def schedule(self):
        order = {e: [] for e in self.ENGS}
        nst = self.stage + 1
        stages = [[] for _ in range(nst)]
        for o in self.all:
            stages[o.stage].append(o)
        t_stage = 0.0
        for ops in stages:
            if not ops:
                continue
            inst = set(id(o) for o in ops)
            pend = {e: [] for e in self.ENGS}
            for o in ops:
                o.nleft = sum(1 for d in o.deps if id(d) in inst)
                o.ready = t_stage
                pend[o.eng].append(o)
            free = {e: t_stage for e in self.ENGS}
            n = len(ops)
            tmax = t_stage
            while n:
                best = None
                for e in self.ENGS:
                    lst = pend[e]
                    cnt = 0
                    for o in lst:
                        if o.nleft == 0:
                            st = o.ready if o.ready > free[e] else free[e]
                            if best is None or st < best[0] - 1e-9:
                                best = (st, o)
                        cnt += 1
                        if cnt >= SCHED_WINDOW:
                            break
                st, o = best
                e = o.eng
                pend[e].remove(o)
                if o.is_dma:
                    free[e] = st + 0.1
                    o.fin = st + o.cost
                else:
                    o.fin = st + o.cost
                    free[e] = o.fin
                tmax = max(tmax, o.fin)
                for u in o.users:
                    if id(u) in inst:
                        u.nleft -= 1
                        lat = 0.05 if (u.eng == e and not o.is_dma) else 0.35
                        if o.fin + lat > u.ready:
                            u.ready = o.fin + lat
                order[e].append(o)
                n -= 1
            t_stage = tmax
        self.est_us = t_stage
        return order

    def emit(self, nc, engines, sems, dma_sems):
        order = self.schedule()
        last_by_stage = {}
        for e in self.ENGS:
            for o in order[e]:
                last_by_stage[(e, o.stage)] = o
        for e in self.ENGS:
            prev_stage = None
            for o in order[e]:
                if o.stage != prev_stage:
                    for e2 in self.ENGS:
                        cands = [v for (ee, st), v in last_by_stage.items() if ee == e2 and st < o.stage]
                        if cands:
                            d = max(cands, key=lambda v: v.stage)
                            if d is not o and d not in o.deps:
                                o.deps.append(d)
                    prev_stage = o.stage
        for e in self.ENGS:
            for o in order[e]:
                for d in o.deps:
                    if (not d.is_dma) and (not o.is_dma) and d.eng == o.eng and o.eng == "pe":
                        continue
                    d.need = True
        NDS = {q: len(dma_sems[q]) for q in dma_sems}
        all_dma = {q: [] for q in dma_sems}
        for e in self.ENGS:
            cnt = 0
            k = 0
            for o in order[e]:
                if o.is_dma:
                    s = dma_sems[e][k % NDS[e]]
                    gen = k // NDS[e]
                    o.sig = (s, 16 * (gen + 1))
                    o.guard = (s, 16 * gen) if gen > 0 else None
                    all_dma[e].append(o)
                    k += 1
                elif o.need:
                    cnt += 1
                    o.sig = (sems[e], cnt)

        def run(e, eng):
            waited = {}
            for o in order[e]:
                need = {}
                if o.guard is not None:
                    need[o.guard[0]] = o.guard[1]
                for d in o.deps:
                    if (not d.is_dma) and (not o.is_dma) and d.eng == e and e == "pe":
                        continue
                    s, v = d.sig
                    if need.get(s, 0) < v:
                        need[s] = v
                for s, v in need.items():
                    if waited.get(s, 0) >= v:
                        continue
                    eng.wait_ge(s, v)
                    waited[s] = v
                ins = o.fn(eng)
                if o.is_dma:
                    ins.then_inc(o.sig[0], 16)
                elif o.sig is not None:
                    ins.then_inc(o.sig[0], 1)
            return waited

        with nc.Block() as block:
            @block.tensor
            def _(pe):
                run("pe", pe)

            @block.scalar
            def _(act):
                run("act", act)

            @block.vector
            def _(dve):
                run("dve", dve)

            @block.gpsimd
            def _(pool):
                run("pool", pool)

            @block.sync
            def _(sp):
                w = run("sp", sp)
                for q, lst in all_dma.items():
                    last = {}
                    for o in lst:
                        last[o.sig[0]] = o.sig[1]
                    for s, v in last.items():
                        if w.get(s, 0) < v:
                            sp.wait_ge(s, v)


GELU_NATIVE = False
DEBUG_ZERO = bool(os.environ.get('DEBUG_ZERO'))
ARENA_BYTES = 98 * 1024

C_IDENT = 0
C_TRI = 128
C_M2 = 256
C_MC = 384
C_LCOL = 640
C_SBB = 768
C_ALB = 1280
C_NTRI = C_ALB + 256
CONST_W = C_NTRI + 128


def alibi_slope(h):
    return float(np.float32(2.0 ** (-8.0 * (h + 1) / 16)))


def _bf16_round(v):
    a = np.asarray(v, np.float32).reshape(1)
    u = a.view(np.uint32)
    r = ((u + 0x7FFF + ((u >> 16) & 1)) & 0xFFFF0000).astype(np.uint32)
    return float(r.view(np.float32)[0])


def make_consts():
    c = np.zeros((128, CONST_W), np.float32)
    p = np.arange(128)
    c[:, C_IDENT:C_IDENT + 128] = np.eye(128, dtype=np.float32)
    c[:, C_TRI:C_TRI + 128] = (p[:, None] <= p[None, :]).astype(np.float32)
    c[:, C_M2:C_M2 + 128] = ((p[:, None] <= p[None, :]) & ((p[:, None] // 64) == (p[None, :] // 64))).astype(np.float32)
    c[:, C_NTRI:C_NTRI + 128] = np.where(p[:, None] <= p[None, :], 0.0, -30000.0).astype(np.float32)
    t = np.arange(256)
    c[:, C_MC:C_MC + 256] = (t % 64 != 0).astype(np.float32)[None, :]
    for G in range(2):
        for hl in range(8):
            sl = alibi_slope(G * 8 + hl)
            hi = _bf16_round(sl)
            lo = _bf16_round(sl - hi)
            for kb in range(8):
                col = C_LCOL + G * 64 + hl * 8 + kb
                c[hl * 16 + kb, col] = 1.0
                c[hl * 16 + 8, col] = -8.0 * hi
                c[hl * 16 + 9, col] = -8.0 * hi
                c[hl * 16 + 10, col] = -8.0 * lo
                c[hl * 16 + 11, col] = -8.0 * lo
    for j in range(4):
        for hl in range(8):
            base = C_SBB + j * 128 + hl * 16
            c[:, base + 8] = (j % 2) * 128 + p
            c[:, base + 9] = 256 * (j // 2)
            c[:, base + 10] = (j % 2) * 128 + p
            c[:, base + 11] = 256 * (j // 2)
    for h in range(16):
        sl = alibi_slope(h)
        for dk in range(-12, 4):
            c[:, C_ALB + h * 16 + dk + 12] = np.float32(sl) * (p + 128.0 * dk).astype(np.float32)
    return c


class Ring:
    def __init__(self, items):
        self.items = items
        self.i = 0

    def next(self):
        it = self.items[self.i % len(self.items)]
        self.i += 1
        return it


class K:
    def __init__(self, nc, stages):
        self.nc = nc
        self.P = Prog()
        self.es = ExitStack()
        self.stages = stages
        self.uid = 0

    def sb(self, shape, dt, name=None):
        self.uid += 1
        return self.es.enter_context(self.nc.sbuf_tensor(f"{name or 't'}_{self.uid}", list(shape), dt))

    def salloc(self, shape, dt):
        n = 1
        for s in shape[1:]:
            n *= s
        nbytes = n * (4 if dt == F32 else 2)
        off = (self.aoff + 31) // 32 * 32
        self.aoff = off + nbytes
        if os.environ.get('DEBUG_ALLOC'):
            print('salloc', shape, off, off + nbytes)
        assert self.aoff <= ARENA_BYTES, f"arena overflow {self.aoff}"
        v = self.arena[:, off // 2:(off + nbytes) // 2]
        if dt == F32:
            v = v.bitcast(F32)
        if len(shape) == 3:
            v = v.rearrange("p (a b) -> p a b", a=shape[1])
        elif len(shape) == 4:
            v = v.rearrange("p (a b c) -> p a b c", a=shape[1], b=shape[2])
        if shape[0] < 128:
            v = v[0:shape[0]]
        return v

    def ring(self, n, shape, dt, name="r"):
        return Ring([(self.salloc(shape, dt), Tok(name)) for _ in range(n)])

    def begin_stage(self):
        self.aoff = 0
        self.P.fence()

    def dram_in(self, name, shape, dt=F32):
        return self.nc.dram_tensor(name, list(shape), dt, kind="ExternalInput").ap()

    def pe(self, fn, r=(), w=()):
        return self.P.op("pe", fn, r, w)

    def act(self, fn, r=(), w=()):
        return self.P.op("act", fn, r, w)

    def dve(self, fn, r=(), w=()):
        return self.P.op("dve", fn, r, w)

    def pool(self, fn, r=(), w=()):
        return self.P.op("pool", fn, r, w)

    def dma(self, q, out, in_, r=(), w=()):
        return self.P.op(q, lambda e: e.dma_start(out=out, in_=in_), r, w, dma=True)

    def dump(self, ap, tok, psum=False):
        if not os.environ.get('DEBUG_DUMP'):
            return
        n = ap.shape[-1]
        c0 = self.dump_off
        self.dump_off += n
        print("DUMP", c0, n)
        if psum:
            scr = self.salloc([128, n], F32)
            st = Tok("dscr")
            self.dve(lambda e: e.tensor_copy(out=scr, in_=ap), r=[tok], w=[st])
            self.dma("pool", self.d["dbg"][:, c0:c0 + n], scr, r=[st])
        else:
            self.dma("pool", self.d["dbg"][:, c0:c0 + n], ap, r=[tok])

    def bank(self):
        b = self.banks[self.bank_i % 6]
        self.bank_i += 1
        return b

    def obank(self):
        b = self.banks[6 + self.obank_i % 2]
        self.obank_i += 1
        return b

    def mm(self, out, lhsT, rhs, start, stop, r, w, **kw):
        n = out.shape[-1]
        return self.P.op("pe", lambda e: e.matmul(out, lhsT=lhsT, rhs=rhs, start=start, stop=stop, **kw), r, w, cost=max(n, 64) / 2400.0 + 0.012)

    def mmB(self, ps, pst, W, wt, col0, tok0, n):
        xts = [self.xT_t[q] for q in range(tok0 // 128, (tok0 + n + 127) // 128)]
        for kc in range(KC):
            self.mm(ps, W[:, kc, col0:col0 + 128], self.xT[:, kc, tok0:tok0 + n], kc == 0, kc == KC - 1, [wt] + xts, [pst])

    def mmA(self, ps, pst, i, W, wt, col0, n):
        for kc in range(KC):
            self.mm(ps, self.xT[:, kc, i * 128:(i + 1) * 128], W[:, kc, col0:col0 + n], kc == 0, kc == KC - 1,
                    [wt, self.xT_t[i]], [pst])

    def wload(self, dst, src, tok):
        self.dma("pool", dst, src, w=[tok])

    def wload_w(self, dst, W_d, col0, n, tok, step=512):
        for c in range(0, n, step):
            m = min(step, n - c)
            self.wload(dst[:, :, c:c + m], W_d[:, col0 + c:col0 + c + m].rearrange("(k p) n -> p k n", p=128), tok)

    def build(self):
        nc = self.nc
        d = {}
        d["x"] = self.dram_in("x", [T, D])
        d["mem"] = self.dram_in("mem", [MEM, D])
        d["lng"] = self.dram_in("ln_g", [6, D])
        d["lnb"] = self.dram_in("ln_b", [6, D])
        d["xwq"] = self.dram_in("x_wq", [2, D, D])
        d["xwkv"] = self.dram_in("x_wkv", [2, D, 2 * D])
        d["xwo"] = self.dram_in("x_wo", [2, D, D])
        d["fwi"] = self.dram_in("ffn_w_in", [2, D, 2 * DFF])
        d["fwo"] = self.dram_in("ffn_w_out", [2, DFF, D])
        d["evi"] = self.dram_in("ev_w_in", [D, 3072])
        d["evo"] = self.dram_in("ev_w_out", [D, D])
        d["awsT"] = self.dram_in("a_wsT", [128, 4, 128])
        d["abs"] = self.dram_in("a_bs", [1, 512])
        d["alng"] = self.dram_in("a_ln_g", [1, 512])
        d["alnb"] = self.dram_in("a_ln_b", [1, 512])
        d["bng"] = self.dram_in("b_norm_gT", [128, 4])
        d["lbl"] = self.dram_in("lb_logitsT", [128, 2, 4])
        d["oqkv"] = self.dram_in("od_w_qkv", [D, 3072])
        d["owo"] = self.dram_in("od_w_out", [D, D])
        d["cst"] = self.dram_in("consts", [128, CONST_W])
        d["out"] = nc.dram_tensor("out", [T, D], F32, kind="ExternalOutput").ap()
        if os.environ.get('DEBUG_DUMP'):
            d["dbg"] = nc.dram_tensor("dbg", [128, 8192], F32, kind="ExternalOutput").ap()
        self.dump_off = 0
        self.d = d

        self.x_tok = self.sb([128, NT, D], F32, "x_tok")
        self.xT = self.sb([128, KC, T], BF16, "xT")
        self.xtok_t = [Tok(f"xtok{i}") for i in range(NT)]
        self.xT_t = [Tok(f"xT{i}") for i in range(NT)]
        self.cst = self.sb([128, CONST_W], F32, "cst")
        self.cst_t = Tok("cst")
        self.ident = self.sb([128, 128], BF16, "ident")
        self.ident_t = Tok("ident")
        self.xb_ring = Ring([(self.sb([128, D], BF16, "xb"), Tok("xb")) for _ in range(2)])
        self.st_ring = Ring([(self.sb([128, 32], F32, "lnst"), Tok("lnst")) for _ in range(3)])
        self.negh = self.sb([128, 512], F32, "negh")
        self.negh_t = Tok("negh")
        self.arena = self.sb([128, ARENA_BYTES // 2], BF16, "arena")
        self.aoff = 0
        self.banks = []
        for i in range(8):
            pt = self.es.enter_context(nc.psum_tensor(f"bank{i}", [128, 512], F32))
            self.banks.append((pt, Tok(f"bank{i}")))
        self.bank_i = 0
        self.obank_i = 0
        self.cp_i = 0

        self.dma("sp", self.cst[:], d["cst"], w=[self.cst_t])
        self.dve(lambda e: e.tensor_copy(out=self.ident[:], in_=self.cst[:, C_IDENT:C_IDENT + 128]),
                 r=[self.cst_t], w=[self.ident_t])
        self.pool(lambda e: e.memset(self.negh[:], -0.5), w=[self.negh_t])

        self.load_x()
        for s in self.stages:
            getattr(self, "stage_" + s[0])(*s[1:])
        self.P.fence()
        self.store_out()
        self.P.fence()

        sems = {e: self.es.enter_context(nc.semaphore(f"s_{e}")) for e in Prog.ENGS}
        dma_sems = {}
        for q, n in (("sp", 24), ("pool", 32), ("act", 2)):
            dma_sems[q] = [self.es.enter_context(nc.semaphore(f"d_{q}{i}")) for i in range(n)]
        self.P.emit(nc, None, sems, dma_sems)
        self.es.close()

    def copy(self, out, in_, r, w):
        self.cp_i += 1
        if self.cp_i % 2:
            return self.act(lambda e: e.copy(out=out, in_=in_), r=r, w=w)
        return self.dve(lambda e: e.tensor_copy(out=out, in_=in_), r=r, w=w)

    def load_x(self):
        for i in range(NT):
            self.dma("sp", self.x_tok[:, i, :], self.d["x"][i * 128:(i + 1) * 128, :], w=[self.xtok_t[i]])
        for i in range(NT):
            self.to_xT(i)

    def store_out(self):
        for i in range(NT):
            self.dma("sp", self.d["out"][i * 128:(i + 1) * 128, :], self.x_tok[:, i, :], r=[self.xtok_t[i]])

    def transpose_to(self, dst_view, src_tiles, r, w):
        bk, bt = self.bank()
        psb = bk[:].bitcast(BF16)
        n = len(src_tiles)
        for k, src in enumerate(src_tiles):
            self.pe(lambda e, k=k, src=src: e.transpose(out=psb[:, k * 128:(k + 1) * 128], in_=src, identity=self.ident[:]),
                    r=list(r) + [self.ident_t], w=[bt])
        self.copy(dst_view, psb[:, 0:n * 128].rearrange("p (k t) -> p k t", k=n), [bt], w)

    def to_xT(self, i):
        xb, xbt = self.xb_ring.next()
        self.act(lambda e: e.copy(out=xb[:], in_=self.x_tok[:, i, :]), r=[self.xtok_t[i]], w=[xbt])
        self.transpose_to(self.xT[:, :, i * 128:(i + 1) * 128], [xb[:, kc * 128:(kc + 1) * 128] for kc in range(KC)],
                          [xbt], [self.xT_t[i]])

    def load_ln(self, idx):
        g = self.salloc([128, D], F32)
        b = self.salloc([128, D], F32)
        t = Tok("lnp")
        self.dma("sp", g, self.d["lng"][idx:idx + 1, :].to_broadcast([128, D]), w=[t])
        self.dma("sp", b, self.d["lnb"][idx:idx + 1, :].to_broadcast([128, D]), w=[t])
        return g, b, t

    def rstd_small(self, out, var, r, w, eps=EPS):
        n = out.shape[-1]
        self.pool(lambda e: e.tensor_scalar(out=out, in0=var, scalar1=eps, scalar2=None, op0=ALU.add), r=r, w=w)
        self.pool(lambda e: e.tensor_tensor(out=out, in0=out, in1=self.negh[:, 0:n], op=ALU.pow), r=list(w) + [self.negh_t], w=w)

    def layer_norm(self, i, lnp):
        g, b, gt = lnp
        xt = self.x_tok[:, i, :]
        xtok = self.xtok_t[i]
        stt, stk = self.st_ring.next()
        self.dve(lambda e: e.bn_stats(out=stt[:, 0:6], in_=self.x_tok[:, i, 0:512]), r=[xtok], w=[stk])
        self.dve(lambda e: e.bn_stats(out=stt[:, 6:12], in_=self.x_tok[:, i, 512:1024]), r=[xtok], w=[stk])
        self.dve(lambda e: e.bn_aggr(out=stt[:, 12:14], in_=stt[:, 0:12].rearrange("p (a b) -> p a b", a=2)), r=[stk], w=[stk])
        self.rstd_small(stt[:, 15:16], stt[:, 13:14], [stk], [stk])
        self.dve(lambda e: e.scalar_tensor_tensor(out=stt[:, 16:17], in0=stt[:, 12:13], scalar=-1.0, in1=stt[:, 15:16],
                                                  op0=ALU.mult, op1=ALU.mult), r=[stk], w=[stk])
        self.act(lambda e: e.activation(out=xt, in_=xt, func=AF.Identity, bias=stt[:, 16:17], scale=stt[:, 15:16]),
                 r=[stk, xtok], w=[xtok])
        self.pool(lambda e: e.tensor_tensor(out=xt, in0=xt, in1=g, op=ALU.mult), r=[xtok, gt], w=[xtok])
        self.dve(lambda e: e.tensor_tensor(out=xt, in0=xt, in1=b, op=ALU.add), r=[xtok, gt], w=[xtok])
        self.to_xT(i)

    def accum(self, i, half, ps, pst, first):
        dst = self.x_tok[:, i, half * 512:(half + 1) * 512]
        if first:
            self.dve(lambda e: e.scalar_tensor_tensor(out=dst, in0=dst, scalar=ALPHA, in1=ps, op0=ALU.mult, op1=ALU.add),
                     r=[pst, self.xtok_t[i]], w=[self.xtok_t[i]])
        else:
            self.dve(lambda e: e.tensor_tensor(out=dst, in0=dst, in1=ps, op=ALU.add),
                     r=[pst, self.xtok_t[i]], w=[self.xtok_t[i]])

    def out_proj(self, i, lhs_fn, nk, lhs_toks, wo, wot, first):
        for half in range(2):
            ps, pst = self.bank()
            for k in range(nk):
                self.mm(ps[:], lhs_fn(k), wo[:, k, half * 512:(half + 1) * 512], k == 0, k == nk - 1, list(lhs_toks) + [wot], [pst])
            self.accum(i, half, ps[:], pst, first)

    def stage_ln_only(self, idx):
        self.begin_stage()
        lnp = self.load_ln(idx)
        for i in range(NT):
            self.layer_norm(i, lnp)

    def stage_ffn(self, l):
        self.begin_stage()
        groups = [list(range(0, 6)), list(range(6, 12)), list(range(12, 17)), list(range(17, 22))]
        fwi = self.d["fwi"][l]
        fwo = self.d["fwo"][l]
        lnp = self.load_ln(l * 3 + 2)
        wi_r = self.ring(3, [128, KC, 256], BF16, "fwi")
        wo_r = self.ring(2, [128, 6, D], BF16, "fwo")
        hT, hTt = self.salloc([128, 6, T], BF16), Tok("hT")
        sg_r = self.ring(2, [128, 512], F32, "sg")
        for gi, js in enumerate(groups):
            wo, wot = wo_r.next()
            j0 = js[0]
            self.wload(wo[:, 0:len(js), :], fwo[j0 * 128:(j0 + len(js)) * 128, :].rearrange("(j p) n -> p j n", p=128), wot)
            for jl, j in enumerate(js):
                wi, wit = wi_r.next()
                self.wload(wi[:, :, 0:128], fwi[:, j * 128:(j + 1) * 128].rearrange("(k p) n -> p k n", p=128), wit)
                self.wload(wi[:, :, 128:256], fwi[:, DFF + j * 128:DFF + (j + 1) * 128].rearrange("(k p) n -> p k n", p=128), wit)
                for tb in range(4):
                    pg, pgt = self.bank()
                    pu, put = self.bank()
                    self.mmB(pg[:], pgt, wi, wit, 0, tb * 512, 512)
                    self.mmB(pu[:], put, wi, wit, 128, tb * 512, 512)
                    sg, sgt = sg_r.next()
                    self.act(lambda e, sg=sg, pg=pg: e.activation(out=sg, in_=pg[:], func=AF.Silu), r=[pgt], w=[sgt])
                    self.dve(lambda e, sg=sg, pu=pu, jl=jl, tb=tb: e.tensor_tensor(out=hT[:, jl, tb * 512:(tb + 1) * 512], in0=sg, in1=pu[:], op=ALU.mult),
                             r=[sgt, put], w=[hTt])
            last = gi == len(groups) - 1
            for i in range(NT):
                self.out_proj(i, lambda k, i=i: hT[:, k, i * 128:(i + 1) * 128], len(js), [hTt], wo, wot, first=(gi == 0))
                if last:
                    self.layer_norm(i, lnp)

    def stage_xattn(self, l):
        self.begin_stage()
        d = self.d
        SC = 1.0 / 16.0
        lnp = self.load_ln(l * 3 + 1)
        wq, wqt = self.salloc([128, KC, D], BF16), Tok("wq")
        wo, wot = self.salloc([128, KC, D], BF16), Tok("wo")
        kT, kTt = self.salloc([128, KC, MEM], BF16), Tok("kT")
        vS, vSt = self.salloc([128, 2, D], BF16), Tok("vS")
        memb, membt = self.salloc([128, 2, D], BF16), Tok("memb")
        memT, memTt = self.salloc([128, KC, MEM], BF16), Tok("memT")
        wkv_r = self.ring(1, [128, KC, 512], BF16, "wkv")
        qT_r = self.ring(1, [128, KC, 512], BF16, "qT")
        p32_r = self.ring(2, [128, 4, 256], F32, "p32")
        pb_r = self.ring(2, [128, 4, 256], BF16, "pb")
        pT_r = self.ring(2, [128, 8, 128], BF16, "pT")
        oT_r = self.ring(2, [128, 8, 128], BF16, "oT")
        sm_r = self.ring(3, [128, 16], F32, "sm")
        self.wload(memb, d["mem"].rearrange("(m p) n -> p m n", p=128), membt)
        for mt in range(2):
            self.transpose_to(memT[:, :, mt * 128:(mt + 1) * 128], [memb[:, mt, kc * 128:(kc + 1) * 128] for kc in range(KC)],
                              [membt], [memTt])
        for c in range(4):
            wk, wkt = wkv_r.next()
            self.wload_w(wk, d["xwkv"][l], c * 512, 512, wkt)
            if c < 2:
                for cc in range(4):
                    fc = c * 4 + cc
                    ps, pst = self.bank()
                    for kc in range(KC):
                        self.mm(ps[:, 0:MEM], wk[:, kc, cc * 128:(cc + 1) * 128], memT[:, kc, :], kc == 0, kc == KC - 1, [wkt, memTt], [pst])
                    self.copy(kT[:, fc, :], ps[:, 0:MEM], [pst], [kTt])
            else:
                for mt in range(2):
                    ps, pst = self.bank()
                    for kc in range(KC):
                        self.mm(ps[:], memT[:, kc, mt * 128:(mt + 1) * 128], wk[:, kc, :], kc == 0, kc == KC - 1, [wkt, memTt], [pst])
                    self.copy(vS[:, mt, (c - 2) * 512:(c - 1) * 512], ps[:], [pst], [vSt])
        self.wload_w(wq, d["xwq"][l], 0, D, wqt)
        self.wload_w(wo, d["xwo"][l], 0, D, wot)
        for tb in range(4):
            qT, qTt = qT_r.next()
            for c in range(KC):
                ps, pst = self.bank()
                self.mmB(ps[:], pst, wq, wqt, c * 128, tb * 512, 512)
                self.copy(qT[:, c, :], ps[:], [pst], [qTt])
            for il in range(4):
                i = tb * 4 + il
                sA, sAt = self.bank()
                sB, sBt = self.bank()
                for h in range(4):
                    bk, bkt = (sA, sAt) if h < 2 else (sB, sBt)
                    for k2 in range(2):
                        self.mm(bk[:, (h % 2) * 256:(h % 2 + 1) * 256], qT[:, 2 * h + k2, il * 128:(il + 1) * 128], kT[:, 2 * h + k2, :],
                                k2 == 0, k2 == 1, [qTt, kTt], [bkt])
                sm, smt = sm_r.next()
                self.dve(lambda e, sm=sm, sA=sA: e.tensor_reduce(out=sm[:, 0:2], in_=sA[:].rearrange("p (h m) -> p h m", h=2), axis=AX.X, op=ALU.max), r=[sAt], w=[smt])
                self.dve(lambda e, sm=sm, sB=sB: e.tensor_reduce(out=sm[:, 2:4], in_=sB[:].rearrange("p (h m) -> p h m", h=2), axis=AX.X, op=ALU.max), r=[sBt], w=[smt])
                self.dve(lambda e, sm=sm: e.tensor_scalar(out=sm[:, 4:8], in0=sm[:, 0:4], scalar1=-SC, scalar2=None, op0=ALU.mult), r=[smt], w=[smt])
                p32, p32t = p32_r.next()
                for h in range(4):
                    bk, bkt = (sA, sAt) if h < 2 else (sB, sBt)
                    self.act(lambda e, h=h, bk=bk, sm=sm, p32=p32: e.activation(out=p32[:, h, :], in_=bk[:, (h % 2) * 256:(h % 2 + 1) * 256], func=AF.Exp,
                                                                             bias=sm[:, 4 + h:5 + h], scale=SC, accum_out=sm[:, 8 + h:9 + h]),
                             r=[bkt, smt], w=[p32t, smt])
                self.dve(lambda e, sm=sm: e.reciprocal(out=sm[:, 12:16], in_=sm[:, 8:12]), r=[smt], w=[smt])
                pb, pbt = pb_r.next()
                for h in range(4):
                    if h % 2:
                        self.dve(lambda e, h=h, pb=pb, p32=p32, sm=sm: e.tensor_scalar(out=pb[:, h, :], in0=p32[:, h, :], scalar1=sm[:, 12 + h:13 + h], scalar2=None, op0=ALU.mult),
                                 r=[p32t, smt], w=[pbt])
                    else:
                        self.act(lambda e, h=h, pb=pb, p32=p32, sm=sm: e.activation(out=pb[:, h, :], in_=p32[:, h, :], func=AF.Copy, scale=sm[:, 12 + h:13 + h]),
                                 r=[p32t, smt], w=[pbt])
                pT, pTt = pT_r.next()
                self.transpose_to(pT, [pb[:, h, mt * 128:(mt + 1) * 128] for h in range(4) for mt in range(2)], [pbt], [pTt])
                oA, oAt = self.bank()
                oB, oBt = self.bank()
                oT, oTt = oT_r.next()
                for c in range(8):
                    bk, bkt = (oA, oAt) if c < 4 else (oB, oBt)
                    h = c // 2
                    for mt in range(2):
                        self.mm(bk[:, (c % 4) * 128:(c % 4 + 1) * 128], vS[:, mt, c * 128:(c + 1) * 128], pT[:, h * 2 + mt, :],
                                mt == 0, mt == 1, [vSt, pTt], [bkt])
                self.copy(oT[:, 0:4, :], oA[:].rearrange("p (c t) -> p c t", c=4), [oAt], [oTt])
                self.copy(oT[:, 4:8, :], oB[:].rearrange("p (c t) -> p c t", c=4), [oBt], [oTt])
                self.out_proj(i, lambda k, oT=oT: oT[:, k, :], KC, [oTt], wo, wot, first=True)
                self.layer_norm(i, lnp)

    def gelu(self, out, ps, pst, outt, scr_r, half=True):
        if GELU_NATIVE:
            self.act(lambda e: e.activation(out=out, in_=ps, func=AF.Gelu_apprx_tanh), r=[pst], w=[outt])
            return 1.0
        C0 = 0.7978845608028654
        C1 = 0.044715
        s, stk = scr_r.next()
        n = ps.shape[-1]
        sv = s[:, 0:n]
        self.act(lambda e: e.activation(out=sv, in_=ps, func=AF.Square), r=[pst], w=[stk])
        self.dve(lambda e: e.tensor_scalar(out=sv, in0=sv, scalar1=C1, scalar2=1.0, op0=ALU.mult, op1=ALU.add), r=[stk], w=[stk])
        self.dve(lambda e: e.tensor_tensor(out=sv, in0=sv, in1=ps, op=ALU.mult), r=[stk, pst], w=[stk])
        self.act(lambda e: e.activation(out=sv, in_=sv, func=AF.Tanh, scale=C0), r=[stk], w=[stk])
        self.dve(lambda e: e.scalar_tensor_tensor(out=out, in0=sv, scalar=1.0, in1=ps, op0=ALU.add, op1=ALU.mult), r=[stk, pst], w=[outt])
        return 0.5

    def stage_mix0(self):
        d = self.d
        self.begin_stage()
        wuv, wut, wvt = self.salloc([128, KC, 1024], BF16), Tok("wu"), Tok("wv")
        woA, woAt = self.salloc([128, 4, D], BF16), Tok("woA")
        ws32, ws32t = self.salloc([128, 4, 128], F32), Tok("ws32")
        wsb, wsbt = self.salloc([128, 4, 128], BF16), Tok("wsb")
        bsB, bsBt = self.salloc([128, 512], F32), Tok("bsB")
        lgB, lgBt = self.salloc([128, 512], F32), Tok("lgB")
        lbB, lbBt = self.salloc([128, 512], F32), Tok("lbB")
        uT_r = self.ring(2, [128, 4, 512], BF16, "uT")
        scr_r = self.ring(3, [128, 512], F32, "gscr")
        vg_r = self.ring(2, [128, 512], F32, "vg")
        vb_r = self.ring(2, [128, 512], BF16, "vb")
        ss_r = self.ring(2, [128, 512], F32, "ssum")
        ya_r = self.ring(2, [128, 4, 128], BF16, "yaT")
        st_r = self.ring(2, [128, 48], F32, "gst")
        self.wload_w(wuv[:, :, 0:512], d["evi"], 0, 512, wut)
        self.wload_w(wuv[:, :, 512:1024], d["evi"], 512, 512, wvt)
        self.wload(woA, d["evo"][0:512, :].rearrange("(g p) n -> p g n", p=128), woAt)
        self.dma("sp", ws32, d["awsT"], w=[ws32t])
        self.dma("sp", bsB, d["abs"].to_broadcast([128, 512]), w=[bsBt])
        self.dma("sp", lgB, d["alng"].to_broadcast([128, 512]), w=[lgBt])
        self.dma("sp", lbB, d["alnb"].to_broadcast([128, 512]), w=[lbBt])
        for g in range(4):
            self.dve(lambda e, g=g: e.tensor_tensor(out=wsb[:, g, :], in0=ws32[:, g, :], in1=self.cst[:, C_TRI:C_TRI + 128], op=ALU.mult),
                     r=[ws32t, self.cst_t], w=[wsbt])
        for tb in range(4):
            uT, uTt = uT_r.next()
            gf = 1.0
            for c in range(4):
                ps, pst = self.bank()
                self.mmB(ps[:], pst, wuv, wut, c * 128, tb * 512, 512)
                gf = self.gelu(uT[:, c, :], ps[:], pst, uTt, scr_r)
            for il in range(4):
                i = tb * 4 + il
                ps, pst = self.bank()
                self.mmA(ps[:], pst, i, wuv, wvt, 512, 512)
                vg, vgt = vg_r.next()
                gv = self.gelu(vg, ps[:], pst, vgt, scr_r)
                st, stt = st_r.next()
                for g in range(4):
                    self.dve(lambda e, g=g, st=st, vg=vg: e.bn_stats(out=st[:, g * 6:(g + 1) * 6], in_=vg[:, g * 128:(g + 1) * 128]), r=[vgt], w=[stt])
                for g in range(4):
                    self.dve(lambda e, g=g, st=st: e.bn_aggr(out=st[:, 24 + g * 2:26 + g * 2], in_=st[:, g * 6:(g + 1) * 6]), r=[stt], w=[stt])
                mv = st[:, 24:32].rearrange("p (g two) -> p g two", two=2)
                self.pool(lambda e, st=st, mv=mv: e.tensor_scalar(out=st[:, 32:36], in0=mv[:, :, 1], scalar1=gv * gv, scalar2=EPS, op0=ALU.mult, op1=ALU.add), r=[stt], w=[stt])
                self.pool(lambda e, st=st: e.tensor_tensor(out=st[:, 32:36], in0=st[:, 32:36], in1=self.negh[:, 0:4], op=ALU.pow), r=[stt, self.negh_t], w=[stt])
                self.dve(lambda e, st=st: e.tensor_scalar(out=st[:, 36:40], in0=st[:, 32:36], scalar1=gv, scalar2=None, op0=ALU.mult), r=[stt], w=[stt])
                self.dve(lambda e, st=st, mv=mv: e.scalar_tensor_tensor(out=st[:, 40:44], in0=mv[:, :, 0], scalar=-1.0, in1=st[:, 36:40], op0=ALU.mult, op1=ALU.mult), r=[stt], w=[stt])
                for g in range(4):
                    self.act(lambda e, g=g, st=st, vg=vg: e.activation(out=vg[:, g * 128:(g + 1) * 128], in_=vg[:, g * 128:(g + 1) * 128], func=AF.Identity,
                                                                      bias=st[:, 40 + g:41 + g], scale=st[:, 36 + g:37 + g]), r=[stt, vgt], w=[vgt])
                self.dve(lambda e, vg=vg: e.tensor_tensor(out=vg, in0=vg, in1=lgB, op=ALU.mult), r=[vgt, lgBt], w=[vgt])
                vb, vbt = vb_r.next()
                self.pool(lambda e, vg=vg, vb=vb: e.tensor_tensor(out=vb, in0=vg, in1=lbB, op=ALU.add), r=[vgt, lbBt], w=[vbt])
                pss, psst = self.bank()
                for g in range(4):
                    self.mm(pss[:, g * 128:(g + 1) * 128], vb[:, g * 128:(g + 1) * 128], wsb[:, g, :], True, True, [vbt, wsbt], [psst])
                ss, sst = ss_r.next()
                self.dve(lambda e, ss=ss, pss=pss: e.tensor_tensor(out=ss, in0=pss[:], in1=bsB, op=ALU.add), r=[psst, bsBt], w=[sst])
                ya, yat = ya_r.next()
                self.dve(lambda e, ya=ya, uT=uT, ss=ss, il=il: e.scalar_tensor_tensor(out=ya, in0=uT[:, :, il * 128:(il + 1) * 128], scalar=gf,
                                                                                   in1=ss.rearrange("p (g t) -> p g t", g=4), op0=ALU.mult, op1=ALU.mult),
                         r=[uTt, sst], w=[yat])
                self.out_proj(i, lambda k, ya=ya: ya[:, k, :], 4, [yat], woA, woAt, first=True)

        self.begin_stage()
        NB = 256
        lnp = self.load_ln(0)
        wB, wBt = self.salloc([128, KC, 2048], BF16), [Tok("wBq"), Tok("wBf"), Tok("wBi"), Tok("wBg")]
        woB, woBt = self.salloc([128, 4, D], BF16), Tok("woB")
        sm, smt = self.salloc([128, 32], F32), Tok("hsm")
        S, St = self.salloc([128, 4, 128], F32), Tok("S")
        Sb_r = self.ring(3, [128, 4, 128], BF16, "Sb")
        m2x4, m2t = self.salloc([128, 512], BF16), Tok("m2x4")
        ones, onest = self.salloc([128, 128], BF16), Tok("ones")
        f1 = self.ring(1, [128, 4, NB], F32, "f1")
        f2 = self.ring(1, [128, 4, NB], F32, "f2")
        f3 = self.ring(1, [128, 4, NB], F32, "f3")
        E_r = self.ring(1, [128, 4, NB], F32, "E")
        qd_r = self.ring(1, [128, 4, NB], BF16, "qdT")
        kd_r = self.ring(1, [128, 4, NB], BF16, "kdT")
        gs_r = self.ring(1, [128, 4, NB], BF16, "gsT")
        vi_r = self.ring(1, [128, 2, 512], BF16, "vi")
        kt_r = self.ring(1, [128, 2, 512], BF16, "kdtok")
        am_r = self.ring(2, [128, 512], BF16, "am")
        tmp_r = self.ring(1, [128, 512], F32, "stmp")
        sq_r = self.ring(2, [128, 512], BF16, "sq")
        r_r = self.ring(1, [128, 512], F32, "rr")
        t1_r = self.ring(1, [128, 512], F32, "t1")
        yb_r = self.ring(2, [128, 4, 128], BF16, "ybT")
        for c in range(4):
            self.wload_w(wB[:, :, c * 512:(c + 1) * 512], d["evi"], 1024 + c * 512, 512, wBt[c])
        self.wload(woB, d["evo"][512:1024, :].rearrange("(g p) n -> p g n", p=128), woBt)
        self.dma("sp", sm[:, 0:8], d["lbl"].rearrange("p l h -> p (l h)"), w=[smt])
        self.dma("sp", sm[:, 20:24], d["bng"], w=[smt])
        self.dve(lambda e: e.tensor_tensor(out=sm[:, 8:12], in0=sm[:, 0:4], in1=sm[:, 4:8], op=ALU.subtract), r=[smt], w=[smt])
        self.act(lambda e: e.activation(out=sm[:, 12:16], in_=sm[:, 8:12], func=AF.Sigmoid), r=[smt], w=[smt])
        self.dve(lambda e: e.tensor_scalar(out=sm[:, 16:20], in0=sm[:, 12:16], scalar1=-1.0, scalar2=1.0, op0=ALU.mult, op1=ALU.add), r=[smt], w=[smt])
        self.pool(lambda e: e.memset(S.rearrange("p h v -> p (h v)"), 0.0), w=[St])
        Sb, Sbt = Sb_r.next()
        self.pool(lambda e, Sb=Sb: e.memset(Sb.rearrange("p h v -> p (h v)"), 0.0), w=[Sbt])
        self.pool(lambda e: e.memset(ones, 1.0), w=[onest])
        epsc, epst = self.salloc([128, 8], F32), Tok("eps")
        self.pool(lambda e: e.memset(epsc, EPS), w=[epst])
        for h in range(4):
            self.dve(lambda e, h=h: e.tensor_copy(out=m2x4[:, h * 128:(h + 1) * 128], in_=self.cst[:, C_M2:C_M2 + 128]), r=[self.cst_t], w=[m2t])
        maskc = self.cst[:, C_MC:C_MC + NB]
        for blk in range(T // NB):
            tok0 = blk * NB
            s1, s1t = f1.next()
            s2, s2t = f2.next()
            s3, s3t = f3.next()
            E, Et = E_r.next()
            qd, qdt = qd_r.next()
            kd, kdt = kd_r.next()
            gs, gst = gs_r.next()
            qf = []
            for h in range(4):
                b1, b1t = self.bank()
                self.mmB(b1[:, 0:NB], b1t, wB, wBt[0], h * 128, tok0, NB)
                self.mmB(b1[:, NB:2 * NB], b1t, wB, wBt[1], 512 + h * 128, tok0, NB)
                qf.append((b1, b1t))
                self.act(lambda e, h=h, b1=b1, s1=s1: e.activation(out=s1[:, h, :], in_=b1[:, NB:2 * NB], func=AF.Sigmoid), r=[b1t], w=[s1t])
            for h in range(4):
                self.dve(lambda e, h=h, s1=s1: e.tensor_scalar(out=s1[:, h, :], in0=s1[:, h, :], scalar1=sm[:, 16 + h:17 + h], scalar2=sm[:, 12 + h:13 + h],
                                                              op0=ALU.mult, op1=ALU.add), r=[s1t, smt], w=[s1t])
            self.act(lambda e, s1=s1, s2=s2: e.activation(out=s2, in_=s1, func=AF.Ln), r=[s1t], w=[s2t])
            for h in range(4):
                self.dve(lambda e, h=h, s2=s2, s3=s3: e.tensor_tensor_scan(out=s3[:, h, :], data0=maskc, data1=s2[:, h, :], initial=0.0, op0=ALU.mult, op1=ALU.add),
                         r=[s2t, self.cst_t], w=[s3t])
            self.pool(lambda e, s1=s1: e.tensor_scalar(out=s1, in0=s1, scalar1=-1.0, scalar2=1.0, op0=ALU.mult, op1=ALU.add), r=[s1t], w=[s1t])
            self.act(lambda e, E=E, s3=s3: e.activation(out=E, in_=s3, func=AF.Exp), r=[s3t], w=[Et])
            self.act(lambda e, s2=s2, s3=s3: e.activation(out=s2, in_=s3, func=AF.Exp, scale=-1.0), r=[s3t, s2t], w=[s2t])
            for h in range(4):
                b1, b1t = qf[h]
                self.dve(lambda e, h=h, b1=b1, E=E, qd=qd: e.tensor_tensor(out=qd[:, h, :], in0=b1[:, 0:NB], in1=E[:, h, :], op=ALU.mult), r=[b1t, Et], w=[qdt])
            self.pool(lambda e, s1=s1, s2=s2, kd=kd: e.tensor_tensor(out=kd, in0=s1, in1=s2, op=ALU.mult), r=[s1t, s2t], w=[kdt])
            for h in range(0, 4, 2):
                b2, b2t = self.bank()
                self.mmB(b2[:, 0:NB], b2t, wB, wBt[3], 1536 + h * 128, tok0, NB)
                self.mmB(b2[:, NB:2 * NB], b2t, wB, wBt[3], 1536 + (h + 1) * 128, tok0, NB)
                self.act(lambda e, h=h, b2=b2, gs=gs: e.activation(out=gs[:, h:h + 2, :], in_=b2[:].rearrange("p (a t) -> p a t", a=2), func=AF.Silu), r=[b2t], w=[gst])
            vi, vit = vi_r.next()
            ktk, ktkt = kt_r.next()
            for il in range(2):
                b3, b3t = self.bank()
                self.mmA(b3[:], b3t, blk * 2 + il, wB, wBt[2], 1024, 512)
                self.act(lambda e, il=il, b3=b3, vi=vi: e.activation(out=vi[:, il, :], in_=b3[:], func=AF.Silu), r=[b3t], w=[vit])
                self.transpose_to(ktk[:, il, :].rearrange("p (h k) -> p h k", h=4), [kd[:, h, il * 128:(il + 1) * 128] for h in range(4)], [kdt], [ktkt])
            for il in range(2):
                i = blk * 2 + il
                tc0 = il * 128
                U = [self.bank(), self.bank()]
                for c in range(2):
                    for h in range(4):
                        self.mm(U[c][0][:, h * 128:(h + 1) * 128], ktk[c * 64:(c + 1) * 64, il, h * 128:(h + 1) * 128],
                                vi[c * 64:(c + 1) * 64, il, h * 128:(h + 1) * 128], True, True, [ktkt, vit], [U[c][1]])
                A, At = self.bank()
                for h in range(4):
                    self.mm(A[:, h * 128:(h + 1) * 128], kd[:, h, tc0:tc0 + 128], qd[:, h, tc0:tc0 + 128], True, True, [kdt, qdt], [At])
                am, amt = am_r.next()
                self.dve(lambda e, am=am, A=A: e.tensor_tensor(out=am, in0=A[:], in1=m2x4, op=ALU.mult), r=[At, m2t], w=[amt])
                Sbs = [(Sb, Sbt)]
                for c in range(2):
                    tmp, tmpt = tmp_r.next()
                    Uc, Uct = U[c]
                    self.dve(lambda e, tmp=tmp, Uc=Uc: e.tensor_tensor(out=tmp, in0=Uc[:], in1=S.rearrange("p h v -> p (h v)"), op=ALU.add), r=[Uct, St], w=[tmpt])
                    Sb, Sbt = Sb_r.next()
                    col = tc0 + c * 64 + 63
                    for h in range(4):
                        self.dve(lambda e, h=h, tmp=tmp, E=E, col=col: e.tensor_scalar(out=S[:, h, :], in0=tmp[:, h * 128:(h + 1) * 128], scalar1=E[:, h, col:col + 1],
                                                                                    scalar2=None, op0=ALU.mult), r=[tmpt, Et], w=[St])
                        self.act(lambda e, h=h, tmp=tmp, E=E, col=col, Sb=Sb: e.activation(out=Sb[:, h, :], in_=tmp[:, h * 128:(h + 1) * 128], func=AF.Copy,
                                                                                        scale=E[:, h, col:col + 1]), r=[tmpt, Et], w=[Sbt])
                    Sbs.append((Sb, Sbt))
                O, Ot = self.bank()
                for h in range(4):
                    self.mm(O[:, h * 128:(h + 1) * 128], vi[:, il, h * 128:(h + 1) * 128], am[:, h * 128:(h + 1) * 128], True, False, [vit, amt], [Ot])
                    for c in range(2):
                        Sc, Sct = Sbs[c]
                        self.mm(O[:, h * 128 + c * 64:h * 128 + (c + 1) * 64], Sc[:, h, :], qd[:, h, tc0 + c * 64:tc0 + (c + 1) * 64], False, c == 1,
                                [Sct, qdt], [Ot], skip_group_check=True)
                sq, sqt = sq_r.next()
                self.act(lambda e, sq=sq, O=O: e.activation(out=sq, in_=O[:], func=AF.Square), r=[Ot], w=[sqt])
                Q, Qt = self.bank()
                self.mm(Q[:], ones, sq, True, True, [onest, sqt], [Qt])
                rr, rrt = r_r.next()
                self.act(lambda e, rr=rr, Q=Q: e.activation(out=rr, in_=Q[:], func=AF.Ln, bias=epsc[:, 0:1], scale=1.0 / 128.0), r=[Qt, epst], w=[rrt])
                self.act(lambda e, rr=rr: e.activation(out=rr, in_=rr, func=AF.Exp, scale=-0.5), r=[rrt], w=[rrt])
                t1, t1t = t1_r.next()
                self.dve(lambda e, t1=t1, O=O, rr=rr: e.tensor_tensor(out=t1, in0=O[:], in1=rr, op=ALU.mult), r=[Ot, rrt], w=[t1t])
                yb, ybt = yb_r.next()
                for h in range(4):
                    self.dve(lambda e, h=h, yb=yb, t1=t1, gs=gs, tc0=tc0: e.scalar_tensor_tensor(out=yb[:, h, :], in0=t1[:, h * 128:(h + 1) * 128], scalar=sm[:, 20 + h:21 + h],
                                                                                              in1=gs[:, h, tc0:tc0 + 128], op0=ALU.mult, op1=ALU.mult),
                             r=[t1t, gst, smt], w=[ybt])
                self.out_proj(i, lambda k, yb=yb: yb[:, k, :], 4, [ybt], woB, woBt, first=False)
                self.layer_norm(i, lnp)

    def stage_moba(self):
        for G in range(2):
            self.moba_group(G)

    def moba_group(self, G):
        d = self.d
        if True:
            self.begin_stage()
            if DEBUG_ZERO:
                zt = Tok("z")
                lo, hi = [int(v) for v in os.environ['DEBUG_ZERO'].split(':')]
                self.pool(lambda e: e.memset(self.arena[:, lo // 2:hi // 2], 0.0), w=[zt])
                self.P.fence()
            lnp = self.load_ln(3) if G == 1 else None
            kT, kTt = self.salloc([128, 4, T], BF16), Tok("kT")
            va, vat = self.salloc([128, NT, 8, 65], BF16), Tok("va")
            wq, wqt = self.salloc([128, KC, 512], BF16), Tok("wq")
            woG, woGt = self.salloc([128, 4, D], BF16), Tok("woG")
            wk_r = self.ring(2, [128, KC, 256], BF16, "wk")
            qz_r = self.ring(1, [128, 8, 512], BF16, "qz")
            R_r = self.ring(1, [128, 512], BF16, "R")
            Rc, Rct = self.salloc([128, 512], BF16), Tok("Rc")
            pT_r = self.ring(3, [128, 512], BF16, "pT")
            otok, otokt = self.salloc([128, 4, 512], BF16), Tok("otok")
            oTb, oTbt = self.salloc([128, 4, 512], BF16), Tok("oTb")
            lcol, lcolt = self.salloc([128, 64], BF16), Tok("lcol")
            tri, trit = self.salloc([128, 128], BF16), Tok("tri")
            SBb, SBbt = self.salloc([128, 4, 128], BF16), Tok("SBb")
            SB_r = self.ring(4, [128, 128], BF16, "SB")
            km32, km32t = self.salloc([128, 4, 8], F32), Tok("km32")
            kmh, kmht = self.salloc([128, 4, 8], BF16), Tok("kmh")
            kml, kmlt = self.salloc([128, 4, 8], BF16), Tok("kml")
            aff_r = self.ring(2, [128, 8, 8], F32, "aff")
            cmp_, cmpt = self.salloc([128, 8, 8, 8], F32), Tok("cmp")
            cnt, cntt = self.salloc([128, 8, 8], F32), Tok("cnt")
            sm_r = self.ring(2, [128, 8], F32, "msm")
            zb = os.environ.get('DEBUG_ZBUF')
            if zb:
                cand = dict(qz=[qz_r.items[0][0]], pT=[x[0] for x in pT_r.items], otok=[otok], oTb=[oTb],
                            R=[R_r.items[0][0]], SB=[x[0] for x in SB_r.items], kT=[kT], va=[va], small=[km32, kmh, kml, cmp_, cnt] + [x[0] for x in aff_r.items] + [x[0] for x in sm_r.items])
                for ap in cand[zb]:
                    nd = len(ap.shape)
                    flat = ap if nd == 2 else ap.rearrange({3: "p a b -> p (a b)", 4: "p a b c -> p (a b c)"}[nd])
                    self.pool(lambda e, flat=flat: e.memset(flat, float(os.environ.get('DEBUG_ZVAL', '0'))), w=[Tok("z")])
                self.P.fence()
            self.dve(lambda e: e.tensor_copy(out=lcol, in_=self.cst[:, C_LCOL + G * 64:C_LCOL + (G + 1) * 64]), r=[self.cst_t], w=[lcolt])
            self.dve(lambda e: e.tensor_copy(out=tri, in_=self.cst[:, C_TRI:C_TRI + 128]), r=[self.cst_t], w=[trit])
            self.dve(lambda e: e.tensor_copy(out=SBb, in_=self.cst[:, C_SBB:C_SBB + 512].rearrange("p (j c) -> p j c", j=4)), r=[self.cst_t], w=[SBbt])
            self.transpose_to(Rc.rearrange("p (j t) -> p j t", j=4), [SBb[:, j, :] for j in range(4)], [SBbt], [Rct])
            if G == 0:
                self.dump(Rc[:, 0:128], Rct)
                self.dump(SBb[:, 0, :], SBbt)
            for s_ in range(qz_r.items.__len__()):
                qz0, qz0t = qz_r.items[s_]
                self.pool(lambda e, qz0=qz0: e.memset(qz0.rearrange("p h t -> p (h t)"), 0.0), w=[qz0t])
            self.pool(lambda e: e.memset(va.rearrange("p a b c -> p (a b c)"), 1.0), w=[vat])
            self.wload_w(wq, d["oqkv"], G * 512, 512, wqt)
            self.wload(woG, d["owo"][G * 512:(G + 1) * 512, :].rearrange("(g p) n -> p g n", p=128), woGt)
            for c2 in range(2):
                wk, wkt = wk_r.next()
                self.wload_w(wk, d["oqkv"], 1024 + G * 512 + c2 * 256, 256, wkt, step=256)
                for pp in range(2):
                    p = c2 * 2 + pp
                    for tb in range(4):
                        ps, pst = self.bank()
                        self.mmB(ps[:], pst, wk, wkt, pp * 128, tb * 512, 512)
                        self.copy(kT[:, p, tb * 512:(tb + 1) * 512], ps[:], [pst], [kTt])
            for c2 in range(2):
                wk, wkt = wk_r.next()
                self.wload_w(wk, d["oqkv"], 2048 + G * 512 + c2 * 256, 256, wkt, step=256)
                for i in range(NT):
                    ps, pst = self.bank()
                    self.mmA(ps[:, 0:256], pst, i, wk, wkt, 0, 256)
                    self.copy(va[:, i, c2 * 4:(c2 + 1) * 4, 0:64], ps[:, 0:256].rearrange("p (h c) -> p h c", h=4), [pst], [vat])
            for p in range(4):
                self.dve(lambda e, p=p: e.tensor_reduce(out=km32[:, p, :], in_=kT[:, p, :].rearrange("p (b t) -> p b t", b=8), axis=AX.X, op=ALU.add), r=[kTt], w=[km32t])
            self.dve(lambda e: e.tensor_scalar(out=km32, in0=km32, scalar1=1.0 / 256.0, scalar2=None, op0=ALU.mult), r=[km32t], w=[km32t])
            self.dve(lambda e: e.tensor_copy(out=kmh, in_=km32), r=[km32t], w=[kmht])
            self.dve(lambda e: e.tensor_tensor(out=km32, in0=km32, in1=kmh, op=ALU.subtract), r=[km32t, kmht], w=[km32t])
            self.dve(lambda e: e.tensor_copy(out=kml, in_=km32), r=[km32t], w=[kmlt])

            for qc in range(4):
                qz, qzt = qz_r.next()
                for p in range(4):
                    ps, pst = self.bank()
                    self.mmB(ps[:], pst, wq, wqt, p * 128, qc * 512, 512)
                    self.act(lambda e, p=p, ps=ps, qz=qz: e.copy(out=qz[0:64, 2 * p, :], in_=ps[0:64, :]), r=[pst], w=[qzt])
                    self.dve(lambda e, p=p, ps=ps, qz=qz: e.tensor_copy(out=qz[64:128, 2 * p + 1, :], in_=ps[64:128, :]), r=[pst], w=[qzt])
                if qc < 2:
                    R, Rt = Rc, Rct
                else:
                    R, Rt = R_r.next()
                    sbs = []
                    for j in range(4):
                        qb = (qc * 4 + j) // 2
                        ab, abt = self.bank()
                        for hl in range(8):
                            self.mm(ab[:, hl * 8:(hl + 1) * 8], qz[:, hl, j * 128:(j + 1) * 128], kmh[:, hl // 2, :], True, False, [qzt, kmht], [abt])
                            self.mm(ab[:, hl * 8:(hl + 1) * 8], qz[:, hl, j * 128:(j + 1) * 128], kml[:, hl // 2, :], False, True, [qzt, kmlt], [abt])
                        aff, afft = aff_r.next()
                        self.dve(lambda e, aff=aff, ab=ab: e.tensor_copy(out=aff, in_=ab[:, 0:64].rearrange("p (h k) -> p h k", h=8)), r=[abt], w=[afft])
                        self.dve(lambda e, aff=aff, qb=qb: e.tensor_tensor(out=cmp_[:, :, 0:qb, 0:qb],
                                                                        in0=aff[:, :, 0:qb].unsqueeze(2).to_broadcast([128, 8, qb, qb]),
                                                                        in1=aff[:, :, 0:qb].unsqueeze(3).to_broadcast([128, 8, qb, qb]), op=ALU.is_gt),
                                 r=[afft], w=[cmpt])
                        self.dve(lambda e, qb=qb: e.tensor_reduce(out=cnt[:, :, 0:qb], in_=cmp_[:, :, 0:qb, 0:qb], axis=AX.X, op=ALU.add), r=[cmpt], w=[cntt])
                        SB, SBt = SB_r.next()
                        self.pool(lambda e, SB=SB, j=j: e.tensor_copy(out=SB, in_=SBb[:, j, :]), r=[SBbt], w=[SBt])
                        self.dve(lambda e, SB=SB, qb=qb: e.tensor_scalar(out=SB.rearrange("p (h s) -> p h s", h=8)[:, :, 0:qb], in0=cnt[:, :, 0:qb],
                                                                      scalar1=3.0, scalar2=-32768.0, op0=ALU.is_ge, op1=ALU.mult), r=[cntt, SBt], w=[SBt])
                        sbs.append((SB, SBt))
                    self.transpose_to(R.rearrange("p (j t) -> p j t", j=4), [s[0] for s in sbs], [s[1] for s in sbs], [Rt])
                nkt = 4 * qc + 4
                for hl in range(8):
                    h = G * 8 + hl
                    O, Ot = self.obank()
                    first = True
                    for kt in range(nkt):
                        j0 = max(0, kt - 4 * qc)
                        c0 = j0 * 128
                        kb = kt // 2
                        S_, S_t = self.bank()
                        self.mm(S_[:, c0:512], kT[:, hl // 2, kt * 128:(kt + 1) * 128], qz[:, hl, c0:512], True, False, [kTt, qzt], [S_t])
                        self.mm(S_[:, c0:512], lcol[:, hl * 8 + kb:hl * 8 + kb + 1].to_broadcast([128, 128]), R[:, c0:512], False, True, [lcolt, Rt], [S_t])
                        if kt >= 4 * qc:
                            self.dve(lambda e, S_=S_, c0=c0: e.tensor_tensor(out=S_[:, c0:c0 + 128], in0=S_[:, c0:c0 + 128], in1=self.cst[:, C_NTRI:C_NTRI + 128], op=ALU.add),
                                     r=[S_t, self.cst_t], w=[S_t])
                        pT, pTt = pT_r.next()
                        bcol = C_ALB + h * 16 + (kt - 4 * qc) + 12
                        self.act(lambda e, pT=pT, S_=S_, c0=c0, bcol=bcol: e.activation(out=pT[:, c0:512], in_=S_[:, c0:512], func=AF.Exp,
                                                                                     bias=self.cst[:, bcol:bcol + 1], scale=0.125),
                                 r=[S_t, self.cst_t], w=[pTt])
                        if G == 0 and qc == 0 and hl == 0 and kt == 0:
                            self.dump(S_[:, 0:512], S_t, psum=True)
                            self.dump(pT, pTt)
                            self.dump(va[:, 0, 0, :], vat)
                            self.dump(qz[:, 0, :], qzt)
                            self.dump(kT[:, 0, 0:128], kTt)
                            self.dump(R[:, 0:128], Rt)
                            self.dump(va[:, 0, :, :].rearrange("p h c -> p (h c)"), vat)
                        for j in range(j0, 4):
                            self.mm(O[:, j * 65:(j + 1) * 65], pT[:, j * 128:(j + 1) * 128], va[:, kt, hl, :], first, kt == 4 * qc + j,
                                    [pTt, vat], [Ot], skip_group_check=True)
                            first = False
                    if G == 0 and qc == 0 and hl == 0:
                        self.dump(O[:, 0:260], Ot, psum=True)
                    sm, smt = sm_r.next()
                    self.dve(lambda e, sm=sm, O=O: e.reciprocal(out=sm[:, 0:4], in_=O[:, 0:260].rearrange("p (j c) -> p j c", j=4)[:, :, 64]), r=[Ot], w=[smt])
                    for j in range(4):
                        if j % 2:
                            self.dve(lambda e, sm=sm, O=O, j=j, hl=hl: e.tensor_scalar(out=otok[:, j, hl * 64:(hl + 1) * 64], in0=O[:, j * 65:j * 65 + 64],
                                                                                    scalar1=sm[:, j:j + 1], scalar2=None, op0=ALU.mult), r=[Ot, smt], w=[otokt])
                        else:
                            self.act(lambda e, sm=sm, O=O, j=j, hl=hl: e.activation(out=otok[:, j, hl * 64:(hl + 1) * 64], in_=O[:, j * 65:j * 65 + 64],
                                                                                 func=AF.Copy, scale=sm[:, j:j + 1]), r=[Ot, smt], w=[otokt])
                if G == 0 and qc == 0:
                    self.dump(otok.rearrange("p j c -> p (j c)"), otokt)
                for j in range(4):
                    i = qc * 4 + j
                    self.transpose_to(oTb[:, :, j * 128:(j + 1) * 128], [otok[:, j, c * 128:(c + 1) * 128] for c in range(4)], [otokt], [oTbt])
                    self.out_proj(i, lambda k, j=j: oTb[:, k, j * 128:(j + 1) * 128], 4, [oTbt], woG, woGt, first=(G == 0))
                    if G == 1:
                        self.layer_norm(i, lnp)


def build_program(stages):
    nc = bass.Bass("TRN2", target_bir_lowering=False)
    k = K(nc, stages)
    k.build()
    return nc


FULL_STAGES = [("mix0",), ("xattn", 0), ("ffn", 0), ("moba",), ("xattn", 1), ("ffn", 1)]


def make_in_maps(inputs, ncores=NCORES):
    f = lambda a: np.ascontiguousarray(np.asarray(a, dtype=np.float32))
    x = f(inputs["x"])
    mem = f(inputs["mem"])
    shared = dict(
        ln_g=f(inputs["ln_g"]).reshape(6, D),
        ln_b=f(inputs["ln_b"]).reshape(6, D),
        x_wq=f(inputs["x_wq"]), x_wkv=f(inputs["x_wkv"]), x_wo=f(inputs["x_wo"]),
        ffn_w_in=f(inputs["ffn_w_in"]), ffn_w_out=f(inputs["ffn_w_out"]),
        ev_w_in=f(inputs["ev_w_in"])[0], ev_w_out=f(inputs["ev_w_out"])[0],
        a_wsT=f(np.transpose(np.asarray(inputs["a_ws"])[0], (2, 0, 1))),
        a_bs=f(inputs["a_bs"]).reshape(1, 512),
        a_ln_g=f(inputs["a_ln_g"]).reshape(1, 512),
        a_ln_b=f(inputs["a_ln_b"]).reshape(1, 512),
        b_norm_gT=f(np.asarray(inputs["b_norm_g"]).reshape(4, 128).T),
        lb_logitsT=f(np.transpose(np.asarray(inputs["hgrn_lb_logits"]).reshape(2, 4, 128), (2, 0, 1))),
        od_w_qkv=f(inputs["od_w_qkv"])[0], od_w_out=f(inputs["od_w_out"])[0],
        consts=make_consts(),
    )
    maps = []
    for c in range(ncores):
        m = dict(shared)
        m["x"] = x[c]
        m["mem"] = mem[c]
        maps.append(m)
    return maps


def kernel(**inputs):
    nc = build_program(FULL_STAGES)
    in_maps = make_in_maps(inputs)
    res = run_bass_kernel_spmd(nc, in_maps, core_ids=list(range(NCORES)))
    return np.stack([np.asarray(r["out"], dtype=np.float32) for r in res.results], axis=0)
```

```python
import math
import os
from contextlib import ExitStack

import numpy as np
import concourse.bass as bass
import concourse.mybir as mybir
from concourse.bass_utils import run_bass_kernel_spmd

F32 = mybir.dt.float32
BF16 = mybir.dt.bfloat16
AF = mybir.ActivationFunctionType
ALU = mybir.AluOpType
AX = mybir.AxisListType

T = 2048
D = 1024
NT = T // 128
KC = D // 128
MEM = 256
DFF = 2816
NJ = DFF // 128
ALPHA = 4.0 ** 0.25
EPS = 1e-5
NCORES = 8


ARENA_REG = {"range": None, "toks": []}


class Tok:
    __slots__ = ("name", "w", "r")

    def __init__(self, name=""):
        self.name = name
        self.w = None
        self.r = []
        rng = ARENA_REG["range"]
        if rng is not None:
            st, lo, hi = rng
            inh = []
            for (st2, lo2, hi2, t2) in ARENA_REG["toks"]:
                if st2 < st and lo2 < hi and lo < hi2:
                    if t2.w is not None:
                        inh.append(t2.w)
                    inh.extend(t2.r)
            seen = set()
            for o in inh:
                if id(o) not in seen:
                    seen.add(id(o))
                    self.r.append(o)
            ARENA_REG["toks"].append((st, lo, hi, self))


class Op:
    __slots__ = ("eng", "fn", "deps", "sig", "is_dma", "need", "idx", "guard", "cost", "stage", "nleft", "users", "ready", "fin")

    def __init__(self, eng, fn, is_dma, cost):
        self.eng = eng
        self.fn = fn
        self.deps = []
        self.sig = None
        self.is_dma = is_dma
        self.need = False
        self.guard = None
        self.cost = cost
        self.users = []


DEFAULT_COST = {"pe": 0.06, "act": 0.45, "dve": 0.35, "pool": 0.6, "sp": 0.1}
SCHED_WINDOW = 160


class Prog:
    ENGS = ("pe", "act", "dve", "pool", "sp")

    def __init__(self):
        self.all = []
        self.stage = 0

    def fence(self):
        pass

    def op(self, eng, fn, reads=(), writes=(), dma=False, cost=None):
        if cost is None:
            cost = 3.0 if dma else DEFAULT_COST[eng]
        o = Op(eng, fn, dma, cost)
        o.stage = self.stage
        o.idx = len(self.all)
        deps = []
        for t in reads:
            if t.w is not None:
                deps.append(t.w)
        for t in writes:
            if t.w is not None:
                deps.append(t.w)
            deps.extend(t.r)
        seen = set()
        for d in deps:
            if id(d) in seen or d is o:
                continue
            seen.add(id(d))
            o.deps.append(d)
            d.users.append(o)
        for t in reads:
            t.r.append(o)
        for t in writes:
            t.w = o
            t.r = []
        self.all.append(o)
        return o

    def schedule(self):
        order = {e: [] for e in self.ENGS}
        nst = self.stage + 1
        stages = [[] for _ in range(nst)]
        for o in self.all:
            stages[o.stage].append(o)
        t_stage = 0.0
        for ops in stages:
            if not ops:
                continue
            inst = set(id(o) for o in ops)
            pend = {e: [] for e in self.ENGS}
            for o in ops:
                o.nleft = sum(1 for d in o.deps if id(d) in inst)
                o.ready = t_stage
                pend[o.eng].append(o)
            free = {e: t_stage for e in self.ENGS}
            n = len(ops)
            tmax = t_stage
            while n:
                best = None
                for e in self.ENGS:
                    lst = pend[e]
                    cnt = 0
                    for o in lst:
                        if o.nleft == 0:
                            st = o.ready if o.ready > free[e] else free[e]
                            if best is None or st < best[0] - 1e-9:
                                best = (st, o)
                        cnt += 1
                        if cnt >= SCHED_WINDOW:
                            break
                st, o = best
                e = o.eng
                pend[e].remove(o)
                if o.is_dma:
                    free[e] = st + 0.1
                    o.fin = st + o.cost
                else:
                    o.fin = st + o.cost
                    free[e] = o.fin
                tmax = max(tmax, o.fin)
                for u in o.users:
                    if id(u) in inst:
                        u.nleft -= 1
                        lat = 0.05 if (u.eng == e and not o.is_dma) else 0.35
                        if o.fin + lat > u.ready:
                            u.ready = o.fin + lat
                order[e].append(o)
                n -= 1
            t_stage = tmax
        self.est_us = t_stage
        return order

    def emit(self, nc, engines, sems, dma_sems):
        order = self.schedule()
        last_by_stage = {}
        for e in self.ENGS:
            for o in order[e]:
                last_by_stage[(e, o.stage)] = o
        for e in self.ENGS:
            prev_stage = None
            for o in order[e]:
                if o.stage != prev_stage:
                    for e2 in self.ENGS:
                        cands = [v for (ee, st), v in last_by_stage.items() if ee == e2 and st < o.stage]
                        if cands:
                            d = max(cands, key=lambda v: v.stage)
                            if d is not o and d not in o.deps:
                                o.deps.append(d)
                    prev_stage = o.stage
        for e in self.ENGS:
            for o in order[e]:
                for d in o.deps:
                    if (not d.is_dma) and (not o.is_dma) and d.eng == o.eng and o.eng == "pe":
                        continue
                    d.need = True
        NDS = {q: len(dma_sems[q]) for q in dma_sems}
        all_dma = {q: [] for q in dma_sems}
        for e in self.ENGS:
            cnt = 0
            k = 0
            for o in order[e]:
                if o.is_dma:
                    s = dma_sems[e][k % NDS[e]]
                    gen = k // NDS[e]
                    o.sig = (s, 16 * (gen + 1))
                    o.guard = (s, 16 * gen) if gen > 0 else None
                    all_dma[e].append(o)
                    k += 1
                elif o.need:
                    cnt += 1
                    o.sig = (sems[e], cnt)

        def run(e, eng):
            waited = {}
            for o in order[e]:
                need = {}
                if o.guard is not None:
                    need[o.guard[0]] = o.guard[1]
                for d in o.deps:
                    if (not d.is_dma) and (not o.is_dma) and d.eng == e and e == "pe":
                        continue
                    s, v = d.sig
                    if need.get(s, 0) < v:
                        need[s] = v
                for s, v in need.items():
                    if waited.get(s, 0) >= v:
                        continue
                    eng.wait_ge(s, v)
                    waited[s] = v
                ins = o.fn(eng)
                if o.is_dma:
                    ins.then_inc(o.sig[0], 16)
                elif o.sig is not None:
                    ins.then_inc(o.sig[0], 1)
            return waited

        with nc.Block() as block:
            @block.tensor
            def _(pe):
                run("pe", pe)

            @block.scalar
            def _(act):
                run("act", act)

            @block.vector
            def _(dve):
                run("dve", dve)

            @block.gpsimd
            def _(pool):
                run("pool", pool)

            @block.sync
            def _(sp):
                w = run("sp", sp)
                for q, lst in all_dma.items():
                    last = {}
                    for o in lst:
                        last[o.sig[0]] = o.sig[1]
                    for s, v in last.items():
                        if w.get(s, 0) < v:
                            sp.wait_ge(s, v)


GELU_NATIVE = False
ARENA_BYTES = 98 * 1024

C_IDENT = 0
C_TRI = 128
C_M2 = 256
C_MC = 384
C_LCOL = 640
C_SBB = 768
C_ALB = 1280
C_NTRI = C_ALB + 256
CONST_W = C_NTRI + 128


def alibi_slope(h):
    return float(np.float32(2.0 ** (-8.0 * (h + 1) / 16)))


def _bf16_round(v):
    a = np.asarray(v, np.float32).reshape(1)
    u = a.view(np.uint32)
    r = ((u + 0x7FFF + ((u >> 16) & 1)) & 0xFFFF0000).astype(np.uint32)
    return float(r.view(np.float32)[0])


def make_consts():
    c = np.zeros((128, CONST_W), np.float32)
    p = np.arange(128)
    c[:, C_IDENT:C_IDENT + 128] = np.eye(128, dtype=np.float32)
    c[:, C_TRI:C_TRI + 128] = (p[:, None] <= p[None, :]).astype(np.float32)
    c[:, C_M2:C_M2 + 128] = ((p[:, None] <= p[None, :]) & ((p[:, None] // 64) == (p[None, :] // 64))).astype(np.float32)
    c[:, C_NTRI:C_NTRI + 128] = np.where(p[:, None] <= p[None, :], 0.0, -30000.0).astype(np.float32)
    t = np.arange(256)
    c[:, C_MC:C_MC + 256] = (t % 64 != 0).astype(np.float32)[None, :]
    for G in range(2):
        for hl in range(8):
            sl = alibi_slope(G * 8 + hl)
            hi = _bf16_round(sl)
            lo = _bf16_round(sl - hi)
            for kb in range(8):
                col = C_LCOL + G * 64 + hl * 8 + kb
                c[hl * 16 + kb, col] = 1.0
                c[hl * 16 + 8, col] = -8.0 * hi
                c[hl * 16 + 9, col] = -8.0 * hi
                c[hl * 16 + 10, col] = -8.0 * lo
                c[hl * 16 + 11, col] = -8.0 * lo
    for j in range(4):
        for hl in range(8):
            base = C_SBB + j * 128 + hl * 16
            c[:, base + 8] = (j % 2) * 128 + p
            c[:, base + 9] = 256 * (j // 2)
            c[:, base + 10] = (j % 2) * 128 + p
            c[:, base + 11] = 256 * (j // 2)
    for h in range(16):
        sl = alibi_slope(h)
        for dk in range(-12, 4):
            c[:, C_ALB + h * 16 + dk + 12] = np.float32(sl) * (p + 128.0 * dk).astype(np.float32)
    return c


class Ring:
    def __init__(self, items):
        self.items = items
        self.i = 0

    def next(self):
        it = self.items[self.i % len(self.items)]
        self.i += 1
        return it


class K:
    def __init__(self, nc, stages):
        self.nc = nc
        self.P = Prog()
        self.es = ExitStack()
        self.stages = stages
        self.uid = 0
        self.stage_no = 0
        ARENA_REG["range"] = None
        ARENA_REG["toks"] = []

    def sb(self, shape, dt, name=None):
        self.uid += 1
        return self.es.enter_context(self.nc.sbuf_tensor(f"{name or 't'}_{self.uid}", list(shape), dt))

    def salloc(self, shape, dt):
        n = 1
        for s in shape[1:]:
            n *= s
        nbytes = n * (4 if dt == F32 else 2)
        off = (self.aoff + 31) // 32 * 32
        self.aoff = off + nbytes
        assert self.aoff <= ARENA_BYTES, f"arena overflow {self.aoff}"
        if self.stage_no % 2:
            off = (ARENA_BYTES - self.aoff) // 32 * 32
        ARENA_REG["range"] = (self.stage_no, off, off + nbytes)
        v = self.arena[:, off // 2:(off + nbytes) // 2]
        if dt == F32:
            v = v.bitcast(F32)
        if len(shape) == 3:
            v = v.rearrange("p (a b) -> p a b", a=shape[1])
        elif len(shape) == 4:
            v = v.rearrange("p (a b c) -> p a b c", a=shape[1], b=shape[2])
        if shape[0] < 128:
            v = v[0:shape[0]]
        return v

    def ring(self, n, shape, dt, name="r"):
        return Ring([(self.salloc(shape, dt), Tok(name)) for _ in range(n)])

    def begin_stage(self):
        self.aoff = 0
        self.stage_no += 1
        ARENA_REG["range"] = None

    def dram_in(self, name, shape, dt=F32):
        return self.nc.dram_tensor(name, list(shape), dt, kind="ExternalInput").ap()

    def pe(self, fn, r=(), w=()):
        return self.P.op("pe", fn, r, w)

    def act(self, fn, r=(), w=()):
        return self.P.op("act", fn, r, w)

    def dve(self, fn, r=(), w=()):
        return self.P.op("dve", fn, r, w)

    def pool(self, fn, r=(), w=()):
        return self.P.op("pool", fn, r, w)

    def dma(self, q, out, in_, r=(), w=()):
        return self.P.op(q, lambda e: e.dma_start(out=out, in_=in_), r, w, dma=True)

    def dump(self, ap, tok, psum=False):
        if not os.environ.get('DEBUG_DUMP'):
            return
        n = ap.shape[-1]
        c0 = self.dump_off
        self.dump_off += n
        print("DUMP", c0, n)
        if psum:
            scr = self.salloc([128, n], F32)
            st = Tok("dscr")
            self.dve(lambda e: e.tensor_copy(out=scr, in_=ap), r=[tok], w=[st])
            self.dma("pool", self.d["dbg"][:, c0:c0 + n], scr, r=[st])
        else:
            self.dma("pool", self.d["dbg"][:, c0:c0 + n], ap, r=[tok])

    def bank(self):
        b = self.banks[self.bank_i % 6]
        self.bank_i += 1
        return b

    def obank(self):
        b = self.banks[6 + self.obank_i % 2]
        self.obank_i += 1
        return b

    def mm(self, out, lhsT, rhs, start, stop, r, w, **kw):
        n = out.shape[-1]
        return self.P.op("pe", lambda e: e.matmul(out, lhsT=lhsT, rhs=rhs, start=start, stop=stop, **kw), r, w, cost=max(n, 64) / 2400.0 + 0.012)

    def mmB(self, ps, pst, W, wt, col0, tok0, n):
        xts = [self.xT_t[q] for q in range(tok0 // 128, (tok0 + n + 127) // 128)]
        for kc in range(KC):
            self.mm(ps, W[:, kc, col0:col0 + 128], self.xT[:, kc, tok0:tok0 + n], kc == 0, kc == KC - 1, [wt] + xts, [pst])

    def mmA(self, ps, pst, i, W, wt, col0, n):
        for kc in range(KC):
            self.mm(ps, self.xT[:, kc, i * 128:(i + 1) * 128], W[:, kc, col0:col0 + n], kc == 0, kc == KC - 1,
                    [wt, self.xT_t[i]], [pst])

    def wload(self, dst, src, tok):
        self.dma("pool", dst, src, w=[tok])

    def wload_w(self, dst, W_d, col0, n, tok, step=512):
        for c in range(0, n, step):
            m = min(step, n - c)
            self.wload(dst[:, :, c:c + m], W_d[:, col0 + c:col0 + c + m].rearrange("(k p) n -> p k n", p=128), tok)

    def build(self):
        nc = self.nc
        d = {}
        d["x"] = self.dram_in("x", [T, D])
        d["mem"] = self.dram_in("mem", [MEM, D])
        d["lng"] = self.dram_in("ln_g", [6, D])
        d["lnb"] = self.dram_in("ln_b", [6, D])
        d["xwq"] = self.dram_in("x_wq", [2, D, D])
        d["xwkv"] = self.dram_in("x_wkv", [2, D, 2 * D])
        d["xwo"] = self.dram_in("x_wo", [2, D, D])
        d["fwi"] = self.dram_in("ffn_w_in", [2, D, 2 * DFF])
        d["fwo"] = self.dram_in("ffn_w_out", [2, DFF, D])
        d["evi"] = self.dram_in("ev_w_in", [D, 3072])
        d["evo"] = self.dram_in("ev_w_out", [D, D])
        d["awsT"] = self.dram_in("a_wsT", [128, 4, 128])
        d["abs"] = self.dram_in("a_bs", [1, 512])
        d["alng"] = self.dram_in("a_ln_g", [1, 512])
        d["alnb"] = self.dram_in("a_ln_b", [1, 512])
        d["bng"] = self.dram_in("b_norm_gT", [128, 4])
        d["lbl"] = self.dram_in("lb_logitsT", [128, 2, 4])
        d["oqkv"] = self.dram_in("od_w_qkv", [D, 3072])
        d["owo"] = self.dram_in("od_w_out", [D, D])
        d["cst"] = self.dram_in("consts", [128, CONST_W])
        d["out"] = nc.dram_tensor("out", [T, D], F32, kind="ExternalOutput").ap()
        if os.environ.get('DEBUG_DUMP'):
            d["dbg"] = nc.dram_tensor("dbg", [128, 8192], F32, kind="ExternalOutput").ap()
        self.dump_off = 0
        self.d = d

        self.x_tok = self.sb([128, NT, D], F32, "x_tok")
        self.xT = self.sb([128, KC, T], BF16, "xT")
        self.xtok_t = [Tok(f"xtok{i}") for i in range(NT)]
        self.xT_t = [Tok(f"xT{i}") for i in range(NT)]
        self.cst = self.sb([128, CONST_W], F32, "cst")
        self.cst_t = Tok("cst")
        self.ident = self.sb([128, 128], BF16, "ident")
        self.ident_t = Tok("ident")
        self.xb_ring = Ring([(self.sb([128, D], BF16, "xb"), Tok("xb")) for _ in range(2)])
        self.st_ring = Ring([(self.sb([128, 32], F32, "lnst"), Tok("lnst")) for _ in range(3)])
        self.negh = self.sb([128, 512], F32, "negh")
        self.negh_t = Tok("negh")
        self.arena = self.sb([128, ARENA_BYTES // 2], BF16, "arena")
        self.aoff = 0
        self.banks = []
        for i in range(8):
            pt = self.es.enter_context(nc.psum_tensor(f"bank{i}", [128, 512], F32))
            self.banks.append((pt, Tok(f"bank{i}")))
        self.bank_i = 0
        self.obank_i = 0
        self.cp_i = 0

        self.dma("sp", self.cst[:], d["cst"], w=[self.cst_t])
        self.dve(lambda e: e.tensor_copy(out=self.ident[:], in_=self.cst[:, C_IDENT:C_IDENT + 128]),
                 r=[self.cst_t], w=[self.ident_t])
        self.pool(lambda e: e.memset(self.negh[:], -0.5), w=[self.negh_t])

        self.load_x()
        for s in self.stages:
            getattr(self, "stage_" + s[0])(*s[1:])
        ARENA_REG["range"] = None
        self.store_out()

        sems = {e: self.es.enter_context(nc.semaphore(f"s_{e}")) for e in Prog.ENGS}
        dma_sems = {}
        for q, n in (("sp", 24), ("pool", 32), ("act", 2)):
            dma_sems[q] = [self.es.enter_context(nc.semaphore(f"d_{q}{i}")) for i in range(n)]
        self.P.emit(nc, None, sems, dma_sems)
        self.es.close()

    def copy(self, out, in_, r, w):
        self.cp_i += 1
        if self.cp_i % 2:
            return self.act(lambda e: e.copy(out=out, in_=in_), r=r, w=w)
        return self.dve(lambda e: e.tensor_copy(out=out, in_=in_), r=r, w=w)

    def load_x(self):
        for i in range(NT):
            self.dma("sp", self.x_tok[:, i, :], self.d["x"][i * 128:(i + 1) * 128, :], w=[self.xtok_t[i]])
        for i in range(NT):
            self.to_xT(i)

    def store_out(self):
        for i in range(NT):
            self.dma("sp", self.d["out"][i * 128:(i + 1) * 128, :], self.x_tok[:, i, :], r=[self.xtok_t[i]])

    def transpose_to(self, dst_view, src_tiles, r, w):
        bk, bt = self.bank()
        psb = bk[:].bitcast(BF16)
        n = len(src_tiles)
        for k, src in enumerate(src_tiles):
            self.pe(lambda e, k=k, src=src: e.transpose(out=psb[:, k * 128:(k + 1) * 128], in_=src, identity=self.ident[:]),
                    r=list(r) + [self.ident_t], w=[bt])
        self.copy(dst_view, psb[:, 0:n * 128].rearrange("p (k t) -> p k t", k=n), [bt], w)

    def to_xT(self, i):
        xb, xbt = self.xb_ring.next()
        self.act(lambda e: e.copy(out=xb[:], in_=self.x_tok[:, i, :]), r=[self.xtok_t[i]], w=[xbt])
        self.transpose_to(self.xT[:, :, i * 128:(i + 1) * 128], [xb[:, kc * 128:(kc + 1) * 128] for kc in range(KC)],
                          [xbt], [self.xT_t[i]])

    def load_ln(self, idx):
        gb = self.salloc([128, 2, D], F32)
        t = Tok("lnp")
        g, b = gb[:, 0, :], gb[:, 1, :]
        self.dma("sp", g, self.d["lng"][idx:idx + 1, :].to_broadcast([128, D]), w=[t])
        self.dma("sp", b, self.d["lnb"][idx:idx + 1, :].to_broadcast([128, D]), w=[t])
        return g, b, t

    def rstd_small(self, out, var, r, w, eps=EPS):
        n = out.shape[-1]
        self.pool(lambda e: e.tensor_scalar(out=out, in0=var, scalar1=eps, scalar2=None, op0=ALU.add), r=r, w=w)
        self.pool(lambda e: e.tensor_tensor(out=out, in0=out, in1=self.negh[:, 0:n], op=ALU.pow), r=list(w) + [self.negh_t], w=w)

    def layer_norm(self, i, lnp):
        g, b, gt = lnp
        xt = self.x_tok[:, i, :]
        xtok = self.xtok_t[i]
        stt, stk = self.st_ring.next()
        self.dve(lambda e: e.bn_stats(out=stt[:, 0:6], in_=self.x_tok[:, i, 0:512]), r=[xtok], w=[stk])
        self.dve(lambda e: e.bn_stats(out=stt[:, 6:12], in_=self.x_tok[:, i, 512:1024]), r=[xtok], w=[stk])
        self.dve(lambda e: e.bn_aggr(out=stt[:, 12:14], in_=stt[:, 0:12].rearrange("p (a b) -> p a b", a=2)), r=[stk], w=[stk])
        self.rstd_small(stt[:, 15:16], stt[:, 13:14], [stk], [stk])
        self.dve(lambda e: e.scalar_tensor_tensor(out=stt[:, 16:17], in0=stt[:, 12:13], scalar=-1.0, in1=stt[:, 15:16],
                                                  op0=ALU.mult, op1=ALU.mult), r=[stk], w=[stk])
        self.act(lambda e: e.activation(out=xt, in_=xt, func=AF.Identity, bias=stt[:, 16:17], scale=stt[:, 15:16]),
                 r=[stk, xtok], w=[xtok])
        self.pool(lambda e: e.tensor_tensor(out=xt, in0=xt, in1=g, op=ALU.mult), r=[xtok, gt], w=[xtok])
        self.dve(lambda e: e.tensor_tensor(out=xt, in0=xt, in1=b, op=ALU.add), r=[xtok, gt], w=[xtok])
        self.to_xT(i)

    def accum(self, i, half, ps, pst, first):
        dst = self.x_tok[:, i, half * 512:(half + 1) * 512]
        if first:
            self.dve(lambda e: e.scalar_tensor_tensor(out=dst, in0=dst, scalar=ALPHA, in1=ps, op0=ALU.mult, op1=ALU.add),
                     r=[pst, self.xtok_t[i]], w=[self.xtok_t[i]])
        else:
            self.dve(lambda e: e.tensor_tensor(out=dst, in0=dst, in1=ps, op=ALU.add),
                     r=[pst, self.xtok_t[i]], w=[self.xtok_t[i]])

    def out_proj(self, i, lhs_fn, nk, lhs_toks, wo, wot, first):
        for half in range(2):
            ps, pst = self.bank()
            for k in range(nk):
                self.mm(ps[:], lhs_fn(k), wo[:, k, half * 512:(half + 1) * 512], k == 0, k == nk - 1, list(lhs_toks) + [wot], [pst])
            self.accum(i, half, ps[:], pst, first)

    def stage_ln_only(self, idx):
        self.begin_stage()
        lnp = self.load_ln(idx)
        for i in range(NT):
            self.layer_norm(i, lnp)

    def stage_ffn(self, l):
        self.begin_stage()
        groups = [list(range(0, 6)), list(range(6, 12)), list(range(12, 17)), list(range(17, 22))]
        fwi = self.d["fwi"][l]
        fwo = self.d["fwo"][l]
        lnp = self.load_ln(l * 3 + 2)
        wi_r = self.ring(3, [128, KC, 256], BF16, "fwi")
        wo_r = self.ring(2, [128, 6, D], BF16, "fwo")
        hT = self.salloc([128, 6, T], BF16)
        hTt = [[Tok("hT") for _ in range(4)] for _ in range(6)]
        sg_r = self.ring(2, [128, 512], F32, "sg")
        for gi, js in enumerate(groups):
            wo, wot = wo_r.next()
            j0 = js[0]
            self.wload(wo[:, 0:len(js), :], fwo[j0 * 128:(j0 + len(js)) * 128, :].rearrange("(j p) n -> p j n", p=128), wot)
            for jl, j in enumerate(js):
                wi, wit = wi_r.next()
                self.wload(wi[:, :, 0:128], fwi[:, j * 128:(j + 1) * 128].rearrange("(k p) n -> p k n", p=128), wit)
                self.wload(wi[:, :, 128:256], fwi[:, DFF + j * 128:DFF + (j + 1) * 128].rearrange("(k p) n -> p k n", p=128), wit)
                for tb in range(4):
                    pg, pgt = self.bank()
                    pu, put = self.bank()
                    self.mmB(pg[:], pgt, wi, wit, 0, tb * 512, 512)
                    self.mmB(pu[:], put, wi, wit, 128, tb * 512, 512)
                    sg, sgt = sg_r.next()
                    self.act(lambda e, sg=sg, pg=pg: e.activation(out=sg, in_=pg[:], func=AF.Silu), r=[pgt], w=[sgt])
                    self.dve(lambda e, sg=sg, pu=pu, jl=jl, tb=tb: e.tensor_tensor(out=hT[:, jl, tb * 512:(tb + 1) * 512], in0=sg, in1=pu[:], op=ALU.mult),
                             r=[sgt, put], w=[hTt[jl][tb]])
            last = gi == len(groups) - 1
            for i in range(NT):
                self.out_proj(i, lambda k, i=i: hT[:, k, i * 128:(i + 1) * 128], len(js), [hTt[k][i // 4] for k in range(len(js))], wo, wot, first=(gi == 0))
                if last:
                    self.layer_norm(i, lnp)

    def stage_xattn(self, l):
        self.begin_stage()
        d = self.d
        SC = 1.0 / 16.0
        lnp = self.load_ln(l * 3 + 1)
        wq, wqt = self.salloc([128, KC, D], BF16), Tok("wq")
        wo, wot = self.salloc([128, KC, D], BF16), Tok("wo")
        kT, kTt = self.salloc([128, KC, MEM], BF16), [Tok("kT") for _ in range(KC)]
        vS, vSt = self.salloc([128, 2, D], BF16), [[Tok("vS") for _ in range(2)] for _ in range(2)]
        memb, membt = self.salloc([128, 2, D], BF16), Tok("memb")
        memT, memTt = self.salloc([128, KC, MEM], BF16), Tok("memT")
        wkv_r = self.ring(1, [128, KC, 512], BF16, "wkv")
        qT_r = Ring([(self.salloc([128, KC, 512], BF16), [Tok("qT") for _ in range(KC)])])
        p32_r = self.ring(2, [128, 4, 256], F32, "p32")
        pb_r = self.ring(2, [128, 4, 256], BF16, "pb")
        pT_r = self.ring(2, [128, 8, 128], BF16, "pT")
        oT_r = self.ring(2, [128, 8, 128], BF16, "oT")
        sm_r = self.ring(3, [128, 16], F32, "sm")
        self.wload(memb, d["mem"].rearrange("(m p) n -> p m n", p=128), membt)
        for mt in range(2):
            self.transpose_to(memT[:, :, mt * 128:(mt + 1) * 128], [memb[:, mt, kc * 128:(kc + 1) * 128] for kc in range(KC)],
                              [membt], [memTt])
        for c in range(4):
            wk, wkt = wkv_r.next()
            self.wload_w(wk, d["xwkv"][l], c * 512, 512, wkt)
            if c < 2:
                for cc in range(4):
                    fc = c * 4 + cc
                    ps, pst = self.bank()
                    for kc in range(KC):
                        self.mm(ps[:, 0:MEM], wk[:, kc, cc * 128:(cc + 1) * 128], memT[:, kc, :], kc == 0, kc == KC - 1, [wkt, memTt], [pst])
                    self.copy(kT[:, fc, :], ps[:, 0:MEM], [pst], [kTt[fc]])
            else:
                for mt in range(2):
                    ps, pst = self.bank()
                    for kc in range(KC):
                        self.mm(ps[:], memT[:, kc, mt * 128:(mt + 1) * 128], wk[:, kc, :], kc == 0, kc == KC - 1, [wkt, memTt], [pst])
                    self.copy(vS[:, mt, (c - 2) * 512:(c - 1) * 512], ps[:], [pst], [vSt[mt][c - 2]])
        self.wload_w(wq, d["xwq"][l], 0, D, wqt)
        self.wload_w(wo, d["xwo"][l], 0, D, wot)
        for tb in range(4):
            qT, qTt = qT_r.next()
            for c in range(KC):
                ps, pst = self.bank()
                self.mmB(ps[:], pst, wq, wqt, c * 128, tb * 512, 512)
                self.copy(qT[:, c, :], ps[:], [pst], [qTt[c]])
            for il in range(4):
                i = tb * 4 + il
                sA, sAt = self.bank()
                sB, sBt = self.bank()
                for h in range(4):
                    bk, bkt = (sA, sAt) if h < 2 else (sB, sBt)
                    for k2 in range(2):
                        self.mm(bk[:, (h % 2) * 256:(h % 2 + 1) * 256], qT[:, 2 * h + k2, il * 128:(il + 1) * 128], kT[:, 2 * h + k2, :],
                                k2 == 0, k2 == 1, [qTt[2 * h + k2], kTt[2 * h + k2]], [bkt])
                sm, smt = sm_r.next()
                self.dve(lambda e, sm=sm, sA=sA: e.tensor_reduce(out=sm[:, 0:2], in_=sA[:].rearrange("p (h m) -> p h m", h=2), axis=AX.X, op=ALU.max), r=[sAt], w=[smt])
                self.dve(lambda e, sm=sm, sB=sB: e.tensor_reduce(out=sm[:, 2:4], in_=sB[:].rearrange("p (h m) -> p h m", h=2), axis=AX.X, op=ALU.max), r=[sBt], w=[smt])
                self.dve(lambda e, sm=sm: e.tensor_scalar(out=sm[:, 4:8], in0=sm[:, 0:4], scalar1=-SC, scalar2=None, op0=ALU.mult), r=[smt], w=[smt])
                p32, p32t = p32_r.next()
                for h in range(4):
                    bk, bkt = (sA, sAt) if h < 2 else (sB, sBt)
                    self.act(lambda e, h=h, bk=bk, sm=sm, p32=p32: e.activation(out=p32[:, h, :], in_=bk[:, (h % 2) * 256:(h % 2 + 1) * 256], func=AF.Exp,
                                                                             bias=sm[:, 4 + h:5 + h], scale=SC, accum_out=sm[:, 8 + h:9 + h]),
                             r=[bkt, smt], w=[p32t, smt])
                self.dve(lambda e, sm=sm: e.reciprocal(out=sm[:, 12:16], in_=sm[:, 8:12]), r=[smt], w=[smt])
                pb, pbt = pb_r.next()
                for h in range(4):
                    if h % 2:
                        self.dve(lambda e, h=h, pb=pb, p32=p32, sm=sm: e.tensor_scalar(out=pb[:, h, :], in0=p32[:, h, :], scalar1=sm[:, 12 + h:13 + h], scalar2=None, op0=ALU.mult),
                                 r=[p32t, smt], w=[pbt])
                    else:
                        self.act(lambda e, h=h, pb=pb, p32=p32, sm=sm: e.activation(out=pb[:, h, :], in_=p32[:, h, :], func=AF.Copy, scale=sm[:, 12 + h:13 + h]),
                                 r=[p32t, smt], w=[pbt])
                pT, pTt = pT_r.next()
                self.transpose_to(pT, [pb[:, h, mt * 128:(mt + 1) * 128] for h in range(4) for mt in range(2)], [pbt], [pTt])
                oA, oAt = self.bank()
                oB, oBt = self.bank()
                oT, oTt = oT_r.next()
                for c in range(8):
                    bk, bkt = (oA, oAt) if c < 4 else (oB, oBt)
                    h = c // 2
                    for mt in range(2):
                        self.mm(bk[:, (c % 4) * 128:(c % 4 + 1) * 128], vS[:, mt, c * 128:(c + 1) * 128], pT[:, h * 2 + mt, :],
                                mt == 0, mt == 1, [vSt[mt][c // 4], pTt], [bkt])
                self.copy(oT[:, 0:4, :], oA[:].rearrange("p (c t) -> p c t", c=4), [oAt], [oTt])
                self.copy(oT[:, 4:8, :], oB[:].rearrange("p (c t) -> p c t", c=4), [oBt], [oTt])
                self.out_proj(i, lambda k, oT=oT: oT[:, k, :], KC, [oTt], wo, wot, first=True)
                self.layer_norm(i, lnp)

    def gelu(self, out, ps, pst, outt, scr_r, half=True):
        if GELU_NATIVE:
            self.act(lambda e: e.activation(out=out, in_=ps, func=AF.Gelu_apprx_tanh), r=[pst], w=[outt])
            return 1.0
        C0 = 0.7978845608028654
        C1 = 0.044715
        s, stk = scr_r.next()
        n = ps.shape[-1]
        sv = s[:, 0:n]
        self.act(lambda e: e.activation(out=sv, in_=ps, func=AF.Square), r=[pst], w=[stk])
        self.dve(lambda e: e.tensor_scalar(out=sv, in0=sv, scalar1=C1, scalar2=1.0, op0=ALU.mult, op1=ALU.add), r=[stk], w=[stk])
        self.dve(lambda e: e.tensor_tensor(out=sv, in0=sv, in1=ps, op=ALU.mult), r=[stk, pst], w=[stk])
        self.act(lambda e: e.activation(out=sv, in_=sv, func=AF.Tanh, scale=C0), r=[stk], w=[stk])
        self.dve(lambda e: e.scalar_tensor_tensor(out=out, in0=sv, scalar=1.0, in1=ps, op0=ALU.add, op1=ALU.mult), r=[stk, pst], w=[outt])
        return 0.5

    def stage_mix0(self):
        d = self.d
        self.begin_stage()
        wuv, wut, wvt = self.salloc([128, KC, 1024], BF16), Tok("wu"), Tok("wv")
        woA, woAt = self.salloc([128, 4, D], BF16), Tok("woA")
        ws32, ws32t = self.salloc([128, 4, 128], F32), Tok("ws32")
        wsb, wsbt = self.salloc([128, 4, 128], BF16), Tok("wsb")
        bsB, bsBt = self.salloc([128, 512], F32), Tok("bsB")
        lgB, lgBt = self.salloc([128, 512], F32), Tok("lgB")
        lbB, lbBt = self.salloc([128, 512], F32), Tok("lbB")
        uT_r = self.ring(2, [128, 4, 512], BF16, "uT")
        scr_r = self.ring(3, [128, 512], F32, "gscr")
        vg_r = self.ring(2, [128, 512], F32, "vg")
        vb_r = self.ring(2, [128, 512], BF16, "vb")
        ss_r = self.ring(2, [128, 512], F32, "ssum")
        ya_r = self.ring(2, [128, 4, 128], BF16, "yaT")
        st_r = self.ring(2, [128, 48], F32, "gst")
        self.wload_w(wuv[:, :, 0:512], d["evi"], 0, 512, wut)
        self.wload_w(wuv[:, :, 512:1024], d["evi"], 512, 512, wvt)
        self.wload(woA, d["evo"][0:512, :].rearrange("(g p) n -> p g n", p=128), woAt)
        self.dma("sp", ws32, d["awsT"], w=[ws32t])
        self.dma("sp", bsB, d["abs"].to_broadcast([128, 512]), w=[bsBt])
        self.dma("sp", lgB, d["alng"].to_broadcast([128, 512]), w=[lgBt])
        self.dma("sp", lbB, d["alnb"].to_broadcast([128, 512]), w=[lbBt])
        for g in range(4):
            self.dve(lambda e, g=g: e.tensor_tensor(out=wsb[:, g, :], in0=ws32[:, g, :], in1=self.cst[:, C_TRI:C_TRI + 128], op=ALU.mult),
                     r=[ws32t, self.cst_t], w=[wsbt])
        for tb in range(4):
            uT, uTt = uT_r.next()
            gf = 1.0
            for c in range(4):
                ps, pst = self.bank()
                self.mmB(ps[:], pst, wuv, wut, c * 128, tb * 512, 512)
                gf = self.gelu(uT[:, c, :], ps[:], pst, uTt, scr_r)
            for il in range(4):
                i = tb * 4 + il
                ps, pst = self.bank()
                self.mmA(ps[:], pst, i, wuv, wvt, 512, 512)
                vg, vgt = vg_r.next()
                gv = self.gelu(vg, ps[:], pst, vgt, scr_r)
                st, stt = st_r.next()
                for g in range(4):
                    self.dve(lambda e, g=g, st=st, vg=vg: e.bn_stats(out=st[:, g * 6:(g + 1) * 6], in_=vg[:, g * 128:(g + 1) * 128]), r=[vgt], w=[stt])
                for g in range(4):
                    self.dve(lambda e, g=g, st=st: e.bn_aggr(out=st[:, 24 + g * 2:26 + g * 2], in_=st[:, g * 6:(g + 1) * 6]), r=[stt], w=[stt])
                mv = st[:, 24:32].rearrange("p (g two) -> p g two", two=2)
                self.pool(lambda e, st=st, mv=mv: e.tensor_scalar(out=st[:, 32:36], in0=mv[:, :, 1], scalar1=gv * gv, scalar2=EPS, op0=ALU.mult, op1=ALU.add), r=[stt], w=[stt])
                self.pool(lambda e, st=st: e.tensor_tensor(out=st[:, 32:36], in0=st[:, 32:36], in1=self.negh[:, 0:4], op=ALU.pow), r=[stt, self.negh_t], w=[stt])
                self.dve(lambda e, st=st: e.tensor_scalar(out=st[:, 36:40], in0=st[:, 32:36], scalar1=gv, scalar2=None, op0=ALU.mult), r=[stt], w=[stt])
                self.dve(lambda e, st=st, mv=mv: e.scalar_tensor_tensor(out=st[:, 40:44], in0=mv[:, :, 0], scalar=-1.0, in1=st[:, 36:40], op0=ALU.mult, op1=ALU.mult), r=[stt], w=[stt])
                for g in range(4):
                    self.act(lambda e, g=g, st=st, vg=vg: e.activation(out=vg[:, g * 128:(g + 1) * 128], in_=vg[:, g * 128:(g + 1) * 128], func=AF.Identity,
                                                                      bias=st[:, 40 + g:41 + g], scale=st[:, 36 + g:37 + g]), r=[stt, vgt], w=[vgt])
                self.dve(lambda e, vg=vg: e.tensor_tensor(out=vg, in0=vg, in1=lgB, op=ALU.mult), r=[vgt, lgBt], w=[vgt])
                vb, vbt = vb_r.next()
                self.pool(lambda e, vg=vg, vb=vb: e.tensor_tensor(out=vb, in0=vg, in1=lbB, op=ALU.add), r=[vgt, lbBt], w=[vbt])
                pss, psst = self.bank()
                for g in range(4):
                    self.mm(pss[:, g * 128:(g + 1) * 128], vb[:, g * 128:(g + 1) * 128], wsb[:, g, :], True, True, [vbt, wsbt], [psst])
                ss, sst = ss_r.next()
                self.dve(lambda e, ss=ss, pss=pss: e.tensor_tensor(out=ss, in0=pss[:], in1=bsB, op=ALU.add), r=[psst, bsBt], w=[sst])
                ya, yat = ya_r.next()
                self.dve(lambda e, ya=ya, uT=uT, ss=ss, il=il: e.scalar_tensor_tensor(out=ya, in0=uT[:, :, il * 128:(il + 1) * 128], scalar=gf,
                                                                                   in1=ss.rearrange("p (g t) -> p g t", g=4), op0=ALU.mult, op1=ALU.mult),
                         r=[uTt, sst], w=[yat])
                self.out_proj(i, lambda k, ya=ya: ya[:, k, :], 4, [yat], woA, woAt, first=True)

        self.begin_stage()
        NB = 256
        lnp = self.load_ln(0)
        wB, wBt = self.salloc([128, KC, 2048], BF16), [Tok("wBq"), Tok("wBf"), Tok("wBi"), Tok("wBg")]
        woB, woBt = self.salloc([128, 4, D], BF16), Tok("woB")
        sm, smt = self.salloc([128, 32], F32), Tok("hsm")
        S, St = self.salloc([128, 4, 128], F32), Tok("S")
        Sb_r = self.ring(3, [128, 4, 128], BF16, "Sb")
        m2x4, m2t = self.salloc([128, 512], BF16), Tok("m2x4")
        ones, onest = self.salloc([128, 128], BF16), Tok("ones")
        f1 = self.ring(1, [128, 4, NB], F32, "f1")
        f2 = self.ring(1, [128, 4, NB], F32, "f2")
        f3 = self.ring(1, [128, 4, NB], F32, "f3")
        E_r = self.ring(1, [128, 4, NB], F32, "E")
        qd_r = self.ring(1, [128, 4, NB], BF16, "qdT")
        kd_r = self.ring(1, [128, 4, NB], BF16, "kdT")
        gs_r = self.ring(1, [128, 4, NB], BF16, "gsT")
        vi_r = self.ring(1, [128, 2, 512], BF16, "vi")
        kt_r = self.ring(1, [128, 2, 512], BF16, "kdtok")
        am_r = self.ring(2, [128, 512], BF16, "am")
        tmp_r = self.ring(1, [128, 512], F32, "stmp")
        sq_r = self.ring(2, [128, 512], BF16, "sq")
        r_r = self.ring(1, [128, 512], F32, "rr")
        t1_r = self.ring(1, [128, 512], F32, "t1")
        yb_r = self.ring(2, [128, 4, 128], BF16, "ybT")
        for c in range(4):
            self.wload_w(wB[:, :, c * 512:(c + 1) * 512], d["evi"], 1024 + c * 512, 512, wBt[c])
        self.wload(woB, d["evo"][512:1024, :].rearrange("(g p) n -> p g n", p=128), woBt)
        self.dma("sp", sm[:, 0:8], d["lbl"].rearrange("p l h -> p (l h)"), w=[smt])
        self.dma("sp", sm[:, 20:24], d["bng"], w=[smt])
        self.dve(lambda e: e.tensor_tensor(out=sm[:, 8:12], in0=sm[:, 0:4], in1=sm[:, 4:8], op=ALU.subtract), r=[smt], w=[smt])
        self.act(lambda e: e.activation(out=sm[:, 12:16], in_=sm[:, 8:12], func=AF.Sigmoid), r=[smt], w=[smt])
        self.dve(lambda e: e.tensor_scalar(out=sm[:, 16:20], in0=sm[:, 12:16], scalar1=-1.0, scalar2=1.0, op0=ALU.mult, op1=ALU.add), r=[smt], w=[smt])
        self.pool(lambda e: e.memset(S.rearrange("p h v -> p (h v)"), 0.0), w=[St])
        Sb, Sbt = Sb_r.next()
        self.pool(lambda e, Sb=Sb: e.memset(Sb.rearrange("p h v -> p (h v)"), 0.0), w=[Sbt])
        self.pool(lambda e: e.memset(ones, 1.0), w=[onest])
        epsc, epst = self.salloc([128, 8], F32), Tok("eps")
        self.pool(lambda e: e.memset(epsc, EPS), w=[epst])
        for h in range(4):
            self.dve(lambda e, h=h: e.tensor_copy(out=m2x4[:, h * 128:(h + 1) * 128], in_=self.cst[:, C_M2:C_M2 + 128]), r=[self.cst_t], w=[m2t])
        maskc = self.cst[:, C_MC:C_MC + NB]
        for blk in range(T // NB):
            tok0 = blk * NB
            s1, s1t = f1.next()
            s2, s2t = f2.next()
            s3, s3t = f3.next()
            E, Et = E_r.next()
            qd, qdt = qd_r.next()
            kd, kdt = kd_r.next()
            gs, gst = gs_r.next()
            qf = []
            for h in range(4):
                b1, b1t = self.bank()
                self.mmB(b1[:, 0:NB], b1t, wB, wBt[0], h * 128, tok0, NB)
                self.mmB(b1[:, NB:2 * NB], b1t, wB, wBt[1], 512 + h * 128, tok0, NB)
                qf.append((b1, b1t))
                self.act(lambda e, h=h, b1=b1, s1=s1: e.activation(out=s1[:, h, :], in_=b1[:, NB:2 * NB], func=AF.Sigmoid), r=[b1t], w=[s1t])
            for h in range(4):
                self.dve(lambda e, h=h, s1=s1: e.tensor_scalar(out=s1[:, h, :], in0=s1[:, h, :], scalar1=sm[:, 16 + h:17 + h], scalar2=sm[:, 12 + h:13 + h],
                                                              op0=ALU.mult, op1=ALU.add), r=[s1t, smt], w=[s1t])
            self.act(lambda e, s1=s1, s2=s2: e.activation(out=s2, in_=s1, func=AF.Ln), r=[s1t], w=[s2t])
            for h in range(4):
                self.dve(lambda e, h=h, s2=s2, s3=s3: e.tensor_tensor_scan(out=s3[:, h, :], data0=maskc, data1=s2[:, h, :], initial=0.0, op0=ALU.mult, op1=ALU.add),
                         r=[s2t, self.cst_t], w=[s3t])
            self.pool(lambda e, s1=s1: e.tensor_scalar(out=s1, in0=s1, scalar1=-1.0, scalar2=1.0, op0=ALU.mult, op1=ALU.add), r=[s1t], w=[s1t])
            self.act(lambda e, E=E, s3=s3: e.activation(out=E, in_=s3, func=AF.Exp), r=[s3t], w=[Et])
            self.act(lambda e, s2=s2, s3=s3: e.activation(out=s2, in_=s3, func=AF.Exp, scale=-1.0), r=[s3t, s2t], w=[s2t])
            for h in range(4):
                b1, b1t = qf[h]
                self.dve(lambda e, h=h, b1=b1, E=E, qd=qd: e.tensor_tensor(out=qd[:, h, :], in0=b1[:, 0:NB], in1=E[:, h, :], op=ALU.mult), r=[b1t, Et], w=[qdt])
            self.pool(lambda e, s1=s1, s2=s2, kd=kd: e.tensor_tensor(out=kd, in0=s1, in1=s2, op=ALU.mult), r=[s1t, s2t], w=[kdt])
            for h in range(0, 4, 2):
                b2, b2t = self.bank()
                self.mmB(b2[:, 0:NB], b2t, wB, wBt[3], 1536 + h * 128, tok0, NB)
                self.mmB(b2[:, NB:2 * NB], b2t, wB, wBt[3], 1536 + (h + 1) * 128, tok0, NB)
                self.act(lambda e, h=h, b2=b2, gs=gs: e.activation(out=gs[:, h:h + 2, :], in_=b2[:].rearrange("p (a t) -> p a t", a=2), func=AF.Silu), r=[b2t], w=[gst])
            vi, vit = vi_r.next()
            ktk, ktkt = kt_r.next()
            for il in range(2):
                b3, b3t = self.bank()
                self.mmA(b3[:], b3t, blk * 2 + il, wB, wBt[2], 1024, 512)
                self.act(lambda e, il=il, b3=b3, vi=vi: e.activation(out=vi[:, il, :], in_=b3[:], func=AF.Silu), r=[b3t], w=[vit])
                self.transpose_to(ktk[:, il, :].rearrange("p (h k) -> p h k", h=4), [kd[:, h, il * 128:(il + 1) * 128] for h in range(4)], [kdt], [ktkt])
            for il in range(2):
                i = blk * 2 + il
                tc0 = il * 128
                U = [self.bank(), self.bank()]
                for c in range(2):
                    for h in range(4):
                        self.mm(U[c][0][:, h * 128:(h + 1) * 128], ktk[c * 64:(c + 1) * 64, il, h * 128:(h + 1) * 128],
                                vi[c * 64:(c + 1) * 64, il, h * 128:(h + 1) * 128], True, True, [ktkt, vit], [U[c][1]])
                A, At = self.bank()
                for h in range(4):
                    self.mm(A[:, h * 128:(h + 1) * 128], kd[:, h, tc0:tc0 + 128], qd[:, h, tc0:tc0 + 128], True, True, [kdt, qdt], [At])
                am, amt = am_r.next()
                self.dve(lambda e, am=am, A=A: e.tensor_tensor(out=am, in0=A[:], in1=m2x4, op=ALU.mult), r=[At, m2t], w=[amt])
                Sbs = [(Sb, Sbt)]
                for c in range(2):
                    tmp, tmpt = tmp_r.next()
                    Uc, Uct = U[c]
                    self.dve(lambda e, tmp=tmp, Uc=Uc: e.tensor_tensor(out=tmp, in0=Uc[:], in1=S.rearrange("p h v -> p (h v)"), op=ALU.add), r=[Uct, St], w=[tmpt])
                    Sb, Sbt = Sb_r.next()
                    col = tc0 + c * 64 + 63
                    for h in range(4):
                        self.dve(lambda e, h=h, tmp=tmp, E=E, col=col: e.tensor_scalar(out=S[:, h, :], in0=tmp[:, h * 128:(h + 1) * 128], scalar1=E[:, h, col:col + 1],
                                                                                    scalar2=None, op0=ALU.mult), r=[tmpt, Et], w=[St])
                        self.act(lambda e, h=h, tmp=tmp, E=E, col=col, Sb=Sb: e.activation(out=Sb[:, h, :], in_=tmp[:, h * 128:(h + 1) * 128], func=AF.Copy,
                                                                                        scale=E[:, h, col:col + 1]), r=[tmpt, Et], w=[Sbt])
                    Sbs.append((Sb, Sbt))
                O, Ot = self.bank()
                for h in range(4):
                    self.mm(O[:, h * 128:(h + 1) * 128], vi[:, il, h * 128:(h + 1) * 128], am[:, h * 128:(h + 1) * 128], True, False, [vit, amt], [Ot])
                    for c in range(2):
                        Sc, Sct = Sbs[c]
                        self.mm(O[:, h * 128 + c * 64:h * 128 + (c + 1) * 64], Sc[:, h, :], qd[:, h, tc0 + c * 64:tc0 + (c + 1) * 64], False, c == 1,
                                [Sct, qdt], [Ot], skip_group_check=True)
                sq, sqt = sq_r.next()
                self.act(lambda e, sq=sq, O=O: e.activation(out=sq, in_=O[:], func=AF.Square), r=[Ot], w=[sqt])
                Q, Qt = self.bank()
                self.mm(Q[:], ones, sq, True, True, [onest, sqt], [Qt])
                rr, rrt = r_r.next()
                self.act(lambda e, rr=rr, Q=Q: e.activation(out=rr, in_=Q[:], func=AF.Ln, bias=epsc[:, 0:1], scale=1.0 / 128.0), r=[Qt, epst], w=[rrt])
                self.act(lambda e, rr=rr: e.activation(out=rr, in_=rr, func=AF.Exp, scale=-0.5), r=[rrt], w=[rrt])
                t1, t1t = t1_r.next()
                self.dve(lambda e, t1=t1, O=O, rr=rr: e.tensor_tensor(out=t1, in0=O[:], in1=rr, op=ALU.mult), r=[Ot, rrt], w=[t1t])
                yb, ybt = yb_r.next()
                for h in range(4):
                    self.dve(lambda e, h=h, yb=yb, t1=t1, gs=gs, tc0=tc0: e.scalar_tensor_tensor(out=yb[:, h, :], in0=t1[:, h * 128:(h + 1) * 128], scalar=sm[:, 20 + h:21 + h],
                                                                                              in1=gs[:, h, tc0:tc0 + 128], op0=ALU.mult, op1=ALU.mult),
                             r=[t1t, gst, smt], w=[ybt])
                self.out_proj(i, lambda k, yb=yb: yb[:, k, :], 4, [ybt], woB, woBt, first=False)
                self.layer_norm(i, lnp)

    def stage_moba(self):
        for G in range(2):
            self.moba_group(G)

    def moba_group(self, G):
        d = self.d
        if True:
            self.begin_stage()
            lnp = self.load_ln(3) if G == 1 else None
            kT, kTt = self.salloc([128, 4, T], BF16), [[Tok("kT") for _ in range(4)] for _ in range(4)]
            va, vat = self.salloc([128, NT, 8, 65], BF16), [Tok("va") for _ in range(NT)]
            wq, wqt = self.salloc([128, KC, 512], BF16), Tok("wq")
            woG, woGt = self.salloc([128, 4, D], BF16), Tok("woG")
            wk_r = self.ring(2, [128, KC, 256], BF16, "wk")
            qz_r = self.ring(1, [128, 8, 512], BF16, "qz")
            R_r = self.ring(1, [128, 512], BF16, "R")
            Rc, Rct = self.salloc([128, 512], BF16), Tok("Rc")
            pT_r = self.ring(3, [128, 512], BF16, "pT")
            otok, otokt = self.salloc([128, 4, 512], BF16), [Tok("otok") for _ in range(8)]
            oTb, oTbt = self.salloc([128, 4, 512], BF16), Tok("oTb")
            lcol, lcolt = self.salloc([128, 64], BF16), Tok("lcol")
            tri, trit = self.salloc([128, 128], BF16), Tok("tri")
            SBb, SBbt = self.salloc([128, 4, 128], BF16), Tok("SBb")
            SB_r = self.ring(4, [128, 128], BF16, "SB")
            km32, km32t = self.salloc([128, 4, 8], F32), Tok("km32")
            kmh, kmht = self.salloc([128, 4, 8], BF16), Tok("kmh")
            kml, kmlt = self.salloc([128, 4, 8], BF16), Tok("kml")
            aff_r = self.ring(2, [128, 8, 8], F32, "aff")
            cmp_, cmpt = self.salloc([128, 8, 8, 8], F32), Tok("cmp")
            cnt, cntt = self.salloc([128, 8, 8], F32), Tok("cnt")
            sm_r = self.ring(2, [128, 8], F32, "msm")
            self.dve(lambda e: e.tensor_copy(out=lcol, in_=self.cst[:, C_LCOL + G * 64:C_LCOL + (G + 1) * 64]), r=[self.cst_t], w=[lcolt])
            self.dve(lambda e: e.tensor_copy(out=tri, in_=self.cst[:, C_TRI:C_TRI + 128]), r=[self.cst_t], w=[trit])
            self.dve(lambda e: e.tensor_copy(out=SBb, in_=self.cst[:, C_SBB:C_SBB + 512].rearrange("p (j c) -> p j c", j=4)), r=[self.cst_t], w=[SBbt])
            self.transpose_to(Rc.rearrange("p (j t) -> p j t", j=4), [SBb[:, j, :] for j in range(4)], [SBbt], [Rct])
            if G == 0:
                self.dump(Rc[:, 0:128], Rct)
                self.dump(SBb[:, 0, :], SBbt)
            for s_ in range(qz_r.items.__len__()):
                qz0, qz0t = qz_r.items[s_]
                self.pool(lambda e, qz0=qz0: e.memset(qz0.rearrange("p h t -> p (h t)"), 0.0), w=[qz0t])
            self.pool(lambda e: e.memset(va.rearrange("p a b c -> p (a b c)"), 1.0), w=vat)
            self.wload_w(wq, d["oqkv"], G * 512, 512, wqt)
            self.wload(woG, d["owo"][G * 512:(G + 1) * 512, :].rearrange("(g p) n -> p g n", p=128), woGt)
            for c2 in range(2):
                wk, wkt = wk_r.next()
                self.wload_w(wk, d["oqkv"], 1024 + G * 512 + c2 * 256, 256, wkt, step=256)
                for pp in range(2):
                    p = c2 * 2 + pp
                    for tb in range(4):
                        ps, pst = self.bank()
                        self.mmB(ps[:], pst, wk, wkt, pp * 128, tb * 512, 512)
                        self.copy(kT[:, p, tb * 512:(tb + 1) * 512], ps[:], [pst], [kTt[p][tb]])
            for c2 in range(2):
                wk, wkt = wk_r.next()
                self.wload_w(wk, d["oqkv"], 2048 + G * 512 + c2 * 256, 256, wkt, step=256)
                for i in range(NT):
                    ps, pst = self.bank()
                    self.mmA(ps[:, 0:256], pst, i, wk, wkt, 0, 256)
                    self.copy(va[:, i, c2 * 4:(c2 + 1) * 4, 0:64], ps[:, 0:256].rearrange("p (h c) -> p h c", h=4), [pst], [vat[i]])
            for p in range(4):
                self.dve(lambda e, p=p: e.tensor_reduce(out=km32[:, p, :], in_=kT[:, p, :].rearrange("p (b t) -> p b t", b=8), axis=AX.X, op=ALU.add), r=kTt[p], w=[km32t])
            self.dve(lambda e: e.tensor_scalar(out=km32, in0=km32, scalar1=1.0 / 256.0, scalar2=None, op0=ALU.mult), r=[km32t], w=[km32t])
            self.dve(lambda e: e.tensor_copy(out=kmh, in_=km32), r=[km32t], w=[kmht])
            self.dve(lambda e: e.tensor_tensor(out=km32, in0=km32, in1=kmh, op=ALU.subtract), r=[km32t, kmht], w=[km32t])
            self.dve(lambda e: e.tensor_copy(out=kml, in_=km32), r=[km32t], w=[kmlt])

            for qc in range(4):
                qz, qzt = qz_r.next()
                for p in range(4):
                    ps, pst = self.bank()
                    self.mmB(ps[:], pst, wq, wqt, p * 128, qc * 512, 512)
                    self.act(lambda e, p=p, ps=ps, qz=qz: e.copy(out=qz[0:64, 2 * p, :], in_=ps[0:64, :]), r=[pst], w=[qzt])
                    self.dve(lambda e, p=p, ps=ps, qz=qz: e.tensor_copy(out=qz[64:128, 2 * p + 1, :], in_=ps[64:128, :]), r=[pst], w=[qzt])
                if qc < 2:
                    R, Rt = Rc, Rct
                else:
                    R, Rt = R_r.next()
                    sbs = []
                    for j in range(4):
                        qb = (qc * 4 + j) // 2
                        ab, abt = self.bank()
                        for hl in range(8):
                            self.mm(ab[:, hl * 8:(hl + 1) * 8], qz[:, hl, j * 128:(j + 1) * 128], kmh[:, hl // 2, :], True, False, [qzt, kmht], [abt])
                            self.mm(ab[:, hl * 8:(hl + 1) * 8], qz[:, hl, j * 128:(j + 1) * 128], kml[:, hl // 2, :], False, True, [qzt, kmlt], [abt])
                        aff, afft = aff_r.next()
                        self.dve(lambda e, aff=aff, ab=ab: e.tensor_copy(out=aff, in_=ab[:, 0:64].rearrange("p (h k) -> p h k", h=8)), r=[abt], w=[afft])
                        self.dve(lambda e, aff=aff, qb=qb: e.tensor_tensor(out=cmp_[:, :, 0:qb, 0:qb],
                                                                        in0=aff[:, :, 0:qb].unsqueeze(2).to_broadcast([128, 8, qb, qb]),
                                                                        in1=aff[:, :, 0:qb].unsqueeze(3).to_broadcast([128, 8, qb, qb]), op=ALU.is_gt),
                                 r=[afft], w=[cmpt])
                        self.dve(lambda e, qb=qb: e.tensor_reduce(out=cnt[:, :, 0:qb], in_=cmp_[:, :, 0:qb, 0:qb], axis=AX.X, op=ALU.add), r=[cmpt], w=[cntt])
                        SB, SBt = SB_r.next()
                        self.pool(lambda e, SB=SB, j=j: e.tensor_copy(out=SB, in_=SBb[:, j, :]), r=[SBbt], w=[SBt])
                        self.dve(lambda e, SB=SB, qb=qb: e.tensor_scalar(out=SB.rearrange("p (h s) -> p h s", h=8)[:, :, 0:qb], in0=cnt[:, :, 0:qb],
                                                                      scalar1=3.0, scalar2=-32768.0, op0=ALU.is_ge, op1=ALU.mult), r=[cntt, SBt], w=[SBt])
                        sbs.append((SB, SBt))
                    self.transpose_to(R.rearrange("p (j t) -> p j t", j=4), [s[0] for s in sbs], [s[1] for s in sbs], [Rt])
                nkt = 4 * qc + 4
                for hl in range(8):
                    h = G * 8 + hl
                    O, Ot = self.obank()
                    first = True
                    for kt in range(nkt):
                        j0 = max(0, kt - 4 * qc)
                        c0 = j0 * 128
                        kb = kt // 2
                        S_, S_t = self.bank()
                        self.mm(S_[:, c0:512], kT[:, hl // 2, kt * 128:(kt + 1) * 128], qz[:, hl, c0:512], True, False, [kTt[hl // 2][kt // 4], qzt], [S_t])
                        self.mm(S_[:, c0:512], lcol[:, hl * 8 + kb:hl * 8 + kb + 1].to_broadcast([128, 128]), R[:, c0:512], False, True, [lcolt, Rt], [S_t])
                        if kt >= 4 * qc:
                            self.dve(lambda e, S_=S_, c0=c0: e.tensor_tensor(out=S_[:, c0:c0 + 128], in0=S_[:, c0:c0 + 128], in1=self.cst[:, C_NTRI:C_NTRI + 128], op=ALU.add),
                                     r=[S_t, self.cst_t], w=[S_t])
                        pT, pTt = pT_r.next()
                        bcol = C_ALB + h * 16 + (kt - 4 * qc) + 12
                        self.act(lambda e, pT=pT, S_=S_, c0=c0, bcol=bcol: e.activation(out=pT[:, c0:512], in_=S_[:, c0:512], func=AF.Exp,
                                                                                     bias=self.cst[:, bcol:bcol + 1], scale=0.125),
                                 r=[S_t, self.cst_t], w=[pTt])
                        if G == 0 and qc == 0 and hl == 0 and kt == 0:
                            self.dump(S_[:, 0:512], S_t, psum=True)
                            self.dump(pT, pTt)
                            self.dump(va[:, 0, 0, :], vat[0])
                            self.dump(qz[:, 0, :], qzt)
                            self.dump(kT[:, 0, 0:128], kTt[0][0])
                            self.dump(R[:, 0:128], Rt)
                            self.dump(va[:, 0, :, :].rearrange("p h c -> p (h c)"), vat[0])
                        for j in range(j0, 4):
                            self.mm(O[:, j * 65:(j + 1) * 65], pT[:, j * 128:(j + 1) * 128], va[:, kt, hl, :], first, kt == 4 * qc + j,
                                    [pTt, vat[kt]], [Ot], skip_group_check=True)
                            first = False
                    if G == 0 and qc == 0 and hl == 0:
                        self.dump(O[:, 0:260], Ot, psum=True)
                    sm, smt = sm_r.next()
                    self.dve(lambda e, sm=sm, O=O: e.reciprocal(out=sm[:, 0:4], in_=O[:, 0:260].rearrange("p (j c) -> p j c", j=4)[:, :, 64]), r=[Ot], w=[smt])
                    for j in range(4):
                        if j % 2:
                            self.dve(lambda e, sm=sm, O=O, j=j, hl=hl: e.tensor_scalar(out=otok[:, j, hl * 64:(hl + 1) * 64], in0=O[:, j * 65:j * 65 + 64],
                                                                                    scalar1=sm[:, j:j + 1], scalar2=None, op0=ALU.mult), r=[Ot, smt], w=[otokt[hl]])
                        else:
                            self.act(lambda e, sm=sm, O=O, j=j, hl=hl: e.activation(out=otok[:, j, hl * 64:(hl + 1) * 64], in_=O[:, j * 65:j * 65 + 64],
                                                                                 func=AF.Copy, scale=sm[:, j:j + 1]), r=[Ot, smt], w=[otokt[hl]])
                if G == 0 and qc == 0:
                    self.dump(otok.rearrange("p j c -> p (j c)"), otokt[0])
                for j in range(4):
                    i = qc * 4 + j
                    self.transpose_to(oTb[:, :, j * 128:(j + 1) * 128], [otok[:, j, c * 128:(c + 1) * 128] for c in range(4)], otokt, [oTbt])
                    self.out_proj(i, lambda k, j=j: oTb[:, k, j * 128:(j + 1) * 128], 4, [oTbt], woG, woGt, first=(G == 0))
                    if G == 1:
                        self.layer_norm(i, lnp)


def build_program(stages):
    nc = bass.Bass("TRN2", target_bir_lowering=False)
    k = K(nc, stages)
    k.build()
    return nc


FULL_STAGES = [("mix0",), ("xattn", 0), ("ffn", 0), ("moba",), ("xattn", 1), ("ffn", 1)]


def make_in_maps(inputs, ncores=NCORES):
    f = lambda a: np.ascontiguousarray(np.asarray(a, dtype=np.float32))
    x = f(inputs["x"])
    mem = f(inputs["mem"])
    shared = dict(
        ln_g=f(inputs["ln_g"]).reshape(6, D),
        ln_b=f(inputs["ln_b"]).reshape(6, D),
        x_wq=f(inputs["x_wq"]), x_wkv=f(inputs["x_wkv"]), x_wo=f(inputs["x_wo"]),
        ffn_w_in=f(inputs["ffn_w_in"]), ffn_w_out=f(inputs["ffn_w_out"]),
        ev_w_in=f(inputs["ev_w_in"])[0], ev_w_out=f(inputs["ev_w_out"])[0],
        a_wsT=f(np.transpose(np.asarray(inputs["a_ws"])[0], (2, 0, 1))),
        a_bs=f(inputs["a_bs"]).reshape(1, 512),
        a_ln_g=f(inputs["a_ln_g"]).reshape(1, 512),
        a_ln_b=f(inputs["a_ln_b"]).reshape(1, 512),
        b_norm_gT=f(np.asarray(inputs["b_norm_g"]).reshape(4, 128).T),
        lb_logitsT=f(np.transpose(np.asarray(inputs["hgrn_lb_logits"]).reshape(2, 4, 128), (2, 0, 1))),
        od_w_qkv=f(inputs["od_w_qkv"])[0], od_w_out=f(inputs["od_w_out"])[0],
        consts=make_consts(),
    )
    maps = []
    for c in range(ncores):
        m = dict(shared)
        m["x"] = x[c]
        m["mem"] = mem[c]
        maps.append(m)
    return maps


def kernel(**inputs):
    nc = build_program(FULL_STAGES)
    in_maps = make_in_maps(inputs)
    res = run_bass_kernel_spmd(nc, in_maps, core_ids=list(range(NCORES)))
    return np.stack([np.asarray(r["out"], dtype=np.float32) for r in res.results], axis=0)
```

```python
import math
import os
from contextlib import ExitStack

import numpy as np
import concourse.bass as bass
import concourse.mybir as mybir
from concourse.bass_utils import run_bass_kernel_spmd

F32 = mybir.dt.float32
BF16 = mybir.dt.bfloat16
AF = mybir.ActivationFunctionType
ALU = mybir.AluOpType
AX = mybir.AxisListType

T = 2048
D = 1024
NT = T // 128
KC = D // 128
MEM = 256
DFF = 2816
NJ = DFF // 128
ALPHA = 4.0 ** 0.25
EPS = 1e-5
NCORES = 8


ARENA_REG = {"range": None, "toks": []}


class Tok:
    __slots__ = ("name", "w", "r")

    def __init__(self, name=""):
        self.name = name
        self.w = None
        self.r = []
        rng = ARENA_REG["range"]
        if rng is not None:
            st, lo, hi = rng
            inh = []
            for (st2, lo2, hi2, t2) in ARENA_REG["toks"]:
                if st2 < st and lo2 < hi and lo < hi2:
                    if t2.w is not None:
                        inh.append(t2.w)
                    inh.extend(t2.r)
            seen = set()
            for o in inh:
                if id(o) not in seen:
                    seen.add(id(o))
                    self.r.append(o)
            ARENA_REG["toks"].append((st, lo, hi, self))


class Op:
    __slots__ = ("eng", "fn", "deps", "sig", "is_dma", "need", "idx", "guard", "cost", "stage", "nleft", "users", "ready", "fin", "raw")

    def __init__(self, eng, fn, is_dma, cost):
        self.eng = eng
        self.fn = fn
        self.deps = []
        self.sig = None
        self.is_dma = is_dma
        self.need = False
        self.guard = None
        self.cost = cost
        self.users = []


DEFAULT_COST = {"pe": 0.06, "act": 0.45, "dve": 0.35, "pool": 0.6, "sp": 0.1}
SCHED_WINDOW = 160
SAME_ENGINE_NOSYNC = tuple(os.environ.get('NOSYNC', 'pe').split(','))
RAW_ONLY = not os.environ.get('ALL_SYNC')


class Prog:
    ENGS = ("pe", "act", "dve", "pool", "sp")

    def __init__(self):
        self.all = []
        self.stage = 0

    def fence(self):
        pass

    def op(self, eng, fn, reads=(), writes=(), dma=False, cost=None):
        if cost is None:
            cost = 3.0 if dma else DEFAULT_COST[eng]
        o = Op(eng, fn, dma, cost)
        o.stage = self.stage
        o.idx = len(self.all)
        deps = []
        raw = set()
        for t in reads:
            if t.w is not None:
                deps.append(t.w)
                raw.add(id(t.w))
        o.raw = raw
        for t in writes:
            if t.w is not None:
                deps.append(t.w)
            deps.extend(t.r)
        seen = set()
        for d in deps:
            if id(d) in seen or d is o:
                continue
            seen.add(id(d))
            o.deps.append(d)
            d.users.append(o)
        for t in reads:
            t.r.append(o)
        for t in writes:
            t.w = o
            t.r = []
        self.all.append(o)
        return o

    def schedule(self):
        order = {e: [] for e in self.ENGS}
        nst = self.stage + 1
        stages = [[] for _ in range(nst)]
        for o in self.all:
            stages[o.stage].append(o)
        t_stage = 0.0
        for ops in stages:
            if not ops:
                continue
            inst = set(id(o) for o in ops)
            pend = {e: [] for e in self.ENGS}
            for o in ops:
                o.nleft = sum(1 for d in o.deps if id(d) in inst)
                o.ready = t_stage
                pend[o.eng].append(o)
            free = {e: t_stage for e in self.ENGS}
            n = len(ops)
            tmax = t_stage
            while n:
                best = None
                for e in self.ENGS:
                    lst = pend[e]
                    cnt = 0
                    for o in lst:
                        if o.nleft == 0:
                            st = o.ready if o.ready > free[e] else free[e]
                            if best is None or st < best[0] - 1e-9:
                                best = (st, o)
                        cnt += 1
                        if cnt >= SCHED_WINDOW:
                            break
                st, o = best
                e = o.eng
                pend[e].remove(o)
                if o.is_dma:
                    free[e] = st + 0.1
                    o.fin = st + o.cost
                else:
                    o.fin = st + o.cost
                    free[e] = o.fin
                tmax = max(tmax, o.fin)
                for u in o.users:
                    if id(u) in inst:
                        u.nleft -= 1
                        lat = 0.05 if (u.eng == e and not o.is_dma) else 0.35
                        if o.fin + lat > u.ready:
                            u.ready = o.fin + lat
                order[e].append(o)
                n -= 1
            t_stage = tmax
        self.est_us = t_stage
        return order

    def emit(self, nc, engines, sems, dma_sems):
        order = self.schedule()
        last_by_stage = {}
        for e in self.ENGS:
            for o in order[e]:
                last_by_stage[(e, o.stage)] = o
        for e in self.ENGS:
            prev_stage = None
            for o in order[e]:
                if o.stage != prev_stage:
                    for e2 in self.ENGS:
                        cands = [v for (ee, st), v in last_by_stage.items() if ee == e2 and st < o.stage]
                        if cands:
                            d = max(cands, key=lambda v: v.stage)
                            if d is not o and d not in o.deps:
                                o.deps.append(d)
                    prev_stage = o.stage
        for e in self.ENGS:
            for o in order[e]:
                for d in o.deps:
                    if (not d.is_dma) and (not o.is_dma) and d.eng == o.eng and (o.eng in SAME_ENGINE_NOSYNC or (RAW_ONLY and id(d) not in o.raw)):
                        continue
                    d.need = True
        NDS = {q: len(dma_sems[q]) for q in dma_sems}
        all_dma = {q: [] for q in dma_sems}
        for e in self.ENGS:
            cnt = 0
            k = 0
            for o in order[e]:
                if o.is_dma:
                    s = dma_sems[e][k % NDS[e]]
                    gen = k // NDS[e]
                    o.sig = (s, 16 * (gen + 1))
                    o.guard = (s, 16 * gen) if gen > 0 else None
                    all_dma[e].append(o)
                    k += 1
                elif o.need:
                    cnt += 1
                    o.sig = (sems[e], cnt)

        def run(e, eng):
            waited = {}
            for o in order[e]:
                need = {}
                if o.guard is not None:
                    need[o.guard[0]] = o.guard[1]
                for d in o.deps:
                    if (not d.is_dma) and (not o.is_dma) and d.eng == e and (e in SAME_ENGINE_NOSYNC or (RAW_ONLY and id(d) not in o.raw)):
                        continue
                    s, v = d.sig
                    if need.get(s, 0) < v:
                        need[s] = v
                for s, v in need.items():
                    if waited.get(s, 0) >= v:
                        continue
                    eng.wait_ge(s, v)
                    waited[s] = v
                ins = o.fn(eng)
                if o.is_dma:
                    ins.then_inc(o.sig[0], 16)
                elif o.sig is not None:
                    ins.then_inc(o.sig[0], 1)
            return waited

        with nc.Block() as block:
            @block.tensor
            def _(pe):
                run("pe", pe)

            @block.scalar
            def _(act):
                run("act", act)

            @block.vector
            def _(dve):
                run("dve", dve)

            @block.gpsimd
            def _(pool):
                run("pool", pool)

            @block.sync
            def _(sp):
                w = run("sp", sp)
                for q, lst in all_dma.items():
                    last = {}
                    for o in lst:
                        last[o.sig[0]] = o.sig[1]
                    for s, v in last.items():
                        if w.get(s, 0) < v:
                            sp.wait_ge(s, v)


GELU_NATIVE = False
ARENA_BYTES = 98 * 1024

C_IDENT = 0
C_TRI = 128
C_M2 = 256
C_MC = 384
C_LCOL = 640
C_SBB = 768
C_ALB = 1280
C_NTRI = C_ALB + 256
CONST_W = C_NTRI + 128


def alibi_slope(h):
    return float(np.float32(2.0 ** (-8.0 * (h + 1) / 16)))


def _bf16_round(v):
    a = np.asarray(v, np.float32).reshape(1)
    u = a.view(np.uint32)
    r = ((u + 0x7FFF + ((u >> 16) & 1)) & 0xFFFF0000).astype(np.uint32)
    return float(r.view(np.float32)[0])


def make_consts():
    c = np.zeros((128, CONST_W), np.float32)
    p = np.arange(128)
    c[:, C_IDENT:C_IDENT + 128] = np.eye(128, dtype=np.float32)
    c[:, C_TRI:C_TRI + 128] = (p[:, None] <= p[None, :]).astype(np.float32)
    c[:, C_M2:C_M2 + 128] = ((p[:, None] <= p[None, :]) & ((p[:, None] // 64) == (p[None, :] // 64))).astype(np.float32)
    c[:, C_NTRI:C_NTRI + 128] = np.where(p[:, None] <= p[None, :], 0.0, -30000.0).astype(np.float32)
    t = np.arange(256)
    c[:, C_MC:C_MC + 256] = (t % 64 != 0).astype(np.float32)[None, :]
    for G in range(2):
        for hl in range(8):
            sl = alibi_slope(G * 8 + hl)
            hi = _bf16_round(sl)
            lo = _bf16_round(sl - hi)
            for kb in range(8):
                col = C_LCOL + G * 64 + hl * 8 + kb
                c[hl * 16 + kb, col] = 1.0
                c[hl * 16 + 8, col] = -8.0 * hi
                c[hl * 16 + 9, col] = -8.0 * hi
                c[hl * 16 + 10, col] = -8.0 * lo
                c[hl * 16 + 11, col] = -8.0 * lo
    for j in range(4):
        for hl in range(8):
            base = C_SBB + j * 128 + hl * 16
            c[:, base + 8] = (j % 2) * 128 + p
            c[:, base + 9] = 256 * (j // 2)
            c[:, base + 10] = (j % 2) * 128 + p
            c[:, base + 11] = 256 * (j // 2)
    for h in range(16):
        sl = alibi_slope(h)
        for dk in range(-12, 4):
            c[:, C_ALB + h * 16 + dk + 12] = np.float32(sl) * (p + 128.0 * dk).astype(np.float32)
    return c


class Ring:
    def __init__(self, items):
        self.items = items
        self.i = 0

    def next(self):
        it = self.items[self.i % len(self.items)]
        self.i += 1
        return it


class K:
    def __init__(self, nc, stages):
        self.nc = nc
        self.P = Prog()
        self.es = ExitStack()
        self.stages = stages
        self.uid = 0
        self.stage_no = 0
        ARENA_REG["range"] = None
        ARENA_REG["toks"] = []

    def sb(self, shape, dt, name=None):
        self.uid += 1
        return self.es.enter_context(self.nc.sbuf_tensor(f"{name or 't'}_{self.uid}", list(shape), dt))

    def salloc(self, shape, dt):
        n = 1
        for s in shape[1:]:
            n *= s
        nbytes = n * (4 if dt == F32 else 2)
        off = (self.aoff + 31) // 32 * 32
        self.aoff = off + nbytes
        assert self.aoff <= ARENA_BYTES, f"arena overflow {self.aoff}"
        if self.stage_no % 2:
            off = (ARENA_BYTES - self.aoff) // 32 * 32
        ARENA_REG["range"] = (self.stage_no, off, off + nbytes)
        v = self.arena[:, off // 2:(off + nbytes) // 2]
        if dt == F32:
            v = v.bitcast(F32)
        if len(shape) == 3:
            v = v.rearrange("p (a b) -> p a b", a=shape[1])
        elif len(shape) == 4:
            v = v.rearrange("p (a b c) -> p a b c", a=shape[1], b=shape[2])
        if shape[0] < 128:
            v = v[0:shape[0]]
        return v

    def ring(self, n, shape, dt, name="r"):
        return Ring([(self.salloc(shape, dt), Tok(name)) for _ in range(n)])

    def begin_stage(self):
        self.aoff = 0
        self.stage_no += 1
        ARENA_REG["range"] = None

    def dram_in(self, name, shape, dt=F32):
        return self.nc.dram_tensor(name, list(shape), dt, kind="ExternalInput").ap()

    def pe(self, fn, r=(), w=()):
        return self.P.op("pe", fn, r, w)

    def act(self, fn, r=(), w=()):
        return self.P.op("act", fn, r, w)

    def dve(self, fn, r=(), w=()):
        return self.P.op("dve", fn, r, w)

    def pool(self, fn, r=(), w=()):
        return self.P.op("pool", fn, r, w)

    def dma(self, q, out, in_, r=(), w=()):
        return self.P.op(q, lambda e: e.dma_start(out=out, in_=in_), r, w, dma=True)

    def dump(self, ap, tok, psum=False):
        if not os.environ.get('DEBUG_DUMP'):
            return
        n = ap.shape[-1]
        c0 = self.dump_off
        self.dump_off += n
        print("DUMP", c0, n)
        if psum:
            scr = self.salloc([128, n], F32)
            st = Tok("dscr")
            self.dve(lambda e: e.tensor_copy(out=scr, in_=ap), r=[tok], w=[st])
            self.dma("pool", self.d["dbg"][:, c0:c0 + n], scr, r=[st])
        else:
            self.dma("pool", self.d["dbg"][:, c0:c0 + n], ap, r=[tok])

    def bank(self):
        b = self.banks[self.bank_i % self.nbank]
        self.bank_i += 1
        return b

    def obank(self):
        b = self.banks[6 + self.obank_i % 2]
        self.obank_i += 1
        return b

    def mm(self, out, lhsT, rhs, start, stop, r, w, **kw):
        n = out.shape[-1]
        return self.P.op("pe", lambda e: e.matmul(out, lhsT=lhsT, rhs=rhs, start=start, stop=stop, **kw), r, w, cost=max(n, 64) / 2400.0 + 0.012)

    def mmB(self, ps, pst, W, wt, col0, tok0, n):
        xts = [self.xT_t[q] for q in range(tok0 // 128, (tok0 + n + 127) // 128)]
        for kc in range(KC):
            self.mm(ps, W[:, kc, col0:col0 + 128], self.xT[:, kc, tok0:tok0 + n], kc == 0, kc == KC - 1, [wt] + xts, [pst])

    def mmA(self, ps, pst, i, W, wt, col0, n):
        for kc in range(KC):
            self.mm(ps, self.xT[:, kc, i * 128:(i + 1) * 128], W[:, kc, col0:col0 + n], kc == 0, kc == KC - 1,
                    [wt, self.xT_t[i]], [pst])

    def wload(self, dst, src, tok):
        self.dma("pool", dst, src, w=[tok])

    def wload_w(self, dst, W_d, col0, n, tok, step=512):
        for c in range(0, n, step):
            m = min(step, n - c)
            self.wload(dst[:, :, c:c + m], W_d[:, col0 + c:col0 + c + m].rearrange("(k p) n -> p k n", p=128), tok)

    def build(self):
        nc = self.nc
        d = {}
        d["x"] = self.dram_in("x", [T, D])
        d["mem"] = self.dram_in("mem", [MEM, D])
        d["lng"] = self.dram_in("ln_g", [6, D])
        d["lnb"] = self.dram_in("ln_b", [6, D])
        d["xwq"] = self.dram_in("x_wq", [2, D, D])
        d["xwkv"] = self.dram_in("x_wkv", [2, D, 2 * D])
        d["xwo"] = self.dram_in("x_wo", [2, D, D])
        d["fwi"] = self.dram_in("ffn_w_in", [2, D, 2 * DFF])
        d["fwo"] = self.dram_in("ffn_w_out", [2, DFF, D])
        d["evi"] = self.dram_in("ev_w_in", [D, 3072])
        d["evo"] = self.dram_in("ev_w_out", [D, D])
        d["awsT"] = self.dram_in("a_wsT", [128, 4, 128])
        d["abs"] = self.dram_in("a_bs", [1, 512])
        d["alng"] = self.dram_in("a_ln_g", [1, 512])
        d["alnb"] = self.dram_in("a_ln_b", [1, 512])
        d["bng"] = self.dram_in("b_norm_gT", [128, 4])
        d["lbl"] = self.dram_in("lb_logitsT", [128, 2, 4])
        d["oqkv"] = self.dram_in("od_w_qkv", [D, 3072])
        d["owo"] = self.dram_in("od_w_out", [D, D])
        d["cst"] = self.dram_in("consts", [128, CONST_W])
        d["out"] = nc.dram_tensor("out", [T, D], F32, kind="ExternalOutput").ap()
        if os.environ.get('DEBUG_DUMP'):
            d["dbg"] = nc.dram_tensor("dbg", [128, 8192], F32, kind="ExternalOutput").ap()
        self.dump_off = 0
        self.d = d

        self.x_tok = self.sb([128, NT, D], F32, "x_tok")
        self.xT = self.sb([128, KC, T], BF16, "xT")
        self.xtok_t = [Tok(f"xtok{i}") for i in range(NT)]
        self.xT_t = [Tok(f"xT{i}") for i in range(NT)]
        self.cst = self.sb([128, CONST_W], F32, "cst")
        self.cst_t = Tok("cst")
        self.ident = self.sb([128, 128], BF16, "ident")
        self.ident_t = Tok("ident")
        self.xb_ring = Ring([(self.sb([128, D], BF16, "xb"), Tok("xb")) for _ in range(2)])
        self.st_ring = Ring([(self.sb([128, 32], F32, "lnst"), Tok("lnst")) for _ in range(3)])
        self.negh = self.sb([128, 512], F32, "negh")
        self.negh_t = Tok("negh")
        self.arena = self.sb([128, ARENA_BYTES // 2], BF16, "arena")
        self.aoff = 0
        self.banks = []
        for i in range(8):
            pt = self.es.enter_context(nc.psum_tensor(f"bank{i}", [128, 512], F32))
            self.banks.append((pt, Tok(f"bank{i}")))
        self.bank_i = 0
        self.obank_i = 0
        self.nbank = 8
        self.cp_i = 0

        self.dma("sp", self.cst[:], d["cst"], w=[self.cst_t])
        self.dve(lambda e: e.tensor_copy(out=self.ident[:], in_=self.cst[:, C_IDENT:C_IDENT + 128]),
                 r=[self.cst_t], w=[self.ident_t])
        self.pool(lambda e: e.memset(self.negh[:], -0.5), w=[self.negh_t])

        self.load_x()
        for s in self.stages:
            getattr(self, "stage_" + s[0])(*s[1:])
        ARENA_REG["range"] = None
        self.store_out()

        sems = {e: self.es.enter_context(nc.semaphore(f"s_{e}")) for e in Prog.ENGS}
        dma_sems = {}
        for q, n in (("sp", 8), ("pool", 6), ("act", 2)):
            dma_sems[q] = [self.es.enter_context(nc.semaphore(f"d_{q}{i}")) for i in range(n)]
        self.P.emit(nc, None, sems, dma_sems)
        self.es.close()

    def copy(self, out, in_, r, w):
        self.cp_i += 1
        if self.cp_i % 2:
            return self.act(lambda e: e.copy(out=out, in_=in_), r=r, w=w)
        return self.dve(lambda e: e.tensor_copy(out=out, in_=in_), r=r, w=w)

    def load_x(self):
        for i in range(NT):
            self.dma("sp", self.x_tok[:, i, :], self.d["x"][i * 128:(i + 1) * 128, :], w=[self.xtok_t[i]])
        for i in range(NT):
            self.to_xT(i)

    def store_out(self):
        for i in range(NT):
            self.dma("sp", self.d["out"][i * 128:(i + 1) * 128, :], self.x_tok[:, i, :], r=[self.xtok_t[i]])

    def transpose_to(self, dst_view, src_tiles, r, w):
        bk, bt = self.bank()
        psb = bk[:].bitcast(BF16)
        n = len(src_tiles)
        for k, src in enumerate(src_tiles):
            self.pe(lambda e, k=k, src=src: e.transpose(out=psb[:, k * 128:(k + 1) * 128], in_=src, identity=self.ident[:]),
                    r=list(r) + [self.ident_t], w=[bt])
        self.copy(dst_view, psb[:, 0:n * 128].rearrange("p (k t) -> p k t", k=n), [bt], w)

    def to_xT(self, i):
        xb, xbt = self.xb_ring.next()
        self.act(lambda e: e.copy(out=xb[:], in_=self.x_tok[:, i, :]), r=[self.xtok_t[i]], w=[xbt])
        self.transpose_to(self.xT[:, :, i * 128:(i + 1) * 128], [xb[:, kc * 128:(kc + 1) * 128] for kc in range(KC)],
                          [xbt], [self.xT_t[i]])

    def load_ln(self, idx):
        gb = self.salloc([128, 2, D], F32)
        t = Tok("lnp")
        g, b = gb[:, 0, :], gb[:, 1, :]
        self.dma("sp", g, self.d["lng"][idx:idx + 1, :].to_broadcast([128, D]), w=[t])
        self.dma("sp", b, self.d["lnb"][idx:idx + 1, :].to_broadcast([128, D]), w=[t])
        return g, b, t

    def rstd_small(self, out, var, r, w, eps=EPS):
        n = out.shape[-1]
        self.pool(lambda e: e.tensor_scalar(out=out, in0=var, scalar1=eps, scalar2=None, op0=ALU.add), r=r, w=w)
        self.pool(lambda e: e.tensor_tensor(out=out, in0=out, in1=self.negh[:, 0:n], op=ALU.pow), r=list(w) + [self.negh_t], w=w)

    def layer_norm(self, i, lnp):
        g, b, gt = lnp
        xt = self.x_tok[:, i, :]
        xtok = self.xtok_t[i]
        stt, stk = self.st_ring.next()
        self.dve(lambda e: e.bn_stats(out=stt[:, 0:6], in_=self.x_tok[:, i, 0:512]), r=[xtok], w=[stk])
        self.dve(lambda e: e.bn_stats(out=stt[:, 6:12], in_=self.x_tok[:, i, 512:1024]), r=[xtok], w=[stk])
        self.dve(lambda e: e.bn_aggr(out=stt[:, 12:14], in_=stt[:, 0:12].rearrange("p (a b) -> p a b", a=2)), r=[stk], w=[stk])
        self.rstd_small(stt[:, 15:16], stt[:, 13:14], [stk], [stk])
        self.dve(lambda e: e.scalar_tensor_tensor(out=stt[:, 16:17], in0=stt[:, 12:13], scalar=-1.0, in1=stt[:, 15:16],
                                                  op0=ALU.mult, op1=ALU.mult), r=[stk], w=[stk])
        self.act(lambda e: e.activation(out=xt, in_=xt, func=AF.Identity, bias=stt[:, 16:17], scale=stt[:, 15:16]),
                 r=[stk, xtok], w=[xtok])
        self.pool(lambda e: e.tensor_tensor(out=xt, in0=xt, in1=g, op=ALU.mult), r=[xtok, gt], w=[xtok])
        self.dve(lambda e: e.tensor_tensor(out=xt, in0=xt, in1=b, op=ALU.add), r=[xtok, gt], w=[xtok])
        self.to_xT(i)

    def accum(self, i, half, ps, pst, first):
        dst = self.x_tok[:, i, half * 512:(half + 1) * 512]
        if first:
            self.dve(lambda e: e.scalar_tensor_tensor(out=dst, in0=dst, scalar=ALPHA, in1=ps, op0=ALU.mult, op1=ALU.add),
                     r=[pst, self.xtok_t[i]], w=[self.xtok_t[i]])
        else:
            self.dve(lambda e: e.tensor_tensor(out=dst, in0=dst, in1=ps, op=ALU.add),
                     r=[pst, self.xtok_t[i]], w=[self.xtok_t[i]])

    def out_proj(self, i, lhs_fn, nk, lhs_toks, wo, wot, first):
        for half in range(2):
            ps, pst = self.bank()
            for k in range(nk):
                self.mm(ps[:], lhs_fn(k), wo[:, k, half * 512:(half + 1) * 512], k == 0, k == nk - 1, list(lhs_toks) + [wot], [pst])
            self.accum(i, half, ps[:], pst, first)

    def stage_ln_only(self, idx):
        self.begin_stage()
        lnp = self.load_ln(idx)
        for i in range(NT):
            self.layer_norm(i, lnp)

    def stage_ffn(self, l):
        self.begin_stage()
        groups = [list(range(0, 6)), list(range(6, 12)), list(range(12, 17)), list(range(17, 22))]
        fwi = self.d["fwi"][l]
        fwo = self.d["fwo"][l]
        lnp = self.load_ln(l * 3 + 2)
        wi_r = self.ring(3, [128, KC, 256], BF16, "fwi")
        wo_r = self.ring(2, [128, 6, D], BF16, "fwo")
        hT = self.salloc([128, 6, T], BF16)
        hTt = [[Tok("hT") for _ in range(4)] for _ in range(6)]
        sg_r = self.ring(2, [128, 512], F32, "sg")
        for gi, js in enumerate(groups):
            wo, wot = wo_r.next()
            j0 = js[0]
            self.wload(wo[:, 0:len(js), :], fwo[j0 * 128:(j0 + len(js)) * 128, :].rearrange("(j p) n -> p j n", p=128), wot)
            for jl, j in enumerate(js):
                wi, wit = wi_r.next()
                self.wload(wi[:, :, 0:128], fwi[:, j * 128:(j + 1) * 128].rearrange("(k p) n -> p k n", p=128), wit)
                self.wload(wi[:, :, 128:256], fwi[:, DFF + j * 128:DFF + (j + 1) * 128].rearrange("(k p) n -> p k n", p=128), wit)
                for tb in range(4):
                    pg, pgt = self.bank()
                    pu, put = self.bank()
                    self.mmB(pg[:], pgt, wi, wit, 0, tb * 512, 512)
                    self.mmB(pu[:], put, wi, wit, 128, tb * 512, 512)
                    sg, sgt = sg_r.next()
                    self.act(lambda e, sg=sg, pg=pg: e.activation(out=sg, in_=pg[:], func=AF.Silu), r=[pgt], w=[sgt])
                    self.dve(lambda e, sg=sg, pu=pu, jl=jl, tb=tb: e.tensor_tensor(out=hT[:, jl, tb * 512:(tb + 1) * 512], in0=sg, in1=pu[:], op=ALU.mult),
                             r=[sgt, put], w=[hTt[jl][tb]])
            last = gi == len(groups) - 1
            for i in range(NT):
                self.out_proj(i, lambda k, i=i: hT[:, k, i * 128:(i + 1) * 128], len(js), [hTt[k][i // 4] for k in range(len(js))], wo, wot, first=(gi == 0))
                if last:
                    self.layer_norm(i, lnp)

    def stage_xattn(self, l):
        self.begin_stage()
        d = self.d
        SC = 1.0 / 16.0
        lnp = self.load_ln(l * 3 + 1)
        wq, wqt = self.salloc([128, KC, D], BF16), Tok("wq")
        wo, wot = self.salloc([128, KC, D], BF16), Tok("wo")
        kT, kTt = self.salloc([128, KC, MEM], BF16), [Tok("kT") for _ in range(KC)]
        vS, vSt = self.salloc([128, 2, D], BF16), [[Tok("vS") for _ in range(2)] for _ in range(2)]
        memb, membt = self.salloc([128, 2, D], BF16), Tok("memb")
        memT, memTt = self.salloc([128, KC, MEM], BF16), Tok("memT")
        wkv_r = self.ring(1, [128, KC, 512], BF16, "wkv")
        qT_r = Ring([(self.salloc([128, KC, 512], BF16), [Tok("qT") for _ in range(KC)])])
        p32_r = self.ring(2, [128, 4, 256], F32, "p32")
        pb_r = self.ring(2, [128, 4, 256], BF16, "pb")
        pT_r = self.ring(2, [128, 8, 128], BF16, "pT")
        oT_r = self.ring(2, [128, 8, 128], BF16, "oT")
        sm_r = self.ring(3, [128, 16], F32, "sm")
        self.wload(memb, d["mem"].rearrange("(m p) n -> p m n", p=128), membt)
        for mt in range(2):
            self.transpose_to(memT[:, :, mt * 128:(mt + 1) * 128], [memb[:, mt, kc * 128:(kc + 1) * 128] for kc in range(KC)],
                              [membt], [memTt])
        for c in range(4):
            wk, wkt = wkv_r.next()
            self.wload_w(wk, d["xwkv"][l], c * 512, 512, wkt)
            if c < 2:
                for cc in range(4):
                    fc = c * 4 + cc
                    ps, pst = self.bank()
                    for kc in range(KC):
                        self.mm(ps[:, 0:MEM], wk[:, kc, cc * 128:(cc + 1) * 128], memT[:, kc, :], kc == 0, kc == KC - 1, [wkt, memTt], [pst])
                    self.copy(kT[:, fc, :], ps[:, 0:MEM], [pst], [kTt[fc]])
            else:
                for mt in range(2):
                    ps, pst = self.bank()
                    for kc in range(KC):
                        self.mm(ps[:], memT[:, kc, mt * 128:(mt + 1) * 128], wk[:, kc, :], kc == 0, kc == KC - 1, [wkt, memTt], [pst])
                    self.copy(vS[:, mt, (c - 2) * 512:(c - 1) * 512], ps[:], [pst], [vSt[mt][c - 2]])
        self.wload_w(wq, d["xwq"][l], 0, D, wqt)
        self.wload_w(wo, d["xwo"][l], 0, D, wot)
        for tb in range(4):
            qT, qTt = qT_r.next()
            for c in range(KC):
                ps, pst = self.bank()
                self.mmB(ps[:], pst, wq, wqt, c * 128, tb * 512, 512)
                self.copy(qT[:, c, :], ps[:], [pst], [qTt[c]])
            for il in range(4):
                i = tb * 4 + il
                sA, sAt = self.bank()
                sB, sBt = self.bank()
                for h in range(4):
                    bk, bkt = (sA, sAt) if h < 2 else (sB, sBt)
                    for k2 in range(2):
                        self.mm(bk[:, (h % 2) * 256:(h % 2 + 1) * 256], qT[:, 2 * h + k2, il * 128:(il + 1) * 128], kT[:, 2 * h + k2, :],
                                k2 == 0, k2 == 1, [qTt[2 * h + k2], kTt[2 * h + k2]], [bkt])
                sm, smt = sm_r.next()
                self.dve(lambda e, sm=sm, sA=sA: e.tensor_reduce(out=sm[:, 0:2], in_=sA[:].rearrange("p (h m) -> p h m", h=2), axis=AX.X, op=ALU.max), r=[sAt], w=[smt])
                self.dve(lambda e, sm=sm, sB=sB: e.tensor_reduce(out=sm[:, 2:4], in_=sB[:].rearrange("p (h m) -> p h m", h=2), axis=AX.X, op=ALU.max), r=[sBt], w=[smt])
                self.dve(lambda e, sm=sm: e.tensor_scalar(out=sm[:, 4:8], in0=sm[:, 0:4], scalar1=-SC, scalar2=None, op0=ALU.mult), r=[smt], w=[smt])
                p32, p32t = p32_r.next()
                for h in range(4):
                    bk, bkt = (sA, sAt) if h < 2 else (sB, sBt)
                    self.act(lambda e, h=h, bk=bk, sm=sm, p32=p32: e.activation(out=p32[:, h, :], in_=bk[:, (h % 2) * 256:(h % 2 + 1) * 256], func=AF.Exp,
                                                                             bias=sm[:, 4 + h:5 + h], scale=SC, accum_out=sm[:, 8 + h:9 + h]),
                             r=[bkt, smt], w=[p32t, smt])
                self.dve(lambda e, sm=sm: e.reciprocal(out=sm[:, 12:16], in_=sm[:, 8:12]), r=[smt], w=[smt])
                pb, pbt = pb_r.next()
                for h in range(4):
                    if h % 2:
                        self.dve(lambda e, h=h, pb=pb, p32=p32, sm=sm: e.tensor_scalar(out=pb[:, h, :], in0=p32[:, h, :], scalar1=sm[:, 12 + h:13 + h], scalar2=None, op0=ALU.mult),
                                 r=[p32t, smt], w=[pbt])
                    else:
                        self.act(lambda e, h=h, pb=pb, p32=p32, sm=sm: e.activation(out=pb[:, h, :], in_=p32[:, h, :], func=AF.Copy, scale=sm[:, 12 + h:13 + h]),
                                 r=[p32t, smt], w=[pbt])
                pT, pTt = pT_r.next()
                self.transpose_to(pT, [pb[:, h, mt * 128:(mt + 1) * 128] for h in range(4) for mt in range(2)], [pbt], [pTt])
                oA, oAt = self.bank()
                oB, oBt = self.bank()
                oT, oTt = oT_r.next()
                for c in range(8):
                    bk, bkt = (oA, oAt) if c < 4 else (oB, oBt)
                    h = c // 2
                    for mt in range(2):
                        self.mm(bk[:, (c % 4) * 128:(c % 4 + 1) * 128], vS[:, mt, c * 128:(c + 1) * 128], pT[:, h * 2 + mt, :],
                                mt == 0, mt == 1, [vSt[mt][c // 4], pTt], [bkt])
                self.copy(oT[:, 0:4, :], oA[:].rearrange("p (c t) -> p c t", c=4), [oAt], [oTt])
                self.copy(oT[:, 4:8, :], oB[:].rearrange("p (c t) -> p c t", c=4), [oBt], [oTt])
                self.out_proj(i, lambda k, oT=oT: oT[:, k, :], KC, [oTt], wo, wot, first=True)
                self.layer_norm(i, lnp)

    def gelu(self, out, ps, pst, outt, scr_r, half=True):
        if GELU_NATIVE:
            self.act(lambda e: e.activation(out=out, in_=ps, func=AF.Gelu_apprx_tanh), r=[pst], w=[outt])
            return 1.0
        C0 = 0.7978845608028654
        C1 = 0.044715
        s, stk = scr_r.next()
        n = ps.shape[-1]
        sv = s[:, 0:n]
        self.act(lambda e: e.activation(out=sv, in_=ps, func=AF.Square), r=[pst], w=[stk])
        self.dve(lambda e: e.tensor_scalar(out=sv, in0=sv, scalar1=C1, scalar2=1.0, op0=ALU.mult, op1=ALU.add), r=[stk], w=[stk])
        self.dve(lambda e: e.tensor_tensor(out=sv, in0=sv, in1=ps, op=ALU.mult), r=[stk, pst], w=[stk])
        self.act(lambda e: e.activation(out=sv, in_=sv, func=AF.Tanh, scale=C0), r=[stk], w=[stk])
        self.dve(lambda e: e.scalar_tensor_tensor(out=out, in0=sv, scalar=1.0, in1=ps, op0=ALU.add, op1=ALU.mult), r=[stk, pst], w=[outt])
        return 0.5

    def stage_mix0(self):
        d = self.d
        self.begin_stage()
        wuv, wut, wvt = self.salloc([128, KC, 1024], BF16), Tok("wu"), Tok("wv")
        woA, woAt = self.salloc([128, 4, D], BF16), Tok("woA")
        ws32, ws32t = self.salloc([128, 4, 128], F32), Tok("ws32")
        wsb, wsbt = self.salloc([128, 4, 128], BF16), Tok("wsb")
        bsB, bsBt = self.salloc([128, 512], F32), Tok("bsB")
        lgB, lgBt = self.salloc([128, 512], F32), Tok("lgB")
        lbB, lbBt = self.salloc([128, 512], F32), Tok("lbB")
        uT_r = self.ring(2, [128, 4, 512], BF16, "uT")
        scr_r = self.ring(5, [128, 512], F32, "gscr")
        vg_r = self.ring(4, [128, 512], F32, "vg")
        vb_r = self.ring(4, [128, 512], BF16, "vb")
        ss_r = self.ring(4, [128, 512], F32, "ssum")
        ya_r = self.ring(4, [128, 4, 128], BF16, "yaT")
        st_r = self.ring(4, [128, 48], F32, "gst")
        self.wload_w(wuv[:, :, 0:512], d["evi"], 0, 512, wut)
        self.wload_w(wuv[:, :, 512:1024], d["evi"], 512, 512, wvt)
        self.wload(woA, d["evo"][0:512, :].rearrange("(g p) n -> p g n", p=128), woAt)
        self.dma("sp", ws32, d["awsT"], w=[ws32t])
        self.dma("sp", bsB, d["abs"].to_broadcast([128, 512]), w=[bsBt])
        self.dma("sp", lgB, d["alng"].to_broadcast([128, 512]), w=[lgBt])
        self.dma("sp", lbB, d["alnb"].to_broadcast([128, 512]), w=[lbBt])
        for g in range(4):
            self.dve(lambda e, g=g: e.tensor_tensor(out=wsb[:, g, :], in0=ws32[:, g, :], in1=self.cst[:, C_TRI:C_TRI + 128], op=ALU.mult),
                     r=[ws32t, self.cst_t], w=[wsbt])
        for tb in range(4):
            uT, uTt = uT_r.next()
            gf = 1.0
            for c in range(4):
                ps, pst = self.bank()
                self.mmB(ps[:], pst, wuv, wut, c * 128, tb * 512, 512)
                gf = self.gelu(uT[:, c, :], ps[:], pst, uTt, scr_r)
            for il in range(4):
                i = tb * 4 + il
                ps, pst = self.bank()
                self.mmA(ps[:], pst, i, wuv, wvt, 512, 512)
                vg, vgt = vg_r.next()
                gv = self.gelu(vg, ps[:], pst, vgt, scr_r)
                st, stt = st_r.next()
                for g in range(4):
                    self.dve(lambda e, g=g, st=st, vg=vg: e.bn_stats(out=st[:, g * 6:(g + 1) * 6], in_=vg[:, g * 128:(g + 1) * 128]), r=[vgt], w=[stt])
                for g in range(4):
                    self.dve(lambda e, g=g, st=st: e.bn_aggr(out=st[:, 24 + g * 2:26 + g * 2], in_=st[:, g * 6:(g + 1) * 6]), r=[stt], w=[stt])
                mv = st[:, 24:32].rearrange("p (g two) -> p g two", two=2)
                self.pool(lambda e, st=st, mv=mv: e.tensor_scalar(out=st[:, 32:36], in0=mv[:, :, 1], scalar1=gv * gv, scalar2=EPS, op0=ALU.mult, op1=ALU.add), r=[stt], w=[stt])
                self.pool(lambda e, st=st: e.tensor_tensor(out=st[:, 32:36], in0=st[:, 32:36], in1=self.negh[:, 0:4], op=ALU.pow), r=[stt, self.negh_t], w=[stt])
                self.dve(lambda e, st=st: e.tensor_scalar(out=st[:, 36:40], in0=st[:, 32:36], scalar1=gv, scalar2=None, op0=ALU.mult), r=[stt], w=[stt])
                self.dve(lambda e, st=st, mv=mv: e.scalar_tensor_tensor(out=st[:, 40:44], in0=mv[:, :, 0], scalar=-1.0, in1=st[:, 36:40], op0=ALU.mult, op1=ALU.mult), r=[stt], w=[stt])
                for g in range(4):
                    self.act(lambda e, g=g, st=st, vg=vg: e.activation(out=vg[:, g * 128:(g + 1) * 128], in_=vg[:, g * 128:(g + 1) * 128], func=AF.Identity,
                                                                      bias=st[:, 40 + g:41 + g], scale=st[:, 36 + g:37 + g]), r=[stt, vgt], w=[vgt])
                self.dve(lambda e, vg=vg: e.tensor_tensor(out=vg, in0=vg, in1=lgB, op=ALU.mult), r=[vgt, lgBt], w=[vgt])
                vb, vbt = vb_r.next()
                self.pool(lambda e, vg=vg, vb=vb: e.tensor_tensor(out=vb, in0=vg, in1=lbB, op=ALU.add), r=[vgt, lbBt], w=[vbt])
                pss, psst = self.bank()
                for g in range(4):
                    self.mm(pss[:, g * 128:(g + 1) * 128], vb[:, g * 128:(g + 1) * 128], wsb[:, g, :], True, True, [vbt, wsbt], [psst])
                ss, sst = ss_r.next()
                self.dve(lambda e, ss=ss, pss=pss: e.tensor_tensor(out=ss, in0=pss[:], in1=bsB, op=ALU.add), r=[psst, bsBt], w=[sst])
                ya, yat = ya_r.next()
                self.dve(lambda e, ya=ya, uT=uT, ss=ss, il=il: e.scalar_tensor_tensor(out=ya, in0=uT[:, :, il * 128:(il + 1) * 128], scalar=gf,
                                                                                   in1=ss.rearrange("p (g t) -> p g t", g=4), op0=ALU.mult, op1=ALU.mult),
                         r=[uTt, sst], w=[yat])
                self.out_proj(i, lambda k, ya=ya: ya[:, k, :], 4, [yat], woA, woAt, first=True)

        self.begin_stage()
        NB = 256
        lnp = self.load_ln(0)
        wB, wBt = self.salloc([128, KC, 2048], BF16), [Tok("wBq"), Tok("wBf"), Tok("wBi"), Tok("wBg")]
        woB, woBt = self.salloc([128, 4, D], BF16), Tok("woB")
        sm, smt = self.salloc([128, 32], F32), Tok("hsm")
        S, St = self.salloc([128, 4, 128], F32), Tok("S")
        Sb_r = self.ring(3, [128, 4, 128], BF16, "Sb")
        m2x4, m2t = self.salloc([128, 512], BF16), Tok("m2x4")
        ones, onest = self.salloc([128, 128], BF16), Tok("ones")
        f1 = self.ring(1, [128, 4, NB], F32, "f1")
        f2 = self.ring(1, [128, 4, NB], F32, "f2")
        f3 = self.ring(1, [128, 4, NB], F32, "f3")
        E_r = self.ring(2, [128, 4, 4], F32, "Elast")
        qd_r = self.ring(1, [128, 4, NB], BF16, "qdT")
        kd_r = self.ring(1, [128, 4, NB], BF16, "kdT")
        gs_r = self.ring(1, [128, 4, NB], BF16, "gsT")
        vi_r = self.ring(1, [128, 2, 512], BF16, "vi")
        kt_r = self.ring(1, [128, 2, 512], BF16, "kdtok")
        am_r = self.ring(3, [128, 512], BF16, "am")
        tmp_r = self.ring(2, [128, 512], F32, "stmp")
        sq_r = self.ring(2, [128, 512], BF16, "sq")
        r_r = self.ring(2, [128, 512], F32, "rr")
        t1_r = self.ring(2, [128, 512], F32, "t1")
        yb_r = self.ring(3, [128, 4, 128], BF16, "ybT")
        for c in range(4):
            self.wload_w(wB[:, :, c * 512:(c + 1) * 512], d["evi"], 1024 + c * 512, 512, wBt[c])
        self.wload(woB, d["evo"][512:1024, :].rearrange("(g p) n -> p g n", p=128), woBt)
        self.dma("sp", sm[:, 0:8], d["lbl"].rearrange("p l h -> p (l h)"), w=[smt])
        self.dma("sp", sm[:, 20:24], d["bng"], w=[smt])
        self.dve(lambda e: e.tensor_tensor(out=sm[:, 8:12], in0=sm[:, 0:4], in1=sm[:, 4:8], op=ALU.subtract), r=[smt], w=[smt])
        self.act(lambda e: e.activation(out=sm[:, 12:16], in_=sm[:, 8:12], func=AF.Sigmoid), r=[smt], w=[smt])
        self.dve(lambda e: e.tensor_scalar(out=sm[:, 16:20], in0=sm[:, 12:16], scalar1=-1.0, scalar2=1.0, op0=ALU.mult, op1=ALU.add), r=[smt], w=[smt])
        self.pool(lambda e: e.memset(S.rearrange("p h v -> p (h v)"), 0.0), w=[St])
        Sb, Sbt = Sb_r.next()
        self.pool(lambda e, Sb=Sb: e.memset(Sb.rearrange("p h v -> p (h v)"), 0.0), w=[Sbt])
        self.pool(lambda e: e.memset(ones, 1.0), w=[onest])
        epsc, epst = self.salloc([128, 8], F32), Tok("eps")
        self.pool(lambda e: e.memset(epsc, EPS), w=[epst])
        for h in range(4):
            self.dve(lambda e, h=h: e.tensor_copy(out=m2x4[:, h * 128:(h + 1) * 128], in_=self.cst[:, C_M2:C_M2 + 128]), r=[self.cst_t], w=[m2t])
        maskc = self.cst[:, C_MC:C_MC + NB]
        for blk in range(T // NB):
            tok0 = blk * NB
            s1, s1t = f1.next()
            s2, s2t = f2.next()
            s3, s3t = f3.next()
            E, Et = E_r.next()
            qd, qdt = qd_r.next()
            kd, kdt = kd_r.next()
            gs, gst = gs_r.next()
            qf = []
            for h in range(4):
                b1, b1t = self.bank()
                self.mmB(b1[:, 0:NB], b1t, wB, wBt[0], h * 128, tok0, NB)
                self.mmB(b1[:, NB:2 * NB], b1t, wB, wBt[1], 512 + h * 128, tok0, NB)
                qf.append((b1, b1t))
                self.act(lambda e, h=h, b1=b1, s1=s1: e.activation(out=s1[:, h, :], in_=b1[:, NB:2 * NB], func=AF.Sigmoid), r=[b1t], w=[s1t])
            for h in range(4):
                self.dve(lambda e, h=h, s1=s1: e.tensor_scalar(out=s1[:, h, :], in0=s1[:, h, :], scalar1=sm[:, 16 + h:17 + h], scalar2=sm[:, 12 + h:13 + h],
                                                              op0=ALU.mult, op1=ALU.add), r=[s1t, smt], w=[s1t])
            self.act(lambda e, s1=s1, s2=s2: e.activation(out=s2, in_=s1, func=AF.Ln), r=[s1t], w=[s2t])
            for h in range(4):
                self.dve(lambda e, h=h, s2=s2, s3=s3: e.tensor_tensor_scan(out=s3[:, h, :], data0=maskc, data1=s2[:, h, :], initial=0.0, op0=ALU.mult, op1=ALU.add),
                         r=[s2t, self.cst_t], w=[s3t])
            self.pool(lambda e, s1=s1: e.tensor_scalar(out=s1, in0=s1, scalar1=-1.0, scalar2=1.0, op0=ALU.mult, op1=ALU.add), r=[s1t], w=[s1t])
            self.act(lambda e, s2=s2, s3=s3: e.activation(out=s2, in_=s3, func=AF.Exp, scale=-1.0), r=[s3t, s2t], w=[s2t])
            self.act(lambda e, s3=s3: e.activation(out=s3, in_=s3, func=AF.Exp), r=[s3t], w=[s3t])
            self.dve(lambda e, E=E, s3=s3: e.tensor_copy(out=E, in_=s3[:, :, 63:NB:64]), r=[s3t], w=[Et])
            for h in range(4):
                b1, b1t = qf[h]
                self.dve(lambda e, h=h, b1=b1, s3=s3, qd=qd: e.tensor_tensor(out=qd[:, h, :], in0=b1[:, 0:NB], in1=s3[:, h, :], op=ALU.mult), r=[b1t, s3t], w=[qdt])
            self.pool(lambda e, s1=s1, s2=s2, kd=kd: e.tensor_tensor(out=kd, in0=s1, in1=s2, op=ALU.mult), r=[s1t, s2t], w=[kdt])
            for h in range(0, 4, 2):
                b2, b2t = self.bank()
                self.mmB(b2[:, 0:NB], b2t, wB, wBt[3], 1536 + h * 128, tok0, NB)
                self.mmB(b2[:, NB:2 * NB], b2t, wB, wBt[3], 1536 + (h + 1) * 128, tok0, NB)
                self.act(lambda e, h=h, b2=b2, gs=gs: e.activation(out=gs[:, h:h + 2, :], in_=b2[:].rearrange("p (a t) -> p a t", a=2), func=AF.Silu), r=[b2t], w=[gst])
            vi, vit = vi_r.next()
            ktk, ktkt = kt_r.next()
            for il in range(2):
                b3, b3t = self.bank()
                self.mmA(b3[:], b3t, blk * 2 + il, wB, wBt[2], 1024, 512)
                self.act(lambda e, il=il, b3=b3, vi=vi: e.activation(out=vi[:, il, :], in_=b3[:], func=AF.Silu), r=[b3t], w=[vit])
                self.transpose_to(ktk[:, il, :].rearrange("p (h k) -> p h k", h=4), [kd[:, h, il * 128:(il + 1) * 128] for h in range(4)], [kdt], [ktkt])
            for il in range(2):
                i = blk * 2 + il
                tc0 = il * 128
                U = [self.bank(), self.bank()]
                for c in range(2):
                    for h in range(4):
                        self.mm(U[c][0][:, h * 128:(h + 1) * 128], ktk[c * 64:(c + 1) * 64, il, h * 128:(h + 1) * 128],
                                vi[c * 64:(c + 1) * 64, il, h * 128:(h + 1) * 128], True, True, [ktkt, vit], [U[c][1]])
                A, At = self.bank()
                for h in range(4):
                    self.mm(A[:, h * 128:(h + 1) * 128], kd[:, h, tc0:tc0 + 128], qd[:, h, tc0:tc0 + 128], True, True, [kdt, qdt], [At])
                am, amt = am_r.next()
                self.dve(lambda e, am=am, A=A: e.tensor_tensor(out=am, in0=A[:], in1=m2x4, op=ALU.mult), r=[At, m2t], w=[amt])
                Sbs = [(Sb, Sbt)]
                for c in range(2):
                    tmp, tmpt = tmp_r.next()
                    Uc, Uct = U[c]
                    self.dve(lambda e, tmp=tmp, Uc=Uc: e.tensor_tensor(out=tmp, in0=Uc[:], in1=S.rearrange("p h v -> p (h v)"), op=ALU.add), r=[Uct, St], w=[tmpt])
                    Sb, Sbt = Sb_r.next()
                    col = il * 2 + c
                    for h in range(4):
                        self.dve(lambda e, h=h, tmp=tmp, E=E, col=col: e.tensor_scalar(out=S[:, h, :], in0=tmp[:, h * 128:(h + 1) * 128], scalar1=E[:, h, col:col + 1],
                                                                                    scalar2=None, op0=ALU.mult), r=[tmpt, Et], w=[St])
                        self.act(lambda e, h=h, tmp=tmp, E=E, col=col, Sb=Sb: e.activation(out=Sb[:, h, :], in_=tmp[:, h * 128:(h + 1) * 128], func=AF.Copy,
                                                                                        scale=E[:, h, col:col + 1]), r=[tmpt, Et], w=[Sbt])
                    Sbs.append((Sb, Sbt))
                O, Ot = self.bank()
                for h in range(4):
                    self.mm(O[:, h * 128:(h + 1) * 128], vi[:, il, h * 128:(h + 1) * 128], am[:, h * 128:(h + 1) * 128], True, False, [vit, amt], [Ot])
                    for c in range(2):
                        Sc, Sct = Sbs[c]
                        self.mm(O[:, h * 128 + c * 64:h * 128 + (c + 1) * 64], Sc[:, h, :], qd[:, h, tc0 + c * 64:tc0 + (c + 1) * 64], False, True,
                                [Sct, qdt], [Ot], skip_group_check=True)
                sq, sqt = sq_r.next()
                self.act(lambda e, sq=sq, O=O: e.activation(out=sq, in_=O[:], func=AF.Square), r=[Ot], w=[sqt])
                Q, Qt = self.bank()
                self.mm(Q[:], ones, sq, True, True, [onest, sqt], [Qt])
                rr, rrt = r_r.next()
                self.act(lambda e, rr=rr, Q=Q: e.activation(out=rr, in_=Q[:], func=AF.Ln, bias=epsc[:, 0:1], scale=1.0 / 128.0), r=[Qt, epst], w=[rrt])
                self.act(lambda e, rr=rr: e.activation(out=rr, in_=rr, func=AF.Exp, scale=-0.5), r=[rrt], w=[rrt])
                t1, t1t = t1_r.next()
                self.dve(lambda e, t1=t1, O=O, rr=rr: e.tensor_tensor(out=t1, in0=O[:], in1=rr, op=ALU.mult), r=[Ot, rrt], w=[t1t])
                yb, ybt = yb_r.next()
                for h in range(4):
                    self.dve(lambda e, h=h, yb=yb, t1=t1, gs=gs, tc0=tc0: e.scalar_tensor_tensor(out=yb[:, h, :], in0=t1[:, h * 128:(h + 1) * 128], scalar=sm[:, 20 + h:21 + h],
                                                                                              in1=gs[:, h, tc0:tc0 + 128], op0=ALU.mult, op1=ALU.mult),
                             r=[t1t, gst, smt], w=[ybt])
                self.out_proj(i, lambda k, yb=yb: yb[:, k, :], 4, [ybt], woB, woBt, first=False)
                self.layer_norm(i, lnp)

    def stage_moba(self):
        self.nbank = 6
        for G in range(2):
            self.moba_group(G)
        self.nbank = 8

    def moba_group(self, G):
        d = self.d
        if True:
            self.begin_stage()
            lnp = self.load_ln(3) if G == 1 else None
            kT, kTt = self.salloc([128, 4, T], BF16), [[Tok("kT") for _ in range(4)] for _ in range(4)]
            va, vat = self.salloc([128, NT, 8, 65], BF16), [Tok("va") for _ in range(NT)]
            wq, wqt = self.salloc([128, KC, 512], BF16), Tok("wq")
            woG, woGt = self.salloc([128, 4, D], BF16), Tok("woG")
            wk_r = self.ring(2, [128, KC, 256], BF16, "wk")
            qz_r = self.ring(1, [128, 8, 512], BF16, "qz")
            R_r = self.ring(1, [128, 512], BF16, "R")
            Rc, Rct = self.salloc([128, 512], BF16), Tok("Rc")
            pT_r = self.ring(3, [128, 512], BF16, "pT")
            otok, otokt = self.salloc([128, 4, 512], BF16), [Tok("otok") for _ in range(8)]
            oTb, oTbt = self.salloc([128, 4, 512], BF16), Tok("oTb")
            lcol, lcolt = self.salloc([128, 64], BF16), Tok("lcol")
            tri, trit = self.salloc([128, 128], BF16), Tok("tri")
            SBb, SBbt = self.salloc([128, 4, 128], BF16), Tok("SBb")
            SB_r = self.ring(4, [128, 128], BF16, "SB")
            km32, km32t = self.salloc([128, 4, 8], F32), Tok("km32")
            kmh, kmht = self.salloc([128, 4, 8], BF16), Tok("kmh")
            kml, kmlt = self.salloc([128, 4, 8], BF16), Tok("kml")
            aff_r = self.ring(2, [128, 8, 8], F32, "aff")
            cmp_, cmpt = self.salloc([128, 8, 8, 8], F32), Tok("cmp")
            cnt, cntt = self.salloc([128, 8, 8], F32), Tok("cnt")
            sm_r = self.ring(2, [128, 8], F32, "msm")
            self.dve(lambda e: e.tensor_copy(out=lcol, in_=self.cst[:, C_LCOL + G * 64:C_LCOL + (G + 1) * 64]), r=[self.cst_t], w=[lcolt])
            self.dve(lambda e: e.tensor_copy(out=tri, in_=self.cst[:, C_TRI:C_TRI + 128]), r=[self.cst_t], w=[trit])
            self.dve(lambda e: e.tensor_copy(out=SBb, in_=self.cst[:, C_SBB:C_SBB + 512].rearrange("p (j c) -> p j c", j=4)), r=[self.cst_t], w=[SBbt])
            self.transpose_to(Rc.rearrange("p (j t) -> p j t", j=4), [SBb[:, j, :] for j in range(4)], [SBbt], [Rct])
            if G == 0:
                self.dump(Rc[:, 0:128], Rct)
                self.dump(SBb[:, 0, :], SBbt)
            for s_ in range(qz_r.items.__len__()):
                qz0, qz0t = qz_r.items[s_]
                self.pool(lambda e, qz0=qz0: e.memset(qz0.rearrange("p h t -> p (h t)"), 0.0), w=[qz0t])
            self.pool(lambda e: e.memset(va.rearrange("p a b c -> p (a b c)"), 1.0), w=vat)
            self.wload_w(wq, d["oqkv"], G * 512, 512, wqt)
            self.wload(woG, d["owo"][G * 512:(G + 1) * 512, :].rearrange("(g p) n -> p g n", p=128), woGt)
            for c2 in range(2):
                wk, wkt = wk_r.next()
                self.wload_w(wk, d["oqkv"], 1024 + G * 512 + c2 * 256, 256, wkt, step=256)
                for pp in range(2):
                    p = c2 * 2 + pp
                    for tb in range(4):
                        ps, pst = self.bank()
                        self.mmB(ps[:], pst, wk, wkt, pp * 128, tb * 512, 512)
                        self.copy(kT[:, p, tb * 512:(tb + 1) * 512], ps[:], [pst], [kTt[p][tb]])
            for c2 in range(2):
                wk, wkt = wk_r.next()
                self.wload_w(wk, d["oqkv"], 2048 + G * 512 + c2 * 256, 256, wkt, step=256)
                for i in range(NT):
                    ps, pst = self.bank()
                    self.mmA(ps[:, 0:256], pst, i, wk, wkt, 0, 256)
                    self.copy(va[:, i, c2 * 4:(c2 + 1) * 4, 0:64], ps[:, 0:256].rearrange("p (h c) -> p h c", h=4), [pst], [vat[i]])
            for p in range(4):
                self.dve(lambda e, p=p: e.tensor_reduce(out=km32[:, p, :], in_=kT[:, p, :].rearrange("p (b t) -> p b t", b=8), axis=AX.X, op=ALU.add), r=kTt[p], w=[km32t])
            self.dve(lambda e: e.tensor_scalar(out=km32, in0=km32, scalar1=1.0 / 256.0, scalar2=None, op0=ALU.mult), r=[km32t], w=[km32t])
            self.dve(lambda e: e.tensor_copy(out=kmh, in_=km32), r=[km32t], w=[kmht])
            self.dve(lambda e: e.tensor_tensor(out=km32, in0=km32, in1=kmh, op=ALU.subtract), r=[km32t, kmht], w=[km32t])
            self.dve(lambda e: e.tensor_copy(out=kml, in_=km32), r=[km32t], w=[kmlt])

            for qc in range(4):
                qz, qzt = qz_r.next()
                for p in range(4):
                    ps, pst = self.bank()
                    self.mmB(ps[:], pst, wq, wqt, p * 128, qc * 512, 512)
                    self.act(lambda e, p=p, ps=ps, qz=qz: e.copy(out=qz[0:64, 2 * p, :], in_=ps[0:64, :]), r=[pst], w=[qzt])
                    self.dve(lambda e, p=p, ps=ps, qz=qz: e.tensor_copy(out=qz[64:128, 2 * p + 1, :], in_=ps[64:128, :]), r=[pst], w=[qzt])
                if qc < 2:
                    R, Rt = Rc, Rct
                else:
                    R, Rt = R_r.next()
                    sbs = []
                    for j in range(4):
                        qb = (qc * 4 + j) // 2
                        ab, abt = self.bank()
                        for hl in range(8):
                            self.mm(ab[:, hl * 8:(hl + 1) * 8], qz[:, hl, j * 128:(j + 1) * 128], kmh[:, hl // 2, :], True, False, [qzt, kmht], [abt])
                            self.mm(ab[:, hl * 8:(hl + 1) * 8], qz[:, hl, j * 128:(j + 1) * 128], kml[:, hl // 2, :], False, True, [qzt, kmlt], [abt])
                        aff, afft = aff_r.next()
                        self.dve(lambda e, aff=aff, ab=ab: e.tensor_copy(out=aff, in_=ab[:, 0:64].rearrange("p (h k) -> p h k", h=8)), r=[abt], w=[afft])
                        self.dve(lambda e, aff=aff, qb=qb: e.tensor_tensor(out=cmp_[:, :, 0:qb, 0:qb],
                                                                        in0=aff[:, :, 0:qb].unsqueeze(2).to_broadcast([128, 8, qb, qb]),
                                                                        in1=aff[:, :, 0:qb].unsqueeze(3).to_broadcast([128, 8, qb, qb]), op=ALU.is_gt),
                                 r=[afft], w=[cmpt])
                        self.dve(lambda e, qb=qb: e.tensor_reduce(out=cnt[:, :, 0:qb], in_=cmp_[:, :, 0:qb, 0:qb], axis=AX.X, op=ALU.add), r=[cmpt], w=[cntt])
                        SB, SBt = SB_r.next()
                        self.pool(lambda e, SB=SB, j=j: e.tensor_copy(out=SB, in_=SBb[:, j, :]), r=[SBbt], w=[SBt])
                        self.dve(lambda e, SB=SB, qb=qb: e.tensor_scalar(out=SB.rearrange("p (h s) -> p h s", h=8)[:, :, 0:qb], in0=cnt[:, :, 0:qb],
                                                                      scalar1=3.0, scalar2=-32768.0, op0=ALU.is_ge, op1=ALU.mult), r=[cntt, SBt], w=[SBt])
                        sbs.append((SB, SBt))
                    self.transpose_to(R.rearrange("p (j t) -> p j t", j=4), [s[0] for s in sbs], [s[1] for s in sbs], [Rt])
                nkt = 4 * qc + 4
                for hl in range(8):
                    h = G * 8 + hl
                    O, Ot = self.obank()
                    first = True
                    for kt in range(nkt):
                        j0 = max(0, kt - 4 * qc)
                        c0 = j0 * 128
                        kb = kt // 2
                        S_, S_t = self.bank()
                        self.mm(S_[:, c0:512], kT[:, hl // 2, kt * 128:(kt + 1) * 128], qz[:, hl, c0:512], True, False, [kTt[hl // 2][kt // 4], qzt], [S_t])
                        self.mm(S_[:, c0:512], lcol[:, hl * 8 + kb:hl * 8 + kb + 1].to_broadcast([128, 128]), R[:, c0:512], False, True, [lcolt, Rt], [S_t])
                        if kt >= 4 * qc:
                            self.dve(lambda e, S_=S_, c0=c0: e.tensor_tensor(out=S_[:, c0:c0 + 128], in0=S_[:, c0:c0 + 128], in1=self.cst[:, C_NTRI:C_NTRI + 128], op=ALU.add),
                                     r=[S_t, self.cst_t], w=[S_t])
                        pT, pTt = pT_r.next()
                        bcol = C_ALB + h * 16 + (kt - 4 * qc) + 12
                        self.act(lambda e, pT=pT, S_=S_, c0=c0, bcol=bcol: e.activation(out=pT[:, c0:512], in_=S_[:, c0:512], func=AF.Exp,
                                                                                     bias=self.cst[:, bcol:bcol + 1], scale=0.125),
                                 r=[S_t, self.cst_t], w=[pTt])
                        if G == 0 and qc == 0 and hl == 0 and kt == 0:
                            self.dump(S_[:, 0:512], S_t, psum=True)
                            self.dump(pT, pTt)
                            self.dump(va[:, 0, 0, :], vat[0])
                            self.dump(qz[:, 0, :], qzt)
                            self.dump(kT[:, 0, 0:128], kTt[0][0])
                            self.dump(R[:, 0:128], Rt)
                            self.dump(va[:, 0, :, :].rearrange("p h c -> p (h c)"), vat[0])
                        for j in range(j0, 4):
                            self.mm(O[:, j * 65:(j + 1) * 65], pT[:, j * 128:(j + 1) * 128], va[:, kt, hl, :], first, kt == 4 * qc + j,
                                    [pTt, vat[kt]], [Ot], skip_group_check=True)
                            first = False
                    if G == 0 and qc == 0 and hl == 0:
                        self.dump(O[:, 0:260], Ot, psum=True)
                    sm, smt = sm_r.next()
                    self.dve(lambda e, sm=sm, O=O: e.reciprocal(out=sm[:, 0:4], in_=O[:, 0:260].rearrange("p (j c) -> p j c", j=4)[:, :, 64]), r=[Ot], w=[smt])
                    for j in range(4):
                        if j % 2:
                            self.dve(lambda e, sm=sm, O=O, j=j, hl=hl: e.tensor_scalar(out=otok[:, j, hl * 64:(hl + 1) * 64], in0=O[:, j * 65:j * 65 + 64],
                                                                                    scalar1=sm[:, j:j + 1], scalar2=None, op0=ALU.mult), r=[Ot, smt], w=[otokt[hl]])
                        else:
                            self.act(lambda e, sm=sm, O=O, j=j, hl=hl: e.activation(out=otok[:, j, hl * 64:(hl + 1) * 64], in_=O[:, j * 65:j * 65 + 64],
                                                                                 func=AF.Copy, scale=sm[:, j:j + 1]), r=[Ot, smt], w=[otokt[hl]])
                if G == 0 and qc == 0:
                    self.dump(otok.rearrange("p j c -> p (j c)"), otokt[0])
                for j in range(4):
                    i = qc * 4 + j
                    self.transpose_to(oTb[:, :, j * 128:(j + 1) * 128], [otok[:, j, c * 128:(c + 1) * 128] for c in range(4)], otokt, [oTbt])
                    self.out_proj(i, lambda k, j=j: oTb[:, k, j * 128:(j + 1) * 128], 4, [oTbt], woG, woGt, first=(G == 0))
                    if G == 1:
                        self.layer_norm(i, lnp)


def build_program(stages):
    nc = bass.Bass("TRN2", target_bir_lowering=False)
    k = K(nc, stages)
    k.build()
    return nc


FULL_STAGES = [("mix0",), ("xattn", 0), ("ffn", 0), ("moba",), ("xattn", 1), ("ffn", 1)]


def make_in_maps(inputs, ncores=NCORES):
    f = lambda a: np.ascontiguousarray(np.asarray(a, dtype=np.float32))
    x = f(inputs["x"])
    mem = f(inputs["mem"])
    shared = dict(
        ln_g=f(inputs["ln_g"]).reshape(6, D),
        ln_b=f(inputs["ln_b"]).reshape(6, D),
        x_wq=f(inputs["x_wq"]), x_wkv=f(inputs["x_wkv"]), x_wo=f(inputs["x_wo"]),
        ffn_w_in=f(inputs["ffn_w_in"]), ffn_w_out=f(inputs["ffn_w_out"]),
        ev_w_in=f(inputs["ev_w_in"])[0], ev_w_out=f(inputs["ev_w_out"])[0],
        a_wsT=f(np.transpose(np.asarray(inputs["a_ws"])[0], (2, 0, 1))),
        a_bs=f(inputs["a_bs"]).reshape(1, 512),
        a_ln_g=f(inputs["a_ln_g"]).reshape(1, 512),
        a_ln_b=f(inputs["a_ln_b"]).reshape(1, 512),
        b_norm_gT=f(np.asarray(inputs["b_norm_g"]).reshape(4, 128).T),
        lb_logitsT=f(np.transpose(np.asarray(inputs["hgrn_lb_logits"]).reshape(2, 4, 128), (2, 0, 1))),
        od_w_qkv=f(inputs["od_w_qkv"])[0], od_w_out=f(inputs["od_w_out"])[0],
        consts=make_consts(),
    )
    maps = []
    for c in range(ncores):
        m = dict(shared)
        m["x"] = x[c]
        m["mem"] = mem[c]
        maps.append(m)
    return maps


def kernel(**inputs):
    nc = build_program(FULL_STAGES)
    in_maps = make_in_maps(inputs)
    res = run_bass_kernel_spmd(nc, in_maps, core_ids=list(range(NCORES)))
    return np.stack([np.asarray(r["out"], dtype=np.float32) for r in res.results], axis=0)
```

```python
import math
import os
from contextlib import ExitStack

import numpy as np
import concourse.bass as bass
import concourse.mybir as mybir
from concourse.bass_utils import run_bass_kernel_spmd

F32 = mybir.dt.float32
BF16 = mybir.dt.bfloat16
AF = mybir.ActivationFunctionType
ALU = mybir.AluOpType
AX = mybir.AxisListType

T = 2048
D = 1024
NT = T // 128
KC = D // 128
MEM = 256
DFF = 2816
NJ = DFF // 128
ALPHA = 4.0 ** 0.25
EPS = 1e-5
NCORES = 8


ARENA_REG = {"range": None, "toks": []}


class Tok:
    __slots__ = ("name", "w", "r")

    def __init__(self, name=""):
        self.name = name
        self.w = None
        self.r = []
        rng = ARENA_REG["range"]
        if rng is not None:
            st, lo, hi = rng
            inh = []
            for (st2, lo2, hi2, t2) in ARENA_REG["toks"]:
                if st2 < st and lo2 < hi and lo < hi2:
                    if t2.w is not None:
                        inh.append(t2.w)
                    inh.extend(t2.r)
            seen = set()
            for o in inh:
                if id(o) not in seen:
                    seen.add(id(o))
                    self.r.append(o)
            ARENA_REG["toks"].append((st, lo, hi, self))


class Op:
    __slots__ = ("eng", "fn", "deps", "sig", "is_dma", "need", "idx", "guard", "cost", "stage", "nleft", "users", "ready", "fin", "raw")

    def __init__(self, eng, fn, is_dma, cost):
        self.eng = eng
        self.fn = fn
        self.deps = []
        self.sig = None
        self.is_dma = is_dma
        self.need = False
        self.guard = None
        self.cost = cost
        self.users = []


CS = float(os.environ.get("CS", "1.6"))
DEFAULT_COST = {"pe": 0.06, "act": 0.45 * CS, "dve": 0.35 * CS, "pool": 0.6 * CS, "sp": 0.1}
SCHED_WINDOW = 160
XLAT = float(os.environ.get('XLAT', '1.0'))
SAME_ENGINE_NOSYNC = tuple(os.environ.get('NOSYNC', 'pe').split(','))
RAW_ONLY = not os.environ.get('ALL_SYNC')


class Prog:
    ENGS = ("pe", "act", "dve", "pool", "sp")

    def __init__(self):
        self.all = []
        self.stage = 0

    def fence(self):
        pass

    def op(self, eng, fn, reads=(), writes=(), dma=False, cost=None):
        if cost is None:
            cost = 3.0 if dma else DEFAULT_COST[eng]
        o = Op(eng, fn, dma, cost)
        o.stage = self.stage
        o.idx = len(self.all)
        deps = []
        raw = set()
        for t in reads:
            if t.w is not None:
                deps.append(t.w)
                raw.add(id(t.w))
        o.raw = raw
        for t in writes:
            if t.w is not None:
                deps.append(t.w)
            deps.extend(t.r)
        seen = set()
        for d in deps:
            if id(d) in seen or d is o:
                continue
            seen.add(id(d))
            o.deps.append(d)
            d.users.append(o)
        for t in reads:
            t.r.append(o)
        for t in writes:
            t.w = o
            t.r = []
        self.all.append(o)
        return o

    def schedule(self):
        order = {e: [] for e in self.ENGS}
        nst = self.stage + 1
        stages = [[] for _ in range(nst)]
        for o in self.all:
            stages[o.stage].append(o)
        t_stage = 0.0
        for ops in stages:
            if not ops:
                continue
            inst = set(id(o) for o in ops)
            pend = {e: [] for e in self.ENGS}
            for o in ops:
                o.nleft = sum(1 for d in o.deps if id(d) in inst)
                o.ready = t_stage
                pend[o.eng].append(o)
            free = {e: t_stage for e in self.ENGS}
            n = len(ops)
            tmax = t_stage
            while n:
                best = None
                for e in self.ENGS:
                    lst = pend[e]
                    cnt = 0
                    for o in lst:
                        if o.nleft == 0:
                            st = o.ready if o.ready > free[e] else free[e]
                            if best is None or st < best[0] - 1e-9:
                                best = (st, o)
                        cnt += 1
                        if cnt >= SCHED_WINDOW:
                            break
                st, o = best
                e = o.eng
                pend[e].remove(o)
                if o.is_dma:
                    free[e] = st + 0.1
                    o.fin = st + o.cost
                else:
                    o.fin = st + o.cost
                    free[e] = o.fin
                tmax = max(tmax, o.fin)
                for u in o.users:
                    if id(u) in inst:
                        u.nleft -= 1
                        lat = 0.05 if (u.eng == e and not o.is_dma) else XLAT
                        if o.fin + lat > u.ready:
                            u.ready = o.fin + lat
                order[e].append(o)
                n -= 1
            t_stage = tmax
        self.est_us = t_stage
        return order

    def emit(self, nc, engines, sems, dma_sems):
        order = self.schedule()
        last_by_stage = {}
        for e in self.ENGS:
            for o in order[e]:
                last_by_stage[(e, o.stage)] = o
        for e in self.ENGS:
            prev_stage = None
            for o in order[e]:
                if o.stage != prev_stage:
                    for e2 in self.ENGS:
                        cands = [v for (ee, st), v in last_by_stage.items() if ee == e2 and st < o.stage]
                        if cands:
                            d = max(cands, key=lambda v: v.stage)
                            if d is not o and d not in o.deps:
                                o.deps.append(d)
                    prev_stage = o.stage
        for e in self.ENGS:
            for o in order[e]:
                for d in o.deps:
                    if (not d.is_dma) and (not o.is_dma) and d.eng == o.eng and (o.eng in SAME_ENGINE_NOSYNC or (RAW_ONLY and id(d) not in o.raw)):
                        continue
                    d.need = True
        NDS = {q: len(dma_sems[q]) for q in dma_sems}
        all_dma = {q: [] for q in dma_sems}
        for e in self.ENGS:
            cnt = 0
            k = 0
            for o in order[e]:
                if o.is_dma:
                    s = dma_sems[e][k % NDS[e]]
                    gen = k // NDS[e]
                    o.sig = (s, 16 * (gen + 1))
                    o.guard = (s, 16 * gen) if gen > 0 else None
                    all_dma[e].append(o)
                    k += 1
                elif o.need:
                    cnt += 1
                    o.sig = (sems[e], cnt)

        def run(e, eng):
            waited = {}
            for o in order[e]:
                need = {}
                if o.guard is not None:
                    need[o.guard[0]] = o.guard[1]
                for d in o.deps:
                    if (not d.is_dma) and (not o.is_dma) and d.eng == e and (e in SAME_ENGINE_NOSYNC or (RAW_ONLY and id(d) not in o.raw)):
                        continue
                    s, v = d.sig
                    if need.get(s, 0) < v:
                        need[s] = v
                for s, v in need.items():
                    if waited.get(s, 0) >= v:
                        continue
                    eng.wait_ge(s, v)
                    waited[s] = v
                ins = o.fn(eng)
                if o.is_dma:
                    ins.then_inc(o.sig[0], 16)
                elif o.sig is not None:
                    ins.then_inc(o.sig[0], 1)
            return waited

        with nc.Block() as block:
            @block.tensor
            def _(pe):
                run("pe", pe)

            @block.scalar
            def _(act):
                run("act", act)

            @block.vector
            def _(dve):
                run("dve", dve)

            @block.gpsimd
            def _(pool):
                run("pool", pool)

            @block.sync
            def _(sp):
                w = run("sp", sp)
                for q, lst in all_dma.items():
                    last = {}
                    for o in lst:
                        last[o.sig[0]] = o.sig[1]
                    for s, v in last.items():
                        if w.get(s, 0) < v:
                            sp.wait_ge(s, v)


GELU_NATIVE = True
ARENA_BYTES = 98 * 1024

C_IDENT = 0
C_TRI = 128
C_M2 = 256
C_MC = 384
C_LCOL = 640
C_SBB = 768
C_ALB = 1280
C_NTRI = C_ALB + 256
CONST_W = C_NTRI + 128


def alibi_slope(h):
    return float(np.float32(2.0 ** (-8.0 * (h + 1) / 16)))


def _bf16_round(v):
    a = np.asarray(v, np.float32).reshape(1)
    u = a.view(np.uint32)
    r = ((u + 0x7FFF + ((u >> 16) & 1)) & 0xFFFF0000).astype(np.uint32)
    return float(r.view(np.float32)[0])


def make_consts():
    c = np.zeros((128, CONST_W), np.float32)
    p = np.arange(128)
    c[:, C_IDENT:C_IDENT + 128] = np.eye(128, dtype=np.float32)
    c[:, C_TRI:C_TRI + 128] = (p[:, None] <= p[None, :]).astype(np.float32)
    c[:, C_M2:C_M2 + 128] = ((p[:, None] <= p[None, :]) & ((p[:, None] // 64) == (p[None, :] // 64))).astype(np.float32)
    c[:, C_NTRI:C_NTRI + 128] = np.where(p[:, None] <= p[None, :], 0.0, -30000.0).astype(np.float32)
    t = np.arange(256)
    c[:, C_MC:C_MC + 256] = (t % 64 != 0).astype(np.float32)[None, :]
    for G in range(2):
        for hl in range(8):
            sl = alibi_slope(G * 8 + hl)
            hi = _bf16_round(sl)
            lo = _bf16_round(sl - hi)
            for kb in range(8):
                col = C_LCOL + G * 64 + hl * 8 + kb
                c[hl * 16 + kb, col] = 1.0
                c[hl * 16 + 8, col] = -8.0 * hi
                c[hl * 16 + 9, col] = -8.0 * hi
                c[hl * 16 + 10, col] = -8.0 * lo
                c[hl * 16 + 11, col] = -8.0 * lo
    for j in range(4):
        for hl in range(8):
            base = C_SBB + j * 128 + hl * 16
            c[:, base + 8] = (j % 2) * 128 + p
            c[:, base + 9] = 256 * (j // 2)
            c[:, base + 10] = (j % 2) * 128 + p
            c[:, base + 11] = 256 * (j // 2)
    for h in range(16):
        sl = alibi_slope(h)
        for dk in range(-12, 4):
            c[:, C_ALB + h * 16 + dk + 12] = np.float32(sl) * (p + 128.0 * dk).astype(np.float32)
    return c


class Ring:
    def __init__(self, items):
        self.items = items
        self.i = 0

    def next(self):
        it = self.items[self.i % len(self.items)]
        self.i += 1
        return it


class K:
    def __init__(self, nc, stages):
        self.nc = nc
        self.P = Prog()
        self.es = ExitStack()
        self.stages = stages
        self.uid = 0
        self.stage_no = 0
        ARENA_REG["range"] = None
        ARENA_REG["toks"] = []

    def sb(self, shape, dt, name=None):
        self.uid += 1
        return self.es.enter_context(self.nc.sbuf_tensor(f"{name or 't'}_{self.uid}", list(shape), dt))

    def salloc(self, shape, dt):
        n = 1
        for s in shape[1:]:
            n *= s
        nbytes = n * (4 if dt == F32 else 2)
        off = (self.aoff + 31) // 32 * 32
        self.aoff = off + nbytes
        assert self.aoff <= ARENA_BYTES, f"arena overflow {self.aoff}"
        if self.stage_no % 2:
            off = (ARENA_BYTES - self.aoff) // 32 * 32
        ARENA_REG["range"] = (self.stage_no, off, off + nbytes)
        v = self.arena[:, off // 2:(off + nbytes) // 2]
        if dt == F32:
            v = v.bitcast(F32)
        if len(shape) == 3:
            v = v.rearrange("p (a b) -> p a b", a=shape[1])
        elif len(shape) == 4:
            v = v.rearrange("p (a b c) -> p a b c", a=shape[1], b=shape[2])
        if shape[0] < 128:
            v = v[0:shape[0]]
        return v

    def ring(self, n, shape, dt, name="r"):
        return Ring([(self.salloc(shape, dt), Tok(name)) for _ in range(n)])

    def begin_stage(self):
        self.aoff = 0
        self.stage_no += 1
        ARENA_REG["range"] = None

    def dram_in(self, name, shape, dt=F32):
        return self.nc.dram_tensor(name, list(shape), dt, kind="ExternalInput").ap()

    def pe(self, fn, r=(), w=()):
        return self.P.op("pe", fn, r, w)

    def act(self, fn, r=(), w=()):
        return self.P.op("act", fn, r, w)

    def dve(self, fn, r=(), w=()):
        return self.P.op("dve", fn, r, w)

    def pool(self, fn, r=(), w=()):
        return self.P.op("pool", fn, r, w)

    def dma(self, q, out, in_, r=(), w=()):
        return self.P.op(q, lambda e: e.dma_start(out=out, in_=in_), r, w, dma=True)

    def dump(self, ap, tok, psum=False):
        if not os.environ.get('DEBUG_DUMP'):
            return
        n = ap.shape[-1]
        c0 = self.dump_off
        self.dump_off += n
        print("DUMP", c0, n)
        if psum:
            scr = self.salloc([128, n], F32)
            st = Tok("dscr")
            self.dve(lambda e: e.tensor_copy(out=scr, in_=ap), r=[tok], w=[st])
            self.dma("pool", self.d["dbg"][:, c0:c0 + n], scr, r=[st])
        else:
            self.dma("pool", self.d["dbg"][:, c0:c0 + n], ap, r=[tok])

    def bank(self):
        b = self.banks[self.bank_i % self.nbank]
        self.bank_i += 1
        return b

    def obank(self):
        b = self.banks[6 + self.obank_i % 2]
        self.obank_i += 1
        return b

    def mm(self, out, lhsT, rhs, start, stop, r, w, **kw):
        n = out.shape[-1]
        return self.P.op("pe", lambda e: e.matmul(out, lhsT=lhsT, rhs=rhs, start=start, stop=stop, **kw), r, w, cost=max(n, 64) / 2400.0 + 0.012)

    def mmB(self, ps, pst, W, wt, col0, tok0, n):
        xts = [self.xT_t[q] for q in range(tok0 // 128, (tok0 + n + 127) // 128)]
        for kc in range(KC):
            self.mm(ps, W[:, kc, col0:col0 + 128], self.xT[:, kc, tok0:tok0 + n], kc == 0, kc == KC - 1, [wt] + xts, [pst])

    def mmA(self, ps, pst, i, W, wt, col0, n):
        for kc in range(KC):
            self.mm(ps, self.xT[:, kc, i * 128:(i + 1) * 128], W[:, kc, col0:col0 + n], kc == 0, kc == KC - 1,
                    [wt, self.xT_t[i]], [pst])

    def wload(self, dst, src, tok):
        self.dma("pool", dst, src, w=[tok])

    def wload_w(self, dst, W_d, col0, n, tok, step=512):
        for c in range(0, n, step):
            m = min(step, n - c)
            self.wload(dst[:, :, c:c + m], W_d[:, col0 + c:col0 + c + m].rearrange("(k p) n -> p k n", p=128), tok)

    def build(self):
        nc = self.nc
        d = {}
        d["x"] = self.dram_in("x", [T, D])
        d["mem"] = self.dram_in("mem", [MEM, D])
        d["lng"] = self.dram_in("ln_g", [6, D])
        d["lnb"] = self.dram_in("ln_b", [6, D])
        d["xwq"] = self.dram_in("x_wq", [2, D, D])
        d["xwkv"] = self.dram_in("x_wkv", [2, D, 2 * D])
        d["xwo"] = self.dram_in("x_wo", [2, D, D])
        d["fwi"] = self.dram_in("ffn_w_in", [2, D, 2 * DFF])
        d["fwo"] = self.dram_in("ffn_w_out", [2, DFF, D])
        d["evi"] = self.dram_in("ev_w_in", [D, 3072])
        d["evo"] = self.dram_in("ev_w_out", [D, D])
        d["awsT"] = self.dram_in("a_wsT", [128, 4, 128])
        d["abs"] = self.dram_in("a_bs", [1, 512])
        d["alng"] = self.dram_in("a_ln_g", [1, 512])
        d["alnb"] = self.dram_in("a_ln_b", [1, 512])
        d["bng"] = self.dram_in("b_norm_gT", [128, 4])
        d["lbl"] = self.dram_in("lb_logitsT", [128, 2, 4])
        d["oqkv"] = self.dram_in("od_w_qkv", [D, 3072])
        d["owo"] = self.dram_in("od_w_out", [D, D])
        d["cst"] = self.dram_in("consts", [128, CONST_W])
        d["out"] = nc.dram_tensor("out", [T, D], F32, kind="ExternalOutput").ap()
        if os.environ.get('DEBUG_DUMP'):
            d["dbg"] = nc.dram_tensor("dbg", [128, 8192], F32, kind="ExternalOutput").ap()
        self.dump_off = 0
        self.d = d

        self.x_tok = self.sb([128, NT, D], F32, "x_tok")
        self.xT = self.sb([128, KC, T], BF16, "xT")
        self.xtok_t = [Tok(f"xtok{i}") for i in range(NT)]
        self.xT_t = [Tok(f"xT{i}") for i in range(NT)]
        self.cst = self.sb([128, CONST_W], F32, "cst")
        self.cst_t = Tok("cst")
        self.ident = self.sb([128, 128], BF16, "ident")
        self.ident_t = Tok("ident")
        self.xb_ring = Ring([(self.sb([128, D], BF16, "xb"), Tok("xb")) for _ in range(2)])
        self.st_ring = Ring([(self.sb([128, 32], F32, "lnst"), Tok("lnst")) for _ in range(3)])
        self.negh = self.sb([128, 512], F32, "negh")
        self.negh_t = Tok("negh")
        self.arena = self.sb([128, ARENA_BYTES // 2], BF16, "arena")
        self.aoff = 0
        self.banks = []
        for i in range(8):
            pt = self.es.enter_context(nc.psum_tensor(f"bank{i}", [128, 512], F32))
            self.banks.append((pt, Tok(f"bank{i}")))
        self.bank_i = 0
        self.obank_i = 0
        self.nbank = 8
        self.cp_i = 0

        self.dma("sp", self.cst[:], d["cst"], w=[self.cst_t])
        self.dve(lambda e: e.tensor_copy(out=self.ident[:], in_=self.cst[:, C_IDENT:C_IDENT + 128]),
                 r=[self.cst_t], w=[self.ident_t])
        self.pool(lambda e: e.memset(self.negh[:], -0.5), w=[self.negh_t])

        self.load_x()
        for s in self.stages:
            getattr(self, "stage_" + s[0])(*s[1:])
        ARENA_REG["range"] = None
        self.store_out()

        sems = {e: self.es.enter_context(nc.semaphore(f"s_{e}")) for e in Prog.ENGS}
        dma_sems = {}
        for q, n in (("sp", 8), ("pool", 6), ("act", 2)):
            dma_sems[q] = [self.es.enter_context(nc.semaphore(f"d_{q}{i}")) for i in range(n)]
        self.P.emit(nc, None, sems, dma_sems)
        self.es.close()

    def copy(self, out, in_, r, w):
        self.cp_i += 1
        if self.cp_i % 2:
            return self.act(lambda e: e.copy(out=out, in_=in_), r=r, w=w)
        return self.dve(lambda e: e.tensor_copy(out=out, in_=in_), r=r, w=w)

    def load_x(self):
        for i in range(NT):
            self.dma("sp", self.x_tok[:, i, :], self.d["x"][i * 128:(i + 1) * 128, :], w=[self.xtok_t[i]])
        for i in range(NT):
            self.to_xT(i)

    def store_out(self):
        for i in range(NT):
            self.dma("sp", self.d["out"][i * 128:(i + 1) * 128, :], self.x_tok[:, i, :], r=[self.xtok_t[i]])

    def transpose_to(self, dst_view, src_tiles, r, w):
        bk, bt = self.bank()
        psb = bk[:].bitcast(BF16)
        n = len(src_tiles)
        for k, src in enumerate(src_tiles):
            self.pe(lambda e, k=k, src=src: e.transpose(out=psb[:, k * 128:(k + 1) * 128], in_=src, identity=self.ident[:]),
                    r=list(r) + [self.ident_t], w=[bt])
        self.copy(dst_view, psb[:, 0:n * 128].rearrange("p (k t) -> p k t", k=n), [bt], w)

    def to_xT(self, i):
        xb, xbt = self.xb_ring.next()
        self.act(lambda e: e.copy(out=xb[:], in_=self.x_tok[:, i, :]), r=[self.xtok_t[i]], w=[xbt])
        self.transpose_to(self.xT[:, :, i * 128:(i + 1) * 128], [xb[:, kc * 128:(kc + 1) * 128] for kc in range(KC)],
                          [xbt], [self.xT_t[i]])

    def load_ln(self, idx):
        gb = self.salloc([128, 2, D], F32)
        t = Tok("lnp")
        g, b = gb[:, 0, :], gb[:, 1, :]
        self.dma("sp", g, self.d["lng"][idx:idx + 1, :].to_broadcast([128, D]), w=[t])
        self.dma("sp", b, self.d["lnb"][idx:idx + 1, :].to_broadcast([128, D]), w=[t])
        return g, b, t

    def rstd_small(self, out, var, r, w, eps=EPS):
        n = out.shape[-1]
        self.pool(lambda e: e.tensor_scalar(out=out, in0=var, scalar1=eps, scalar2=None, op0=ALU.add), r=r, w=w)
        self.pool(lambda e: e.tensor_tensor(out=out, in0=out, in1=self.negh[:, 0:n], op=ALU.pow), r=list(w) + [self.negh_t], w=w)

    def layer_norm(self, i, lnp):
        g, b, gt = lnp
        xt = self.x_tok[:, i, :]
        xtok = self.xtok_t[i]
        stt, stk = self.st_ring.next()
        self.dve(lambda e: e.bn_stats(out=stt[:, 0:6], in_=self.x_tok[:, i, 0:512]), r=[xtok], w=[stk])
        self.dve(lambda e: e.bn_stats(out=stt[:, 6:12], in_=self.x_tok[:, i, 512:1024]), r=[xtok], w=[stk])
        self.dve(lambda e: e.bn_aggr(out=stt[:, 12:14], in_=stt[:, 0:12].rearrange("p (a b) -> p a b", a=2)), r=[stk], w=[stk])
        self.rstd_small(stt[:, 15:16], stt[:, 13:14], [stk], [stk])
        self.dve(lambda e: e.scalar_tensor_tensor(out=stt[:, 16:17], in0=stt[:, 12:13], scalar=-1.0, in1=stt[:, 15:16],
                                                  op0=ALU.mult, op1=ALU.mult), r=[stk], w=[stk])
        self.act(lambda e: e.activation(out=xt, in_=xt, func=AF.Identity, bias=stt[:, 16:17], scale=stt[:, 15:16]),
                 r=[stk, xtok], w=[xtok])
        self.pool(lambda e: e.tensor_tensor(out=xt, in0=xt, in1=g, op=ALU.mult), r=[xtok, gt], w=[xtok])
        self.dve(lambda e: e.tensor_tensor(out=xt, in0=xt, in1=b, op=ALU.add), r=[xtok, gt], w=[xtok])
        self.to_xT(i)

    def accum(self, i, half, ps, pst, first):
        dst = self.x_tok[:, i, half * 512:(half + 1) * 512]
        if first:
            self.dve(lambda e: e.scalar_tensor_tensor(out=dst, in0=dst, scalar=ALPHA, in1=ps, op0=ALU.mult, op1=ALU.add),
                     r=[pst, self.xtok_t[i]], w=[self.xtok_t[i]])
        else:
            self.dve(lambda e: e.tensor_tensor(out=dst, in0=dst, in1=ps, op=ALU.add),
                     r=[pst, self.xtok_t[i]], w=[self.xtok_t[i]])

    def out_proj(self, i, lhs_fn, nk, lhs_toks, wo, wot, first):
        for half in range(2):
            ps, pst = self.bank()
            for k in range(nk):
                self.mm(ps[:], lhs_fn(k), wo[:, k, half * 512:(half + 1) * 512], k == 0, k == nk - 1, list(lhs_toks) + [wot], [pst])
            self.accum(i, half, ps[:], pst, first)

    def stage_ln_only(self, idx):
        self.begin_stage()
        lnp = self.load_ln(idx)
        for i in range(NT):
            self.layer_norm(i, lnp)

    def stage_ffn(self, l):
        self.begin_stage()
        groups = [list(range(0, 6)), list(range(6, 12)), list(range(12, 17)), list(range(17, 22))]
        fwi = self.d["fwi"][l]
        fwo = self.d["fwo"][l]
        lnp = self.load_ln(l * 3 + 2)
        wi_r = self.ring(3, [128, KC, 256], BF16, "fwi")
        wo_r = self.ring(2, [128, 6, D], BF16, "fwo")
        hT = self.salloc([128, 6, T], BF16)
        hTt = [[Tok("hT") for _ in range(4)] for _ in range(6)]
        sg_r = self.ring(2, [128, 512], F32, "sg")
        for gi, js in enumerate(groups):
            wo, wot = wo_r.next()
            j0 = js[0]
            self.wload(wo[:, 0:len(js), :], fwo[j0 * 128:(j0 + len(js)) * 128, :].rearrange("(j p) n -> p j n", p=128), wot)
            for jl, j in enumerate(js):
                wi, wit = wi_r.next()
                self.wload(wi[:, :, 0:128], fwi[:, j * 128:(j + 1) * 128].rearrange("(k p) n -> p k n", p=128), wit)
                self.wload(wi[:, :, 128:256], fwi[:, DFF + j * 128:DFF + (j + 1) * 128].rearrange("(k p) n -> p k n", p=128), wit)
                for tb in range(4):
                    pg, pgt = self.bank()
                    pu, put = self.bank()
                    self.mmB(pg[:], pgt, wi, wit, 0, tb * 512, 512)
                    self.mmB(pu[:], put, wi, wit, 128, tb * 512, 512)
                    sg, sgt = sg_r.next()
                    self.act(lambda e, sg=sg, pg=pg: e.activation(out=sg, in_=pg[:], func=AF.Silu), r=[pgt], w=[sgt])
                    self.dve(lambda e, sg=sg, pu=pu, jl=jl, tb=tb: e.tensor_tensor(out=hT[:, jl, tb * 512:(tb + 1) * 512], in0=sg, in1=pu[:], op=ALU.mult),
                             r=[sgt, put], w=[hTt[jl][tb]])
            last = gi == len(groups) - 1
            for i in range(NT):
                self.out_proj(i, lambda k, i=i: hT[:, k, i * 128:(i + 1) * 128], len(js), [hTt[k][i // 4] for k in range(len(js))], wo, wot, first=(gi == 0))
                if last:
                    self.layer_norm(i, lnp)

    def stage_xattn(self, l):
        self.begin_stage()
        d = self.d
        SC = 1.0 / 16.0
        lnp = self.load_ln(l * 3 + 1)
        wq, wqt = self.salloc([128, KC, D], BF16), Tok("wq")
        wo, wot = self.salloc([128, KC, D], BF16), Tok("wo")
        kT, kTt = self.salloc([128, KC, MEM], BF16), [Tok("kT") for _ in range(KC)]
        vS, vSt = self.salloc([128, 2, D], BF16), [[Tok("vS") for _ in range(2)] for _ in range(2)]
        memb, membt = self.salloc([128, 2, D], BF16), Tok("memb")
        memT, memTt = self.salloc([128, KC, MEM], BF16), Tok("memT")
        wkv_r = self.ring(1, [128, KC, 512], BF16, "wkv")
        qT_r = Ring([(self.salloc([128, KC, 512], BF16), [Tok("qT") for _ in range(KC)])])
        p32_r = self.ring(2, [128, 4, 256], F32, "p32")
        pb_r = self.ring(2, [128, 4, 256], BF16, "pb")
        pT_r = self.ring(2, [128, 8, 128], BF16, "pT")
        oT_r = self.ring(2, [128, 8, 128], BF16, "oT")
        sm_r = self.ring(3, [128, 16], F32, "sm")
        self.wload(memb, d["mem"].rearrange("(m p) n -> p m n", p=128), membt)
        for mt in range(2):
            self.transpose_to(memT[:, :, mt * 128:(mt + 1) * 128], [memb[:, mt, kc * 128:(kc + 1) * 128] for kc in range(KC)],
                              [membt], [memTt])
        for c in range(4):
            wk, wkt = wkv_r.next()
            self.wload_w(wk, d["xwkv"][l], c * 512, 512, wkt)
            if c < 2:
                for cc in range(4):
                    fc = c * 4 + cc
                    ps, pst = self.bank()
                    for kc in range(KC):
                        self.mm(ps[:, 0:MEM], wk[:, kc, cc * 128:(cc + 1) * 128], memT[:, kc, :], kc == 0, kc == KC - 1, [wkt, memTt], [pst])
                    self.copy(kT[:, fc, :], ps[:, 0:MEM], [pst], [kTt[fc]])
            else:
                for mt in range(2):
                    ps, pst = self.bank()
                    for kc in range(KC):
                        self.mm(ps[:], memT[:, kc, mt * 128:(mt + 1) * 128], wk[:, kc, :], kc == 0, kc == KC - 1, [wkt, memTt], [pst])
                    self.copy(vS[:, mt, (c - 2) * 512:(c - 1) * 512], ps[:], [pst], [vSt[mt][c - 2]])
        self.wload_w(wq, d["xwq"][l], 0, D, wqt)
        self.wload_w(wo, d["xwo"][l], 0, D, wot)
        for tb in range(4):
            qT, qTt = qT_r.next()
            for c in range(KC):
                ps, pst = self.bank()
                self.mmB(ps[:], pst, wq, wqt, c * 128, tb * 512, 512)
                self.copy(qT[:, c, :], ps[:], [pst], [qTt[c]])
            for il in range(4):
                i = tb * 4 + il
                sA, sAt = self.bank()
                sB, sBt = self.bank()
                for h in range(4):
                    bk, bkt = (sA, sAt) if h < 2 else (sB, sBt)
                    for k2 in range(2):
                        self.mm(bk[:, (h % 2) * 256:(h % 2 + 1) * 256], qT[:, 2 * h + k2, il * 128:(il + 1) * 128], kT[:, 2 * h + k2, :],
                                k2 == 0, k2 == 1, [qTt[2 * h + k2], kTt[2 * h + k2]], [bkt])
                sm, smt = sm_r.next()
                self.dve(lambda e, sm=sm, sA=sA: e.tensor_reduce(out=sm[:, 0:2], in_=sA[:].rearrange("p (h m) -> p h m", h=2), axis=AX.X, op=ALU.max), r=[sAt], w=[smt])
                self.dve(lambda e, sm=sm, sB=sB: e.tensor_reduce(out=sm[:, 2:4], in_=sB[:].rearrange("p (h m) -> p h m", h=2), axis=AX.X, op=ALU.max), r=[sBt], w=[smt])
                self.dve(lambda e, sm=sm: e.tensor_scalar(out=sm[:, 4:8], in0=sm[:, 0:4], scalar1=-SC, scalar2=None, op0=ALU.mult), r=[smt], w=[smt])
                p32, p32t = p32_r.next()
                for h in range(4):
                    bk, bkt = (sA, sAt) if h < 2 else (sB, sBt)
                    self.act(lambda e, h=h, bk=bk, sm=sm, p32=p32: e.activation(out=p32[:, h, :], in_=bk[:, (h % 2) * 256:(h % 2 + 1) * 256], func=AF.Exp,
                                                                             bias=sm[:, 4 + h:5 + h], scale=SC, accum_out=sm[:, 8 + h:9 + h]),
                             r=[bkt, smt], w=[p32t, smt])
                self.dve(lambda e, sm=sm: e.reciprocal(out=sm[:, 12:16], in_=sm[:, 8:12]), r=[smt], w=[smt])
                pb, pbt = pb_r.next()
                for h in range(4):
                    if h % 2:
                        self.dve(lambda e, h=h, pb=pb, p32=p32, sm=sm: e.tensor_scalar(out=pb[:, h, :], in0=p32[:, h, :], scalar1=sm[:, 12 + h:13 + h], scalar2=None, op0=ALU.mult),
                                 r=[p32t, smt], w=[pbt])
                    else:
                        self.act(lambda e, h=h, pb=pb, p32=p32, sm=sm: e.activation(out=pb[:, h, :], in_=p32[:, h, :], func=AF.Copy, scale=sm[:, 12 + h:13 + h]),
                                 r=[p32t, smt], w=[pbt])
                pT, pTt = pT_r.next()
                self.transpose_to(pT, [pb[:, h, mt * 128:(mt + 1) * 128] for h in range(4) for mt in range(2)], [pbt], [pTt])
                oA, oAt = self.bank()
                oB, oBt = self.bank()
                oT, oTt = oT_r.next()
                for c in range(8):
                    bk, bkt = (oA, oAt) if c < 4 else (oB, oBt)
                    h = c // 2
                    for mt in range(2):
                        self.mm(bk[:, (c % 4) * 128:(c % 4 + 1) * 128], vS[:, mt, c * 128:(c + 1) * 128], pT[:, h * 2 + mt, :],
                                mt == 0, mt == 1, [vSt[mt][c // 4], pTt], [bkt])
                self.copy(oT[:, 0:4, :], oA[:].rearrange("p (c t) -> p c t", c=4), [oAt], [oTt])
                self.copy(oT[:, 4:8, :], oB[:].rearrange("p (c t) -> p c t", c=4), [oBt], [oTt])
                self.out_proj(i, lambda k, oT=oT: oT[:, k, :], KC, [oTt], wo, wot, first=True)
                self.layer_norm(i, lnp)

    def gelu(self, out, ps, pst, outt, scr_r, half=True):
        if GELU_NATIVE:
            self.act(lambda e: e.activation(out=out, in_=ps, func=AF.Gelu_apprx_tanh), r=[pst], w=[outt])
            return 1.0
        C0 = 0.7978845608028654
        C1 = 0.044715
        s, stk = scr_r.next()
        n = ps.shape[-1]
        sv = s[:, 0:n]
        self.act(lambda e: e.activation(out=sv, in_=ps, func=AF.Square), r=[pst], w=[stk])
        self.dve(lambda e: e.tensor_scalar(out=sv, in0=sv, scalar1=C1, scalar2=1.0, op0=ALU.mult, op1=ALU.add), r=[stk], w=[stk])
        self.dve(lambda e: e.tensor_tensor(out=sv, in0=sv, in1=ps, op=ALU.mult), r=[stk, pst], w=[stk])
        self.act(lambda e: e.activation(out=sv, in_=sv, func=AF.Tanh, scale=C0), r=[stk], w=[stk])
        self.dve(lambda e: e.scalar_tensor_tensor(out=out, in0=sv, scalar=1.0, in1=ps, op0=ALU.add, op1=ALU.mult), r=[stk, pst], w=[outt])
        return 0.5

    def stage_mix0(self):
        d = self.d
        self.begin_stage()
        wuv, wut, wvt = self.salloc([128, KC, 1024], BF16), Tok("wu"), Tok("wv")
        woA, woAt = self.salloc([128, 4, D], BF16), Tok("woA")
        ws32, ws32t = self.salloc([128, 4, 128], F32), Tok("ws32")
        wsb, wsbt = self.salloc([128, 4, 128], BF16), Tok("wsb")
        bsB, bsBt = self.salloc([128, 512], F32), Tok("bsB")
        lgB, lgBt = self.salloc([128, 512], F32), Tok("lgB")
        lbB, lbBt = self.salloc([128, 512], F32), Tok("lbB")
        uT_r = self.ring(2, [128, 4, 512], BF16, "uT")
        scr_r = self.ring(5, [128, 512], F32, "gscr")
        vg_r = self.ring(4, [128, 512], F32, "vg")
        vb_r = self.ring(4, [128, 512], BF16, "vb")
        ss_r = self.ring(4, [128, 512], F32, "ssum")
        ya_r = self.ring(4, [128, 4, 128], BF16, "yaT")
        st_r = self.ring(4, [128, 48], F32, "gst")
        self.wload_w(wuv[:, :, 0:512], d["evi"], 0, 512, wut)
        self.wload_w(wuv[:, :, 512:1024], d["evi"], 512, 512, wvt)
        self.wload(woA, d["evo"][0:512, :].rearrange("(g p) n -> p g n", p=128), woAt)
        self.dma("sp", ws32, d["awsT"], w=[ws32t])
        self.dma("sp", bsB, d["abs"].to_broadcast([128, 512]), w=[bsBt])
        self.dma("sp", lgB, d["alng"].to_broadcast([128, 512]), w=[lgBt])
        self.dma("sp", lbB, d["alnb"].to_broadcast([128, 512]), w=[lbBt])
        for g in range(4):
            self.dve(lambda e, g=g: e.tensor_tensor(out=wsb[:, g, :], in0=ws32[:, g, :], in1=self.cst[:, C_TRI:C_TRI + 128], op=ALU.mult),
                     r=[ws32t, self.cst_t], w=[wsbt])
        for tb in range(4):
            uT, uTt = uT_r.next()
            gf = 1.0
            for c in range(4):
                ps, pst = self.bank()
                self.mmB(ps[:], pst, wuv, wut, c * 128, tb * 512, 512)
                gf = self.gelu(uT[:, c, :], ps[:], pst, uTt, scr_r)
            for il in range(4):
                i = tb * 4 + il
                ps, pst = self.bank()
                self.mmA(ps[:], pst, i, wuv, wvt, 512, 512)
                vg, vgt = vg_r.next()
                gv = self.gelu(vg, ps[:], pst, vgt, scr_r)
                st, stt = st_r.next()
                for g in range(4):
                    self.dve(lambda e, g=g, st=st, vg=vg: e.bn_stats(out=st[:, g * 6:(g + 1) * 6], in_=vg[:, g * 128:(g + 1) * 128]), r=[vgt], w=[stt])
                for g in range(4):
                    self.dve(lambda e, g=g, st=st: e.bn_aggr(out=st[:, 24 + g * 2:26 + g * 2], in_=st[:, g * 6:(g + 1) * 6]), r=[stt], w=[stt])
                mv = st[:, 24:32].rearrange("p (g two) -> p g two", two=2)
                self.pool(lambda e, st=st, mv=mv: e.tensor_scalar(out=st[:, 32:36], in0=mv[:, :, 1], scalar1=gv * gv, scalar2=EPS, op0=ALU.mult, op1=ALU.add), r=[stt], w=[stt])
                self.pool(lambda e, st=st: e.tensor_tensor(out=st[:, 32:36], in0=st[:, 32:36], in1=self.negh[:, 0:4], op=ALU.pow), r=[stt, self.negh_t], w=[stt])
                self.dve(lambda e, st=st: e.tensor_scalar(out=st[:, 36:40], in0=st[:, 32:36], scalar1=gv, scalar2=None, op0=ALU.mult), r=[stt], w=[stt])
                self.dve(lambda e, st=st, mv=mv: e.scalar_tensor_tensor(out=st[:, 40:44], in0=mv[:, :, 0], scalar=-1.0, in1=st[:, 36:40], op0=ALU.mult, op1=ALU.mult), r=[stt], w=[stt])
                for g in range(4):
                    self.act(lambda e, g=g, st=st, vg=vg: e.activation(out=vg[:, g * 128:(g + 1) * 128], in_=vg[:, g * 128:(g + 1) * 128], func=AF.Identity,
                                                                      bias=st[:, 40 + g:41 + g], scale=st[:, 36 + g:37 + g]), r=[stt, vgt], w=[vgt])
                self.dve(lambda e, vg=vg: e.tensor_tensor(out=vg, in0=vg, in1=lgB, op=ALU.mult), r=[vgt, lgBt], w=[vgt])
                vb, vbt = vb_r.next()
                self.pool(lambda e, vg=vg, vb=vb: e.tensor_tensor(out=vb, in0=vg, in1=lbB, op=ALU.add), r=[vgt, lbBt], w=[vbt])
                pss, psst = self.bank()
                for g in range(4):
                    self.mm(pss[:, g * 128:(g + 1) * 128], vb[:, g * 128:(g + 1) * 128], wsb[:, g, :], True, True, [vbt, wsbt], [psst])
                ss, sst = ss_r.next()
                self.dve(lambda e, ss=ss, pss=pss: e.tensor_tensor(out=ss, in0=pss[:], in1=bsB, op=ALU.add), r=[psst, bsBt], w=[sst])
                ya, yat = ya_r.next()
                self.dve(lambda e, ya=ya, uT=uT, ss=ss, il=il: e.scalar_tensor_tensor(out=ya, in0=uT[:, :, il * 128:(il + 1) * 128], scalar=gf,
                                                                                   in1=ss.rearrange("p (g t) -> p g t", g=4), op0=ALU.mult, op1=ALU.mult),
                         r=[uTt, sst], w=[yat])
                self.out_proj(i, lambda k, ya=ya: ya[:, k, :], 4, [yat], woA, woAt, first=True)

        self.begin_stage()
        NB = 256
        lnp = self.load_ln(0)
        wB, wBt = self.salloc([128, KC, 2048], BF16), [Tok("wBq"), Tok("wBf"), Tok("wBi"), Tok("wBg")]
        woB, woBt = self.salloc([128, 4, D], BF16), Tok("woB")
        sm, smt = self.salloc([128, 32], F32), Tok("hsm")
        S, St = self.salloc([128, 4, 128], F32), Tok("S")
        Sb_r = self.ring(3, [128, 4, 128], BF16, "Sb")
        m2x4, m2t = self.salloc([128, 512], BF16), Tok("m2x4")
        ones, onest = self.salloc([128, 128], BF16), Tok("ones")
        f1 = self.ring(1, [128, 4, NB], F32, "f1")
        f2 = self.ring(1, [128, 4, NB], F32, "f2")
        f3 = self.ring(1, [128, 4, NB], F32, "f3")
        E_r = self.ring(2, [128, 4, 4], F32, "Elast")
        qd_r = self.ring(1, [128, 4, NB], BF16, "qdT")
        kd_r = self.ring(1, [128, 4, NB], BF16, "kdT")
        gs_r = self.ring(1, [128, 4, NB], BF16, "gsT")
        vi_r = self.ring(1, [128, 2, 512], BF16, "vi")
        kt_r = self.ring(1, [128, 2, 512], BF16, "kdtok")
        am_r = self.ring(3, [128, 512], BF16, "am")
        tmp_r = self.ring(2, [128, 512], F32, "stmp")
        sq_r = self.ring(2, [128, 512], BF16, "sq")
        r_r = self.ring(2, [128, 512], F32, "rr")
        t1_r = self.ring(2, [128, 512], F32, "t1")
        yb_r = self.ring(3, [128, 4, 128], BF16, "ybT")
        for c in range(4):
            self.wload_w(wB[:, :, c * 512:(c + 1) * 512], d["evi"], 1024 + c * 512, 512, wBt[c])
        self.wload(woB, d["evo"][512:1024, :].rearrange("(g p) n -> p g n", p=128), woBt)
        self.dma("sp", sm[:, 0:8], d["lbl"].rearrange("p l h -> p (l h)"), w=[smt])
        self.dma("sp", sm[:, 20:24], d["bng"], w=[smt])
        self.dve(lambda e: e.tensor_tensor(out=sm[:, 8:12], in0=sm[:, 0:4], in1=sm[:, 4:8], op=ALU.subtract), r=[smt], w=[smt])
        self.act(lambda e: e.activation(out=sm[:, 12:16], in_=sm[:, 8:12], func=AF.Sigmoid), r=[smt], w=[smt])
        self.dve(lambda e: e.tensor_scalar(out=sm[:, 16:20], in0=sm[:, 12:16], scalar1=-1.0, scalar2=1.0, op0=ALU.mult, op1=ALU.add), r=[smt], w=[smt])
        self.pool(lambda e: e.memset(S.rearrange("p h v -> p (h v)"), 0.0), w=[St])
        Sb, Sbt = Sb_r.next()
        self.pool(lambda e, Sb=Sb: e.memset(Sb.rearrange("p h v -> p (h v)"), 0.0), w=[Sbt])
        self.pool(lambda e: e.memset(ones, 1.0), w=[onest])
        epsc, epst = self.salloc([128, 8], F32), Tok("eps")
        self.pool(lambda e: e.memset(epsc, EPS), w=[epst])
        for h in range(4):
            self.dve(lambda e, h=h: e.tensor_copy(out=m2x4[:, h * 128:(h + 1) * 128], in_=self.cst[:, C_M2:C_M2 + 128]), r=[self.cst_t], w=[m2t])
        maskc = self.cst[:, C_MC:C_MC + NB]
        for blk in range(T // NB):
            tok0 = blk * NB
            s1, s1t = f1.next()
            s2, s2t = f2.next()
            s3, s3t = f3.next()
            E, Et = E_r.next()
            qd, qdt = qd_r.next()
            kd, kdt = kd_r.next()
            gs, gst = gs_r.next()
            qf = []
            for h in range(4):
                b1, b1t = self.bank()
                self.mmB(b1[:, 0:NB], b1t, wB, wBt[0], h * 128, tok0, NB)
                self.mmB(b1[:, NB:2 * NB], b1t, wB, wBt[1], 512 + h * 128, tok0, NB)
                qf.append((b1, b1t))
                self.act(lambda e, h=h, b1=b1, s1=s1: e.activation(out=s1[:, h, :], in_=b1[:, NB:2 * NB], func=AF.Sigmoid), r=[b1t], w=[s1t])
            for h in range(4):
                self.dve(lambda e, h=h, s1=s1: e.tensor_scalar(out=s1[:, h, :], in0=s1[:, h, :], scalar1=sm[:, 16 + h:17 + h], scalar2=sm[:, 12 + h:13 + h],
                                                              op0=ALU.mult, op1=ALU.add), r=[s1t, smt], w=[s1t])
            self.act(lambda e, s1=s1, s2=s2: e.activation(out=s2, in_=s1, func=AF.Ln), r=[s1t], w=[s2t])
            for h in range(4):
                self.dve(lambda e, h=h, s2=s2, s3=s3: e.tensor_tensor_scan(out=s3[:, h, :], data0=maskc, data1=s2[:, h, :], initial=0.0, op0=ALU.mult, op1=ALU.add),
                         r=[s2t, self.cst_t], w=[s3t])
            self.pool(lambda e, s1=s1: e.tensor_scalar(out=s1, in0=s1, scalar1=-1.0, scalar2=1.0, op0=ALU.mult, op1=ALU.add), r=[s1t], w=[s1t])
            self.act(lambda e, s2=s2, s3=s3: e.activation(out=s2, in_=s3, func=AF.Exp, scale=-1.0), r=[s3t, s2t], w=[s2t])
            self.act(lambda e, s3=s3: e.activation(out=s3, in_=s3, func=AF.Exp), r=[s3t], w=[s3t])
            self.dve(lambda e, E=E, s3=s3: e.tensor_copy(out=E, in_=s3[:, :, 63:NB:64]), r=[s3t], w=[Et])
            for h in range(4):
                b1, b1t = qf[h]
                self.dve(lambda e, h=h, b1=b1, s3=s3, qd=qd: e.tensor_tensor(out=qd[:, h, :], in0=b1[:, 0:NB], in1=s3[:, h, :], op=ALU.mult), r=[b1t, s3t], w=[qdt])
            self.pool(lambda e, s1=s1, s2=s2, kd=kd: e.tensor_tensor(out=kd, in0=s1, in1=s2, op=ALU.mult), r=[s1t, s2t], w=[kdt])
            for h in range(0, 4, 2):
                b2, b2t = self.bank()
                self.mmB(b2[:, 0:NB], b2t, wB, wBt[3], 1536 + h * 128, tok0, NB)
                self.mmB(b2[:, NB:2 * NB], b2t, wB, wBt[3], 1536 + (h + 1) * 128, tok0, NB)
                self.act(lambda e, h=h, b2=b2, gs=gs: e.activation(out=gs[:, h:h + 2, :], in_=b2[:].rearrange("p (a t) -> p a t", a=2), func=AF.Silu), r=[b2t], w=[gst])
            vi, vit = vi_r.next()
            ktk, ktkt = kt_r.next()
            for il in range(2):
                b3, b3t = self.bank()
                self.mmA(b3[:], b3t, blk * 2 + il, wB, wBt[2], 1024, 512)
                self.act(lambda e, il=il, b3=b3, vi=vi: e.activation(out=vi[:, il, :], in_=b3[:], func=AF.Silu), r=[b3t], w=[vit])
                self.transpose_to(ktk[:, il, :].rearrange("p (h k) -> p h k", h=4), [kd[:, h, il * 128:(il + 1) * 128] for h in range(4)], [kdt], [ktkt])
            for il in range(2):
                i = blk * 2 + il
                tc0 = il * 128
                U = [self.bank(), self.bank()]
                for c in range(2):
                    for h in range(4):
                        self.mm(U[c][0][:, h * 128:(h + 1) * 128], ktk[c * 64:(c + 1) * 64, il, h * 128:(h + 1) * 128],
                                vi[c * 64:(c + 1) * 64, il, h * 128:(h + 1) * 128], True, True, [ktkt, vit], [U[c][1]])
                A, At = self.bank()
                for h in range(4):
                    self.mm(A[:, h * 128:(h + 1) * 128], kd[:, h, tc0:tc0 + 128], qd[:, h, tc0:tc0 + 128], True, True, [kdt, qdt], [At])
                am, amt = am_r.next()
                self.dve(lambda e, am=am, A=A: e.tensor_tensor(out=am, in0=A[:], in1=m2x4, op=ALU.mult), r=[At, m2t], w=[amt])
                Sbs = [(Sb, Sbt)]
                for c in range(2):
                    tmp, tmpt = tmp_r.next()
                    Uc, Uct = U[c]
                    self.dve(lambda e, tmp=tmp, Uc=Uc: e.tensor_tensor(out=tmp, in0=Uc[:], in1=S.rearrange("p h v -> p (h v)"), op=ALU.add), r=[Uct, St], w=[tmpt])
                    Sb, Sbt = Sb_r.next()
                    col = il * 2 + c
                    for h in range(4):
                        self.dve(lambda e, h=h, tmp=tmp, E=E, col=col: e.tensor_scalar(out=S[:, h, :], in0=tmp[:, h * 128:(h + 1) * 128], scalar1=E[:, h, col:col + 1],
                                                                                    scalar2=None, op0=ALU.mult), r=[tmpt, Et], w=[St])
                        self.act(lambda e, h=h, tmp=tmp, E=E, col=col, Sb=Sb: e.activation(out=Sb[:, h, :], in_=tmp[:, h * 128:(h + 1) * 128], func=AF.Copy,
                                                                                        scale=E[:, h, col:col + 1]), r=[tmpt, Et], w=[Sbt])
                    Sbs.append((Sb, Sbt))
                O, Ot = self.bank()
                for h in range(4):
                    self.mm(O[:, h * 128:(h + 1) * 128], vi[:, il, h * 128:(h + 1) * 128], am[:, h * 128:(h + 1) * 128], True, False, [vit, amt], [Ot])
                    for c in range(2):
                        Sc, Sct = Sbs[c]
                        self.mm(O[:, h * 128 + c * 64:h * 128 + (c + 1) * 64], Sc[:, h, :], qd[:, h, tc0 + c * 64:tc0 + (c + 1) * 64], False, True,
                                [Sct, qdt], [Ot], skip_group_check=True)
                sq, sqt = sq_r.next()
                self.act(lambda e, sq=sq, O=O: e.activation(out=sq, in_=O[:], func=AF.Square), r=[Ot], w=[sqt])
                Q, Qt = self.bank()
                self.mm(Q[:], ones, sq, True, True, [onest, sqt], [Qt])
                rr, rrt = r_r.next()
                self.act(lambda e, rr=rr, Q=Q: e.activation(out=rr, in_=Q[:], func=AF.Ln, bias=epsc[:, 0:1], scale=1.0 / 128.0), r=[Qt, epst], w=[rrt])
                self.act(lambda e, rr=rr: e.activation(out=rr, in_=rr, func=AF.Exp, scale=-0.5), r=[rrt], w=[rrt])
                t1, t1t = t1_r.next()
                self.dve(lambda e, t1=t1, O=O, rr=rr: e.tensor_tensor(out=t1, in0=O[:], in1=rr, op=ALU.mult), r=[Ot, rrt], w=[t1t])
                yb, ybt = yb_r.next()
                for h in range(4):
                    self.dve(lambda e, h=h, yb=yb, t1=t1, gs=gs, tc0=tc0: e.scalar_tensor_tensor(out=yb[:, h, :], in0=t1[:, h * 128:(h + 1) * 128], scalar=sm[:, 20 + h:21 + h],
                                                                                              in1=gs[:, h, tc0:tc0 + 128], op0=ALU.mult, op1=ALU.mult),
                             r=[t1t, gst, smt], w=[ybt])
                self.out_proj(i, lambda k, yb=yb: yb[:, k, :], 4, [ybt], woB, woBt, first=False)
                self.layer_norm(i, lnp)

    def stage_moba(self):
        self.nbank = 6
        for G in range(2):
            self.moba_group(G)
        self.nbank = 8

    def moba_group(self, G):
        d = self.d
        if True:
            self.begin_stage()
            lnp = self.load_ln(3) if G == 1 else None
            kT, kTt = self.salloc([128, 4, T], BF16), [[Tok("kT") for _ in range(4)] for _ in range(4)]
            va, vat = self.salloc([128, NT, 8, 65], BF16), [Tok("va") for _ in range(NT)]
            wq, wqt = self.salloc([128, KC, 512], BF16), Tok("wq")
            woG, woGt = self.salloc([128, 4, D], BF16), Tok("woG")
            wk_r = self.ring(2, [128, KC, 256], BF16, "wk")
            qz_r = self.ring(1, [128, 8, 512], BF16, "qz")
            R_r = self.ring(1, [128, 512], BF16, "R")
            Rc, Rct = self.salloc([128, 512], BF16), Tok("Rc")
            pT_r = self.ring(3, [128, 512], BF16, "pT")
            otok, otokt = self.salloc([128, 4, 512], BF16), [Tok("otok") for _ in range(8)]
            oTb, oTbt = self.salloc([128, 4, 512], BF16), Tok("oTb")
            lcol, lcolt = self.salloc([128, 64], BF16), Tok("lcol")
            tri, trit = self.salloc([128, 128], BF16), Tok("tri")
            SBb, SBbt = self.salloc([128, 4, 128], BF16), Tok("SBb")
            SB_r = self.ring(4, [128, 128], BF16, "SB")
            km32, km32t = self.salloc([128, 4, 8], F32), Tok("km32")
            kmh, kmht = self.salloc([128, 4, 8], BF16), Tok("kmh")
            kml, kmlt = self.salloc([128, 4, 8], BF16), Tok("kml")
            aff_r = self.ring(2, [128, 8, 8], F32, "aff")
            cmp_, cmpt = self.salloc([128, 8, 8, 8], F32), Tok("cmp")
            cnt, cntt = self.salloc([128, 8, 8], F32), Tok("cnt")
            sm_r = self.ring(2, [128, 8], F32, "msm")
            self.dve(lambda e: e.tensor_copy(out=lcol, in_=self.cst[:, C_LCOL + G * 64:C_LCOL + (G + 1) * 64]), r=[self.cst_t], w=[lcolt])
            self.dve(lambda e: e.tensor_copy(out=tri, in_=self.cst[:, C_TRI:C_TRI + 128]), r=[self.cst_t], w=[trit])
            self.dve(lambda e: e.tensor_copy(out=SBb, in_=self.cst[:, C_SBB:C_SBB + 512].rearrange("p (j c) -> p j c", j=4)), r=[self.cst_t], w=[SBbt])
            self.transpose_to(Rc.rearrange("p (j t) -> p j t", j=4), [SBb[:, j, :] for j in range(4)], [SBbt], [Rct])
            if G == 0:
                self.dump(Rc[:, 0:128], Rct)
                self.dump(SBb[:, 0, :], SBbt)
            for s_ in range(qz_r.items.__len__()):
                qz0, qz0t = qz_r.items[s_]
                self.pool(lambda e, qz0=qz0: e.memset(qz0.rearrange("p h t -> p (h t)"), 0.0), w=[qz0t])
            self.pool(lambda e: e.memset(va.rearrange("p a b c -> p (a b c)"), 1.0), w=vat)
            self.wload_w(wq, d["oqkv"], G * 512, 512, wqt)
            self.wload(woG, d["owo"][G * 512:(G + 1) * 512, :].rearrange("(g p) n -> p g n", p=128), woGt)
            for c2 in range(2):
                wk, wkt = wk_r.next()
                self.wload_w(wk, d["oqkv"], 1024 + G * 512 + c2 * 256, 256, wkt, step=256)
                for pp in range(2):
                    p = c2 * 2 + pp
                    for tb in range(4):
                        ps, pst = self.bank()
                        self.mmB(ps[:], pst, wk, wkt, pp * 128, tb * 512, 512)
                        self.copy(kT[:, p, tb * 512:(tb + 1) * 512], ps[:], [pst], [kTt[p][tb]])
            for c2 in range(2):
                wk, wkt = wk_r.next()
                self.wload_w(wk, d["oqkv"], 2048 + G * 512 + c2 * 256, 256, wkt, step=256)
                for i in range(NT):
                    ps, pst = self.bank()
                    self.mmA(ps[:, 0:256], pst, i, wk, wkt, 0, 256)
                    self.copy(va[:, i, c2 * 4:(c2 + 1) * 4, 0:64], ps[:, 0:256].rearrange("p (h c) -> p h c", h=4), [pst], [vat[i]])
            for p in range(4):
                self.dve(lambda e, p=p: e.tensor_reduce(out=km32[:, p, :], in_=kT[:, p, :].rearrange("p (b t) -> p b t", b=8), axis=AX.X, op=ALU.add), r=kTt[p], w=[km32t])
            self.dve(lambda e: e.tensor_scalar(out=km32, in0=km32, scalar1=1.0 / 256.0, scalar2=None, op0=ALU.mult), r=[km32t], w=[km32t])
            self.dve(lambda e: e.tensor_copy(out=kmh, in_=km32), r=[km32t], w=[kmht])
            self.dve(lambda e: e.tensor_tensor(out=km32, in0=km32, in1=kmh, op=ALU.subtract), r=[km32t, kmht], w=[km32t])
            self.dve(lambda e: e.tensor_copy(out=kml, in_=km32), r=[km32t], w=[kmlt])

            for qc in range(4):
                qz, qzt = qz_r.next()
                for p in range(4):
                    ps, pst = self.bank()
                    self.mmB(ps[:], pst, wq, wqt, p * 128, qc * 512, 512)
                    self.act(lambda e, p=p, ps=ps, qz=qz: e.copy(out=qz[0:64, 2 * p, :], in_=ps[0:64, :]), r=[pst], w=[qzt])
                    self.dve(lambda e, p=p, ps=ps, qz=qz: e.tensor_copy(out=qz[64:128, 2 * p + 1, :], in_=ps[64:128, :]), r=[pst], w=[qzt])
                if qc < 2:
                    R, Rt = Rc, Rct
                else:
                    R, Rt = R_r.next()
                    sbs = []
                    for j in range(4):
                        qb = (qc * 4 + j) // 2
                        ab, abt = self.bank()
                        for hl in range(8):
                            self.mm(ab[:, hl * 8:(hl + 1) * 8], qz[:, hl, j * 128:(j + 1) * 128], kmh[:, hl // 2, :], True, False, [qzt, kmht], [abt])
                            self.mm(ab[:, hl * 8:(hl + 1) * 8], qz[:, hl, j * 128:(j + 1) * 128], kml[:, hl // 2, :], False, True, [qzt, kmlt], [abt])
                        aff, afft = aff_r.next()
                        self.dve(lambda e, aff=aff, ab=ab: e.tensor_copy(out=aff, in_=ab[:, 0:64].rearrange("p (h k) -> p h k", h=8)), r=[abt], w=[afft])
                        self.dve(lambda e, aff=aff, qb=qb: e.tensor_tensor(out=cmp_[:, :, 0:qb, 0:qb],
                                                                        in0=aff[:, :, 0:qb].unsqueeze(2).to_broadcast([128, 8, qb, qb]),
                                                                        in1=aff[:, :, 0:qb].unsqueeze(3).to_broadcast([128, 8, qb, qb]), op=ALU.is_gt),
                                 r=[afft], w=[cmpt])
                        self.dve(lambda e, qb=qb: e.tensor_reduce(out=cnt[:, :, 0:qb], in_=cmp_[:, :, 0:qb, 0:qb], axis=AX.X, op=ALU.add), r=[cmpt], w=[cntt])
                        SB, SBt = SB_r.next()
                        self.pool(lambda e, SB=SB, j=j: e.tensor_copy(out=SB, in_=SBb[:, j, :]), r=[SBbt], w=[SBt])
                        self.dve(lambda e, SB=SB, qb=qb: e.tensor_scalar(out=SB.rearrange("p (h s) -> p h s", h=8)[:, :, 0:qb], in0=cnt[:, :, 0:qb],
                                                                      scalar1=3.0, scalar2=-32768.0, op0=ALU.is_ge, op1=ALU.mult), r=[cntt, SBt], w=[SBt])
                        sbs.append((SB, SBt))
                    self.transpose_to(R.rearrange("p (j t) -> p j t", j=4), [s[0] for s in sbs], [s[1] for s in sbs], [Rt])
                nkt = 4 * qc + 4
                for hl in range(8):
                    h = G * 8 + hl
                    O, Ot = self.obank()
                    first = True
                    for kt in range(nkt):
                        j0 = max(0, kt - 4 * qc)
                        c0 = j0 * 128
                        kb = kt // 2
                        S_, S_t = self.bank()
                        self.mm(S_[:, c0:512], kT[:, hl // 2, kt * 128:(kt + 1) * 128], qz[:, hl, c0:512], True, False, [kTt[hl // 2][kt // 4], qzt], [S_t])
                        self.mm(S_[:, c0:512], lcol[:, hl * 8 + kb:hl * 8 + kb + 1].to_broadcast([128, 128]), R[:, c0:512], False, True, [lcolt, Rt], [S_t])
                        if kt >= 4 * qc:
                            self.dve(lambda e, S_=S_, c0=c0: e.tensor_tensor(out=S_[:, c0:c0 + 128], in0=S_[:, c0:c0 + 128], in1=self.cst[:, C_NTRI:C_NTRI + 128], op=ALU.add),
                                     r=[S_t, self.cst_t], w=[S_t])
                        pT, pTt = pT_r.next()
                        bcol = C_ALB + h * 16 + (kt - 4 * qc) + 12
                        self.act(lambda e, pT=pT, S_=S_, c0=c0, bcol=bcol: e.activation(out=pT[:, c0:512], in_=S_[:, c0:512], func=AF.Exp,
                                                                                     bias=self.cst[:, bcol:bcol + 1], scale=0.125),
                                 r=[S_t, self.cst_t], w=[pTt])
                        if G == 0 and qc == 0 and hl == 0 and kt == 0:
                            self.dump(S_[:, 0:512], S_t, psum=True)
                            self.dump(pT, pTt)
                            self.dump(va[:, 0, 0, :], vat[0])
                            self.dump(qz[:, 0, :], qzt)
                            self.dump(kT[:, 0, 0:128], kTt[0][0])
                            self.dump(R[:, 0:128], Rt)
                            self.dump(va[:, 0, :, :].rearrange("p h c -> p (h c)"), vat[0])
                        for j in range(j0, 4):
                            self.mm(O[:, j * 65:(j + 1) * 65], pT[:, j * 128:(j + 1) * 128], va[:, kt, hl, :], first, kt == 4 * qc + j,
                                    [pTt, vat[kt]], [Ot], skip_group_check=True)
                            first = False
                    if G == 0 and qc == 0 and hl == 0:
                        self.dump(O[:, 0:260], Ot, psum=True)
                    sm, smt = sm_r.next()
                    self.dve(lambda e, sm=sm, O=O: e.reciprocal(out=sm[:, 0:4], in_=O[:, 0:260].rearrange("p (j c) -> p j c", j=4)[:, :, 64]), r=[Ot], w=[smt])
                    for j in range(4):
                        if j % 2:
                            self.dve(lambda e, sm=sm, O=O, j=j, hl=hl: e.tensor_scalar(out=otok[:, j, hl * 64:(hl + 1) * 64], in0=O[:, j * 65:j * 65 + 64],
                                                                                    scalar1=sm[:, j:j + 1], scalar2=None, op0=ALU.mult), r=[Ot, smt], w=[otokt[hl]])
                        else:
                            self.act(lambda e, sm=sm, O=O, j=j, hl=hl: e.activation(out=otok[:, j, hl * 64:(hl + 1) * 64], in_=O[:, j * 65:j * 65 + 64],
                                                                                 func=AF.Copy, scale=sm[:, j:j + 1]), r=[Ot, smt], w=[otokt[hl]])
                if G == 0 and qc == 0:
                    self.dump(otok.rearrange("p j c -> p (j c)"), otokt[0])
                for j in range(4):
                    i = qc * 4 + j
                    self.transpose_to(oTb[:, :, j * 128:(j + 1) * 128], [otok[:, j, c * 128:(c + 1) * 128] for c in range(4)], otokt, [oTbt])
                    self.out_proj(i, lambda k, j=j: oTb[:, k, j * 128:(j + 1) * 128], 4, [oTbt], woG, woGt, first=(G == 0))
                    if G == 1:
                        self.layer_norm(i, lnp)


def build_program(stages):
    nc = bass.Bass("TRN2", target_bir_lowering=False)
    k = K(nc, stages)
    k.build()
    return nc


FULL_STAGES = [("mix0",), ("xattn", 0), ("ffn", 0), ("moba",), ("xattn", 1), ("ffn", 1)]


def make_in_maps(inputs, ncores=NCORES):
    f = lambda a: np.ascontiguousarray(np.asarray(a, dtype=np.float32))
    x = f(inputs["x"])
    mem = f(inputs["mem"])
    shared = dict(
        ln_g=f(inputs["ln_g"]).reshape(6, D),
        ln_b=f(inputs["ln_b"]).reshape(6, D),
        x_wq=f(inputs["x_wq"]), x_wkv=f(inputs["x_wkv"]), x_wo=f(inputs["x_wo"]),
        ffn_w_in=f(inputs["ffn_w_in"]), ffn_w_out=f(inputs["ffn_w_out"]),
        ev_w_in=f(inputs["ev_w_in"])[0], ev_w_out=f(inputs["ev_w_out"])[0],
        a_wsT=f(np.transpose(np.asarray(inputs["a_ws"])[0], (2, 0, 1))),
        a_bs=f(inputs["a_bs"]).reshape(1, 512),
        a_ln_g=f(inputs["a_ln_g"]).reshape(1, 512),
        a_ln_b=f(inputs["a_ln_b"]).reshape(1, 512),
        b_norm_gT=f(np.asarray(inputs["b_norm_g"]).reshape(4, 128).T),
        lb_logitsT=f(np.transpose(np.asarray(inputs["hgrn_lb_logits"]).reshape(2, 4, 128), (2, 0, 1))),
        od_w_qkv=f(inputs["od_w_qkv"])[0], od_w_out=f(inputs["od_w_out"])[0],
        consts=make_consts(),
    )
    maps = []
    for c in range(ncores):
        m = dict(shared)
        m["x"] = x[c]
        m["mem"] = mem[c]
        maps.append(m)
    return maps


def kernel(**inputs):
    nc = build_program(FULL_STAGES)
    in_maps = make_in_maps(inputs)
    res = run_bass_kernel_spmd(nc, in_maps, core_ids=list(range(NCORES)))
    return np.stack([np.asarray(r["out"], dtype=np.float32) for r in res.results], axis=0)
```

```python
import math
import os
from contextlib import ExitStack

import numpy as np
import concourse.bass as bass
import concourse.mybir as mybir
from concourse.bass_utils import run_bass_kernel_spmd

F32 = mybir.dt.float32
BF16 = mybir.dt.bfloat16
AF = mybir.ActivationFunctionType
ALU = mybir.AluOpType
AX = mybir.AxisListType

T = 2048
D = 1024
NT = T // 128
KC = D // 128
MEM = 256
DFF = 2816
NJ = DFF // 128
ALPHA = 4.0 ** 0.25
EPS = 1e-5
NCORES = 8


ARENA_REG = {"range": None, "toks": []}


class Tok:
    __slots__ = ("name", "w", "r")

    def __init__(self, name=""):
        self.name = name
        self.w = None
        self.r = []
        rng = ARENA_REG["range"]
        if rng is not None:
            st, lo, hi = rng
            inh = []
            for (st2, lo2, hi2, t2) in ARENA_REG["toks"]:
                if st2 < st and lo2 < hi and lo < hi2:
                    if t2.w is not None:
                        inh.append(t2.w)
                    inh.extend(t2.r)
            seen = set()
            for o in inh:
                if id(o) not in seen:
                    seen.add(id(o))
                    self.r.append(o)
            ARENA_REG["toks"].append((st, lo, hi, self))


class Op:
    __slots__ = ("eng", "fn", "deps", "sig", "is_dma", "need", "idx", "guard", "cost", "stage", "nleft", "users", "ready", "fin", "raw")

    def __init__(self, eng, fn, is_dma, cost):
        self.eng = eng
        self.fn = fn
        self.deps = []
        self.sig = None
        self.is_dma = is_dma
        self.need = False
        self.guard = None
        self.cost = cost
        self.users = []


CS = float(os.environ.get("CS", "1.6"))
CS2 = float(os.environ.get("CS2", "1.3"))


class _Probe:
    def __init__(self):
        self.n = None
        self.name = None

    def __getattr__(self, name):
        def f(*a, **k):
            out = k.get("out")
            if out is None and a:
                out = a[0]
            try:
                n = 1
                for s in out.shape[1:]:
                    n *= s
                self.n = n
            except Exception:
                self.n = None
            self.name = name
            return self
        return f


DEFAULT_COST = {"pe": 0.06, "act": 0.45 * CS, "dve": 0.35 * CS, "pool": 0.6 * CS, "sp": 0.1}
SCHED_WINDOW = 160
XLAT = float(os.environ.get('XLAT', '1.0'))
SAME_ENGINE_NOSYNC = tuple(os.environ.get('NOSYNC', 'pe').split(','))
RAW_ONLY = not os.environ.get('ALL_SYNC')


class Prog:
    ENGS = ("pe", "act", "dve", "pool", "sp")

    def __init__(self):
        self.all = []
        self.stage = 0

    def fence(self):
        pass

    def op(self, eng, fn, reads=(), writes=(), dma=False, cost=None):
        if cost is None:
            pr = _Probe()
            try:
                fn(pr)
            except Exception:
                pr.n = None
            n = pr.n
            if n is None:
                cost = 3.0 if dma else DEFAULT_COST[eng]
            elif dma:
                cost = 2.0 + n * 128 * 4 / 150e3
            elif eng == "pe":
                cost = max(n, 64) / 2400.0 + 0.03
            elif eng == "act":
                cost = (0.22 + n * 0.00085) * CS2
            elif eng == "dve":
                cost = (0.12 + n * 0.0011 * (2.0 if pr.name == "tensor_tensor_scan" else 1.0)) * CS2
            else:
                cost = (0.35 + n * 0.0022) * CS2
        o = Op(eng, fn, dma, cost)
        o.stage = self.stage
        o.idx = len(self.all)
        deps = []
        raw = set()
        for t in reads:
            if t.w is not None:
                deps.append(t.w)
                raw.add(id(t.w))
        o.raw = raw
        for t in writes:
            if t.w is not None:
                deps.append(t.w)
            deps.extend(t.r)
        seen = set()
        for d in deps:
            if id(d) in seen or d is o:
                continue
            seen.add(id(d))
            o.deps.append(d)
            d.users.append(o)
        for t in reads:
            t.r.append(o)
        for t in writes:
            t.w = o
            t.r = []
        self.all.append(o)
        return o

    def schedule(self):
        order = {e: [] for e in self.ENGS}
        nst = self.stage + 1
        stages = [[] for _ in range(nst)]
        for o in self.all:
            stages[o.stage].append(o)
        t_stage = 0.0
        for ops in stages:
            if not ops:
                continue
            inst = set(id(o) for o in ops)
            pend = {e: [] for e in self.ENGS}
            for o in ops:
                o.nleft = sum(1 for d in o.deps if id(d) in inst)
                o.ready = t_stage
                pend[o.eng].append(o)
            free = {e: t_stage for e in self.ENGS}
            n = len(ops)
            tmax = t_stage
            while n:
                best = None
                for e in self.ENGS:
                    lst = pend[e]
                    cnt = 0
                    for o in lst:
                        if o.nleft == 0:
                            st = o.ready if o.ready > free[e] else free[e]
                            if best is None or st < best[0] - 1e-9:
                                best = (st, o)
                        cnt += 1
                        if cnt >= SCHED_WINDOW:
                            break
                st, o = best
                e = o.eng
                pend[e].remove(o)
                if o.is_dma:
                    free[e] = st + 0.1
                    o.fin = st + o.cost
                else:
                    o.fin = st + o.cost
                    free[e] = o.fin
                tmax = max(tmax, o.fin)
                for u in o.users:
                    if id(u) in inst:
                        u.nleft -= 1
                        lat = 0.05 if (u.eng == e and not o.is_dma) else XLAT
                        if o.fin + lat > u.ready:
                            u.ready = o.fin + lat
                order[e].append(o)
                n -= 1
            t_stage = tmax
        self.est_us = t_stage
        return order

    def emit(self, nc, engines, sems, dma_sems):
        order = self.schedule()
        last_by_stage = {}
        for e in self.ENGS:
            for o in order[e]:
                last_by_stage[(e, o.stage)] = o
        for e in self.ENGS:
            prev_stage = None
            for o in order[e]:
                if o.stage != prev_stage:
                    for e2 in self.ENGS:
                        cands = [v for (ee, st), v in last_by_stage.items() if ee == e2 and st < o.stage]
                        if cands:
                            d = max(cands, key=lambda v: v.stage)
                            if d is not o and d not in o.deps:
                                o.deps.append(d)
                    prev_stage = o.stage
        for e in self.ENGS:
            for o in order[e]:
                for d in o.deps:
                    if (not d.is_dma) and (not o.is_dma) and d.eng == o.eng and (o.eng in SAME_ENGINE_NOSYNC or (RAW_ONLY and id(d) not in o.raw)):
                        continue
                    d.need = True
        NDS = {q: len(dma_sems[q]) for q in dma_sems}
        all_dma = {q: [] for q in dma_sems}
        for e in self.ENGS:
            cnt = 0
            k = 0
            for o in order[e]:
                if o.is_dma:
                    s = dma_sems[e][k % NDS[e]]
                    gen = k // NDS[e]
                    o.sig = (s, 16 * (gen + 1))
                    o.guard = (s, 16 * gen) if gen > 0 else None
                    all_dma[e].append(o)
                    k += 1
                elif o.need:
                    cnt += 1
                    o.sig = (sems[e], cnt)

        def run(e, eng):
            waited = {}
            for o in order[e]:
                need = {}
                if o.guard is not None:
                    need[o.guard[0]] = o.guard[1]
                for d in o.deps:
                    if (not d.is_dma) and (not o.is_dma) and d.eng == e and (e in SAME_ENGINE_NOSYNC or (RAW_ONLY and id(d) not in o.raw)):
                        continue
                    s, v = d.sig
                    if need.get(s, 0) < v:
                        need[s] = v
                for s, v in need.items():
                    if waited.get(s, 0) >= v:
                        continue
                    eng.wait_ge(s, v)
                    waited[s] = v
                ins = o.fn(eng)
                if o.is_dma:
                    ins.then_inc(o.sig[0], 16)
                elif o.sig is not None:
                    ins.then_inc(o.sig[0], 1)
            return waited

        with nc.Block() as block:
            @block.tensor
            def _(pe):
                run("pe", pe)

            @block.scalar
            def _(act):
                run("act", act)

            @block.vector
            def _(dve):
                run("dve", dve)

            @block.gpsimd
            def _(pool):
                run("pool", pool)

            @block.sync
            def _(sp):
                w = run("sp", sp)
                for q, lst in all_dma.items():
                    last = {}
                    for o in lst:
                        last[o.sig[0]] = o.sig[1]
                    for s, v in last.items():
                        if w.get(s, 0) < v:
                            sp.wait_ge(s, v)


GELU_NATIVE = True
ARENA_BYTES = 98 * 1024

C_IDENT = 0
C_TRI = 128
C_M2 = 256
C_MC = 384
C_LCOL = 640
C_SBB = 768
C_ALB = 1280
C_NTRI = C_ALB + 256
CONST_W = C_NTRI + 128


def alibi_slope(h):
    return float(np.float32(2.0 ** (-8.0 * (h + 1) / 16)))


def _bf16_round(v):
    a = np.asarray(v, np.float32).reshape(1)
    u = a.view(np.uint32)
    r = ((u + 0x7FFF + ((u >> 16) & 1)) & 0xFFFF0000).astype(np.uint32)
    return float(r.view(np.float32)[0])


def make_consts():
    c = np.zeros((128, CONST_W), np.float32)
    p = np.arange(128)
    c[:, C_IDENT:C_IDENT + 128] = np.eye(128, dtype=np.float32)
    c[:, C_TRI:C_TRI + 128] = (p[:, None] <= p[None, :]).astype(np.float32)
    c[:, C_M2:C_M2 + 128] = ((p[:, None] <= p[None, :]) & ((p[:, None] // 64) == (p[None, :] // 64))).astype(np.float32)
    c[:, C_NTRI:C_NTRI + 128] = np.where(p[:, None] <= p[None, :], 0.0, -30000.0).astype(np.float32)
    t = np.arange(256)
    c[:, C_MC:C_MC + 256] = (t % 64 != 0).astype(np.float32)[None, :]
    for G in range(2):
        for hl in range(8):
            sl = alibi_slope(G * 8 + hl)
            hi = _bf16_round(sl)
            lo = _bf16_round(sl - hi)
            for kb in range(8):
                col = C_LCOL + G * 64 + hl * 8 + kb
                c[hl * 16 + kb, col] = 1.0
                c[hl * 16 + 8, col] = -8.0 * hi
                c[hl * 16 + 9, col] = -8.0 * hi
                c[hl * 16 + 10, col] = -8.0 * lo
                c[hl * 16 + 11, col] = -8.0 * lo
    for j in range(4):
        for hl in range(8):
            base = C_SBB + j * 128 + hl * 16
            c[:, base + 8] = (j % 2) * 128 + p
            c[:, base + 9] = 256 * (j // 2)
            c[:, base + 10] = (j % 2) * 128 + p
            c[:, base + 11] = 256 * (j // 2)
    for h in range(16):
        sl = alibi_slope(h)
        for dk in range(-12, 4):
            c[:, C_ALB + h * 16 + dk + 12] = np.float32(sl) * (p + 128.0 * dk).astype(np.float32)
    return c


class Ring:
    def __init__(self, items):
        self.items = items
        self.i = 0

    def next(self):
        it = self.items[self.i % len(self.items)]
        self.i += 1
        return it


class K:
    def __init__(self, nc, stages):
        self.nc = nc
        self.P = Prog()
        self.es = ExitStack()
        self.stages = stages
        self.uid = 0
        self.stage_no = 0
        ARENA_REG["range"] = None
        ARENA_REG["toks"] = []

    def sb(self, shape, dt, name=None):
        self.uid += 1
        return self.es.enter_context(self.nc.sbuf_tensor(f"{name or 't'}_{self.uid}", list(shape), dt))

    def salloc(self, shape, dt):
        n = 1
        for s in shape[1:]:
            n *= s
        nbytes = n * (4 if dt == F32 else 2)
        off = (self.aoff + 31) // 32 * 32
        self.aoff = off + nbytes
        assert self.aoff <= ARENA_BYTES, f"arena overflow {self.aoff}"
        if self.stage_no % 2:
            off = (ARENA_BYTES - self.aoff) // 32 * 32
        ARENA_REG["range"] = (self.stage_no, off, off + nbytes)
        v = self.arena[:, off // 2:(off + nbytes) // 2]
        if dt == F32:
            v = v.bitcast(F32)
        if len(shape) == 3:
            v = v.rearrange("p (a b) -> p a b", a=shape[1])
        elif len(shape) == 4:
            v = v.rearrange("p (a b c) -> p a b c", a=shape[1], b=shape[2])
        if shape[0] < 128:
            v = v[0:shape[0]]
        return v

    def ring(self, n, shape, dt, name="r"):
        return Ring([(self.salloc(shape, dt), Tok(name)) for _ in range(n)])

    def begin_stage(self):
        self.aoff = 0
        self.stage_no += 1
        ARENA_REG["range"] = None

    def dram_in(self, name, shape, dt=F32):
        return self.nc.dram_tensor(name, list(shape), dt, kind="ExternalInput").ap()

    def pe(self, fn, r=(), w=()):
        return self.P.op("pe", fn, r, w)

    def act(self, fn, r=(), w=()):
        return self.P.op("act", fn, r, w)

    def dve(self, fn, r=(), w=()):
        return self.P.op("dve", fn, r, w)

    def pool(self, fn, r=(), w=()):
        return self.P.op("pool", fn, r, w)

    def dma(self, q, out, in_, r=(), w=()):
        return self.P.op(q, lambda e: e.dma_start(out=out, in_=in_), r, w, dma=True)

    def dump(self, ap, tok, psum=False):
        if not os.environ.get('DEBUG_DUMP'):
            return
        n = ap.shape[-1]
        c0 = self.dump_off
        self.dump_off += n
        print("DUMP", c0, n)
        if psum:
            scr = self.salloc([128, n], F32)
            st = Tok("dscr")
            self.dve(lambda e: e.tensor_copy(out=scr, in_=ap), r=[tok], w=[st])
            self.dma("pool", self.d["dbg"][:, c0:c0 + n], scr, r=[st])
        else:
            self.dma("pool", self.d["dbg"][:, c0:c0 + n], ap, r=[tok])

    def bank(self):
        b = self.banks[self.bank_i % self.nbank]
        self.bank_i += 1
        return b

    def obank(self):
        b = self.banks[6 + self.obank_i % 2]
        self.obank_i += 1
        return b

    def mm(self, out, lhsT, rhs, start, stop, r, w, **kw):
        n = out.shape[-1]
        return self.P.op("pe", lambda e: e.matmul(out, lhsT=lhsT, rhs=rhs, start=start, stop=stop, **kw), r, w, cost=max(n, 64) / 2400.0 + 0.012)

    def mmB(self, ps, pst, W, wt, col0, tok0, n):
        xts = [self.xT_t[q] for q in range(tok0 // 128, (tok0 + n + 127) // 128)]
        for kc in range(KC):
            self.mm(ps, W[:, kc, col0:col0 + 128], self.xT[:, kc, tok0:tok0 + n], kc == 0, kc == KC - 1, [wt] + xts, [pst])

    def mmA(self, ps, pst, i, W, wt, col0, n):
        for kc in range(KC):
            self.mm(ps, self.xT[:, kc, i * 128:(i + 1) * 128], W[:, kc, col0:col0 + n], kc == 0, kc == KC - 1,
                    [wt, self.xT_t[i]], [pst])

    def wload(self, dst, src, tok):
        self.dma("pool", dst, src, w=[tok])

    def wload_w(self, dst, W_d, col0, n, tok, step=512):
        for c in range(0, n, step):
            m = min(step, n - c)
            self.wload(dst[:, :, c:c + m], W_d[:, col0 + c:col0 + c + m].rearrange("(k p) n -> p k n", p=128), tok)

    def build(self):
        nc = self.nc
        d = {}
        d["x"] = self.dram_in("x", [T, D])
        d["mem"] = self.dram_in("mem", [MEM, D])
        d["lng"] = self.dram_in("ln_g", [6, D])
        d["lnb"] = self.dram_in("ln_b", [6, D])
        d["xwq"] = self.dram_in("x_wq", [2, D, D])
        d["xwkv"] = self.dram_in("x_wkv", [2, D, 2 * D])
        d["xwo"] = self.dram_in("x_wo", [2, D, D])
        d["fwi"] = self.dram_in("ffn_w_in", [2, D, 2 * DFF])
        d["fwo"] = self.dram_in("ffn_w_out", [2, DFF, D])
        d["evi"] = self.dram_in("ev_w_in", [D, 3072])
        d["evo"] = self.dram_in("ev_w_out", [D, D])
        d["awsT"] = self.dram_in("a_wsT", [128, 4, 128])
        d["abs"] = self.dram_in("a_bs", [1, 512])
        d["alng"] = self.dram_in("a_ln_g", [1, 512])
        d["alnb"] = self.dram_in("a_ln_b", [1, 512])
        d["bng"] = self.dram_in("b_norm_gT", [128, 4])
        d["lbl"] = self.dram_in("lb_logitsT", [128, 2, 4])
        d["oqkv"] = self.dram_in("od_w_qkv", [D, 3072])
        d["owo"] = self.dram_in("od_w_out", [D, D])
        d["cst"] = self.dram_in("consts", [128, CONST_W])
        d["out"] = nc.dram_tensor("out", [T, D], F32, kind="ExternalOutput").ap()
        if os.environ.get('DEBUG_DUMP'):
            d["dbg"] = nc.dram_tensor("dbg", [128, 8192], F32, kind="ExternalOutput").ap()
        self.dump_off = 0
        self.d = d

        self.x_tok = self.sb([128, NT, D], F32, "x_tok")
        self.xT = self.sb([128, KC, T], BF16, "xT")
        self.xtok_t = [Tok(f"xtok{i}") for i in range(NT)]
        self.xT_t = [Tok(f"xT{i}") for i in range(NT)]
        self.cst = self.sb([128, CONST_W], F32, "cst")
        self.cst_t = Tok("cst")
        self.ident = self.sb([128, 128], BF16, "ident")
        self.ident_t = Tok("ident")
        self.xb_ring = Ring([(self.sb([128, D], BF16, "xb"), Tok("xb")) for _ in range(2)])
        self.st_ring = Ring([(self.sb([128, 32], F32, "lnst"), Tok("lnst")) for _ in range(3)])
        self.negh = self.sb([128, 512], F32, "negh")
        self.negh_t = Tok("negh")
        self.arena = self.sb([128, ARENA_BYTES // 2], BF16, "arena")
        self.aoff = 0
        self.banks = []
        for i in range(8):
            pt = self.es.enter_context(nc.psum_tensor(f"bank{i}", [128, 512], F32))
            self.banks.append((pt, Tok(f"bank{i}")))
        self.bank_i = 0
        self.obank_i = 0
        self.nbank = 8
        self.cp_i = 0

        self.dma("sp", self.cst[:], d["cst"], w=[self.cst_t])
        self.dve(lambda e: e.tensor_copy(out=self.ident[:], in_=self.cst[:, C_IDENT:C_IDENT + 128]),
                 r=[self.cst_t], w=[self.ident_t])
        self.pool(lambda e: e.memset(self.negh[:], -0.5), w=[self.negh_t])

        self.load_x()
        for s in self.stages:
            getattr(self, "stage_" + s[0])(*s[1:])
        ARENA_REG["range"] = None
        self.store_out()

        sems = {e: self.es.enter_context(nc.semaphore(f"s_{e}")) for e in Prog.ENGS}
        dma_sems = {}
        for q, n in (("sp", 8), ("pool", 6), ("act", 2)):
            dma_sems[q] = [self.es.enter_context(nc.semaphore(f"d_{q}{i}")) for i in range(n)]
        self.P.emit(nc, None, sems, dma_sems)
        self.es.close()

    def copy(self, out, in_, r, w):
        self.cp_i += 1
        if self.cp_i % 2:
            return self.act(lambda e: e.copy(out=out, in_=in_), r=r, w=w)
        return self.dve(lambda e: e.tensor_copy(out=out, in_=in_), r=r, w=w)

    def load_x(self):
        for i in range(NT):
            self.dma("sp", self.x_tok[:, i, :], self.d["x"][i * 128:(i + 1) * 128, :], w=[self.xtok_t[i]])
        for i in range(NT):
            self.to_xT(i)

    def store_out(self):
        for i in range(NT):
            self.dma("sp", self.d["out"][i * 128:(i + 1) * 128, :], self.x_tok[:, i, :], r=[self.xtok_t[i]])

    def transpose_to(self, dst_view, src_tiles, r, w):
        bk, bt = self.bank()
        psb = bk[:].bitcast(BF16)
        n = len(src_tiles)
        for k, src in enumerate(src_tiles):
            self.pe(lambda e, k=k, src=src: e.transpose(out=psb[:, k * 128:(k + 1) * 128], in_=src, identity=self.ident[:]),
                    r=list(r) + [self.ident_t], w=[bt])
        self.copy(dst_view, psb[:, 0:n * 128].rearrange("p (k t) -> p k t", k=n), [bt], w)

    def to_xT(self, i):
        xb, xbt = self.xb_ring.next()
        self.act(lambda e: e.copy(out=xb[:], in_=self.x_tok[:, i, :]), r=[self.xtok_t[i]], w=[xbt])
        self.transpose_to(self.xT[:, :, i * 128:(i + 1) * 128], [xb[:, kc * 128:(kc + 1) * 128] for kc in range(KC)],
                          [xbt], [self.xT_t[i]])

    def load_ln(self, idx):
        gb = self.salloc([128, 2, D], F32)
        t = Tok("lnp")
        g, b = gb[:, 0, :], gb[:, 1, :]
        self.dma("sp", g, self.d["lng"][idx:idx + 1, :].to_broadcast([128, D]), w=[t])
        self.dma("sp", b, self.d["lnb"][idx:idx + 1, :].to_broadcast([128, D]), w=[t])
        return g, b, t

    def rstd_small(self, out, var, r, w, eps=EPS):
        n = out.shape[-1]
        self.pool(lambda e: e.tensor_scalar(out=out, in0=var, scalar1=eps, scalar2=None, op0=ALU.add), r=r, w=w)
        self.pool(lambda e: e.tensor_tensor(out=out, in0=out, in1=self.negh[:, 0:n], op=ALU.pow), r=list(w) + [self.negh_t], w=w)

    def layer_norm(self, i, lnp):
        g, b, gt = lnp
        xt = self.x_tok[:, i, :]
        xtok = self.xtok_t[i]
        stt, stk = self.st_ring.next()
        self.dve(lambda e: e.bn_stats(out=stt[:, 0:6], in_=self.x_tok[:, i, 0:512]), r=[xtok], w=[stk])
        self.dve(lambda e: e.bn_stats(out=stt[:, 6:12], in_=self.x_tok[:, i, 512:1024]), r=[xtok], w=[stk])
        self.dve(lambda e: e.bn_aggr(out=stt[:, 12:14], in_=stt[:, 0:12].rearrange("p (a b) -> p a b", a=2)), r=[stk], w=[stk])
        self.rstd_small(stt[:, 15:16], stt[:, 13:14], [stk], [stk])
        self.dve(lambda e: e.scalar_tensor_tensor(out=stt[:, 16:17], in0=stt[:, 12:13], scalar=-1.0, in1=stt[:, 15:16],
                                                  op0=ALU.mult, op1=ALU.mult), r=[stk], w=[stk])
        self.act(lambda e: e.activation(out=xt, in_=xt, func=AF.Identity, bias=stt[:, 16:17], scale=stt[:, 15:16]),
                 r=[stk, xtok], w=[xtok])
        self.pool(lambda e: e.tensor_tensor(out=xt, in0=xt, in1=g, op=ALU.mult), r=[xtok, gt], w=[xtok])
        self.dve(lambda e: e.tensor_tensor(out=xt, in0=xt, in1=b, op=ALU.add), r=[xtok, gt], w=[xtok])
        self.to_xT(i)

    def accum(self, i, half, ps, pst, first):
        dst = self.x_tok[:, i, half * 512:(half + 1) * 512]
        if first:
            self.dve(lambda e: e.scalar_tensor_tensor(out=dst, in0=dst, scalar=ALPHA, in1=ps, op0=ALU.mult, op1=ALU.add),
                     r=[pst, self.xtok_t[i]], w=[self.xtok_t[i]])
        else:
            self.dve(lambda e: e.tensor_tensor(out=dst, in0=dst, in1=ps, op=ALU.add),
                     r=[pst, self.xtok_t[i]], w=[self.xtok_t[i]])

    def out_proj(self, i, lhs_fn, nk, lhs_toks, wo, wot, first):
        for half in range(2):
            ps, pst = self.bank()
            for k in range(nk):
                self.mm(ps[:], lhs_fn(k), wo[:, k, half * 512:(half + 1) * 512], k == 0, k == nk - 1, list(lhs_toks) + [wot], [pst])
            self.accum(i, half, ps[:], pst, first)

    def stage_ln_only(self, idx):
        self.begin_stage()
        lnp = self.load_ln(idx)
        for i in range(NT):
            self.layer_norm(i, lnp)

    def stage_ffn(self, l):
        self.begin_stage()
        groups = [list(range(0, 6)), list(range(6, 12)), list(range(12, 17)), list(range(17, 22))]
        fwi = self.d["fwi"][l]
        fwo = self.d["fwo"][l]
        lnp = self.load_ln(l * 3 + 2)
        wi_r = self.ring(3, [128, KC, 256], BF16, "fwi")
        wo_r = self.ring(2, [128, 6, D], BF16, "fwo")
        hT = self.salloc([128, 6, T], BF16)
        hTt = [[Tok("hT") for _ in range(4)] for _ in range(6)]
        sg_r = self.ring(2, [128, 512], F32, "sg")
        for gi, js in enumerate(groups):
            wo, wot = wo_r.next()
            j0 = js[0]
            self.wload(wo[:, 0:len(js), :], fwo[j0 * 128:(j0 + len(js)) * 128, :].rearrange("(j p) n -> p j n", p=128), wot)
            for jl, j in enumerate(js):
                wi, wit = wi_r.next()
                self.wload(wi[:, :, 0:128], fwi[:, j * 128:(j + 1) * 128].rearrange("(k p) n -> p k n", p=128), wit)
                self.wload(wi[:, :, 128:256], fwi[:, DFF + j * 128:DFF + (j + 1) * 128].rearrange("(k p) n -> p k n", p=128), wit)
                for tb in range(4):
                    pg, pgt = self.bank()
                    pu, put = self.bank()
                    self.mmB(pg[:], pgt, wi, wit, 0, tb * 512, 512)
                    self.mmB(pu[:], put, wi, wit, 128, tb * 512, 512)
                    sg, sgt = sg_r.next()
                    self.act(lambda e, sg=sg, pg=pg: e.activation(out=sg, in_=pg[:], func=AF.Silu), r=[pgt], w=[sgt])
                    self.dve(lambda e, sg=sg, pu=pu, jl=jl, tb=tb: e.tensor_tensor(out=hT[:, jl, tb * 512:(tb + 1) * 512], in0=sg, in1=pu[:], op=ALU.mult),
                             r=[sgt, put], w=[hTt[jl][tb]])
            last = gi == len(groups) - 1
            for i in range(NT):
                self.out_proj(i, lambda k, i=i: hT[:, k, i * 128:(i + 1) * 128], len(js), [hTt[k][i // 4] for k in range(len(js))], wo, wot, first=(gi == 0))
                if last:
                    self.layer_norm(i, lnp)

    def stage_xattn(self, l):
        self.begin_stage()
        d = self.d
        SC = 1.0 / 16.0
        lnp = self.load_ln(l * 3 + 1)
        wq, wqt = self.salloc([128, KC, D], BF16), Tok("wq")
        wo, wot = self.salloc([128, KC, D], BF16), Tok("wo")
        kT, kTt = self.salloc([128, KC, MEM], BF16), [Tok("kT") for _ in range(KC)]
        vS, vSt = self.salloc([128, 2, D], BF16), [[Tok("vS") for _ in range(2)] for _ in range(2)]
        memb, membt = self.salloc([128, 2, D], BF16), Tok("memb")
        memT, memTt = self.salloc([128, KC, MEM], BF16), Tok("memT")
        wkv_r = self.ring(1, [128, KC, 512], BF16, "wkv")
        qT_r = Ring([(self.salloc([128, KC, 512], BF16), [Tok("qT") for _ in range(KC)])])
        p32_r = self.ring(2, [128, 4, 256], F32, "p32")
        pb_r = self.ring(2, [128, 4, 256], BF16, "pb")
        pT_r = self.ring(2, [128, 8, 128], BF16, "pT")
        oT_r = self.ring(2, [128, 8, 128], BF16, "oT")
        sm_r = self.ring(3, [128, 16], F32, "sm")
        self.wload(memb, d["mem"].rearrange("(m p) n -> p m n", p=128), membt)
        for mt in range(2):
            self.transpose_to(memT[:, :, mt * 128:(mt + 1) * 128], [memb[:, mt, kc * 128:(kc + 1) * 128] for kc in range(KC)],
                              [membt], [memTt])
        for c in range(4):
            wk, wkt = wkv_r.next()
            self.wload_w(wk, d["xwkv"][l], c * 512, 512, wkt)
            if c < 2:
                for cc in range(4):
                    fc = c * 4 + cc
                    ps, pst = self.bank()
                    for kc in range(KC):
                        self.mm(ps[:, 0:MEM], wk[:, kc, cc * 128:(cc + 1) * 128], memT[:, kc, :], kc == 0, kc == KC - 1, [wkt, memTt], [pst])
                    self.copy(kT[:, fc, :], ps[:, 0:MEM], [pst], [kTt[fc]])
            else:
                for mt in range(2):
                    ps, pst = self.bank()
                    for kc in range(KC):
                        self.mm(ps[:], memT[:, kc, mt * 128:(mt + 1) * 128], wk[:, kc, :], kc == 0, kc == KC - 1, [wkt, memTt], [pst])
                    self.copy(vS[:, mt, (c - 2) * 512:(c - 1) * 512], ps[:], [pst], [vSt[mt][c - 2]])
        self.wload_w(wq, d["xwq"][l], 0, D, wqt)
        self.wload_w(wo, d["xwo"][l], 0, D, wot)
        for tb in range(4):
            qT, qTt = qT_r.next()
            for c in range(KC):
                ps, pst = self.bank()
                self.mmB(ps[:], pst, wq, wqt, c * 128, tb * 512, 512)
                self.copy(qT[:, c, :], ps[:], [pst], [qTt[c]])
            for il in range(4):
                i = tb * 4 + il
                sA, sAt = self.bank()
                sB, sBt = self.bank()
                for h in range(4):
                    bk, bkt = (sA, sAt) if h < 2 else (sB, sBt)
                    for k2 in range(2):
                        self.mm(bk[:, (h % 2) * 256:(h % 2 + 1) * 256], qT[:, 2 * h + k2, il * 128:(il + 1) * 128], kT[:, 2 * h + k2, :],
                                k2 == 0, k2 == 1, [qTt[2 * h + k2], kTt[2 * h + k2]], [bkt])
                sm, smt = sm_r.next()
                self.dve(lambda e, sm=sm, sA=sA: e.tensor_reduce(out=sm[:, 0:2], in_=sA[:].rearrange("p (h m) -> p h m", h=2), axis=AX.X, op=ALU.max), r=[sAt], w=[smt])
                self.dve(lambda e, sm=sm, sB=sB: e.tensor_reduce(out=sm[:, 2:4], in_=sB[:].rearrange("p (h m) -> p h m", h=2), axis=AX.X, op=ALU.max), r=[sBt], w=[smt])
                self.dve(lambda e, sm=sm: e.tensor_scalar(out=sm[:, 4:8], in0=sm[:, 0:4], scalar1=-SC, scalar2=None, op0=ALU.mult), r=[smt], w=[smt])
                p32, p32t = p32_r.next()
                for h in range(4):
                    bk, bkt = (sA, sAt) if h < 2 else (sB, sBt)
                    self.act(lambda e, h=h, bk=bk, sm=sm, p32=p32: e.activation(out=p32[:, h, :], in_=bk[:, (h % 2) * 256:(h % 2 + 1) * 256], func=AF.Exp,
                                                                             bias=sm[:, 4 + h:5 + h], scale=SC, accum_out=sm[:, 8 + h:9 + h]),
                             r=[bkt, smt], w=[p32t, smt])
                self.dve(lambda e, sm=sm: e.reciprocal(out=sm[:, 12:16], in_=sm[:, 8:12]), r=[smt], w=[smt])
                pb, pbt = pb_r.next()
                for h in range(4):
                    if h % 2:
                        self.dve(lambda e, h=h, pb=pb, p32=p32, sm=sm: e.tensor_scalar(out=pb[:, h, :], in0=p32[:, h, :], scalar1=sm[:, 12 + h:13 + h], scalar2=None, op0=ALU.mult),
                                 r=[p32t, smt], w=[pbt])
                    else:
                        self.act(lambda e, h=h, pb=pb, p32=p32, sm=sm: e.activation(out=pb[:, h, :], in_=p32[:, h, :], func=AF.Copy, scale=sm[:, 12 + h:13 + h]),
                                 r=[p32t, smt], w=[pbt])
                pT, pTt = pT_r.next()
                self.transpose_to(pT, [pb[:, h, mt * 128:(mt + 1) * 128] for h in range(4) for mt in range(2)], [pbt], [pTt])
                oA, oAt = self.bank()
                oB, oBt = self.bank()
                oT, oTt = oT_r.next()
                for c in range(8):
                    bk, bkt = (oA, oAt) if c < 4 else (oB, oBt)
                    h = c // 2
                    for mt in range(2):
                        self.mm(bk[:, (c % 4) * 128:(c % 4 + 1) * 128], vS[:, mt, c * 128:(c + 1) * 128], pT[:, h * 2 + mt, :],
                                mt == 0, mt == 1, [vSt[mt][c // 4], pTt], [bkt])
                self.copy(oT[:, 0:4, :], oA[:].rearrange("p (c t) -> p c t", c=4), [oAt], [oTt])
                self.copy(oT[:, 4:8, :], oB[:].rearrange("p (c t) -> p c t", c=4), [oBt], [oTt])
                self.out_proj(i, lambda k, oT=oT: oT[:, k, :], KC, [oTt], wo, wot, first=True)
                self.layer_norm(i, lnp)

    def gelu(self, out, ps, pst, outt, scr_r, half=True):
        if GELU_NATIVE:
            self.act(lambda e: e.activation(out=out, in_=ps, func=AF.Gelu_apprx_tanh), r=[pst], w=[outt])
            return 1.0
        C0 = 0.7978845608028654
        C1 = 0.044715
        s, stk = scr_r.next()
        n = ps.shape[-1]
        sv = s[:, 0:n]
        self.act(lambda e: e.activation(out=sv, in_=ps, func=AF.Square), r=[pst], w=[stk])
        self.dve(lambda e: e.tensor_scalar(out=sv, in0=sv, scalar1=C1, scalar2=1.0, op0=ALU.mult, op1=ALU.add), r=[stk], w=[stk])
        self.dve(lambda e: e.tensor_tensor(out=sv, in0=sv, in1=ps, op=ALU.mult), r=[stk, pst], w=[stk])
        self.act(lambda e: e.activation(out=sv, in_=sv, func=AF.Tanh, scale=C0), r=[stk], w=[stk])
        self.dve(lambda e: e.scalar_tensor_tensor(out=out, in0=sv, scalar=1.0, in1=ps, op0=ALU.add, op1=ALU.mult), r=[stk, pst], w=[outt])
        return 0.5

    def stage_mix0(self):
        d = self.d
        self.begin_stage()
        wuv, wut, wvt = self.salloc([128, KC, 1024], BF16), Tok("wu"), Tok("wv")
        woA, woAt = self.salloc([128, 4, D], BF16), Tok("woA")
        ws32, ws32t = self.salloc([128, 4, 128], F32), Tok("ws32")
        wsb, wsbt = self.salloc([128, 4, 128], BF16), Tok("wsb")
        bsB, bsBt = self.salloc([128, 512], F32), Tok("bsB")
        lgB, lgBt = self.salloc([128, 512], F32), Tok("lgB")
        lbB, lbBt = self.salloc([128, 512], F32), Tok("lbB")
        uT_r = self.ring(2, [128, 4, 512], BF16, "uT")
        scr_r = self.ring(5, [128, 512], F32, "gscr")
        vg_r = self.ring(4, [128, 512], F32, "vg")
        vb_r = self.ring(4, [128, 512], BF16, "vb")
        ss_r = self.ring(4, [128, 512], F32, "ssum")
        ya_r = self.ring(4, [128, 4, 128], BF16, "yaT")
        st_r = self.ring(4, [128, 48], F32, "gst")
        self.wload_w(wuv[:, :, 0:512], d["evi"], 0, 512, wut)
        self.wload_w(wuv[:, :, 512:1024], d["evi"], 512, 512, wvt)
        self.wload(woA, d["evo"][0:512, :].rearrange("(g p) n -> p g n", p=128), woAt)
        self.dma("sp", ws32, d["awsT"], w=[ws32t])
        self.dma("sp", bsB, d["abs"].to_broadcast([128, 512]), w=[bsBt])
        self.dma("sp", lgB, d["alng"].to_broadcast([128, 512]), w=[lgBt])
        self.dma("sp", lbB, d["alnb"].to_broadcast([128, 512]), w=[lbBt])
        for g in range(4):
            self.dve(lambda e, g=g: e.tensor_tensor(out=wsb[:, g, :], in0=ws32[:, g, :], in1=self.cst[:, C_TRI:C_TRI + 128], op=ALU.mult),
                     r=[ws32t, self.cst_t], w=[wsbt])
        for tb in range(4):
            uT, uTt = uT_r.next()
            gf = 1.0
            for c in range(4):
                ps, pst = self.bank()
                self.mmB(ps[:], pst, wuv, wut, c * 128, tb * 512, 512)
                gf = self.gelu(uT[:, c, :], ps[:], pst, uTt, scr_r)
            for il in range(4):
                i = tb * 4 + il
                ps, pst = self.bank()
                self.mmA(ps[:], pst, i, wuv, wvt, 512, 512)
                vg, vgt = vg_r.next()
                gv = self.gelu(vg, ps[:], pst, vgt, scr_r)
                st, stt = st_r.next()
                for g in range(4):
                    self.dve(lambda e, g=g, st=st, vg=vg: e.bn_stats(out=st[:, g * 6:(g + 1) * 6], in_=vg[:, g * 128:(g + 1) * 128]), r=[vgt], w=[stt])
                for g in range(4):
                    self.dve(lambda e, g=g, st=st: e.bn_aggr(out=st[:, 24 + g * 2:26 + g * 2], in_=st[:, g * 6:(g + 1) * 6]), r=[stt], w=[stt])
                mv = st[:, 24:32].rearrange("p (g two) -> p g two", two=2)
                self.pool(lambda e, st=st, mv=mv: e.tensor_scalar(out=st[:, 32:36], in0=mv[:, :, 1], scalar1=gv * gv, scalar2=EPS, op0=ALU.mult, op1=ALU.add), r=[stt], w=[stt])
                self.pool(lambda e, st=st: e.tensor_tensor(out=st[:, 32:36], in0=st[:, 32:36], in1=self.negh[:, 0:4], op=ALU.pow), r=[stt, self.negh_t], w=[stt])
                self.dve(lambda e, st=st: e.tensor_scalar(out=st[:, 36:40], in0=st[:, 32:36], scalar1=gv, scalar2=None, op0=ALU.mult), r=[stt], w=[stt])
                self.dve(lambda e, st=st, mv=mv: e.scalar_tensor_tensor(out=st[:, 40:44], in0=mv[:, :, 0], scalar=-1.0, in1=st[:, 36:40], op0=ALU.mult, op1=ALU.mult), r=[stt], w=[stt])
                for g in range(4):
                    self.act(lambda e, g=g, st=st, vg=vg: e.activation(out=vg[:, g * 128:(g + 1) * 128], in_=vg[:, g * 128:(g + 1) * 128], func=AF.Identity,
                                                                      bias=st[:, 40 + g:41 + g], scale=st[:, 36 + g:37 + g]), r=[stt, vgt], w=[vgt])
                self.dve(lambda e, vg=vg: e.tensor_tensor(out=vg, in0=vg, in1=lgB, op=ALU.mult), r=[vgt, lgBt], w=[vgt])
                vb, vbt = vb_r.next()
                self.pool(lambda e, vg=vg, vb=vb: e.tensor_tensor(out=vb, in0=vg, in1=lbB, op=ALU.add), r=[vgt, lbBt], w=[vbt])
                pss, psst = self.bank()
                for g in range(4):
                    self.mm(pss[:, g * 128:(g + 1) * 128], vb[:, g * 128:(g + 1) * 128], wsb[:, g, :], True, True, [vbt, wsbt], [psst])
                ss, sst = ss_r.next()
                self.dve(lambda e, ss=ss, pss=pss: e.tensor_tensor(out=ss, in0=pss[:], in1=bsB, op=ALU.add), r=[psst, bsBt], w=[sst])
                ya, yat = ya_r.next()
                self.dve(lambda e, ya=ya, uT=uT, ss=ss, il=il: e.scalar_tensor_tensor(out=ya, in0=uT[:, :, il * 128:(il + 1) * 128], scalar=gf,
                                                                                   in1=ss.rearrange("p (g t) -> p g t", g=4), op0=ALU.mult, op1=ALU.mult),
                         r=[uTt, sst], w=[yat])
                self.out_proj(i, lambda k, ya=ya: ya[:, k, :], 4, [yat], woA, woAt, first=True)

        self.begin_stage()
        NB = 256
        lnp = self.load_ln(0)
        wB, wBt = self.salloc([128, KC, 2048], BF16), [Tok("wBq"), Tok("wBf"), Tok("wBi"), Tok("wBg")]
        woB, woBt = self.salloc([128, 4, D], BF16), Tok("woB")
        sm, smt = self.salloc([128, 32], F32), Tok("hsm")
        S, St = self.salloc([128, 4, 128], F32), Tok("S")
        Sb_r = self.ring(3, [128, 4, 128], BF16, "Sb")
        m2x4, m2t = self.salloc([128, 512], BF16), Tok("m2x4")
        ones, onest = self.salloc([128, 128], BF16), Tok("ones")
        f1 = self.ring(1, [128, 4, NB], F32, "f1")
        f2 = self.ring(1, [128, 4, NB], F32, "f2")
        f3 = self.ring(1, [128, 4, NB], F32, "f3")
        E_r = self.ring(2, [128, 4, 4], F32, "Elast")
        qd_r = self.ring(1, [128, 4, NB], BF16, "qdT")
        kd_r = self.ring(1, [128, 4, NB], BF16, "kdT")
        gs_r = self.ring(1, [128, 4, NB], BF16, "gsT")
        vi_r = self.ring(1, [128, 2, 512], BF16, "vi")
        kt_r = self.ring(1, [128, 2, 512], BF16, "kdtok")
        am_r = self.ring(3, [128, 512], BF16, "am")
        tmp_r = self.ring(2, [128, 512], F32, "stmp")
        sq_r = self.ring(2, [128, 512], BF16, "sq")
        r_r = self.ring(2, [128, 512], F32, "rr")
        t1_r = self.ring(2, [128, 512], F32, "t1")
        yb_r = self.ring(3, [128, 4, 128], BF16, "ybT")
        for c in range(4):
            self.wload_w(wB[:, :, c * 512:(c + 1) * 512], d["evi"], 1024 + c * 512, 512, wBt[c])
        self.wload(woB, d["evo"][512:1024, :].rearrange("(g p) n -> p g n", p=128), woBt)
        self.dma("sp", sm[:, 0:8], d["lbl"].rearrange("p l h -> p (l h)"), w=[smt])
        self.dma("sp", sm[:, 20:24], d["bng"], w=[smt])
        self.dve(lambda e: e.tensor_tensor(out=sm[:, 8:12], in0=sm[:, 0:4], in1=sm[:, 4:8], op=ALU.subtract), r=[smt], w=[smt])
        self.act(lambda e: e.activation(out=sm[:, 12:16], in_=sm[:, 8:12], func=AF.Sigmoid), r=[smt], w=[smt])
        self.dve(lambda e: e.tensor_scalar(out=sm[:, 16:20], in0=sm[:, 12:16], scalar1=-1.0, scalar2=1.0, op0=ALU.mult, op1=ALU.add), r=[smt], w=[smt])
        self.pool(lambda e: e.memset(S.rearrange("p h v -> p (h v)"), 0.0), w=[St])
        Sb, Sbt = Sb_r.next()
        self.pool(lambda e, Sb=Sb: e.memset(Sb.rearrange("p h v -> p (h v)"), 0.0), w=[Sbt])
        self.pool(lambda e: e.memset(ones, 1.0), w=[onest])
        epsc, epst = self.salloc([128, 8], F32), Tok("eps")
        self.pool(lambda e: e.memset(epsc, EPS), w=[epst])
        for h in range(4):
            self.dve(lambda e, h=h: e.tensor_copy(out=m2x4[:, h * 128:(h + 1) * 128], in_=self.cst[:, C_M2:C_M2 + 128]), r=[self.cst_t], w=[m2t])
        maskc = self.cst[:, C_MC:C_MC + NB]
        for blk in range(T // NB):
            tok0 = blk * NB
            s1, s1t = f1.next()
            s2, s2t = f2.next()
            s3, s3t = f3.next()
            E, Et = E_r.next()
            qd, qdt = qd_r.next()
            kd, kdt = kd_r.next()
            gs, gst = gs_r.next()
            qf = []
            for h in range(4):
                b1, b1t = self.bank()
                self.mmB(b1[:, 0:NB], b1t, wB, wBt[0], h * 128, tok0, NB)
                self.mmB(b1[:, NB:2 * NB], b1t, wB, wBt[1], 512 + h * 128, tok0, NB)
                qf.append((b1, b1t))
                self.act(lambda e, h=h, b1=b1, s1=s1: e.activation(out=s1[:, h, :], in_=b1[:, NB:2 * NB], func=AF.Sigmoid), r=[b1t], w=[s1t])
            for h in range(4):
                self.dve(lambda e, h=h, s1=s1: e.tensor_scalar(out=s1[:, h, :], in0=s1[:, h, :], scalar1=sm[:, 16 + h:17 + h], scalar2=sm[:, 12 + h:13 + h],
                                                              op0=ALU.mult, op1=ALU.add), r=[s1t, smt], w=[s1t])
            self.act(lambda e, s1=s1, s2=s2: e.activation(out=s2, in_=s1, func=AF.Ln), r=[s1t], w=[s2t])
            for h in range(4):
                self.dve(lambda e, h=h, s2=s2, s3=s3: e.tensor_tensor_scan(out=s3[:, h, :], data0=maskc, data1=s2[:, h, :], initial=0.0, op0=ALU.mult, op1=ALU.add),
                         r=[s2t, self.cst_t], w=[s3t])
            self.pool(lambda e, s1=s1: e.tensor_scalar(out=s1, in0=s1, scalar1=-1.0, scalar2=1.0, op0=ALU.mult, op1=ALU.add), r=[s1t], w=[s1t])
            self.act(lambda e, s2=s2, s3=s3: e.activation(out=s2, in_=s3, func=AF.Exp, scale=-1.0), r=[s3t, s2t], w=[s2t])
            self.act(lambda e, s3=s3: e.activation(out=s3, in_=s3, func=AF.Exp), r=[s3t], w=[s3t])
            self.dve(lambda e, E=E, s3=s3: e.tensor_copy(out=E, in_=s3[:, :, 63:NB:64]), r=[s3t], w=[Et])
            for h in range(4):
                b1, b1t = qf[h]
                self.dve(lambda e, h=h, b1=b1, s3=s3, qd=qd: e.tensor_tensor(out=qd[:, h, :], in0=b1[:, 0:NB], in1=s3[:, h, :], op=ALU.mult), r=[b1t, s3t], w=[qdt])
            self.pool(lambda e, s1=s1, s2=s2, kd=kd: e.tensor_tensor(out=kd, in0=s1, in1=s2, op=ALU.mult), r=[s1t, s2t], w=[kdt])
            for h in range(0, 4, 2):
                b2, b2t = self.bank()
                self.mmB(b2[:, 0:NB], b2t, wB, wBt[3], 1536 + h * 128, tok0, NB)
                self.mmB(b2[:, NB:2 * NB], b2t, wB, wBt[3], 1536 + (h + 1) * 128, tok0, NB)
                self.act(lambda e, h=h, b2=b2, gs=gs: e.activation(out=gs[:, h:h + 2, :], in_=b2[:].rearrange("p (a t) -> p a t", a=2), func=AF.Silu), r=[b2t], w=[gst])
            vi, vit = vi_r.next()
            ktk, ktkt = kt_r.next()
            for il in range(2):
                b3, b3t = self.bank()
                self.mmA(b3[:], b3t, blk * 2 + il, wB, wBt[2], 1024, 512)
                self.act(lambda e, il=il, b3=b3, vi=vi: e.activation(out=vi[:, il, :], in_=b3[:], func=AF.Silu), r=[b3t], w=[vit])
                self.transpose_to(ktk[:, il, :].rearrange("p (h k) -> p h k", h=4), [kd[:, h, il * 128:(il + 1) * 128] for h in range(4)], [kdt], [ktkt])
            for il in range(2):
                i = blk * 2 + il
                tc0 = il * 128
                U = [self.bank(), self.bank()]
                for c in range(2):
                    for h in range(4):
                        self.mm(U[c][0][:, h * 128:(h + 1) * 128], ktk[c * 64:(c + 1) * 64, il, h * 128:(h + 1) * 128],
                                vi[c * 64:(c + 1) * 64, il, h * 128:(h + 1) * 128], True, True, [ktkt, vit], [U[c][1]])
                A, At = self.bank()
                for h in range(4):
                    self.mm(A[:, h * 128:(h + 1) * 128], kd[:, h, tc0:tc0 + 128], qd[:, h, tc0:tc0 + 128], True, True, [kdt, qdt], [At])
                am, amt = am_r.next()
                self.dve(lambda e, am=am, A=A: e.tensor_tensor(out=am, in0=A[:], in1=m2x4, op=ALU.mult), r=[At, m2t], w=[amt])
                Sbs = [(Sb, Sbt)]
                for c in range(2):
                    tmp, tmpt = tmp_r.next()
                    Uc, Uct = U[c]
                    self.dve(lambda e, tmp=tmp, Uc=Uc: e.tensor_tensor(out=tmp, in0=Uc[:], in1=S.rearrange("p h v -> p (h v)"), op=ALU.add), r=[Uct, St], w=[tmpt])
                    Sb, Sbt = Sb_r.next()
                    col = il * 2 + c
                    for h in range(4):
                        self.dve(lambda e, h=h, tmp=tmp, E=E, col=col: e.tensor_scalar(out=S[:, h, :], in0=tmp[:, h * 128:(h + 1) * 128], scalar1=E[:, h, col:col + 1],
                                                                                    scalar2=None, op0=ALU.mult), r=[tmpt, Et], w=[St])
                        self.act(lambda e, h=h, tmp=tmp, E=E, col=col, Sb=Sb: e.activation(out=Sb[:, h, :], in_=tmp[:, h * 128:(h + 1) * 128], func=AF.Copy,
                                                                                        scale=E[:, h, col:col + 1]), r=[tmpt, Et], w=[Sbt])
                    Sbs.append((Sb, Sbt))
                O, Ot = self.bank()
                for h in range(4):
                    self.mm(O[:, h * 128:(h + 1) * 128], vi[:, il, h * 128:(h + 1) * 128], am[:, h * 128:(h + 1) * 128], True, True, [vit, amt], [Ot])
                    for c in range(2):
                        Sc, Sct = Sbs[c]
                        self.mm(O[:, h * 128 + c * 64:h * 128 + (c + 1) * 64], Sc[:, h, :], qd[:, h, tc0 + c * 64:tc0 + (c + 1) * 64], False, True,
                                [Sct, qdt], [Ot], skip_group_check=True)
                sq, sqt = sq_r.next()
                self.act(lambda e, sq=sq, O=O: e.activation(out=sq, in_=O[:], func=AF.Square), r=[Ot], w=[sqt])
                Q, Qt = self.bank()
                self.mm(Q[:], ones, sq, True, True, [onest, sqt], [Qt])
                rr, rrt = r_r.next()
                self.act(lambda e, rr=rr, Q=Q: e.activation(out=rr, in_=Q[:], func=AF.Ln, bias=epsc[:, 0:1], scale=1.0 / 128.0), r=[Qt, epst], w=[rrt])
                self.act(lambda e, rr=rr: e.activation(out=rr, in_=rr, func=AF.Exp, scale=-0.5), r=[rrt], w=[rrt])
                t1, t1t = t1_r.next()
                self.dve(lambda e, t1=t1, O=O, rr=rr: e.tensor_tensor(out=t1, in0=O[:], in1=rr, op=ALU.mult), r=[Ot, rrt], w=[t1t])
                yb, ybt = yb_r.next()
                for h in range(4):
                    self.dve(lambda e, h=h, yb=yb, t1=t1, gs=gs, tc0=tc0: e.scalar_tensor_tensor(out=yb[:, h, :], in0=t1[:, h * 128:(h + 1) * 128], scalar=sm[:, 20 + h:21 + h],
                                                                                              in1=gs[:, h, tc0:tc0 + 128], op0=ALU.mult, op1=ALU.mult),
                             r=[t1t, gst, smt], w=[ybt])
                self.out_proj(i, lambda k, yb=yb: yb[:, k, :], 4, [ybt], woB, woBt, first=False)
                self.layer_norm(i, lnp)

    def stage_moba(self):
        self.nbank = 6
        for G in range(2):
            self.moba_group(G)
        self.nbank = 8

    def moba_group(self, G):
        d = self.d
        if True:
            self.begin_stage()
            lnp = self.load_ln(3) if G == 1 else None
            kT, kTt = self.salloc([128, 4, T], BF16), [[Tok("kT") for _ in range(4)] for _ in range(4)]
            va, vat = self.salloc([128, NT, 8, 65], BF16), [Tok("va") for _ in range(NT)]
            wq, wqt = self.salloc([128, KC, 512], BF16), Tok("wq")
            woG, woGt = self.salloc([128, 4, D], BF16), Tok("woG")
            wk_r = self.ring(2, [128, KC, 256], BF16, "wk")
            qz_r = self.ring(1, [128, 8, 512], BF16, "qz")
            R_r = self.ring(1, [128, 512], BF16, "R")
            Rc, Rct = self.salloc([128, 512], BF16), Tok("Rc")
            pT_r = self.ring(3, [128, 512], BF16, "pT")
            otok, otokt = self.salloc([128, 4, 512], BF16), [Tok("otok") for _ in range(8)]
            oTb, oTbt = self.salloc([128, 4, 512], BF16), Tok("oTb")
            lcol, lcolt = self.salloc([128, 64], BF16), Tok("lcol")
            tri, trit = self.salloc([128, 128], BF16), Tok("tri")
            SBb, SBbt = self.salloc([128, 4, 128], BF16), Tok("SBb")
            SB_r = self.ring(4, [128, 128], BF16, "SB")
            km32, km32t = self.salloc([128, 4, 8], F32), Tok("km32")
            kmh, kmht = self.salloc([128, 4, 8], BF16), Tok("kmh")
            kml, kmlt = self.salloc([128, 4, 8], BF16), Tok("kml")
            aff_r = self.ring(2, [128, 8, 8], F32, "aff")
            cmp_, cmpt = self.salloc([128, 8, 8, 8], F32), Tok("cmp")
            cnt, cntt = self.salloc([128, 8, 8], F32), Tok("cnt")
            sm_r = self.ring(2, [128, 8], F32, "msm")
            self.dve(lambda e: e.tensor_copy(out=lcol, in_=self.cst[:, C_LCOL + G * 64:C_LCOL + (G + 1) * 64]), r=[self.cst_t], w=[lcolt])
            self.dve(lambda e: e.tensor_copy(out=tri, in_=self.cst[:, C_TRI:C_TRI + 128]), r=[self.cst_t], w=[trit])
            self.dve(lambda e: e.tensor_copy(out=SBb, in_=self.cst[:, C_SBB:C_SBB + 512].rearrange("p (j c) -> p j c", j=4)), r=[self.cst_t], w=[SBbt])
            self.transpose_to(Rc.rearrange("p (j t) -> p j t", j=4), [SBb[:, j, :] for j in range(4)], [SBbt], [Rct])
            if G == 0:
                self.dump(Rc[:, 0:128], Rct)
                self.dump(SBb[:, 0, :], SBbt)
            for s_ in range(qz_r.items.__len__()):
                qz0, qz0t = qz_r.items[s_]
                self.pool(lambda e, qz0=qz0: e.memset(qz0.rearrange("p h t -> p (h t)"), 0.0), w=[qz0t])
            self.pool(lambda e: e.memset(va.rearrange("p a b c -> p (a b c)"), 1.0), w=vat)
            self.wload_w(wq, d["oqkv"], G * 512, 512, wqt)
            self.wload(woG, d["owo"][G * 512:(G + 1) * 512, :].rearrange("(g p) n -> p g n", p=128), woGt)
            for c2 in range(2):
                wk, wkt = wk_r.next()
                self.wload_w(wk, d["oqkv"], 1024 + G * 512 + c2 * 256, 256, wkt, step=256)
                for pp in range(2):
                    p = c2 * 2 + pp
                    for tb in range(4):
                        ps, pst = self.bank()
                        self.mmB(ps[:], pst, wk, wkt, pp * 128, tb * 512, 512)
                        self.copy(kT[:, p, tb * 512:(tb + 1) * 512], ps[:], [pst], [kTt[p][tb]])
            for c2 in range(2):
                wk, wkt = wk_r.next()
                self.wload_w(wk, d["oqkv"], 2048 + G * 512 + c2 * 256, 256, wkt, step=256)
                for i in range(NT):
                    ps, pst = self.bank()
                    self.mmA(ps[:, 0:256], pst, i, wk, wkt, 0, 256)
                    self.copy(va[:, i, c2 * 4:(c2 + 1) * 4, 0:64], ps[:, 0:256].rearrange("p (h c) -> p h c", h=4), [pst], [vat[i]])
            for p in range(4):
                self.dve(lambda e, p=p: e.tensor_reduce(out=km32[:, p, :], in_=kT[:, p, :].rearrange("p (b t) -> p b t", b=8), axis=AX.X, op=ALU.add), r=kTt[p], w=[km32t])
            self.dve(lambda e: e.tensor_scalar(out=km32, in0=km32, scalar1=1.0 / 256.0, scalar2=None, op0=ALU.mult), r=[km32t], w=[km32t])
            self.dve(lambda e: e.tensor_copy(out=kmh, in_=km32), r=[km32t], w=[kmht])
            self.dve(lambda e: e.tensor_tensor(out=km32, in0=km32, in1=kmh, op=ALU.subtract), r=[km32t, kmht], w=[km32t])
            self.dve(lambda e: e.tensor_copy(out=kml, in_=km32), r=[km32t], w=[kmlt])

            for qc in range(4):
                qz, qzt = qz_r.next()
                for p in range(4):
                    ps, pst = self.bank()
                    self.mmB(ps[:], pst, wq, wqt, p * 128, qc * 512, 512)
                    self.act(lambda e, p=p, ps=ps, qz=qz: e.copy(out=qz[0:64, 2 * p, :], in_=ps[0:64, :]), r=[pst], w=[qzt])
                    self.dve(lambda e, p=p, ps=ps, qz=qz: e.tensor_copy(out=qz[64:128, 2 * p + 1, :], in_=ps[64:128, :]), r=[pst], w=[qzt])
                if qc < 2:
                    R, Rt = Rc, Rct
                else:
                    R, Rt = R_r.next()
                    sbs = []
                    for j in range(4):
                        qb = (qc * 4 + j) // 2
                        ab, abt = self.bank()
                        for hl in range(8):
                            self.mm(ab[:, hl * 8:(hl + 1) * 8], qz[:, hl, j * 128:(j + 1) * 128], kmh[:, hl // 2, :], True, False, [qzt, kmht], [abt])
                            self.mm(ab[:, hl * 8:(hl + 1) * 8], qz[:, hl, j * 128:(j + 1) * 128], kml[:, hl // 2, :], False, True, [qzt, kmlt], [abt])
                        aff, afft = aff_r.next()
                        self.dve(lambda e, aff=aff, ab=ab: e.tensor_copy(out=aff, in_=ab[:, 0:64].rearrange("p (h k) -> p h k", h=8)), r=[abt], w=[afft])
                        self.dve(lambda e, aff=aff, qb=qb: e.tensor_tensor(out=cmp_[:, :, 0:qb, 0:qb],
                                                                        in0=aff[:, :, 0:qb].unsqueeze(2).to_broadcast([128, 8, qb, qb]),
                                                                        in1=aff[:, :, 0:qb].unsqueeze(3).to_broadcast([128, 8, qb, qb]), op=ALU.is_gt),
                                 r=[afft], w=[cmpt])
                        self.dve(lambda e, qb=qb: e.tensor_reduce(out=cnt[:, :, 0:qb], in_=cmp_[:, :, 0:qb, 0:qb], axis=AX.X, op=ALU.add), r=[cmpt], w=[cntt])
                        SB, SBt = SB_r.next()
                        self.pool(lambda e, SB=SB, j=j: e.tensor_copy(out=SB, in_=SBb[:, j, :]), r=[SBbt], w=[SBt])
                        self.dve(lambda e, SB=SB, qb=qb: e.tensor_scalar(out=SB.rearrange("p (h s) -> p h s", h=8)[:, :, 0:qb], in0=cnt[:, :, 0:qb],
                                                                      scalar1=3.0, scalar2=-32768.0, op0=ALU.is_ge, op1=ALU.mult), r=[cntt, SBt], w=[SBt])
                        sbs.append((SB, SBt))
                    self.transpose_to(R.rearrange("p (j t) -> p j t", j=4), [s[0] for s in sbs], [s[1] for s in sbs], [Rt])
                nkt = 4 * qc + 4
                for hl in range(8):
                    h = G * 8 + hl
                    O, Ot = self.obank()
                    first = True
                    for kt in range(nkt):
                        j0 = max(0, kt - 4 * qc)
                        c0 = j0 * 128
                        kb = kt // 2
                        S_, S_t = self.bank()
                        self.mm(S_[:, c0:512], kT[:, hl // 2, kt * 128:(kt + 1) * 128], qz[:, hl, c0:512], True, False, [kTt[hl // 2][kt // 4], qzt], [S_t])
                        self.mm(S_[:, c0:512], lcol[:, hl * 8 + kb:hl * 8 + kb + 1].to_broadcast([128, 128]), R[:, c0:512], False, True, [lcolt, Rt], [S_t])
                        if kt >= 4 * qc:
                            self.dve(lambda e, S_=S_, c0=c0: e.tensor_tensor(out=S_[:, c0:c0 + 128], in0=S_[:, c0:c0 + 128], in1=self.cst[:, C_NTRI:C_NTRI + 128], op=ALU.add),
                                     r=[S_t, self.cst_t], w=[S_t])
                        pT, pTt = pT_r.next()
                        bcol = C_ALB + h * 16 + (kt - 4 * qc) + 12
                        self.act(lambda e, pT=pT, S_=S_, c0=c0, bcol=bcol: e.activation(out=pT[:, c0:512], in_=S_[:, c0:512], func=AF.Exp,
                                                                                     bias=self.cst[:, bcol:bcol + 1], scale=0.125),
                                 r=[S_t, self.cst_t], w=[pTt])
                        if G == 0 and qc == 0 and hl == 0 and kt == 0:
                            self.dump(S_[:, 0:512], S_t, psum=True)
                            self.dump(pT, pTt)
                            self.dump(va[:, 0, 0, :], vat[0])
                            self.dump(qz[:, 0, :], qzt)
                            self.dump(kT[:, 0, 0:128], kTt[0][0])
                            self.dump(R[:, 0:128], Rt)
                            self.dump(va[:, 0, :, :].rearrange("p h c -> p (h c)"), vat[0])
                        for j in range(j0, 4):
                            self.mm(O[:, j * 65:(j + 1) * 65], pT[:, j * 128:(j + 1) * 128], va[:, kt, hl, :], first, kt == 4 * qc + j,
                                    [pTt, vat[kt]], [Ot], skip_group_check=True)
                            first = False
                    if G == 0 and qc == 0 and hl == 0:
                        self.dump(O[:, 0:260], Ot, psum=True)
                    sm, smt = sm_r.next()
                    self.dve(lambda e, sm=sm, O=O: e.reciprocal(out=sm[:, 0:4], in_=O[:, 0:260].rearrange("p (j c) -> p j c", j=4)[:, :, 64]), r=[Ot], w=[smt])
                    for j in range(4):
                        if j % 2:
                            self.dve(lambda e, sm=sm, O=O, j=j, hl=hl: e.tensor_scalar(out=otok[:, j, hl * 64:(hl + 1) * 64], in0=O[:, j * 65:j * 65 + 64],
                                                                                    scalar1=sm[:, j:j + 1], scalar2=None, op0=ALU.mult), r=[Ot, smt], w=[otokt[hl]])
                        else:
                            self.act(lambda e, sm=sm, O=O, j=j, hl=hl: e.activation(out=otok[:, j, hl * 64:(hl + 1) * 64], in_=O[:, j * 65:j * 65 + 64],
                                                                                 func=AF.Copy, scale=sm[:, j:j + 1]), r=[Ot, smt], w=[otokt[hl]])
                if G == 0 and qc == 0:
                    self.dump(otok.rearrange("p j c -> p (j c)"), otokt[0])
                for j in range(4):
                    i = qc * 4 + j
                    self.transpose_to(oTb[:, :, j * 128:(j + 1) * 128], [otok[:, j, c * 128:(c + 1) * 128] for c in range(4)], otokt, [oTbt])
                    self.out_proj(i, lambda k, j=j: oTb[:, k, j * 128:(j + 1) * 128], 4, [oTbt], woG, woGt, first=(G == 0))
                    if G == 1:
                        self.layer_norm(i, lnp)


def build_program(stages):
    nc = bass.Bass("TRN2", target_bir_lowering=False)
    k = K(nc, stages)
    k.build()
    return nc


FULL_STAGES = [("mix0",), ("xattn", 0), ("ffn", 0), ("moba",), ("xattn", 1), ("ffn", 1)]


def make_in_maps(inputs, ncores=NCORES):
    f = lambda a: np.ascontiguousarray(np.asarray(a, dtype=np.float32))
    x = f(inputs["x"])
    mem = f(inputs["mem"])
    shared = dict(
        ln_g=f(inputs["ln_g"]).reshape(6, D),
        ln_b=f(inputs["ln_b"]).reshape(6, D),
        x_wq=f(inputs["x_wq"]), x_wkv=f(inputs["x_wkv"]), x_wo=f(inputs["x_wo"]),
        ffn_w_in=f(inputs["ffn_w_in"]), ffn_w_out=f(inputs["ffn_w_out"]),
        ev_w_in=f(inputs["ev_w_in"])[0], ev_w_out=f(inputs["ev_w_out"])[0],
        a_wsT=f(np.transpose(np.asarray(inputs["a_ws"])[0], (2, 0, 1))),
        a_bs=f(inputs["a_bs"]).reshape(1, 512),
        a_ln_g=f(inputs["a_ln_g"]).reshape(1, 512),
        a_ln_b=f(inputs["a_ln_b"]).reshape(1, 512),
        b_norm_gT=f(np.asarray(inputs["b_norm_g"]).reshape(4, 128).T),
        lb_logitsT=f(np.transpose(np.asarray(inputs["hgrn_lb_logits"]).reshape(2, 4, 128), (2, 0, 1))),
        od_w_qkv=f(inputs["od_w_qkv"])[0], od_w_out=f(inputs["od_w_out"])[0],
        consts=make_consts(),
    )
    maps = []
    for c in range(ncores):
        m = dict(shared)
        m["x"] = x[c]
        m["mem"] = mem[c]
        maps.append(m)
    return maps


def kernel(**inputs):
    nc = build_program(FULL_STAGES)
    in_maps = make_in_maps(inputs)
    res = run_bass_kernel_spmd(nc, in_maps, core_ids=list(range(NCORES)))
    return np.stack([np.asarray(r["out"], dtype=np.float32) for r in res.results], axis=0)
```

```python
import math
import os
from contextlib import ExitStack

import numpy as np
import concourse.bass as bass
import concourse.mybir as mybir
from concourse.bass_utils import run_bass_kernel_spmd

F32 = mybir.dt.float32
BF16 = mybir.dt.bfloat16
AF = mybir.ActivationFunctionType
ALU = mybir.AluOpType
AX = mybir.AxisListType

T = 2048
D = 1024
NT = T // 128
KC = D // 128
MEM = 256
DFF = 2816
NJ = DFF // 128
ALPHA = 4.0 ** 0.25
EPS = 1e-5
NCORES = 8


ARENA_REG = {"range": None, "toks": []}


class Tok:
    __slots__ = ("name", "w", "r")

    def __init__(self, name=""):
        self.name = name
        self.w = None
        self.r = []
        rng = ARENA_REG["range"]
        if rng is not None:
            st, lo, hi = rng
            inh = []
            for (st2, lo2, hi2, t2) in ARENA_REG["toks"]:
                if st2 < st and lo2 < hi and lo < hi2:
                    if t2.w is not None:
                        inh.append(t2.w)
                    inh.extend(t2.r)
            seen = set()
            for o in inh:
                if id(o) not in seen:
                    seen.add(id(o))
                    self.r.append(o)
            ARENA_REG["toks"].append((st, lo, hi, self))


class Op:
    __slots__ = ("eng", "fn", "deps", "sig", "is_dma", "need", "idx", "guard", "cost", "stage", "nleft", "users", "ready", "fin", "raw")

    def __init__(self, eng, fn, is_dma, cost):
        self.eng = eng
        self.fn = fn
        self.deps = []
        self.sig = None
        self.is_dma = is_dma
        self.need = False
        self.guard = None
        self.cost = cost
        self.users = []


CS = float(os.environ.get("CS", "1.6"))
CS2 = float(os.environ.get("CS2", "1.3"))


class _Probe:
    def __init__(self):
        self.n = None
        self.name = None

    def __getattr__(self, name):
        def f(*a, **k):
            out = k.get("out")
            if out is None and a:
                out = a[0]
            try:
                n = 1
                for s in out.shape[1:]:
                    n *= s
                self.n = n
            except Exception:
                self.n = None
            self.name = name
            return self
        return f


DEFAULT_COST = {"pe": 0.06, "act": 0.45 * CS, "dve": 0.35 * CS, "pool": 0.6 * CS, "sp": 0.1}
SCHED_WINDOW = 160
XLAT = float(os.environ.get('XLAT', '1.0'))
SLAT = float(os.environ.get('SLAT', '0.05'))
SAME_ENGINE_NOSYNC = tuple(os.environ.get('NOSYNC', 'pe').split(','))
RAW_ONLY = not os.environ.get('ALL_SYNC')


class Prog:
    ENGS = ("pe", "act", "dve", "pool", "sp")

    def __init__(self):
        self.all = []
        self.stage = 0

    def fence(self):
        pass

    def op(self, eng, fn, reads=(), writes=(), dma=False, cost=None):
        if cost is None:
            pr = _Probe()
            try:
                fn(pr)
            except Exception:
                pr.n = None
            n = pr.n
            if n is None:
                cost = 3.0 if dma else DEFAULT_COST[eng]
            elif dma:
                cost = 2.0 + n * 128 * 4 / 150e3
            elif eng == "pe":
                cost = max(n, 64) / 2400.0 + 0.03
            elif eng == "act":
                cost = (0.22 + n * 0.00085) * CS2
            elif eng == "dve":
                cost = (0.12 + n * 0.0011 * (2.0 if pr.name == "tensor_tensor_scan" else 1.0)) * CS2
            else:
                cost = (0.35 + n * 0.0022) * CS2
        o = Op(eng, fn, dma, cost)
        o.stage = self.stage
        o.idx = len(self.all)
        deps = []
        raw = set()
        for t in reads:
            if t.w is not None:
                deps.append(t.w)
                raw.add(id(t.w))
        o.raw = raw
        for t in writes:
            if t.w is not None:
                deps.append(t.w)
            deps.extend(t.r)
        seen = set()
        for d in deps:
            if id(d) in seen or d is o:
                continue
            seen.add(id(d))
            o.deps.append(d)
            d.users.append(o)
        for t in reads:
            t.r.append(o)
        for t in writes:
            t.w = o
            t.r = []
        self.all.append(o)
        return o

    def schedule(self):
        order = {e: [] for e in self.ENGS}
        nst = self.stage + 1
        stages = [[] for _ in range(nst)]
        for o in self.all:
            stages[o.stage].append(o)
        t_stage = 0.0
        for ops in stages:
            if not ops:
                continue
            inst = set(id(o) for o in ops)
            pend = {e: [] for e in self.ENGS}
            for o in ops:
                o.nleft = sum(1 for d in o.deps if id(d) in inst)
                o.ready = t_stage
                pend[o.eng].append(o)
            free = {e: t_stage for e in self.ENGS}
            n = len(ops)
            tmax = t_stage
            while n:
                best = None
                for e in self.ENGS:
                    lst = pend[e]
                    cnt = 0
                    for o in lst:
                        if o.nleft == 0:
                            st = o.ready if o.ready > free[e] else free[e]
                            if best is None or st < best[0] - 1e-9:
                                best = (st, o)
                        cnt += 1
                        if cnt >= SCHED_WINDOW:
                            break
                st, o = best
                e = o.eng
                pend[e].remove(o)
                if o.is_dma:
                    free[e] = st + 0.1
                    o.fin = st + o.cost
                else:
                    o.fin = st + o.cost
                    free[e] = o.fin
                tmax = max(tmax, o.fin)
                for u in o.users:
                    if id(u) in inst:
                        u.nleft -= 1
                        lat = SLAT if (u.eng == e and not o.is_dma) else XLAT
                        if o.fin + lat > u.ready:
                            u.ready = o.fin + lat
                order[e].append(o)
                n -= 1
            t_stage = tmax
        self.est_us = t_stage
        return order

    def emit(self, nc, engines, sems, dma_sems):
        order = self.schedule()
        last_by_stage = {}
        for e in self.ENGS:
            for o in order[e]:
                last_by_stage[(e, o.stage)] = o
        for e in self.ENGS:
            prev_stage = None
            for o in order[e]:
                if o.stage != prev_stage:
                    for e2 in self.ENGS:
                        cands = [v for (ee, st), v in last_by_stage.items() if ee == e2 and st < o.stage]
                        if cands:
                            d = max(cands, key=lambda v: v.stage)
                            if d is not o and d not in o.deps:
                                o.deps.append(d)
                    prev_stage = o.stage
        for e in self.ENGS:
            for o in order[e]:
                for d in o.deps:
                    if (not d.is_dma) and (not o.is_dma) and d.eng == o.eng and (o.eng in SAME_ENGINE_NOSYNC or (RAW_ONLY and id(d) not in o.raw)):
                        continue
                    d.need = True
        NDS = {q: len(dma_sems[q]) for q in dma_sems}
        all_dma = {q: [] for q in dma_sems}
        for e in self.ENGS:
            cnt = 0
            k = 0
            for o in order[e]:
                if o.is_dma:
                    s = dma_sems[e][k % NDS[e]]
                    gen = k // NDS[e]
                    o.sig = (s, 16 * (gen + 1))
                    o.guard = (s, 16 * gen) if gen > 0 else None
                    all_dma[e].append(o)
                    k += 1
                elif o.need:
                    cnt += 1
                    o.sig = (sems[e], cnt)

        def run(e, eng):
            waited = {}
            for o in order[e]:
                need = {}
                if o.guard is not None:
                    need[o.guard[0]] = o.guard[1]
                for d in o.deps:
                    if (not d.is_dma) and (not o.is_dma) and d.eng == e and (e in SAME_ENGINE_NOSYNC or (RAW_ONLY and id(d) not in o.raw)):
                        continue
                    s, v = d.sig
                    if need.get(s, 0) < v:
                        need[s] = v
                for s, v in need.items():
                    if waited.get(s, 0) >= v:
                        continue
                    eng.wait_ge(s, v)
                    waited[s] = v
                ins = o.fn(eng)
                if o.is_dma:
                    ins.then_inc(o.sig[0], 16)
                elif o.sig is not None:
                    ins.then_inc(o.sig[0], 1)
            return waited

        with nc.Block() as block:
            @block.tensor
            def _(pe):
                run("pe", pe)

            @block.scalar
            def _(act):
                run("act", act)

            @block.vector
            def _(dve):
                run("dve", dve)

            @block.gpsimd
            def _(pool):
                run("pool", pool)

            @block.sync
            def _(sp):
                w = run("sp", sp)
                for q, lst in all_dma.items():
                    last = {}
                    for o in lst:
                        last[o.sig[0]] = o.sig[1]
                    for s, v in last.items():
                        if w.get(s, 0) < v:
                            sp.wait_ge(s, v)


GELU_NATIVE = True
ARENA_BYTES = 98 * 1024

C_IDENT = 0
C_TRI = 128
C_M2 = 256
C_MC = 384
C_LCOL = 640
C_SBB = 768
C_ALB = 1280
C_NTRI = C_ALB + 256
CONST_W = C_NTRI + 128


def alibi_slope(h):
    return float(np.float32(2.0 ** (-8.0 * (h + 1) / 16)))


def _bf16_round(v):
    a = np.asarray(v, np.float32).reshape(1)
    u = a.view(np.uint32)
    r = ((u + 0x7FFF + ((u >> 16) & 1)) & 0xFFFF0000).astype(np.uint32)
    return float(r.view(np.float32)[0])


def make_consts():
    c = np.zeros((128, CONST_W), np.float32)
    p = np.arange(128)
    c[:, C_IDENT:C_IDENT + 128] = np.eye(128, dtype=np.float32)
    c[:, C_TRI:C_TRI + 128] = (p[:, None] <= p[None, :]).astype(np.float32)
    c[:, C_M2:C_M2 + 128] = ((p[:, None] <= p[None, :]) & ((p[:, None] // 64) == (p[None, :] // 64))).astype(np.float32)
    c[:, C_NTRI:C_NTRI + 128] = np.where(p[:, None] <= p[None, :], 0.0, -30000.0).astype(np.float32)
    t = np.arange(256)
    c[:, C_MC:C_MC + 256] = (t % 64 != 0).astype(np.float32)[None, :]
    for G in range(2):
        for hl in range(8):
            sl = alibi_slope(G * 8 + hl)
            hi = _bf16_round(sl)
            lo = _bf16_round(sl - hi)
            for kb in range(8):
                col = C_LCOL + G * 64 + hl * 8 + kb
                c[hl * 16 + kb, col] = 1.0
                c[hl * 16 + 8, col] = -8.0 * hi
                c[hl * 16 + 9, col] = -8.0 * hi
                c[hl * 16 + 10, col] = -8.0 * lo
                c[hl * 16 + 11, col] = -8.0 * lo
    for j in range(4):
        for hl in range(8):
            base = C_SBB + j * 128 + hl * 16
            c[:, base + 8] = (j % 2) * 128 + p
            c[:, base + 9] = 256 * (j // 2)
            c[:, base + 10] = (j % 2) * 128 + p
            c[:, base + 11] = 256 * (j // 2)
    for h in range(16):
        sl = alibi_slope(h)
        for dk in range(-12, 4):
            c[:, C_ALB + h * 16 + dk + 12] = np.float32(sl) * (p + 128.0 * dk).astype(np.float32)
    return c


class Ring:
    def __init__(self, items):
        self.items = items
        self.i = 0

    def next(self):
        it = self.items[self.i % len(self.items)]
        self.i += 1
        return it


class K:
    def __init__(self, nc, stages):
        self.nc = nc
        self.P = Prog()
        self.es = ExitStack()
        self.stages = stages
        self.uid = 0
        self.stage_no = 0
        ARENA_REG["range"] = None
        ARENA_REG["toks"] = []

    def sb(self, shape, dt, name=None):
        self.uid += 1
        return self.es.enter_context(self.nc.sbuf_tensor(f"{name or 't'}_{self.uid}", list(shape), dt))

    def salloc(self, shape, dt):
        n = 1
        for s in shape[1:]:
            n *= s
        nbytes = n * (4 if dt == F32 else 2)
        off = (self.aoff + 31) // 32 * 32
        self.aoff = off + nbytes
        assert self.aoff <= ARENA_BYTES, f"arena overflow {self.aoff}"
        if self.stage_no % 2:
            off = (ARENA_BYTES - self.aoff) // 32 * 32
        ARENA_REG["range"] = (self.stage_no, off, off + nbytes)
        v = self.arena[:, off // 2:(off + nbytes) // 2]
        if dt == F32:
            v = v.bitcast(F32)
        if len(shape) == 3:
            v = v.rearrange("p (a b) -> p a b", a=shape[1])
        elif len(shape) == 4:
            v = v.rearrange("p (a b c) -> p a b c", a=shape[1], b=shape[2])
        if shape[0] < 128:
            v = v[0:shape[0]]
        return v

    def ring(self, n, shape, dt, name="r"):
        return Ring([(self.salloc(shape, dt), Tok(name)) for _ in range(n)])

    def begin_stage(self):
        self.aoff = 0
        self.stage_no += 1
        ARENA_REG["range"] = None

    def dram_in(self, name, shape, dt=F32):
        return self.nc.dram_tensor(name, list(shape), dt, kind="ExternalInput").ap()

    def pe(self, fn, r=(), w=()):
        return self.P.op("pe", fn, r, w)

    def act(self, fn, r=(), w=()):
        return self.P.op("act", fn, r, w)

    def dve(self, fn, r=(), w=()):
        return self.P.op("dve", fn, r, w)

    def pool(self, fn, r=(), w=()):
        return self.P.op("pool", fn, r, w)

    def dma(self, q, out, in_, r=(), w=()):
        return self.P.op(q, lambda e: e.dma_start(out=out, in_=in_), r, w, dma=True)

    def dump(self, ap, tok, psum=False):
        if not os.environ.get('DEBUG_DUMP'):
            return
        n = ap.shape[-1]
        c0 = self.dump_off
        self.dump_off += n
        print("DUMP", c0, n)
        if psum:
            scr = self.salloc([128, n], F32)
            st = Tok("dscr")
            self.dve(lambda e: e.tensor_copy(out=scr, in_=ap), r=[tok], w=[st])
            self.dma("pool", self.d["dbg"][:, c0:c0 + n], scr, r=[st])
        else:
            self.dma("pool", self.d["dbg"][:, c0:c0 + n], ap, r=[tok])

    def bank(self):
        b = self.banks[self.bank_i % self.nbank]
        self.bank_i += 1
        return b

    def obank(self):
        b = self.banks[6 + self.obank_i % 2]
        self.obank_i += 1
        return b

    def mm(self, out, lhsT, rhs, start, stop, r, w, **kw):
        n = out.shape[-1]
        return self.P.op("pe", lambda e: e.matmul(out, lhsT=lhsT, rhs=rhs, start=start, stop=stop, **kw), r, w, cost=max(n, 64) / 2400.0 + 0.012)

    def mmB(self, ps, pst, W, wt, col0, tok0, n):
        xts = [self.xT_t[q] for q in range(tok0 // 128, (tok0 + n + 127) // 128)]
        for kc in range(KC):
            self.mm(ps, W[:, kc, col0:col0 + 128], self.xT[:, kc, tok0:tok0 + n], kc == 0, kc == KC - 1, [wt] + xts, [pst])

    def mmA(self, ps, pst, i, W, wt, col0, n):
        for kc in range(KC):
            self.mm(ps, self.xT[:, kc, i * 128:(i + 1) * 128], W[:, kc, col0:col0 + n], kc == 0, kc == KC - 1,
                    [wt, self.xT_t[i]], [pst])

    def wload(self, dst, src, tok):
        self.dma("pool", dst, src, w=[tok])

    def wload_w(self, dst, W_d, col0, n, tok, step=512):
        for c in range(0, n, step):
            m = min(step, n - c)
            self.wload(dst[:, :, c:c + m], W_d[:, col0 + c:col0 + c + m].rearrange("(k p) n -> p k n", p=128), tok)

    def build(self):
        nc = self.nc
        d = {}
        d["x"] = self.dram_in("x", [T, D])
        d["mem"] = self.dram_in("mem", [MEM, D])
        d["lng"] = self.dram_in("ln_g", [6, D])
        d["lnb"] = self.dram_in("ln_b", [6, D])
        d["xwq"] = self.dram_in("x_wq", [2, D, D])
        d["xwkv"] = self.dram_in("x_wkv", [2, D, 2 * D])
        d["xwo"] = self.dram_in("x_wo", [2, D, D])
        d["fwi"] = self.dram_in("ffn_w_in", [2, D, 2 * DFF])
        d["fwo"] = self.dram_in("ffn_w_out", [2, DFF, D])
        d["evi"] = self.dram_in("ev_w_in", [D, 3072])
        d["evo"] = self.dram_in("ev_w_out", [D, D])
        d["awsT"] = self.dram_in("a_wsT", [128, 4, 128])
        d["abs"] = self.dram_in("a_bs", [1, 512])
        d["alng"] = self.dram_in("a_ln_g", [1, 512])
        d["alnb"] = self.dram_in("a_ln_b", [1, 512])
        d["bng"] = self.dram_in("b_norm_gT", [128, 4])
        d["lbl"] = self.dram_in("lb_logitsT", [128, 2, 4])
        d["oqkv"] = self.dram_in("od_w_qkv", [D, 3072])
        d["owo"] = self.dram_in("od_w_out", [D, D])
        d["cst"] = self.dram_in("consts", [128, CONST_W])
        d["out"] = nc.dram_tensor("out", [T, D], F32, kind="ExternalOutput").ap()
        if os.environ.get('DEBUG_DUMP'):
            d["dbg"] = nc.dram_tensor("dbg", [128, 8192], F32, kind="ExternalOutput").ap()
        self.dump_off = 0
        self.d = d

        self.x_tok = self.sb([128, NT, D], F32, "x_tok")
        self.xT = self.sb([128, KC, T], BF16, "xT")
        self.xtok_t = [Tok(f"xtok{i}") for i in range(NT)]
        self.xT_t = [Tok(f"xT{i}") for i in range(NT)]
        self.cst = self.sb([128, CONST_W], F32, "cst")
        self.cst_t = Tok("cst")
        self.ident = self.sb([128, 128], BF16, "ident")
        self.ident_t = Tok("ident")
        self.xb_ring = Ring([(self.sb([128, D], BF16, "xb"), Tok("xb")) for _ in range(2)])
        self.st_ring = Ring([(self.sb([128, 32], F32, "lnst"), Tok("lnst")) for _ in range(3)])
        self.negh = self.sb([128, 512], F32, "negh")
        self.negh_t = Tok("negh")
        self.arena = self.sb([128, ARENA_BYTES // 2], BF16, "arena")
        self.aoff = 0
        self.banks = []
        for i in range(8):
            pt = self.es.enter_context(nc.psum_tensor(f"bank{i}", [128, 512], F32))
            self.banks.append((pt, Tok(f"bank{i}")))
        self.bank_i = 0
        self.obank_i = 0
        self.nbank = 8
        self.cp_i = 0

        self.dma("sp", self.cst[:], d["cst"], w=[self.cst_t])
        self.dve(lambda e: e.tensor_copy(out=self.ident[:], in_=self.cst[:, C_IDENT:C_IDENT + 128]),
                 r=[self.cst_t], w=[self.ident_t])
        self.pool(lambda e: e.memset(self.negh[:], -0.5), w=[self.negh_t])

        self.load_x()
        for s in self.stages:
            getattr(self, "stage_" + s[0])(*s[1:])
        ARENA_REG["range"] = None
        self.store_out()

        sems = {e: self.es.enter_context(nc.semaphore(f"s_{e}")) for e in Prog.ENGS}
        dma_sems = {}
        for q, n in (("sp", 8), ("pool", 6), ("act", 2)):
            dma_sems[q] = [self.es.enter_context(nc.semaphore(f"d_{q}{i}")) for i in range(n)]
        self.P.emit(nc, None, sems, dma_sems)
        self.es.close()

    def copy(self, out, in_, r, w):
        self.cp_i += 1
        if self.cp_i % 2:
            return self.act(lambda e: e.copy(out=out, in_=in_), r=r, w=w)
        return self.dve(lambda e: e.tensor_copy(out=out, in_=in_), r=r, w=w)

    def load_x(self):
        for i in range(NT):
            self.dma("sp", self.x_tok[:, i, :], self.d["x"][i * 128:(i + 1) * 128, :], w=[self.xtok_t[i]])
        for i in range(NT):
            self.to_xT(i)

    def store_out(self):
        for i in range(NT):
            self.dma("sp", self.d["out"][i * 128:(i + 1) * 128, :], self.x_tok[:, i, :], r=[self.xtok_t[i]])

    def transpose_to(self, dst_view, src_tiles, r, w):
        bk, bt = self.bank()
        psb = bk[:].bitcast(BF16)
        n = len(src_tiles)
        for k, src in enumerate(src_tiles):
            self.pe(lambda e, k=k, src=src: e.transpose(out=psb[:, k * 128:(k + 1) * 128], in_=src, identity=self.ident[:]),
                    r=list(r) + [self.ident_t], w=[bt])
        self.copy(dst_view, psb[:, 0:n * 128].rearrange("p (k t) -> p k t", k=n), [bt], w)

    def to_xT(self, i):
        xb, xbt = self.xb_ring.next()
        self.act(lambda e: e.copy(out=xb[:], in_=self.x_tok[:, i, :]), r=[self.xtok_t[i]], w=[xbt])
        self.transpose_to(self.xT[:, :, i * 128:(i + 1) * 128], [xb[:, kc * 128:(kc + 1) * 128] for kc in range(KC)],
                          [xbt], [self.xT_t[i]])

    def load_ln(self, idx):
        gb = self.salloc([128, 2, D], F32)
        t = Tok("lnp")
        g, b = gb[:, 0, :], gb[:, 1, :]
        self.dma("sp", g, self.d["lng"][idx:idx + 1, :].to_broadcast([128, D]), w=[t])
        self.dma("sp", b, self.d["lnb"][idx:idx + 1, :].to_broadcast([128, D]), w=[t])
        return g, b, t

    def rstd_small(self, out, var, r, w, eps=EPS):
        n = out.shape[-1]
        self.pool(lambda e: e.tensor_scalar(out=out, in0=var, scalar1=eps, scalar2=None, op0=ALU.add), r=r, w=w)
        self.pool(lambda e: e.tensor_tensor(out=out, in0=out, in1=self.negh[:, 0:n], op=ALU.pow), r=list(w) + [self.negh_t], w=w)

    def layer_norm(self, i, lnp):
        g, b, gt = lnp
        xt = self.x_tok[:, i, :]
        xtok = self.xtok_t[i]
        stt, stk = self.st_ring.next()
        self.dve(lambda e: e.bn_stats(out=stt[:, 0:6], in_=self.x_tok[:, i, 0:512]), r=[xtok], w=[stk])
        self.dve(lambda e: e.bn_stats(out=stt[:, 6:12], in_=self.x_tok[:, i, 512:1024]), r=[xtok], w=[stk])
        self.dve(lambda e: e.bn_aggr(out=stt[:, 12:14], in_=stt[:, 0:12].rearrange("p (a b) -> p a b", a=2)), r=[stk], w=[stk])
        self.rstd_small(stt[:, 15:16], stt[:, 13:14], [stk], [stk])
        self.dve(lambda e: e.scalar_tensor_tensor(out=stt[:, 16:17], in0=stt[:, 12:13], scalar=-1.0, in1=stt[:, 15:16],
                                                  op0=ALU.mult, op1=ALU.mult), r=[stk], w=[stk])
        self.act(lambda e: e.activation(out=xt, in_=xt, func=AF.Identity, bias=stt[:, 16:17], scale=stt[:, 15:16]),
                 r=[stk, xtok], w=[xtok])
        self.pool(lambda e: e.tensor_tensor(out=xt, in0=xt, in1=g, op=ALU.mult), r=[xtok, gt], w=[xtok])
        self.dve(lambda e: e.tensor_tensor(out=xt, in0=xt, in1=b, op=ALU.add), r=[xtok, gt], w=[xtok])
        self.to_xT(i)

    def accum(self, i, half, ps, pst, first):
        dst = self.x_tok[:, i, half * 512:(half + 1) * 512]
        if first:
            self.dve(lambda e: e.scalar_tensor_tensor(out=dst, in0=dst, scalar=ALPHA, in1=ps, op0=ALU.mult, op1=ALU.add),
                     r=[pst, self.xtok_t[i]], w=[self.xtok_t[i]])
        else:
            self.dve(lambda e: e.tensor_tensor(out=dst, in0=dst, in1=ps, op=ALU.add),
                     r=[pst, self.xtok_t[i]], w=[self.xtok_t[i]])

    def out_proj(self, i, lhs_fn, nk, lhs_toks, wo, wot, first):
        for half in range(2):
            ps, pst = self.bank()
            for k in range(nk):
                self.mm(ps[:], lhs_fn(k), wo[:, k, half * 512:(half + 1) * 512], k == 0, k == nk - 1, list(lhs_toks) + [wot], [pst])
            self.accum(i, half, ps[:], pst, first)

    def stage_ln_only(self, idx):
        self.begin_stage()
        lnp = self.load_ln(idx)
        for i in range(NT):
            self.layer_norm(i, lnp)

    def stage_ffn(self, l):
        self.begin_stage()
        groups = [list(range(0, 6)), list(range(6, 12)), list(range(12, 17)), list(range(17, 22))]
        fwi = self.d["fwi"][l]
        fwo = self.d["fwo"][l]
        lnp = self.load_ln(l * 3 + 2)
        wi_r = self.ring(3, [128, KC, 256], BF16, "fwi")
        wo_r = self.ring(2, [128, 6, D], BF16, "fwo")
        hT = self.salloc([128, 6, T], BF16)
        hTt = [[Tok("hT") for _ in range(4)] for _ in range(6)]
        sg_r = self.ring(2, [128, 512], F32, "sg")
        for gi, js in enumerate(groups):
            wo, wot = wo_r.next()
            j0 = js[0]
            self.wload(wo[:, 0:len(js), :], fwo[j0 * 128:(j0 + len(js)) * 128, :].rearrange("(j p) n -> p j n", p=128), wot)
            for jl, j in enumerate(js):
                wi, wit = wi_r.next()
                self.wload(wi[:, :, 0:128], fwi[:, j * 128:(j + 1) * 128].rearrange("(k p) n -> p k n", p=128), wit)
                self.wload(wi[:, :, 128:256], fwi[:, DFF + j * 128:DFF + (j + 1) * 128].rearrange("(k p) n -> p k n", p=128), wit)
                for tb in range(4):
                    pg, pgt = self.bank()
                    pu, put = self.bank()
                    self.mmB(pg[:], pgt, wi, wit, 0, tb * 512, 512)
                    self.mmB(pu[:], put, wi, wit, 128, tb * 512, 512)
                    sg, sgt = sg_r.next()
                    self.act(lambda e, sg=sg, pg=pg: e.activation(out=sg, in_=pg[:], func=AF.Silu), r=[pgt], w=[sgt])
                    self.dve(lambda e, sg=sg, pu=pu, jl=jl, tb=tb: e.tensor_tensor(out=hT[:, jl, tb * 512:(tb + 1) * 512], in0=sg, in1=pu[:], op=ALU.mult),
                             r=[sgt, put], w=[hTt[jl][tb]])
            last = gi == len(groups) - 1
            for i in range(NT):
                self.out_proj(i, lambda k, i=i: hT[:, k, i * 128:(i + 1) * 128], len(js), [hTt[k][i // 4] for k in range(len(js))], wo, wot, first=(gi == 0))
                if last:
                    self.layer_norm(i, lnp)

    def stage_xattn(self, l):
        self.begin_stage()
        d = self.d
        SC = 1.0 / 16.0
        lnp = self.load_ln(l * 3 + 1)
        wq, wqt = self.salloc([128, KC, D], BF16), Tok("wq")
        wo, wot = self.salloc([128, KC, D], BF16), Tok("wo")
        kT, kTt = self.salloc([128, KC, MEM], BF16), [Tok("kT") for _ in range(KC)]
        vS, vSt = self.salloc([128, 2, D], BF16), [[Tok("vS") for _ in range(2)] for _ in range(2)]
        memb, membt = self.salloc([128, 2, D], BF16), Tok("memb")
        memT, memTt = self.salloc([128, KC, MEM], BF16), Tok("memT")
        wkv_r = self.ring(1, [128, KC, 512], BF16, "wkv")
        qT_r = Ring([(self.salloc([128, KC, 512], BF16), [Tok("qT") for _ in range(KC)])])
        p32_r = self.ring(2, [128, 4, 256], F32, "p32")
        pb_r = self.ring(2, [128, 4, 256], BF16, "pb")
        pT_r = self.ring(2, [128, 8, 128], BF16, "pT")
        oT_r = self.ring(2, [128, 8, 128], BF16, "oT")
        sm_r = self.ring(3, [128, 16], F32, "sm")
        self.wload(memb, d["mem"].rearrange("(m p) n -> p m n", p=128), membt)
        for mt in range(2):
            self.transpose_to(memT[:, :, mt * 128:(mt + 1) * 128], [memb[:, mt, kc * 128:(kc + 1) * 128] for kc in range(KC)],
                              [membt], [memTt])
        for c in range(4):
            wk, wkt = wkv_r.next()
            self.wload_w(wk, d["xwkv"][l], c * 512, 512, wkt)
            if c < 2:
                for cc in range(4):
                    fc = c * 4 + cc
                    ps, pst = self.bank()
                    for kc in range(KC):
                        self.mm(ps[:, 0:MEM], wk[:, kc, cc * 128:(cc + 1) * 128], memT[:, kc, :], kc == 0, kc == KC - 1, [wkt, memTt], [pst])
                    self.copy(kT[:, fc, :], ps[:, 0:MEM], [pst], [kTt[fc]])
            else:
                for mt in range(2):
                    ps, pst = self.bank()
                    for kc in range(KC):
                        self.mm(ps[:], memT[:, kc, mt * 128:(mt + 1) * 128], wk[:, kc, :], kc == 0, kc == KC - 1, [wkt, memTt], [pst])
                    self.copy(vS[:, mt, (c - 2) * 512:(c - 1) * 512], ps[:], [pst], [vSt[mt][c - 2]])
        self.wload_w(wq, d["xwq"][l], 0, D, wqt)
        self.wload_w(wo, d["xwo"][l], 0, D, wot)
        for tb in range(4):
            qT, qTt = qT_r.next()
            for c in range(KC):
                ps, pst = self.bank()
                self.mmB(ps[:], pst, wq, wqt, c * 128, tb * 512, 512)
                self.copy(qT[:, c, :], ps[:], [pst], [qTt[c]])
            for il in range(4):
                i = tb * 4 + il
                sA, sAt = self.bank()
                sB, sBt = self.bank()
                for h in range(4):
                    bk, bkt = (sA, sAt) if h < 2 else (sB, sBt)
                    for k2 in range(2):
                        self.mm(bk[:, (h % 2) * 256:(h % 2 + 1) * 256], qT[:, 2 * h + k2, il * 128:(il + 1) * 128], kT[:, 2 * h + k2, :],
                                k2 == 0, k2 == 1, [qTt[2 * h + k2], kTt[2 * h + k2]], [bkt])
                sm, smt = sm_r.next()
                self.dve(lambda e, sm=sm, sA=sA: e.tensor_reduce(out=sm[:, 0:2], in_=sA[:].rearrange("p (h m) -> p h m", h=2), axis=AX.X, op=ALU.max), r=[sAt], w=[smt])
                self.dve(lambda e, sm=sm, sB=sB: e.tensor_reduce(out=sm[:, 2:4], in_=sB[:].rearrange("p (h m) -> p h m", h=2), axis=AX.X, op=ALU.max), r=[sBt], w=[smt])
                self.dve(lambda e, sm=sm: e.tensor_scalar(out=sm[:, 4:8], in0=sm[:, 0:4], scalar1=-SC, scalar2=None, op0=ALU.mult), r=[smt], w=[smt])
                p32, p32t = p32_r.next()
                for h in range(4):
                    bk, bkt = (sA, sAt) if h < 2 else (sB, sBt)
                    self.act(lambda e, h=h, bk=bk, sm=sm, p32=p32: e.activation(out=p32[:, h, :], in_=bk[:, (h % 2) * 256:(h % 2 + 1) * 256], func=AF.Exp,
                                                                             bias=sm[:, 4 + h:5 + h], scale=SC, accum_out=sm[:, 8 + h:9 + h]),
                             r=[bkt, smt], w=[p32t, smt])
                self.dve(lambda e, sm=sm: e.reciprocal(out=sm[:, 12:16], in_=sm[:, 8:12]), r=[smt], w=[smt])
                pb, pbt = pb_r.next()
                for h in range(4):
                    if h % 2:
                        self.dve(lambda e, h=h, pb=pb, p32=p32, sm=sm: e.tensor_scalar(out=pb[:, h, :], in0=p32[:, h, :], scalar1=sm[:, 12 + h:13 + h], scalar2=None, op0=ALU.mult),
                                 r=[p32t, smt], w=[pbt])
                    else:
                        self.act(lambda e, h=h, pb=pb, p32=p32, sm=sm: e.activation(out=pb[:, h, :], in_=p32[:, h, :], func=AF.Copy, scale=sm[:, 12 + h:13 + h]),
                                 r=[p32t, smt], w=[pbt])
                pT, pTt = pT_r.next()
                self.transpose_to(pT, [pb[:, h, mt * 128:(mt + 1) * 128] for h in range(4) for mt in range(2)], [pbt], [pTt])
                oA, oAt = self.bank()
                oB, oBt = self.bank()
                oT, oTt = oT_r.next()
                for c in range(8):
                    bk, bkt = (oA, oAt) if c < 4 else (oB, oBt)
                    h = c // 2
                    for mt in range(2):
                        self.mm(bk[:, (c % 4) * 128:(c % 4 + 1) * 128], vS[:, mt, c * 128:(c + 1) * 128], pT[:, h * 2 + mt, :],
                                mt == 0, mt == 1, [vSt[mt][c // 4], pTt], [bkt])
                self.copy(oT[:, 0:4, :], oA[:].rearrange("p (c t) -> p c t", c=4), [oAt], [oTt])
                self.copy(oT[:, 4:8, :], oB[:].rearrange("p (c t) -> p c t", c=4), [oBt], [oTt])
                self.out_proj(i, lambda k, oT=oT: oT[:, k, :], KC, [oTt], wo, wot, first=True)
                self.layer_norm(i, lnp)

    def gelu(self, out, ps, pst, outt, scr_r, half=True):
        if GELU_NATIVE:
            self.act(lambda e: e.activation(out=out, in_=ps, func=AF.Gelu_apprx_tanh), r=[pst], w=[outt])
            return 1.0
        C0 = 0.7978845608028654
        C1 = 0.044715
        s, stk = scr_r.next()
        n = ps.shape[-1]
        sv = s[:, 0:n]
        self.act(lambda e: e.activation(out=sv, in_=ps, func=AF.Square), r=[pst], w=[stk])
        self.dve(lambda e: e.tensor_scalar(out=sv, in0=sv, scalar1=C1, scalar2=1.0, op0=ALU.mult, op1=ALU.add), r=[stk], w=[stk])
        self.dve(lambda e: e.tensor_tensor(out=sv, in0=sv, in1=ps, op=ALU.mult), r=[stk, pst], w=[stk])
        self.act(lambda e: e.activation(out=sv, in_=sv, func=AF.Tanh, scale=C0), r=[stk], w=[stk])
        self.dve(lambda e: e.scalar_tensor_tensor(out=out, in0=sv, scalar=1.0, in1=ps, op0=ALU.add, op1=ALU.mult), r=[stk, pst], w=[outt])
        return 0.5

    def stage_mix0(self):
        d = self.d
        self.begin_stage()
        wuv, wut, wvt = self.salloc([128, KC, 1024], BF16), Tok("wu"), Tok("wv")
        woA, woAt = self.salloc([128, 4, D], BF16), Tok("woA")
        ws32, ws32t = self.salloc([128, 4, 128], F32), Tok("ws32")
        wsb, wsbt = self.salloc([128, 4, 128], BF16), Tok("wsb")
        bsB, bsBt = self.salloc([128, 512], F32), Tok("bsB")
        lgB, lgBt = self.salloc([128, 512], F32), Tok("lgB")
        lbB, lbBt = self.salloc([128, 512], F32), Tok("lbB")
        uT_r = self.ring(2, [128, 4, 512], BF16, "uT")
        scr_r = self.ring(5, [128, 512], F32, "gscr")
        vg_r = self.ring(4, [128, 512], F32, "vg")
        vb_r = self.ring(4, [128, 512], BF16, "vb")
        ss_r = self.ring(4, [128, 512], F32, "ssum")
        ya_r = self.ring(4, [128, 4, 128], BF16, "yaT")
        st_r = self.ring(4, [128, 48], F32, "gst")
        self.wload_w(wuv[:, :, 0:512], d["evi"], 0, 512, wut)
        self.wload_w(wuv[:, :, 512:1024], d["evi"], 512, 512, wvt)
        self.wload(woA, d["evo"][0:512, :].rearrange("(g p) n -> p g n", p=128), woAt)
        self.dma("sp", ws32, d["awsT"], w=[ws32t])
        self.dma("sp", bsB, d["abs"].to_broadcast([128, 512]), w=[bsBt])
        self.dma("sp", lgB, d["alng"].to_broadcast([128, 512]), w=[lgBt])
        self.dma("sp", lbB, d["alnb"].to_broadcast([128, 512]), w=[lbBt])
        for g in range(4):
            self.dve(lambda e, g=g: e.tensor_tensor(out=wsb[:, g, :], in0=ws32[:, g, :], in1=self.cst[:, C_TRI:C_TRI + 128], op=ALU.mult),
                     r=[ws32t, self.cst_t], w=[wsbt])
        for tb in range(4):
            uT, uTt = uT_r.next()
            gf = 1.0
            for c in range(4):
                ps, pst = self.bank()
                self.mmB(ps[:], pst, wuv, wut, c * 128, tb * 512, 512)
                gf = self.gelu(uT[:, c, :], ps[:], pst, uTt, scr_r)
            for il in range(4):
                i = tb * 4 + il
                ps, pst = self.bank()
                self.mmA(ps[:], pst, i, wuv, wvt, 512, 512)
                vg, vgt = vg_r.next()
                gv = self.gelu(vg, ps[:], pst, vgt, scr_r)
                st, stt = st_r.next()
                for g in range(4):
                    self.dve(lambda e, g=g, st=st, vg=vg: e.bn_stats(out=st[:, g * 6:(g + 1) * 6], in_=vg[:, g * 128:(g + 1) * 128]), r=[vgt], w=[stt])
                for g in range(4):
                    self.dve(lambda e, g=g, st=st: e.bn_aggr(out=st[:, 24 + g * 2:26 + g * 2], in_=st[:, g * 6:(g + 1) * 6]), r=[stt], w=[stt])
                mv = st[:, 24:32].rearrange("p (g two) -> p g two", two=2)
                self.pool(lambda e, st=st, mv=mv: e.tensor_scalar(out=st[:, 32:36], in0=mv[:, :, 1], scalar1=gv * gv, scalar2=EPS, op0=ALU.mult, op1=ALU.add), r=[stt], w=[stt])
                self.pool(lambda e, st=st: e.tensor_tensor(out=st[:, 32:36], in0=st[:, 32:36], in1=self.negh[:, 0:4], op=ALU.pow), r=[stt, self.negh_t], w=[stt])
                self.dve(lambda e, st=st: e.tensor_scalar(out=st[:, 36:40], in0=st[:, 32:36], scalar1=gv, scalar2=None, op0=ALU.mult), r=[stt], w=[stt])
                self.dve(lambda e, st=st, mv=mv: e.scalar_tensor_tensor(out=st[:, 40:44], in0=mv[:, :, 0], scalar=-1.0, in1=st[:, 36:40], op0=ALU.mult, op1=ALU.mult), r=[stt], w=[stt])
                for g in range(4):
                    self.act(lambda e, g=g, st=st, vg=vg: e.activation(out=vg[:, g * 128:(g + 1) * 128], in_=vg[:, g * 128:(g + 1) * 128], func=AF.Identity,
                                                                      bias=st[:, 40 + g:41 + g], scale=st[:, 36 + g:37 + g]), r=[stt, vgt], w=[vgt])
                self.dve(lambda e, vg=vg: e.tensor_tensor(out=vg, in0=vg, in1=lgB, op=ALU.mult), r=[vgt, lgBt], w=[vgt])
                vb, vbt = vb_r.next()
                self.pool(lambda e, vg=vg, vb=vb: e.tensor_tensor(out=vb, in0=vg, in1=lbB, op=ALU.add), r=[vgt, lbBt], w=[vbt])
                pss, psst = self.bank()
                for g in range(4):
                    self.mm(pss[:, g * 128:(g + 1) * 128], vb[:, g * 128:(g + 1) * 128], wsb[:, g, :], True, True, [vbt, wsbt], [psst])
                ss, sst = ss_r.next()
                self.dve(lambda e, ss=ss, pss=pss: e.tensor_tensor(out=ss, in0=pss[:], in1=bsB, op=ALU.add), r=[psst, bsBt], w=[sst])
                ya, yat = ya_r.next()
                self.dve(lambda e, ya=ya, uT=uT, ss=ss, il=il: e.scalar_tensor_tensor(out=ya, in0=uT[:, :, il * 128:(il + 1) * 128], scalar=gf,
                                                                                   in1=ss.rearrange("p (g t) -> p g t", g=4), op0=ALU.mult, op1=ALU.mult),
                         r=[uTt, sst], w=[yat])
                self.out_proj(i, lambda k, ya=ya: ya[:, k, :], 4, [yat], woA, woAt, first=True)

        self.begin_stage()
        NB = 256
        lnp = self.load_ln(0)
        wB, wBt = self.salloc([128, KC, 2048], BF16), [Tok("wBq"), Tok("wBf"), Tok("wBi"), Tok("wBg")]
        woB, woBt = self.salloc([128, 4, D], BF16), Tok("woB")
        sm, smt = self.salloc([128, 32], F32), Tok("hsm")
        S, St = self.salloc([128, 4, 128], F32), Tok("S")
        Sb_r = self.ring(3, [128, 4, 128], BF16, "Sb")
        m2x4, m2t = self.salloc([128, 512], BF16), Tok("m2x4")
        ones, onest = self.salloc([128, 128], BF16), Tok("ones")
        f1 = self.ring(1, [128, 4, NB], F32, "f1")
        f2 = self.ring(1, [128, 4, NB], F32, "f2")
        f3 = self.ring(1, [128, 4, NB], F32, "f3")
        E_r = self.ring(2, [128, 4, 4], F32, "Elast")
        qd_r = self.ring(1, [128, 4, NB], BF16, "qdT")
        kd_r = self.ring(1, [128, 4, NB], BF16, "kdT")
        gs_r = self.ring(1, [128, 4, NB], BF16, "gsT")
        vi_r = self.ring(1, [128, 2, 512], BF16, "vi")
        kt_r = self.ring(1, [128, 2, 512], BF16, "kdtok")
        am_r = self.ring(3, [128, 512], BF16, "am")
        tmp_r = self.ring(2, [128, 512], F32, "stmp")
        sq_r = self.ring(2, [128, 512], BF16, "sq")
        r_r = self.ring(2, [128, 512], F32, "rr")
        t1_r = self.ring(2, [128, 512], F32, "t1")
        yb_r = self.ring(3, [128, 4, 128], BF16, "ybT")
        for c in range(4):
            self.wload_w(wB[:, :, c * 512:(c + 1) * 512], d["evi"], 1024 + c * 512, 512, wBt[c])
        self.wload(woB, d["evo"][512:1024, :].rearrange("(g p) n -> p g n", p=128), woBt)
        self.dma("sp", sm[:, 0:8], d["lbl"].rearrange("p l h -> p (l h)"), w=[smt])
        self.dma("sp", sm[:, 20:24], d["bng"], w=[smt])
        self.dve(lambda e: e.tensor_tensor(out=sm[:, 8:12], in0=sm[:, 0:4], in1=sm[:, 4:8], op=ALU.subtract), r=[smt], w=[smt])
        self.act(lambda e: e.activation(out=sm[:, 12:16], in_=sm[:, 8:12], func=AF.Sigmoid), r=[smt], w=[smt])
        self.dve(lambda e: e.tensor_scalar(out=sm[:, 16:20], in0=sm[:, 12:16], scalar1=-1.0, scalar2=1.0, op0=ALU.mult, op1=ALU.add), r=[smt], w=[smt])
        self.pool(lambda e: e.memset(S.rearrange("p h v -> p (h v)"), 0.0), w=[St])
        Sb, Sbt = Sb_r.next()
        self.pool(lambda e, Sb=Sb: e.memset(Sb.rearrange("p h v -> p (h v)"), 0.0), w=[Sbt])
        self.pool(lambda e: e.memset(ones, 1.0), w=[onest])
        epsc, epst = self.salloc([128, 8], F32), Tok("eps")
        self.pool(lambda e: e.memset(epsc, EPS), w=[epst])
        for h in range(4):
            self.dve(lambda e, h=h: e.tensor_copy(out=m2x4[:, h * 128:(h + 1) * 128], in_=self.cst[:, C_M2:C_M2 + 128]), r=[self.cst_t], w=[m2t])
        maskc = self.cst[:, C_MC:C_MC + NB]
        for blk in range(T // NB):
            tok0 = blk * NB
            s1, s1t = f1.next()
            s2, s2t = f2.next()
            s3, s3t = f3.next()
            E, Et = E_r.next()
            qd, qdt = qd_r.next()
            kd, kdt = kd_r.next()
            gs, gst = gs_r.next()
            qf = []
            for h in range(4):
                b1, b1t = self.bank()
                self.mmB(b1[:, 0:NB], b1t, wB, wBt[0], h * 128, tok0, NB)
                self.mmB(b1[:, NB:2 * NB], b1t, wB, wBt[1], 512 + h * 128, tok0, NB)
                qf.append((b1, b1t))
                self.act(lambda e, h=h, b1=b1, s1=s1: e.activation(out=s1[:, h, :], in_=b1[:, NB:2 * NB], func=AF.Sigmoid), r=[b1t], w=[s1t])
            for h in range(4):
                self.dve(lambda e, h=h, s1=s1: e.tensor_scalar(out=s1[:, h, :], in0=s1[:, h, :], scalar1=sm[:, 16 + h:17 + h], scalar2=sm[:, 12 + h:13 + h],
                                                              op0=ALU.mult, op1=ALU.add), r=[s1t, smt], w=[s1t])
            self.act(lambda e, s1=s1, s2=s2: e.activation(out=s2, in_=s1, func=AF.Ln), r=[s1t], w=[s2t])
            for h in range(4):
                self.dve(lambda e, h=h, s2=s2, s3=s3: e.tensor_tensor_scan(out=s3[:, h, :], data0=maskc, data1=s2[:, h, :], initial=0.0, op0=ALU.mult, op1=ALU.add),
                         r=[s2t, self.cst_t], w=[s3t])
            self.pool(lambda e, s1=s1: e.tensor_scalar(out=s1, in0=s1, scalar1=-1.0, scalar2=1.0, op0=ALU.mult, op1=ALU.add), r=[s1t], w=[s1t])
            self.act(lambda e, s2=s2, s3=s3: e.activation(out=s2, in_=s3, func=AF.Exp, scale=-1.0), r=[s3t, s2t], w=[s2t])
            self.act(lambda e, s3=s3: e.activation(out=s3, in_=s3, func=AF.Exp), r=[s3t], w=[s3t])
            self.dve(lambda e, E=E, s3=s3: e.tensor_copy(out=E, in_=s3[:, :, 63:NB:64]), r=[s3t], w=[Et])
            for h in range(4):
                b1, b1t = qf[h]
                self.dve(lambda e, h=h, b1=b1, s3=s3, qd=qd: e.tensor_tensor(out=qd[:, h, :], in0=b1[:, 0:NB], in1=s3[:, h, :], op=ALU.mult), r=[b1t, s3t], w=[qdt])
            self.pool(lambda e, s1=s1, s2=s2, kd=kd: e.tensor_tensor(out=kd, in0=s1, in1=s2, op=ALU.mult), r=[s1t, s2t], w=[kdt])
            for h in range(0, 4, 2):
                b2, b2t = self.bank()
                self.mmB(b2[:, 0:NB], b2t, wB, wBt[3], 1536 + h * 128, tok0, NB)
                self.mmB(b2[:, NB:2 * NB], b2t, wB, wBt[3], 1536 + (h + 1) * 128, tok0, NB)
                self.act(lambda e, h=h, b2=b2, gs=gs: e.activation(out=gs[:, h:h + 2, :], in_=b2[:].rearrange("p (a t) -> p a t", a=2), func=AF.Silu), r=[b2t], w=[gst])
            vi, vit = vi_r.next()
            ktk, ktkt = kt_r.next()
            for il in range(2):
                b3, b3t = self.bank()
                self.mmA(b3[:], b3t, blk * 2 + il, wB, wBt[2], 1024, 512)
                self.act(lambda e, il=il, b3=b3, vi=vi: e.activation(out=vi[:, il, :], in_=b3[:], func=AF.Silu), r=[b3t], w=[vit])
                self.transpose_to(ktk[:, il, :].rearrange("p (h k) -> p h k", h=4), [kd[:, h, il * 128:(il + 1) * 128] for h in range(4)], [kdt], [ktkt])
            for il in range(2):
                i = blk * 2 + il
                tc0 = il * 128
                U = [self.bank(), self.bank()]
                for c in range(2):
                    for h in range(4):
                        self.mm(U[c][0][:, h * 128:(h + 1) * 128], ktk[c * 64:(c + 1) * 64, il, h * 128:(h + 1) * 128],
                                vi[c * 64:(c + 1) * 64, il, h * 128:(h + 1) * 128], True, True, [ktkt, vit], [U[c][1]])
                A, At = self.bank()
                for h in range(4):
                    self.mm(A[:, h * 128:(h + 1) * 128], kd[:, h, tc0:tc0 + 128], qd[:, h, tc0:tc0 + 128], True, True, [kdt, qdt], [At])
                am, amt = am_r.next()
                self.dve(lambda e, am=am, A=A: e.tensor_tensor(out=am, in0=A[:], in1=m2x4, op=ALU.mult), r=[At, m2t], w=[amt])
                Sbs = [(Sb, Sbt)]
                for c in range(2):
                    tmp, tmpt = tmp_r.next()
                    Uc, Uct = U[c]
                    self.dve(lambda e, tmp=tmp, Uc=Uc: e.tensor_tensor(out=tmp, in0=Uc[:], in1=S.rearrange("p h v -> p (h v)"), op=ALU.add), r=[Uct, St], w=[tmpt])
                    Sb, Sbt = Sb_r.next()
                    col = il * 2 + c
                    for h in range(4):
                        self.dve(lambda e, h=h, tmp=tmp, E=E, col=col: e.tensor_scalar(out=S[:, h, :], in0=tmp[:, h * 128:(h + 1) * 128], scalar1=E[:, h, col:col + 1],
                                                                                    scalar2=None, op0=ALU.mult), r=[tmpt, Et], w=[St])
                        self.act(lambda e, h=h, tmp=tmp, E=E, col=col, Sb=Sb: e.activation(out=Sb[:, h, :], in_=tmp[:, h * 128:(h + 1) * 128], func=AF.Copy,
                                                                                        scale=E[:, h, col:col + 1]), r=[tmpt, Et], w=[Sbt])
                    Sbs.append((Sb, Sbt))
                O, Ot = self.bank()
                for h in range(4):
                    self.mm(O[:, h * 128:(h + 1) * 128], vi[:, il, h * 128:(h + 1) * 128], am[:, h * 128:(h + 1) * 128], True, True, [vit, amt], [Ot])
                    for c in range(2):
                        Sc, Sct = Sbs[c]
                        self.mm(O[:, h * 128 + c * 64:h * 128 + (c + 1) * 64], Sc[:, h, :], qd[:, h, tc0 + c * 64:tc0 + (c + 1) * 64], False, True,
                                [Sct, qdt], [Ot], skip_group_check=True)
                sq, sqt = sq_r.next()
                self.act(lambda e, sq=sq, O=O: e.activation(out=sq, in_=O[:], func=AF.Square), r=[Ot], w=[sqt])
                Q, Qt = self.bank()
                self.mm(Q[:], ones, sq, True, True, [onest, sqt], [Qt])
                rr, rrt = r_r.next()
                self.act(lambda e, rr=rr, Q=Q: e.activation(out=rr, in_=Q[:], func=AF.Ln, bias=epsc[:, 0:1], scale=1.0 / 128.0), r=[Qt, epst], w=[rrt])
                self.act(lambda e, rr=rr: e.activation(out=rr, in_=rr, func=AF.Exp, scale=-0.5), r=[rrt], w=[rrt])
                t1, t1t = t1_r.next()
                self.dve(lambda e, t1=t1, O=O, rr=rr: e.tensor_tensor(out=t1, in0=O[:], in1=rr, op=ALU.mult), r=[Ot, rrt], w=[t1t])
                yb, ybt = yb_r.next()
                for h in range(4):
                    self.dve(lambda e, h=h, yb=yb, t1=t1, gs=gs, tc0=tc0: e.scalar_tensor_tensor(out=yb[:, h, :], in0=t1[:, h * 128:(h + 1) * 128], scalar=sm[:, 20 + h:21 + h],
                                                                                              in1=gs[:, h, tc0:tc0 + 128], op0=ALU.mult, op1=ALU.mult),
                             r=[t1t, gst, smt], w=[ybt])
                self.out_proj(i, lambda k, yb=yb: yb[:, k, :], 4, [ybt], woB, woBt, first=False)
                self.layer_norm(i, lnp)

    def stage_moba(self):
        self.nbank = 6
        for G in range(2):
            self.moba_group(G)
        self.nbank = 8

    def moba_group(self, G):
        d = self.d
        if True:
            self.begin_stage()
            lnp = self.load_ln(3) if G == 1 else None
            kT, kTt = self.salloc([128, 4, T], BF16), [[Tok("kT") for _ in range(4)] for _ in range(4)]
            va, vat = self.salloc([128, NT, 8, 65], BF16), [Tok("va") for _ in range(NT)]
            wq, wqt = self.salloc([128, KC, 512], BF16), Tok("wq")
            woG, woGt = self.salloc([128, 4, D], BF16), Tok("woG")
            wk_r = self.ring(2, [128, KC, 256], BF16, "wk")
            qz_r = self.ring(1, [128, 8, 512], BF16, "qz")
            R_r = self.ring(1, [128, 512], BF16, "R")
            Rc, Rct = self.salloc([128, 512], BF16), Tok("Rc")
            pT_r = self.ring(5, [128, 512], BF16, "pT")
            otok_r = Ring([(self.salloc([128, 4, 512], BF16), [Tok("otok") for _ in range(8)]) for _ in range(2)])
            oTb, oTbt = self.salloc([128, 4, 512], BF16), Tok("oTb")
            lcol, lcolt = self.salloc([128, 64], BF16), Tok("lcol")
            SBb, SBbt = self.salloc([128, 4, 128], BF16), Tok("SBb")
            SB_r = self.ring(4, [128, 128], BF16, "SB")
            km32, km32t = self.salloc([128, 4, 8], F32), Tok("km32")
            kmh, kmht = self.salloc([128, 4, 8], BF16), Tok("kmh")
            kml, kmlt = self.salloc([128, 4, 8], BF16), Tok("kml")
            aff_r = self.ring(2, [128, 8, 8], F32, "aff")
            cmp_, cmpt = self.salloc([128, 8, 8, 8], F32), Tok("cmp")
            cnt, cntt = self.salloc([128, 8, 8], F32), Tok("cnt")
            sm_r = self.ring(2, [128, 8], F32, "msm")
            self.dve(lambda e: e.tensor_copy(out=lcol, in_=self.cst[:, C_LCOL + G * 64:C_LCOL + (G + 1) * 64]), r=[self.cst_t], w=[lcolt])
            self.dve(lambda e: e.tensor_copy(out=SBb, in_=self.cst[:, C_SBB:C_SBB + 512].rearrange("p (j c) -> p j c", j=4)), r=[self.cst_t], w=[SBbt])
            self.transpose_to(Rc.rearrange("p (j t) -> p j t", j=4), [SBb[:, j, :] for j in range(4)], [SBbt], [Rct])
            if G == 0:
                self.dump(Rc[:, 0:128], Rct)
                self.dump(SBb[:, 0, :], SBbt)
            for s_ in range(qz_r.items.__len__()):
                qz0, qz0t = qz_r.items[s_]
                self.pool(lambda e, qz0=qz0: e.memset(qz0.rearrange("p h t -> p (h t)"), 0.0), w=[qz0t])
            self.pool(lambda e: e.memset(va.rearrange("p a b c -> p (a b c)"), 1.0), w=vat)
            self.wload_w(wq, d["oqkv"], G * 512, 512, wqt)
            self.wload(woG, d["owo"][G * 512:(G + 1) * 512, :].rearrange("(g p) n -> p g n", p=128), woGt)
            for c2 in range(2):
                wk, wkt = wk_r.next()
                self.wload_w(wk, d["oqkv"], 1024 + G * 512 + c2 * 256, 256, wkt, step=256)
                for pp in range(2):
                    p = c2 * 2 + pp
                    for tb in range(4):
                        ps, pst = self.bank()
                        self.mmB(ps[:], pst, wk, wkt, pp * 128, tb * 512, 512)
                        self.copy(kT[:, p, tb * 512:(tb + 1) * 512], ps[:], [pst], [kTt[p][tb]])
            for c2 in range(2):
                wk, wkt = wk_r.next()
                self.wload_w(wk, d["oqkv"], 2048 + G * 512 + c2 * 256, 256, wkt, step=256)
                for i in range(NT):
                    ps, pst = self.bank()
                    self.mmA(ps[:, 0:256], pst, i, wk, wkt, 0, 256)
                    self.copy(va[:, i, c2 * 4:(c2 + 1) * 4, 0:64], ps[:, 0:256].rearrange("p (h c) -> p h c", h=4), [pst], [vat[i]])
            for p in range(4):
                self.dve(lambda e, p=p: e.tensor_reduce(out=km32[:, p, :], in_=kT[:, p, :].rearrange("p (b t) -> p b t", b=8), axis=AX.X, op=ALU.add), r=kTt[p], w=[km32t])
            self.dve(lambda e: e.tensor_scalar(out=km32, in0=km32, scalar1=1.0 / 256.0, scalar2=None, op0=ALU.mult), r=[km32t], w=[km32t])
            self.dve(lambda e: e.tensor_copy(out=kmh, in_=km32), r=[km32t], w=[kmht])
            self.dve(lambda e: e.tensor_tensor(out=km32, in0=km32, in1=kmh, op=ALU.subtract), r=[km32t, kmht], w=[km32t])
            self.dve(lambda e: e.tensor_copy(out=kml, in_=km32), r=[km32t], w=[kmlt])

            for qc in range(4):
                qz, qzt = qz_r.next()
                for p in range(4):
                    ps, pst = self.bank()
                    self.mmB(ps[:], pst, wq, wqt, p * 128, qc * 512, 512)
                    self.act(lambda e, p=p, ps=ps, qz=qz: e.copy(out=qz[0:64, 2 * p, :], in_=ps[0:64, :]), r=[pst], w=[qzt])
                    self.dve(lambda e, p=p, ps=ps, qz=qz: e.tensor_copy(out=qz[64:128, 2 * p + 1, :], in_=ps[64:128, :]), r=[pst], w=[qzt])
                if qc < 2:
                    R, Rt = Rc, Rct
                else:
                    R, Rt = R_r.next()
                    sbs = []
                    for j in range(4):
                        qb = (qc * 4 + j) // 2
                        ab, abt = self.bank()
                        for hl in range(8):
                            self.mm(ab[:, hl * 8:(hl + 1) * 8], qz[:, hl, j * 128:(j + 1) * 128], kmh[:, hl // 2, :], True, False, [qzt, kmht], [abt])
                            self.mm(ab[:, hl * 8:(hl + 1) * 8], qz[:, hl, j * 128:(j + 1) * 128], kml[:, hl // 2, :], False, True, [qzt, kmlt], [abt])
                        aff, afft = aff_r.next()
                        self.dve(lambda e, aff=aff, ab=ab: e.tensor_copy(out=aff, in_=ab[:, 0:64].rearrange("p (h k) -> p h k", h=8)), r=[abt], w=[afft])
                        self.dve(lambda e, aff=aff, qb=qb: e.tensor_tensor(out=cmp_[:, :, 0:qb, 0:qb],
                                                                        in0=aff[:, :, 0:qb].unsqueeze(2).to_broadcast([128, 8, qb, qb]),
                                                                        in1=aff[:, :, 0:qb].unsqueeze(3).to_broadcast([128, 8, qb, qb]), op=ALU.is_gt),
                                 r=[afft], w=[cmpt])
                        self.dve(lambda e, qb=qb: e.tensor_reduce(out=cnt[:, :, 0:qb], in_=cmp_[:, :, 0:qb, 0:qb], axis=AX.X, op=ALU.add), r=[cmpt], w=[cntt])
                        SB, SBt = SB_r.next()
                        self.pool(lambda e, SB=SB, j=j: e.tensor_copy(out=SB, in_=SBb[:, j, :]), r=[SBbt], w=[SBt])
                        self.dve(lambda e, SB=SB, qb=qb: e.tensor_scalar(out=SB.rearrange("p (h s) -> p h s", h=8)[:, :, 0:qb], in0=cnt[:, :, 0:qb],
                                                                      scalar1=3.0, scalar2=-32768.0, op0=ALU.is_ge, op1=ALU.mult), r=[cntt, SBt], w=[SBt])
                        sbs.append((SB, SBt))
                    self.transpose_to(R.rearrange("p (j t) -> p j t", j=4), [s[0] for s in sbs], [s[1] for s in sbs], [Rt])
                nkt = 4 * qc + 4
                otok, otokt = otok_r.next()
                for hl in range(8):
                    h = G * 8 + hl
                    O, Ot = self.obank()
                    first = True
                    for kt in range(nkt):
                        j0 = max(0, kt - 4 * qc)
                        c0 = j0 * 128
                        kb = kt // 2
                        S_, S_t = self.bank()
                        self.mm(S_[:, c0:512], kT[:, hl // 2, kt * 128:(kt + 1) * 128], qz[:, hl, c0:512], True, False, [kTt[hl // 2][kt // 4], qzt], [S_t])
                        self.mm(S_[:, c0:512], lcol[:, hl * 8 + kb:hl * 8 + kb + 1].to_broadcast([128, 128]), R[:, c0:512], False, True, [lcolt, Rt], [S_t])
                        if kt >= 4 * qc:
                            self.dve(lambda e, S_=S_, c0=c0: e.tensor_tensor(out=S_[:, c0:c0 + 128], in0=S_[:, c0:c0 + 128], in1=self.cst[:, C_NTRI:C_NTRI + 128], op=ALU.add),
                                     r=[S_t, self.cst_t], w=[S_t])
                        pT, pTt = pT_r.next()
                        bcol = C_ALB + h * 16 + (kt - 4 * qc) + 12
                        self.act(lambda e, pT=pT, S_=S_, c0=c0, bcol=bcol: e.activation(out=pT[:, c0:512], in_=S_[:, c0:512], func=AF.Exp,
                                                                                     bias=self.cst[:, bcol:bcol + 1], scale=0.125),
                                 r=[S_t, self.cst_t], w=[pTt])
                        if G == 0 and qc == 0 and hl == 0 and kt == 0:
                            self.dump(S_[:, 0:512], S_t, psum=True)
                            self.dump(pT, pTt)
                            self.dump(va[:, 0, 0, :], vat[0])
                            self.dump(qz[:, 0, :], qzt)
                            self.dump(kT[:, 0, 0:128], kTt[0][0])
                            self.dump(R[:, 0:128], Rt)
                            self.dump(va[:, 0, :, :].rearrange("p h c -> p (h c)"), vat[0])
                        for j in range(j0, 4):
                            self.mm(O[:, j * 65:(j + 1) * 65], pT[:, j * 128:(j + 1) * 128], va[:, kt, hl, :], first, kt == 4 * qc + j,
                                    [pTt, vat[kt]], [Ot], skip_group_check=True)
                            first = False
                    if G == 0 and qc == 0 and hl == 0:
                        self.dump(O[:, 0:260], Ot, psum=True)
                    sm, smt = sm_r.next()
                    self.dve(lambda e, sm=sm, O=O: e.reciprocal(out=sm[:, 0:4], in_=O[:, 0:260].rearrange("p (j c) -> p j c", j=4)[:, :, 64]), r=[Ot], w=[smt])
                    for j in range(4):
                        if j % 2:
                            self.dve(lambda e, sm=sm, O=O, j=j, hl=hl, otok=otok: e.tensor_scalar(out=otok[:, j, hl * 64:(hl + 1) * 64], in0=O[:, j * 65:j * 65 + 64],
                                                                                    scalar1=sm[:, j:j + 1], scalar2=None, op0=ALU.mult), r=[Ot, smt], w=[otokt[hl]])
                        else:
                            self.act(lambda e, sm=sm, O=O, j=j, hl=hl, otok=otok: e.activation(out=otok[:, j, hl * 64:(hl + 1) * 64], in_=O[:, j * 65:j * 65 + 64],
                                                                                 func=AF.Copy, scale=sm[:, j:j + 1]), r=[Ot, smt], w=[otokt[hl]])
                if G == 0 and qc == 0:
                    self.dump(otok.rearrange("p j c -> p (j c)"), otokt[0])
                for j in range(4):
                    i = qc * 4 + j
                    self.transpose_to(oTb[:, :, j * 128:(j + 1) * 128], [otok[:, j, c * 128:(c + 1) * 128] for c in range(4)], otokt, [oTbt])
                    self.out_proj(i, lambda k, j=j: oTb[:, k, j * 128:(j + 1) * 128], 4, [oTbt], woG, woGt, first=(G == 0))
                    if G == 1:
                        self.layer_norm(i, lnp)


def build_program(stages):
    nc = bass.Bass("TRN2", target_bir_lowering=False)
    k = K(nc, stages)
    k.build()
    return nc


FULL_STAGES = [("mix0",), ("xattn", 0), ("ffn", 0), ("moba",), ("xattn", 1), ("ffn", 1)]


def make_in_maps(inputs, ncores=NCORES):
    f = lambda a: np.ascontiguousarray(np.asarray(a, dtype=np.float32))
    x = f(inputs["x"])
    mem = f(inputs["mem"])
    shared = dict(
        ln_g=f(inputs["ln_g"]).reshape(6, D),
        ln_b=f(inputs["ln_b"]).reshape(6, D),
        x_wq=f(inputs["x_wq"]), x_wkv=f(inputs["x_wkv"]), x_wo=f(inputs["x_wo"]),
        ffn_w_in=f(inputs["ffn_w_in"]), ffn_w_out=f(inputs["ffn_w_out"]),
        ev_w_in=f(inputs["ev_w_in"])[0], ev_w_out=f(inputs["ev_w_out"])[0],
        a_wsT=f(np.transpose(np.asarray(inputs["a_ws"])[0], (2, 0, 1))),
        a_bs=f(inputs["a_bs"]).reshape(1, 512),
        a_ln_g=f(inputs["a_ln_g"]).reshape(1, 512),
        a_ln_b=f(inputs["a_ln_b"]).reshape(1, 512),
        b_norm_gT=f(np.asarray(inputs["b_norm_g"]).reshape(4, 128).T),
        lb_logitsT=f(np.transpose(np.asarray(inputs["hgrn_lb_logits"]).reshape(2, 4, 128), (2, 0, 1))),
        od_w_qkv=f(inputs["od_w_qkv"])[0], od_w_out=f(inputs["od_w_out"])[0],
        consts=make_consts(),
    )
    maps = []
    for c in range(ncores):
        m = dict(shared)
        m["x"] = x[c]
        m["mem"] = mem[c]
        maps.append(m)
    return maps


def kernel(**inputs):
    nc = build_program(FULL_STAGES)
    in_maps = make_in_maps(inputs)
    res = run_bass_kernel_spmd(nc, in_maps, core_ids=list(range(NCORES)))
    return np.stack([np.asarray(r["out"], dtype=np.float32) for r in res.results], axis=0)
```
